# Optimizing a Trainium2 kernel written in Bass

```python
import jax
import jax.numpy as jnp
from jax import lax
import numpy as np

D_MODEL = 2048
BATCH = 8
SEQ = 2048
DEPTH = 1

HEAD_DIM = 128
NSA_WIDTH = D_MODEL // 2
NSA_HEADS = NSA_WIDTH // HEAD_DIM
NSA_KV_HEADS = max(1, NSA_HEADS // 4)
NSA_GROUP = NSA_HEADS // NSA_KV_HEADS
CMP_LEN = 32
CMP_STRIDE = 16
SLC_LEN = 64
N_SELECT = 16
FORCED_SCORE = 1e4
WINDOW = 512
WIN_BLOCK = 128
SLC_Q_CHUNK = 32
RNN_WIDTH = D_MODEL - NSA_WIDTH
RNN_BLOCKS = RNN_WIDTH // HEAD_DIM
RNN_BLOCK_DIM = RNN_WIDTH // RNN_BLOCKS
CONV_WIDTH = 4
LRU_C = 8.0
PEER_HEADS = 8
PEER_KEYS = 128
PEER_EXPERTS = PEER_KEYS * PEER_KEYS
PEER_TOPK = 16
PEER_DQ = 256
PEER_TOK_CHUNK = 128
ROPE_THETA = 10000.0
EPS = 1e-6

Q_COLS = NSA_HEADS * HEAD_DIM
KV_COLS = NSA_KV_HEADS * HEAD_DIM
GATE_COLS = 3 * NSA_HEADS
N_IN = Q_COLS + 6 * KV_COLS + GATE_COLS + 2 * RNN_WIDTH

kernel_name = 'hybrid_nsa_rglru_peer_adaln'


def rms_norm(x, g):
    x32 = x.astype(jnp.float32)
    y = x32 * lax.rsqrt(jnp.mean(x32 * x32, axis=-1, keepdims=True) + EPS)
    return (y * g.astype(jnp.float32)).astype(x.dtype)


def rope(x, pos):
    half = x.shape[-1] // 2
    freqs = ROPE_THETA ** (-jnp.arange(half, dtype=jnp.float32) / half)
    ang = pos.astype(jnp.float32)[:, None] * freqs[None, :]
    cos = jnp.cos(ang)[:, None, :]
    sin = jnp.sin(ang)[:, None, :]
    x32 = x.astype(jnp.float32)
    x1, x2 = x32[..., :half], x32[..., half:]
    return jnp.concatenate([x1 * cos - x2 * sin, x2 * cos + x1 * sin], axis=-1).astype(x.dtype)


def masked_softmax(s, mask):
    s = jnp.where(mask, s.astype(jnp.float32), -jnp.inf)
    m = jnp.max(s, axis=-1, keepdims=True)
    m = jnp.where(jnp.isfinite(m), m, 0.0)
    p = jnp.exp(s - m)
    d = jnp.sum(p, axis=-1, keepdims=True)
    return p / jnp.where(d > 0, d, 1.0)


def compress_blocks(k, pe, w):
    B, S, G, dh = k.shape
    n_cmp = (S - CMP_LEN) // CMP_STRIDE + 1
    idx = (jnp.arange(n_cmp) * CMP_STRIDE)[:, None] + jnp.arange(CMP_LEN)[None, :]
    blk = k[:, idx] + pe[None, None, :, None, :]
    blk = jnp.transpose(blk, (0, 3, 1, 2, 4)).reshape(B, G, n_cmp, CMP_LEN * dh)
    return blk @ w


def nsa_attention(q, kc, vc, ks, vs, kw, vw, gates):
    B, G, S, R, dh = q.shape
    scale = dh ** -0.5
    t = jnp.arange(S)
    n_cmp = kc.shape[2]
    cmp_start = jnp.arange(n_cmp) * CMP_STRIDE
    mask_c = (cmp_start + CMP_LEN - 1)[None, :] <= t[:, None]
    p_c = masked_softmax(jnp.einsum('bgsrd,bgcd->bgsrc', q, kc) * scale, mask_c[None, None, :, None, :])
    o_cmp = jnp.einsum('bgsrc,bgcd->bgsrd', p_c.astype(vc.dtype), vc)
    n_slc = S // SLC_LEN
    slc_start = jnp.arange(n_slc) * SLC_LEN
    overlap = jnp.maximum(
        jnp.minimum(cmp_start[:, None] + CMP_LEN, slc_start[None, :] + SLC_LEN)
        - jnp.maximum(cmp_start[:, None], slc_start[None, :]), 0).astype(jnp.float32) / CMP_LEN
    imp = jnp.einsum('bgsrc,cj->bgsj', p_c, overlap)
    cur = t // SLC_LEN
    jb = jnp.arange(n_slc)
    forced = (jb[None, :] == 0) | (jb[None, :] == cur[:, None]) | (jb[None, :] == cur[:, None] - 1)
    valid = slc_start[None, :] <= t[:, None]
    imp = jnp.where(forced, FORCED_SCORE, jnp.where(valid, imp, -FORCED_SCORE))
    n_sel = min(N_SELECT, n_slc)
    _, sel = lax.top_k(imp, n_sel)
    kb = ks.reshape(B, G, n_slc, SLC_LEN, dh)
    vb = vs.reshape(B, G, n_slc, SLC_LEN, dh)
    nq = S // SLC_Q_CHUNK
    q_chunks = jnp.moveaxis(q.reshape(B, G, nq, SLC_Q_CHUNK, R, dh), 2, 0)
    sel_chunks = jnp.moveaxis(sel.reshape(B, G, nq, SLC_Q_CHUNK, n_sel), 2, 0)
    t_chunks = t.reshape(nq, SLC_Q_CHUNK)
    bi = jnp.arange(B)[:, None, None, None]
    gi = jnp.arange(G)[None, :, None, None]
    n_key = n_sel * SLC_LEN

    def slc_chunk(args):
        q_i, sel_i, t_i = args
        kg = kb[bi, gi, sel_i].reshape(B, G, SLC_Q_CHUNK, n_key, dh)
        vg = vb[bi, gi, sel_i].reshape(B, G, SLC_Q_CHUNK, n_key, dh)
        kpos = (sel_i[..., None] * SLC_LEN + jnp.arange(SLC_LEN)).reshape(B, G, SLC_Q_CHUNK, n_key)
        mask = (kpos <= t_i[None, None, :, None])[:, :, :, None, :]
        p = masked_softmax(jnp.einsum('bgqrd,bgqkd->bgqrk', q_i, kg) * scale, mask)
        return jnp.einsum('bgqrk,bgqkd->bgqrd', p.astype(vg.dtype), vg)

    o_slc = lax.map(slc_chunk, (q_chunks, sel_chunks, t_chunks))
    o_slc = jnp.moveaxis(o_slc, 0, 2).reshape(B, G, S, R, dh)
    nb = S // WIN_BLOCK
    npre = WINDOW // WIN_BLOCK
    pad = ((0, 0), (0, 0), (WINDOW, 0), (0, 0))
    kp = jnp.pad(kw, pad).reshape(B, G, nb + npre, WIN_BLOCK, dh)
    vp = jnp.pad(vw, pad).reshape(B, G, nb + npre, WIN_BLOCK, dh)
    kband = jnp.concatenate([kp[:, :, i:i + nb] for i in range(npre + 1)], axis=3)
    vband = jnp.concatenate([vp[:, :, i:i + nb] for i in range(npre + 1)], axis=3)
    qw = q.reshape(B, G, nb, WIN_BLOCK, R, dh)
    qi = jnp.arange(WIN_BLOCK)
    kj = jnp.arange((npre + 1) * WIN_BLOCK)
    rel = kj[None, :] - WINDOW - qi[:, None]
    kabs = (jnp.arange(nb) * WIN_BLOCK)[:, None] + kj[None, :] - WINDOW
    mask_w = ((rel <= 0) & (rel > -WINDOW))[None, :, :] & (kabs >= 0)[:, None, :]
    p_w = masked_softmax(jnp.einsum('bgnqrd,bgnkd->bgnqrk', qw, kband) * scale,
                         mask_w[None, None, :, :, None, :])
    o_win = jnp.einsum('bgnqrk,bgnkd->bgnqrd', p_w.astype(vband.dtype), vband).reshape(B, G, S, R, dh)
    return gates[..., 0:1] * o_cmp + gates[..., 1:2] * o_slc + gates[..., 2:3] * o_win


def rglru_branch(xr, xg, conv_w, conv_b, wa, ba, wi, bi, lam):
    B, S, W = xr.shape
    xp = jnp.pad(xr, ((0, 0), (CONV_WIDTH - 1, 0), (0, 0)))
    u = conv_b + sum(xp[:, i:i + S] * conv_w[i] for i in range(CONV_WIDTH))
    ub = u.reshape(B, S, RNN_BLOCKS, RNN_BLOCK_DIM)
    r = jax.nn.sigmoid((jnp.einsum('bsnd,nde->bsne', ub, wa).reshape(B, S, W) + ba).astype(jnp.float32))
    ig = jax.nn.sigmoid((jnp.einsum('bsnd,nde->bsne', ub, wi).reshape(B, S, W) + bi).astype(jnp.float32))
    log_a = -LRU_C * r * jax.nn.softplus(-lam.astype(jnp.float32))
    a = jnp.exp(log_a)
    b = jnp.sqrt(-jnp.expm1(2.0 * log_a)) * ig * u.astype(jnp.float32)

    def combine(left, right):
        a1, b1 = left
        a2, b2 = right
        return a1 * a2, a2 * b1 + b2

    _, hs = lax.associative_scan(combine, (a, b), axis=1)
    return (jax.nn.gelu(xg.astype(jnp.float32), approximate=False) * hs).astype(xr.dtype)


def hybrid_mixer(h, w_in, w_out, q_norm_g, k_norm_g, cmp_pe_k, cmp_pe_v, cmp_w_k, cmp_w_v, gate_b,
                 conv_w, conv_b, lru_wa, lru_ba, lru_wi, lru_bi, lru_lam, out_g_attn, out_g_rnn):
    B, S, _ = h.shape
    G, R, dh = NSA_KV_HEADS, NSA_GROUP, HEAD_DIM
    z = h @ w_in
    cuts = [int(v) for v in np.cumsum([Q_COLS] + [KV_COLS] * 6 + [GATE_COLS, RNN_WIDTH])]
    q, kc, vc, ks, vs, kw, vw, gl, xr, xg = jnp.split(z, cuts, axis=-1)
    pos = jnp.arange(S)
    q = rope(rms_norm(q.reshape(B, S, NSA_HEADS, dh), q_norm_g), pos)
    kc = rms_norm(compress_blocks(rope(kc.reshape(B, S, G, dh), pos), cmp_pe_k, cmp_w_k), k_norm_g[0])
    vc = compress_blocks(vc.reshape(B, S, G, dh), cmp_pe_v, cmp_w_v)
    ks = rope(rms_norm(ks.reshape(B, S, G, dh), k_norm_g[1]), pos)
    kw = rope(rms_norm(kw.reshape(B, S, G, dh), k_norm_g[2]), pos)
    qg = q.reshape(B, S, G, R, dh).transpose(0, 2, 1, 3, 4)
    gates = jax.nn.sigmoid(gl + gate_b).reshape(B, S, G, R, 3).transpose(0, 2, 1, 3, 4)
    o_attn = nsa_attention(qg, kc, vc,
                           jnp.swapaxes(ks, 1, 2), jnp.swapaxes(vs.reshape(B, S, G, dh), 1, 2),
                           jnp.swapaxes(kw, 1, 2), jnp.swapaxes(vw.reshape(B, S, G, dh), 1, 2), gates)
    o_attn = o_attn.transpose(0, 2, 1, 3, 4).reshape(B, S, NSA_WIDTH)
    o_rnn = rglru_branch(xr, xg, conv_w, conv_b, lru_wa, lru_ba, lru_wi, lru_bi, lru_lam)
    y = jnp.concatenate([rms_norm(o_attn, out_g_attn), rms_norm(o_rnn, out_g_rnn)], axis=-1)
    return y @ w_out


def peer_ffn(h, wq, keys, down, up):
    B, S, D = h.shape
    T = B * S
    hf = h.reshape(T, D)
    q = (hf @ wq).reshape(T, PEER_HEADS, 2, PEER_DQ // 2)
    s = jnp.einsum('thpd,hpkd->thpk', q, keys).astype(jnp.float32)
    v1, i1 = lax.top_k(s[:, :, 0], PEER_TOPK)
    v2, i2 = lax.top_k(s[:, :, 1], PEER_TOPK)
    cand = (v1[..., :, None] + v2[..., None, :]).reshape(T, PEER_HEADS, PEER_TOPK * PEER_TOPK)
    cidx = (i1[..., :, None] * PEER_KEYS + i2[..., None, :]).reshape(T, PEER_HEADS, PEER_TOPK * PEER_TOPK)
    vals, pick = lax.top_k(cand, PEER_TOPK)
    eidx = jnp.take_along_axis(cidx, pick, axis=-1)
    g = jax.nn.softmax(vals, axis=-1)
    nc = T // PEER_TOK_CHUNK

    def chunk(args):
        x_c, e_c, g_c = args
        act = jax.nn.gelu(jnp.einsum('td,thkd->thk', x_c, down[e_c]).astype(jnp.float32),
                          approximate=False) * g_c
        return jnp.einsum('thk,thkd->td', act.astype(up.dtype), up[e_c])

    out = lax.map(chunk, (hf.reshape(nc, PEER_TOK_CHUNK, D),
                          eidx.reshape(nc, PEER_TOK_CHUNK, PEER_HEADS, PEER_TOPK),
                          g.reshape(nc, PEER_TOK_CHUNK, PEER_HEADS, PEER_TOPK)))
    return out.reshape(B, S, D).astype(h.dtype)


def setup_inputs(seed: int = 0) -> dict:
    key = jax.random.key(seed)
    ks = jax.random.split(key, 32)
    f32 = jnp.float32
    L = DEPTH

    def nrm(k, shape, s):
        return jax.random.normal(k, shape, f32) * s

    u = jax.random.uniform(ks[26], (L, RNN_WIDTH), f32, 0.9, 0.999)
    return {
        'x': nrm(ks[0], (BATCH, SEQ, D_MODEL), 1.0),
        'c': nrm(ks[1], (BATCH, D_MODEL), 1.0),
        'ada_w': nrm(ks[2], (L, D_MODEL, 6 * D_MODEL), 0.5 * D_MODEL ** -0.5),
        'ada_b': nrm(ks[3], (L, 6 * D_MODEL), 0.02),
        'norm_mix_g': 1.0 + nrm(ks[4], (L, D_MODEL), 0.02),
        'norm_ffn_g': 1.0 + nrm(ks[5], (L, D_MODEL), 0.02),
        'w_in': nrm(ks[6], (L, D_MODEL, N_IN), D_MODEL ** -0.5),
        'w_out': nrm(ks[7], (L, NSA_WIDTH + RNN_WIDTH, D_MODEL), (NSA_WIDTH + RNN_WIDTH) ** -0.5),
        'q_norm_g': 1.0 + nrm(ks[8], (L, HEAD_DIM), 0.02),
        'k_norm_g': 1.0 + nrm(ks[9], (L, 3, HEAD_DIM), 0.02),
        'cmp_pe_k': nrm(ks[10], (L, CMP_LEN, HEAD_DIM), 0.02),
        'cmp_pe_v': nrm(ks[11], (L, CMP_LEN, HEAD_DIM), 0.02),
        'cmp_w_k': nrm(ks[12], (L, CMP_LEN * HEAD_DIM, HEAD_DIM), (CMP_LEN * HEAD_DIM) ** -0.5),
        'cmp_w_v': nrm(ks[13], (L, CMP_LEN * HEAD_DIM, HEAD_DIM), (CMP_LEN * HEAD_DIM) ** -0.5),
        'gate_b': nrm(ks[14], (L, GATE_COLS), 0.1),
        'conv_w': nrm(ks[15], (L, CONV_WIDTH, RNN_WIDTH), CONV_WIDTH ** -0.5),
        'conv_b': nrm(ks[16], (L, RNN_WIDTH), 0.02),
        'lru_wa': nrm(ks[17], (L, RNN_BLOCKS, RNN_BLOCK_DIM, RNN_BLOCK_DIM), RNN_BLOCK_DIM ** -0.5),
        'lru_ba': nrm(ks[18], (L, RNN_WIDTH), 0.1),
        'lru_wi': nrm(ks[19], (L, RNN_BLOCKS, RNN_BLOCK_DIM, RNN_BLOCK_DIM), RNN_BLOCK_DIM ** -0.5),
        'lru_bi': nrm(ks[20], (L, RNN_WIDTH), 0.1),
        'lru_lam': jnp.log(u) - jnp.log1p(-u),
        'out_g_attn': 1.0 + nrm(ks[21], (L, NSA_WIDTH), 0.02),
        'out_g_rnn': 1.0 + nrm(ks[22], (L, RNN_WIDTH), 0.02),
        'peer_wq': nrm(ks[23], (L, D_MODEL, PEER_HEADS * PEER_DQ), D_MODEL ** -0.5),
        'peer_keys': nrm(ks[24], (L, PEER_HEADS, 2, PEER_KEYS, PEER_DQ // 2), (PEER_DQ // 2) ** -0.5),
        'peer_down': nrm(ks[25], (L, PEER_EXPERTS, D_MODEL), D_MODEL ** -0.5),
        'peer_up': nrm(ks[27], (L, PEER_EXPERTS, D_MODEL), PEER_HEADS ** -0.5),
    }


def reference(x, c, ada_w, ada_b, norm_mix_g, norm_ffn_g, w_in, w_out, q_norm_g, k_norm_g,
              cmp_pe_k, cmp_pe_v, cmp_w_k, cmp_w_v, gate_b, conv_w, conv_b, lru_wa, lru_ba,
              lru_wi, lru_bi, lru_lam, out_g_attn, out_g_rnn, peer_wq, peer_keys, peer_down, peer_up):
    for l in range(DEPTH):
        mod = jax.nn.silu(c) @ ada_w[l] + ada_b[l]
        sh1, sc1, gt1, sh2, sc2, gt2 = jnp.split(mod, 6, axis=-1)
        h = rms_norm(x, norm_mix_g[l]) * (1 + sc1[:, None]) + sh1[:, None]
        x = x + gt1[:, None] * hybrid_mixer(
            h, w_in[l], w_out[l], q_norm_g[l], k_norm_g[l], cmp_pe_k[l], cmp_pe_v[l], cmp_w_k[l],
            cmp_w_v[l], gate_b[l], conv_w[l], conv_b[l], lru_wa[l], lru_ba[l], lru_wi[l], lru_bi[l],
            lru_lam[l], out_g_attn[l], out_g_rnn[l])
        h = rms_norm(x, norm_ffn_g[l]) * (1 + sc2[:, None]) + sh2[:, None]
        x = x + gt2[:, None] * peer_ffn(h, peer_wq[l], peer_keys[l], peer_down[l], peer_up[l])
    return x
```

```python
import numpy as np
from contextlib import ExitStack
import concourse.bass as bass
import concourse.mybir as mybir
from concourse.bass_utils import run_bass_kernel_spmd

F32 = mybir.dt.float32
BF16 = mybir.dt.bfloat16
AF = mybir.ActivationFunctionType
ALU = mybir.AluOpType
AX = mybir.AxisListType

D = 2048
S = 2048
NT = 16
NIN = 4632
C_Q, C_KC, C_VC, C_KS, C_VS, C_KW, C_VW, C_GL, C_XR, C_XG = 0, 1024, 1280, 1536, 1792, 2048, 2304, 2560, 2584, 3608
EPS = 1e-6
NCMP = 127
SCALE = 128 ** -0.5


class Res:
    __slots__ = ("name", "w", "r")

    def __init__(self, name=""):
        self.name = name
        self.w = None
        self.r = []


class Op:
    __slots__ = ("eng", "fn", "deps", "flag", "cval", "dma", "dsem", "dval")

    def __init__(self, eng, fn, dma=False):
        self.eng = eng
        self.fn = fn
        self.deps = []
        self.flag = False
        self.cval = 0
        self.dma = dma
        self.dsem = None
        self.dval = 0


class Sched:
    ENGS = ("pe", "act", "dve", "pool", "sp")

    def __init__(self, nc, n_dma_sems=16):
        self.nc = nc
        self.q = {e: [] for e in self.ENGS}
        self.n_dma_sems = n_dma_sems
        self.dma_rr = 0
        self.dma_last = [None] * n_dma_sems
        self.dma_cnt = [0] * n_dma_sems
        self.pending = {e: [] for e in self.ENGS}

    def _add(self, eng, fn, reads, writes, dma=False):
        op = Op(eng, fn, dma)
        deps = list(self.pending[eng])
        self.pending[eng] = []
        for r in reads:
            if r.w is not None:
                deps.append(r.w)
        for w in writes:
            if w.w is not None:
                deps.append(w.w)
            deps.extend(w.r)
        for r in reads:
            r.r.append(op)
        for w in writes:
            w.w = op
            w.r = []
        if dma:
            s = self.dma_rr
            self.dma_rr = (self.dma_rr + 1) % self.n_dma_sems
            prev = self.dma_last[s]
            if prev is not None:
                deps.append(prev)
            self.dma_last[s] = op
            self.dma_cnt[s] += 1
            op.dsem = s
            op.dval = 16 * self.dma_cnt[s]
        seen = set()
        for d in deps:
            if d is op or id(d) in seen:
                continue
            if (not d.dma) and d.eng == eng and eng == "pe":
                continue
            seen.add(id(d))
            op.deps.append(d)
            d.flag = True
        self.q[eng].append(op)
        return op

    def op(self, eng, fn, reads=(), writes=()):
        return self._add(eng, fn, list(reads), list(writes))

    def dma(self, eng, out, in_, reads=(), writes=()):
        return self._add(eng, lambda e: e.dma_start(out=out, in_=in_), list(reads), list(writes), dma=True)

    def barrier(self):
        lasts = []
        for e in self.ENGS:
            for op in reversed(self.q[e]):
                if not op.dma:
                    lasts.append(op)
                    break
        for s in range(self.n_dma_sems):
            if self.dma_last[s] is not None:
                lasts.append(self.dma_last[s])
        for e in self.ENGS:
            self.pending[e] = list(lasts)

    def emit(self):
        nc = self.nc
        with ExitStack() as st:
            esem = {e: st.enter_context(nc.semaphore(f"s_{e}")) for e in self.ENGS}
            dsem = [st.enter_context(nc.semaphore(f"d_{i}")) for i in range(self.n_dma_sems)]
            for e in self.ENGS:
                c = 0
                for op in self.q[e]:
                    if op.dma:
                        continue
                    if op.flag:
                        c += 1
                        op.cval = c
            block = st.enter_context(nc.Block())

            def run(ename, eobj):
                waited = {}
                for op in self.q[ename]:
                    need = {}
                    for d in op.deps:
                        if d.dma:
                            key, val = ("d", d.dsem), d.dval
                        else:
                            key, val = ("e", d.eng), d.cval
                        if val > need.get(key, 0):
                            need[key] = val
                    for key, val in need.items():
                        if waited.get(key, 0) >= val:
                            continue
                        waited[key] = val
                        sem = dsem[key[1]] if key[0] == "d" else esem[key[1]]
                        eobj.wait_ge(sem, val)
                    ins = op.fn(eobj)
                    if op.dma:
                        ins.then_inc(dsem[op.dsem], 16)
                    elif op.flag:
                        ins.then_inc(esem[ename], 1)
                if ename == "sp":
                    for s in range(self.n_dma_sems):
                        if self.dma_cnt[s] > 0:
                            eobj.wait_ge(dsem[s], 16 * self.dma_cnt[s])

            block.tensor(lambda e: run("pe", e))
            block.scalar(lambda e: run("act", e))
            block.vector(lambda e: run("dve", e))
            block.gpsimd(lambda e: run("pool", e))
            block.sync(lambda e: run("sp", e))


class T:
    __slots__ = ("t", "r")

    def __init__(self, t, name=""):
        self.t = t
        self.r = Res(name)

    def __getitem__(self, k):
        return self.t[k]


class K:
    def __init__(self, nc):
        self.nc = nc
        self.S = Sched(nc)
        self.uid = 0

    def sb(self, st, shape, dt, name=None):
        self.uid += 1
        n = f"{name or 't'}_{self.uid}"
        return T(st.enter_context(self.nc.sbuf_tensor(n, list(shape), dt)), n)

    def ps(self, st, shape, dt=F32, name=None):
        self.uid += 1
        n = f"{name or 'p'}_{self.uid}"
        return T(st.enter_context(self.nc.psum_tensor(n, list(shape), dt)), n)

    @staticmethod
    def _rs(xs):
        return [x.r if isinstance(x, T) else x for x in xs]

    def mm(self, out, lhsT, rhs, start, stop, R, W):
        self.S.op("pe", lambda e: e.matmul(out, lhsT, rhs, start=start, stop=stop), self._rs(R), self._rs(W))

    def act(self, out, in_, func, R, W, bias=None, scale=None, accum_out=None, eng="act"):
        kw = {}
        if bias is not None:
            kw["bias"] = bias
        if scale is not None:
            kw["scale"] = scale
        if accum_out is not None:
            kw["accum_out"] = accum_out
        self.S.op(eng, lambda e: e.activation(out, in_, func, **kw), self._rs(R), self._rs(W))

    def tt(self, out, in0, in1, op, R, W, eng="dve"):
        self.S.op(eng, lambda e: e.tensor_tensor(out, in0, in1, op), self._rs(R), self._rs(W))

    def ts(self, out, in0, s1, s2, op0, op1, R, W, eng="dve", accum_out=None):
        if accum_out is None:
            if op1 is None:
                self.S.op(eng, lambda e: e.tensor_scalar(out, in0, s1, None, op0), self._rs(R), self._rs(W))
            else:
                self.S.op(eng, lambda e: e.tensor_scalar(out, in0, s1, s2, op0, op1), self._rs(R), self._rs(W))
        else:
            self.S.op(eng, lambda e: e.tensor_scalar(out, in0, s1, s2, op0, op1, accum_out=accum_out),
                      self._rs(R), self._rs(W))

    def stt(self, out, in0, scalar, in1, op0, op1, R, W, eng="dve"):
        self.S.op(eng, lambda e: e.scalar_tensor_tensor(out, in0, scalar, in1, op0, op1), self._rs(R), self._rs(W))

    def copy(self, out, in_, R, W, eng="dve"):
        if eng == "act":
            self.S.op("act", lambda e: e.copy(out, in_), self._rs(R), self._rs(W))
        else:
            self.S.op(eng, lambda e: e.tensor_copy(out, in_), self._rs(R), self._rs(W))

    def memset(self, ap, val, W, eng="pool"):
        self.S.op(eng, lambda e: e.memset(ap, val), [], self._rs(W))

    def recip(self, out, in_, R, W):
        self.S.op("dve", lambda e: e.reciprocal(out, in_), self._rs(R), self._rs(W))

    def dma(self, out, in_, R, W, eng="sp"):
        self.S.dma(eng, out, in_, self._rs(R), self._rs(W))

    def fn(self, eng, f, R, W):
        self.S.op(eng, f, self._rs(R), self._rs(W))

    def barrier(self):
        self.S.barrier()


IN_SPECS = [
    ("xT", [D, S], F32), ("c_col", [128, 16], F32), ("ada_w", [D, 6 * D], F32), ("ada_bT", [128, 96], F32),
    ("g1T", [128, 16], F32), ("g2T", [128, 16], F32), ("w_in", [D, NIN], F32), ("w_out", [D, D], F32),
    ("wq", [D, D], F32), ("qg", [128, 1], F32), ("kgT", [128, 3], F32), ("kg0b", [128, 128], F32),
    ("pekT", [128, 32], F32), ("pevT", [128, 32], F32), ("cwk", [4096, 128], F32), ("cwv", [4096, 128], F32),
    ("gateb", [128, 24], F32), ("convw", [128, 32], F32), ("convb", [128, 8], F32), ("lba", [128, 8], F32),
    ("lbi", [128, 8], F32), ("lam", [128, 8], F32), ("gr", [128, 8], F32), ("gab", [128, 1024], F32),
    ("wa", [8, 128, 128], F32), ("wi", [8, 128, 128], F32), ("keysT", [128, 16 * 128], F32),
    ("downB", [128 * 128, 16 * 128], F32), ("up", [16384, D], F32),
    ("cosT", [128, S], F32), ("sinT", [128, S], F32), ("rotm", [128, 128], F32), ("ident", [128, 128], F32),
    ("maskc", [128, S], F32), ("cm", [128, 4 * 512], F32), ("wm", [128, 8 * 512], F32),
    ("ex", [32, 16 * 128], F32), ("vm", [128, 16 * 32], F32), ("fb", [128, 16 * 32], F32), ("ovl", [128, 32], F32),
]


class Prog:
    def __init__(self, stop_after=99, dbg=()):
        self.nc = nc = bass.Bass("TRN2", target_bir_lowering=False)
        self.k = K(nc)
        self.stop_after = stop_after
        self.dbg = set(dbg)
        for d_ in self.dbg:
            if d_.startswith("qkind="):
                self.qkind = d_.split("=")[1]
        self.I = {}
        for name, shape, dt in IN_SPECS:
            self.I[name] = nc.dram_tensor(name, shape, dt, kind="ExternalInput").ap()
        self.outT = nc.dram_tensor("outT", [D, S], F32, kind="ExternalOutput").ap()
        self.scr = {}

    def scratch(self, name, shape, dt):
        kind = "ExternalOutput" if name in self.dbg else "Internal"
        ap = self.nc.dram_tensor(name, list(shape), dt, kind=kind).ap()
        self.scr[name] = (ap, Res(name))
        return ap, self.scr[name][1]

    def build(self):
        k = self.k
        with ExitStack() as g:
            self.g = g
            self.modT = k.sb(g, [128, 96], F32, "modT")
            self.G1 = k.sb(g, [128, 16], F32, "G1")
            self.G2 = k.sb(g, [128, 16], F32, "G2")
            self.eps_t = k.sb(g, [128, 1], F32, "eps")
            self.ones16 = k.sb(g, [128, 128], BF16, "ones16")
            self.ident16 = k.sb(g, [128, 128], BF16, "ident16")
            k.memset(self.eps_t[:], EPS, [self.eps_t])
            k.memset(self.ones16[:], 1.0, [self.ones16])
            self.ident32 = k.sb(g, [128, 128], F32, "ident32")
            k.dma(self.ident32[:], self.I["ident"], [], [self.ident32])
            k.copy(self.ident16[:], self.ident32[:], [self.ident32], [self.ident16])
            names = ["p0_mod", "p12_proj", "p3_cmp", "p4_attn", "p5_rnn", "p6_out", "p7_peer"]
            phases = [getattr(self, n) for n in names if hasattr(self, n)]
            for i, ph in enumerate(phases):
                if i > self.stop_after:
                    break
                ph()
                k.barrier()
            k.S.emit()
        return self.nc

    def p0_mod(self):
        k, I = self.k, self.I
        with ExitStack() as st:
            cc = k.sb(st, [128, 16], F32, "cc")
            sc = k.sb(st, [128, 16], F32, "sc")
            abT = k.sb(st, [128, 96], F32, "abT")
            g1 = k.sb(st, [128, 16], F32, "g1")
            g2 = k.sb(st, [128, 16], F32, "g2")
            tmp = k.sb(st, [128, 16], F32, "tmp")
            wb = [k.sb(st, [128, 16, 512], F32, f"adaw{i}") for i in range(2)]
            pm = k.ps(st, [128, 96], F32, "pm")
            k.dma(cc[:], I["c_col"], [], [cc])
            k.dma(abT[:], I["ada_bT"], [], [abT])
            k.dma(g1[:], I["g1T"], [], [g1])
            k.dma(g2[:], I["g2T"], [], [g2])
            k.act(sc[:], cc[:], AF.Silu, [cc], [sc])
            wv = I["ada_w"].rearrange("(k p) n -> p k n", p=128)
            for gi in range(24):
                b = wb[gi % 2]
                k.dma(b[:], wv[:, :, gi * 512:(gi + 1) * 512], [], [b], eng=("sp" if gi % 2 == 0 else "pool"))
                for j in range(4):
                    col = gi * 4 + j
                    for kk in range(16):
                        k.mm(pm[:, col:col + 1], b[:, kk, j * 128:(j + 1) * 128], sc[:, kk:kk + 1],
                             kk == 0, kk == 15, [b, sc], [pm])
            k.tt(self.modT[:], pm[:], abT[:], ALU.add, [pm, abT], [self.modT])
            k.ts(tmp[:], self.modT[:, 16:32], 1.0, None, ALU.add, None, [self.modT], [tmp])
            k.tt(self.G1[:], tmp[:], g1[:], ALU.mult, [tmp, g1], [self.G1])
            k.ts(tmp[:], self.modT[:, 64:80], 1.0, None, ALU.add, None, [self.modT, self.G1], [tmp])
            k.tt(self.G2[:], tmp[:], g2[:], ALU.mult, [tmp, g2], [self.G2])
            if "modT" in self.dbg:
                d, r = self.scratch("modT", [128, 96], F32)
                k.dma(d, self.modT[:], [self.modT], [r])

    def rms_stats_fm(self, st, loader, nchunks, width, scale_div, name):
        k = self.k
        rstd = k.sb(st, [128, S], F32, name)
        with ExitStack() as s2:
            xb = [k.sb(s2, [128, S], F32, "xld") for _ in range(2)]
            sq = [k.sb(s2, [128, S], BF16, "sq") for _ in range(2)]
            pss = [k.ps(s2, [128, 512], F32, "pss") for _ in range(4)]
            for kk in range(nchunks):
                xt, sqt = xb[kk % 2], sq[kk % 2]
                loader(kk, xt)
                k.act(sqt[:], xt[:], AF.Square, [xt], [sqt])
                for tg in range(4):
                    k.mm(pss[tg][:], self.ones16[:], sqt[:, tg * 512:(tg + 1) * 512], kk == 0, kk == nchunks - 1,
                         [self.ones16, sqt], [pss[tg]])
            for tg in range(4):
                sl = slice(tg * 512, (tg + 1) * 512)
                k.act(rstd[:, sl], pss[tg][:], AF.Sqrt, [pss[tg], self.eps_t], [rstd], bias=self.eps_t[:], scale=1.0 / scale_div)
            k.recip(rstd[:], rstd[:], [rstd], [rstd])
        k.barrier()
        return rstd

    def p12_proj(self):
        k, I = self.k, self.I
        xTv = I["xT"].rearrange("(k p) t -> k p t", p=128)
        qT_d, qT_r = self.scratch("qT", [8, 128, S], BF16)
        kcT_d, kcT_r = self.scratch("kcT", [2, 128, S], BF16)
        vcT_d, vcT_r = self.scratch("vcT", [2, 128, S], BF16)
        ksT_d, ksT_r = self.scratch("ksT", [2, 128, S], BF16)
        kwT_d, kwT_r = self.scratch("kwT", [2, 128, S], BF16)
        vs_d, vs_r = self.scratch("vs", [S, 256], BF16)
        vw_d, vw_r = self.scratch("vw", [S, 256], BF16)
        gt_d, gt_r = self.scratch("gates", [128, 16 * 24], F32)
        xrT_d, xrT_r = self.scratch("xrT", [8, 128, S], F32)
        xgT_d, xgT_r = self.scratch("xgT", [8, 128, S], F32)
        with ExitStack() as st:
            hT = k.sb(st, [128, 16, S], BF16, "hT")
            with ExitStack() as s1:
                rstd = self.rms_stats_fm(s1, lambda kk, dst: k.dma(dst[:], xTv[kk], [], [dst]), 16, S, float(D), "rstd1")
                xb = [k.sb(s1, [128, S], F32, "xld2") for _ in range(2)]
                tmp = [k.sb(s1, [128, S], F32, "tmp") for _ in range(2)]
                for kk in range(16):
                    xt, tp = xb[kk % 2], tmp[kk % 2]
                    k.dma(xt[:], xTv[kk], [], [xt])
                    k.stt(tp[:], xt[:], self.G1[:, kk:kk + 1], rstd[:], ALU.mult, ALU.mult, [xt, self.G1, rstd], [tp])
                    k.act(hT[:, kk, :], tp[:], AF.Identity, [tp, self.modT], [hT], bias=self.modT[:, kk:kk + 1], scale=1.0)
            if "hT" in self.dbg:
                d, r = self.scratch("hT", [16, 128, S], BF16)
                k.dma(d.rearrange("k p t -> p k t"), hT[:], [hT], [r])
            k.barrier()
            if "stop_p1" in self.dbg:
                return
            with ExitStack() as s2:
                wb = [k.sb(s2, [128, 16, 544], BF16, f"win{i}") for i in range(2)]
                cosT = k.sb(s2, [128, S], F32, "cosT")
                sinT = k.sb(s2, [128, S], F32, "sinT")
                rot16 = k.sb(s2, [128, 128], BF16, "rot16")
                gq = k.sb(s2, [128, 4], F32, "gq")
                gateb = k.sb(s2, [128, 24], F32, "gateb")
                k.dma(cosT[:], I["cosT"], [], [cosT])
                k.dma(sinT[:], I["sinT"], [], [sinT])
                rot32 = k.sb(s2, [128, 128], F32, "rot32")
                k.dma(rot32[:], I["rotm"], [], [rot32])
                k.copy(rot16[:], rot32[:], [rot32], [rot16])
                k.dma(gq[:, 0:1], I["qg"], [], [gq])
                k.dma(gq[:, 1:4], I["kgT"], [], [gq])
                k.dma(gateb[:], I["gateb"], [], [gateb])
                pz = [k.ps(s2, [128, 512], F32, "pz") for _ in range(2)]
                pss = [k.ps(s2, [128, 512], F32, "pss2") for _ in range(2)]
                prot = [k.ps(s2, [128, 512], F32, "prot") for _ in range(2)]
                ptm = [k.ps(s2, [128, 512], F32, "ptm") for _ in range(2)]
                sq = [k.sb(s2, [128, 512], BF16, "sq2") for _ in range(2)]
                rs = [k.sb(s2, [128, 512], F32, "rs") for _ in range(2)]
                xn = [k.sb(s2, [128, 512], F32, "xn") for _ in range(2)]
                xn16 = [k.sb(s2, [128, 512], BF16, "xn16") for _ in range(2)]
                t1 = [k.sb(s2, [128, 512], F32, "t1") for _ in range(2)]
                t2 = [k.sb(s2, [128, 512], F32, "t2") for _ in range(2)]
                o16 = [k.sb(s2, [128, S], BF16, "o16") for _ in range(2)]
                o32 = [k.sb(s2, [128, S], F32, "o32") for _ in range(2)]
                vtm = k.sb(s2, [128, 16, 256], BF16, "vtm")
                gtm = k.sb(s2, [128, 16, 24], F32, "gtm")
                gtmp = k.sb(s2, [128, 24], F32, "gtmp")
                wv = I["w_in"].rearrange("(k p) n -> p k n", p=128)
                cnt = {"c": 0, "i": 0}

                def fm_chunk(b, co, kind, gcol, dst_ap, dst_res):
                    ci = cnt["c"]
                    cnt["c"] += 1
                    ob = (o32 if kind == "f32" else o16)[ci % 2]
                    for tg in range(4):
                        i = cnt["i"]
                        cnt["i"] += 1
                        p = pz[i % 2]
                        sl = slice(tg * 512, (tg + 1) * 512)
                        for kk in range(16):
                            k.mm(p[:], b[:, kk, co:co + 128], hT[:, kk, sl], kk == 0, kk == 15, [b, hT], [p])
                        if kind in ("f32", "bf16"):
                            k.copy(ob[:, sl], p[:], [p], [ob], eng=("act" if tg % 2 == 0 else "dve"))
                            continue
                        a, a16 = xn[i % 2], xn16[i % 2]
                        if kind == "normrope":
                            sqt, pst, rst = sq[i % 2], pss[i % 2], rs[i % 2]
                            k.act(sqt[:], p[:], AF.Square, [p], [sqt])
                            k.mm(pst[:], self.ones16[:], sqt[:], True, True, [self.ones16, sqt], [pst])
                            k.act(rst[:], pst[:], AF.Sqrt, [pst, self.eps_t], [rst], bias=self.eps_t[:], scale=1.0 / 128.0)
                            k.recip(rst[:], rst[:], [rst], [rst])
                            k.stt(a[:], p[:], gq[:, gcol:gcol + 1], rst[:], ALU.mult, ALU.mult, [p, gq, rst], [a])
                        else:
                            k.copy(a[:], p[:], [p], [a], eng="dve")
                        k.copy(a16[:], a[:], [a], [a16], eng="act")
                        pr = prot[i % 2]
                        k.mm(pr[:], rot16[:], a16[:], True, True, [rot16, a16], [pr])
                        k.tt(t1[i % 2][:], a[:], cosT[:, sl], ALU.mult, [a, cosT], [t1[i % 2]])
                        k.tt(t2[i % 2][:], pr[:], sinT[:, sl], ALU.mult, [pr, sinT], [t2[i % 2]])
                        k.tt(ob[:, sl], t1[i % 2][:], t2[i % 2][:], ALU.add, [t1[i % 2], t2[i % 2]], [ob], eng="pool")
                    k.dma(dst_ap, ob[:], [ob], [dst_res])

                stg = [k.sb(s2, [128, 16, 272], F32, f"stg{i}") for i in range(2)]
                lcnt = {"g": 0, "s": 0}

                def load_group(c0, c1):
                    b = wb[lcnt["g"] % 2]
                    lcnt["g"] += 1
                    w = c1 - c0
                    pieces = [(a, min(a + 256, w)) for a in range(0, w, 256)]
                    for (a0, a1) in pieces:
                        sg = stg[lcnt["s"] % 2]
                        lcnt["s"] += 1
                        k.dma(sg[:, :, 0:a1 - a0], wv[:, :, c0 + a0:c0 + a1], [], [sg], eng=("sp" if lcnt["s"] % 2 else "pool"))
                        k.copy(b[:, :, a0:a1], sg[:, :, 0:a1 - a0], [sg], [b], eng="pool")
                    return b

                def tm_block(b, co, width, post):
                    for tt_ in range(16):
                        p = ptm[tt_ % 2]
                        for kk in range(16):
                            k.mm(p[:, 0:width], hT[:, kk, tt_ * 128:(tt_ + 1) * 128], b[:, kk, co:co + width], kk == 0, kk == 15, [b, hT], [p])
                        post(tt_, p)

                b = load_group(0, 512)
                for j in range(4):
                    fm_chunk(b, j * 128, self.qkind if hasattr(self, "qkind") else "normrope", 0, qT_d[j], qT_r)
                b = load_group(512, 1024)
                for j in range(4):
                    fm_chunk(b, j * 128, self.qkind if hasattr(self, "qkind") else "normrope", 0, qT_d[4 + j], qT_r)
                if "stop_g0" in self.dbg:
                    return
                b = load_group(1024, 1536)
                for j in range(2):
                    fm_chunk(b, j * 128, "rope", 0, kcT_d[j], kcT_r)
                for j in range(2):
                    fm_chunk(b, 256 + j * 128, "bf16", 0, vcT_d[j], vcT_r)
                b = load_group(1536, 2048)
                for j in range(2):
                    fm_chunk(b, j * 128, "normrope", 2, ksT_d[j], ksT_r)
                tm_block(b, 256, 256, lambda tt_, p: k.copy(vtm[:, tt_, :], p[:, 0:256], [p], [vtm], eng=("act" if tt_ % 2 == 0 else "dve")))
                k.dma(vs_d.rearrange("(t p) c -> p t c", p=128), vtm[:], [vtm], [vs_r])
                if "stop_g1" in self.dbg:
                    return
                if "rep_g3" in self.dbg:
                    b = load_group(1536, 2048)
                    for j in range(2):
                        fm_chunk(b, j * 128, "normrope", 2, ksT_d[j], ksT_r)
                    tm_block(b, 256, 256, lambda tt_, p: k.copy(vtm[:, tt_, :], p[:, 0:256], [p], [vtm], eng=("act" if tt_ % 2 == 0 else "dve")))
                    k.dma(vs_d.rearrange("(t p) c -> p t c", p=128), vtm[:], [vtm], [vs_r])
                    return
                b = load_group(2048, 2560)
                for j in range(2):
                    fm_chunk(b, j * 128, "normrope", 3, kwT_d[j], kwT_r)
                tm_block(b, 256, 256, lambda tt_, p: k.copy(vtm[:, tt_, :], p[:, 0:256], [p], [vtm], eng=("act" if tt_ % 2 == 0 else "dve")))
                b = load_group(2560, 2584)

                def post_g(tt_, p):
                    k.tt(gtmp[:], p[:, 0:24], gateb[:], ALU.add, [p, gateb], [gtmp])
                    k.act(gtmp[:], gtmp[:], AF.Exp, [gtmp], [gtmp], scale=-1.0)
                    k.ts(gtmp[:], gtmp[:], 1.0, None, ALU.add, None, [gtmp], [gtmp])
                    k.recip(gtm[:, tt_, :], gtmp[:], [gtmp], [gtm])
                tm_block(b, 0, 24, post_g)
                k.dma(vw_d.rearrange("(t p) c -> p t c", p=128), vtm[:], [vtm], [vw_r])
                k.dma(gt_d, gtm[:].rearrange("p t c -> p (t c)"), [gtm], [gt_r])
                if "stop_g2" in self.dbg:
                    return
                for half in range(2):
                    b = load_group(C_XR + half * 512, C_XR + (half + 1) * 512)
                    for j in range(4):
                        fm_chunk(b, j * 128, "f32", 0, xrT_d[half * 4 + j], xrT_r)
                for half in range(2):
                    b = load_group(C_XG + half * 512, C_XG + (half + 1) * 512)
                    for j in range(4):
                        fm_chunk(b, j * 128, "f32", 0, xgT_d[half * 4 + j], xgT_r)

    def p3_cmp(self):
        k, I = self.k, self.I
        kcT_d = self.scr["kcT"][0]
        vcT_d = self.scr["vcT"][0]
        kcmpT_d, kcmpT_r = self.scratch("kcmpT", [128, 2, 128], BF16)
        vcmp_d, vcmp_r = self.scratch("vcmp", [128, 2, 162], BF16)
        with ExitStack() as st:
            src = k.sb(st, [128, 4, S], BF16, "cmpsrc")
            wst = k.sb(st, [128, 32, 128], F32, "wst")
            w16 = [k.sb(st, [128, 32, 128], BF16, f"w16{i}") for i in range(2)]
            pe32 = k.sb(st, [128, 2, 32], F32, "pe32")
            peB = [k.sb(st, [128, 32, 127], BF16, f"peB{i}") for i in range(2)]
            kg0b = k.sb(st, [128, 128], F32, "kg0b")
            ovl = k.sb(st, [128, 32], F32, "ovl")
            ss = k.sb(st, [128, 1], F32, "ss")
            junk = k.sb(st, [128, 128], F32, "junk")
            kn16 = k.sb(st, [128, 128], BF16, "kn16")
            kT16 = k.sb(st, [128, 2, 128], BF16, "kT16")
            va16 = k.sb(st, [128, 2, 162], BF16, "va16")
            pc = [k.ps(st, [128, 128], F32, "pc") for _ in range(2)]
            ptr = k.ps(st, [128, 128], F32, "ptr")
            for g in range(2):
                k.dma(src[:, g, :], kcT_d[g], [self.scr["kcT"][1]], [src])
                k.dma(src[:, 2 + g, :], vcT_d[g], [self.scr["vcT"][1]], [src])
            k.dma(pe32[:, 0, :], I["pekT"], [], [pe32])
            k.dma(pe32[:, 1, :], I["pevT"], [], [pe32])
            k.dma(kg0b[:], I["kg0b"], [], [kg0b])
            k.dma(ovl[:], I["ovl"], [], [ovl])
            k.memset(kT16[:], 0.0, [kT16])
            k.memset(va16[:], 0.0, [va16])
            for kv, wname in enumerate(("cwk", "cwv")):
                k.dma(wst[:], I[wname].rearrange("(l d) o -> d l o", d=128), [], [wst])
                k.copy(w16[kv][:], wst[:], [wst], [w16[kv]], eng="pool")
                k.copy(peB[kv][:], pe32[:, kv, :].unsqueeze(2).to_broadcast([128, 32, 127]), [pe32], [peB[kv]])
            for kv in range(2):
                for g in range(2):
                    p = pc[(kv * 2 + g) % 2]
                    for l in range(32):
                        k.mm(p[0:127, :], src[:, kv * 2 + g, l:l + 16 * 126 + 1:16], w16[kv][:, l, :], l == 0, False, [src, w16[kv]], [p])
                    for l in range(32):
                        k.mm(p[0:127, :], peB[kv][:, l, :], w16[kv][:, l, :], False, l == 31, [peB[kv], w16[kv]], [p])
                    if kv == 0:
                        k.act(junk[0:127, :], p[0:127, :], AF.Square, [p], [junk, ss], accum_out=ss[0:127, :])
                        k.act(ss[0:127, :], ss[0:127, :], AF.Sqrt, [ss, self.eps_t], [ss], bias=self.eps_t[0:127, :], scale=1.0 / 128.0)
                        k.recip(ss[0:127, :], ss[0:127, :], [ss], [ss])
                        k.stt(kn16[0:127, :], p[0:127, :], ss[0:127, :], kg0b[0:127, :], ALU.mult, ALU.mult, [p, ss, kg0b], [kn16])
                        k.mm(ptr[:, 0:127], kn16[0:127, :], self.ident16[0:127, 0:127], True, True, [kn16, self.ident16], [ptr])
                        k.copy(kT16[:, g, 0:127], ptr[:, 0:127], [ptr], [kT16])
                    else:
                        k.copy(va16[0:127, g, 0:128], p[0:127, :], [p], [va16])
                        k.memset(va16[0:127, g, 128:129], 1.0, [va16])
                        k.copy(va16[0:127, g, 129:161], ovl[0:127, :], [ovl], [va16])
            k.dma(kcmpT_d, kT16[:], [kT16], [kcmpT_r])
            k.dma(vcmp_d, va16[:], [va16], [vcmp_r])

    def p4_attn(self):
        k, I = self.k, self.I
        sc = self.scr
        yT_d, yT_r = self.scratch("yT", [16, 128, S], BF16)
        with ExitStack() as st:
            qT = k.sb(st, [128, 8, S], BF16, "qT")
            ksT = k.sb(st, [128, 2, S], BF16, "ksT")
            kwT = k.sb(st, [128, 2, S], BF16, "kwT")
            kcT = k.sb(st, [128, 2, 128], BF16, "kcT")
            vsa = k.sb(st, [128, 16, 2, 130], BF16, "vsa")
            vwa = k.sb(st, [128, 16, 2, 130], BF16, "vwa")
            vca = k.sb(st, [128, 2, 162], BF16, "vca")
            gts = k.sb(st, [128, 16, 24], F32, "gts")
            maskc = k.sb(st, [128, S], F32, "maskc")
            cm = k.sb(st, [128, 4, 512], F32, "cm")
            wm = k.sb(st, [128, 8, 512], F32, "wm")
            ex32 = k.sb(st, [32, 16, 128], F32, "ex32")
            ex16 = k.sb(st, [32, 16, 128], BF16, "ex16")
            vm = k.sb(st, [128, 16, 32], F32, "vm")
            fb = k.sb(st, [128, 16, 32], F32, "fb")
            gab = k.sb(st, [128, 1024], F32, "gab")
            for j in range(8):
                k.dma(qT[:, j, :], sc["qT"][0][j], [sc["qT"][1]], [qT], eng=("sp" if j % 2 else "pool"))
            for g in range(2):
                k.dma(ksT[:, g, :], sc["ksT"][0][g], [sc["ksT"][1]], [ksT])
                k.dma(kwT[:, g, :], sc["kwT"][0][g], [sc["kwT"][1]], [kwT])
            k.dma(kcT[:], sc["kcmpT"][0], [sc["kcmpT"][1]], [kcT])
            k.dma(vca[:], sc["vcmp"][0], [sc["vcmp"][1]], [vca])
            k.memset(vsa[:], 1.0, [vsa])
            k.memset(vwa[:], 1.0, [vwa])
            for g in range(2):
                k.dma(vsa[:, :, g, 0:128], sc["vs"][0].rearrange("(c p) x -> p c x", p=128)[:, :, g * 128:(g + 1) * 128], [sc["vs"][1]], [vsa])
                k.dma(vwa[:, :, g, 0:128], sc["vw"][0].rearrange("(c p) x -> p c x", p=128)[:, :, g * 128:(g + 1) * 128], [sc["vw"][1]], [vwa])
            k.dma(gts[:].rearrange("p t c -> p (t c)"), sc["gates"][0], [sc["gates"][1]], [gts])
            k.dma(maskc[:], I["maskc"], [], [maskc])
            k.dma(cm[:].rearrange("p a b -> p (a b)"), I["cm"], [], [cm])
            k.dma(wm[:].rearrange("p a b -> p (a b)"), I["wm"], [], [wm])
            k.dma(ex32[:].rearrange("p a b -> p (a b)"), I["ex"], [], [ex32])
            k.copy(ex16[:], ex32[:], [ex32], [ex16])
            k.dma(vm[:].rearrange("p a b -> p (a b)"), I["vm"], [], [vm])
            k.dma(fb[:].rearrange("p a b -> p (a b)"), I["fb"], [], [fb])
            k.dma(gab[:], I["gab"], [], [gab])
            O = k.sb(st, [128, 4, 1024], F32, "O")
            imp = [k.sb(st, [128, 32], F32, f"imp{i}") for i in range(4)]
            e16 = [k.sb(st, [128, 512], BF16, f"e16{i}") for i in range(3)]
            p16 = [k.sb(st, [128, 512], BF16, f"p16{i}") for i in range(3)]
            mskS = k.sb(st, [128, 16, 512], BF16, "mskS")
            selT16 = k.sb(st, [32, 512], BF16, "selT16")
            sel16 = [k.sb(st, [128, 32], BF16, f"sel16{i}") for i in range(2)]
            imp2 = [k.sb(st, [128, 32], F32, f"imp2{i}") for i in range(2)]
            wk = [k.sb(st, [128, 32], F32, f"wk{i}") for i in range(2)]
            m8 = [k.sb(st, [128, 16], F32, f"m8{i}") for i in range(2)]
            den = [k.sb(st, [128, 2], F32, f"den{i}") for i in range(4)]
            ssq = k.sb(st, [128, 1], F32, "ssq")
            junk = k.sb(st, [128, 1024], F32, "junk4")
            yn16 = k.sb(st, [128, 1024], BF16, "yn16")
            yT16 = [k.sb(st, [128, 512], BF16, f"yT16{i}") for i in range(2)]
            pS = [k.ps(st, [128, 512], F32, "pS") for _ in range(2)]
            pA = k.ps(st, [128, 512], F32, "pA")
            pACC = [k.ps(st, [128, 512], F32, "pACC") for _ in range(4)]
            pM = [k.ps(st, [128, 512], F32, "pM")] * 2
            pT = pA
            cnt = {"s": 0, "e": 0, "m": 0, "d": 0, "y": 0, "sel": 0}

            def finish_head(sub, hd, acc_ap, den_ap, tt_, gcol, first, Rp):
                dn = den[cnt["d"] % 4]
                cnt["d"] += 1
                k.ts(dn[:, 0:1], den_ap, 1e-30, None, ALU.max, None, Rp, [dn])
                k.recip(dn[:, 0:1], dn[:, 0:1], [dn], [dn])
                k.tt(dn[:, 1:2], dn[:, 0:1], gts[:, tt_, gcol:gcol + 1], ALU.mult, [dn, gts], [dn])
                osl = O[:, sub, hd * 128:(hd + 1) * 128]
                if first:
                    k.ts(osl, acc_ap, dn[:, 1:2], None, ALU.mult, None, Rp + [dn], [O])
                else:
                    k.stt(osl, acc_ap, dn[:, 1:2], osl, ALU.mult, ALU.add, Rp + [dn, O], [O])
                return dn

            def branch(i, g, r, keyT, vaug, chunks, mask_of, gbranch):
                hd = g * 4 + r
                qsl = slice(i * 512, (i + 1) * 512)
                for ci, kc in enumerate(chunks):
                    ps_ = pS[cnt["s"] % 2]
                    cnt["s"] += 1
                    k.mm(ps_[:], keyT[:, g, kc * 128:(kc + 1) * 128], qT[:, hd, qsl], True, True, [keyT, qT], [ps_])
                    e = e16[cnt["e"] % 3]
                    p_ = p16[cnt["e"] % 3]
                    cnt["e"] += 1
                    k.act(e[:], ps_[:], AF.Exp, [ps_], [e], scale=SCALE)
                    mk, mkR, eng = mask_of(kc)
                    k.tt(p_[:], e[:], mk, ALU.mult, [e, mkR], [p_], eng=eng)
                    for sub in range(4):
                        acc = pACC[sub]
                        k.mm(acc[:, 0:129], p_[:, sub * 128:(sub + 1) * 128], vaug[:, kc, g, 0:129],
                             ci == 0, ci == len(chunks) - 1, [p_, vaug], [acc])
                for sub in range(4):
                    acc = pACC[sub]
                    finish_head(sub, hd, acc[:, 0:128], acc[:, 128:129], i * 4 + sub,
                                g * 12 + r * 3 + gbranch, False, [acc])

            for i in range(4):
                qsl = slice(i * 512, (i + 1) * 512)
                for g in range(2):
                    for r in range(4):
                        hd = g * 4 + r
                        ps_ = pS[cnt["s"] % 2]
                        cnt["s"] += 1
                        k.mm(ps_[0:127, :], kcT[:, g, 0:127], qT[:, hd, qsl], True, True, [kcT, qT], [ps_])
                        e = e16[cnt["e"] % 3]
                        p_ = p16[cnt["e"] % 3]
                        cnt["e"] += 1
                        k.act(e[0:127, :], ps_[0:127, :], AF.Exp, [ps_], [e], scale=SCALE)
                        k.tt(p_[0:127, :], e[0:127, :], maskc[0:127, qsl], ALU.mult, [e, maskc], [p_])
                        for sub in range(4):
                            k.mm(pA[:, 0:161], p_[0:127, sub * 128:(sub + 1) * 128], vca[0:127, g, 0:161], True, True, [p_, vca], [pA])
                            dn = finish_head(sub, hd, pA[:, 0:128], pA[:, 128:129], i * 4 + sub, g * 12 + r * 3, True, [pA])
                            if r == 0:
                                k.ts(imp[sub][:], pA[:, 129:161], dn[:, 0:1], None, ALU.mult, None, [pA, dn], [imp[sub]])
                            else:
                                k.stt(imp[sub][:], pA[:, 129:161], dn[:, 0:1], imp[sub][:], ALU.mult, ALU.add, [pA, dn, imp[sub]], [imp[sub]])
                    psel = pM[cnt["m"] % 2]
                    cnt["m"] += 1
                    for sub in range(4):
                        tt_ = i * 4 + sub
                        j = cnt["sel"] % 2
                        cnt["sel"] += 1
                        k.tt(imp2[j][:], imp[sub][:], vm[:, tt_, :], ALU.mult, [imp[sub], vm], [imp2[j]])
                        k.tt(imp2[j][:], imp2[j][:], fb[:, tt_, :], ALU.add, [imp2[j], fb], [imp2[j]])
                        k.fn("dve", lambda e, o=m8[j][:, 0:8], a=imp2[j][:]: e.max(out=o, in_=a), [imp2[j]], [m8[j]])
                        k.fn("dve", lambda e, o=wk[j][:], a=m8[j][:, 0:8], b=imp2[j][:]: e.match_replace(out=o, in_to_replace=a, in_values=b, imm_value=-1e30), [imp2[j], m8[j]], [wk[j]])
                        k.fn("dve", lambda e, o=m8[j][:, 8:16], a=wk[j][:]: e.max(out=o, in_=a), [wk[j]], [m8[j]])
                        k.ts(sel16[j][:], imp2[j][:], m8[j][:, 15:16], None, ALU.is_ge, None, [imp2[j], m8[j]], [sel16[j]])
                        k.mm(psel[0:32, sub * 128:(sub + 1) * 128], sel16[j][:], self.ident16[:], True, True, [sel16[j], self.ident16], [psel])
                    k.copy(selT16[:], psel[0:32, :], [psel], [selT16], eng="act")
                    nkc = 4 * i + 4
                    for kc in range(nkc):
                        pm_ = pM[cnt["m"] % 2]
                        cnt["m"] += 1
                        k.mm(pm_[:], ex16[:, kc, :], selT16[:], True, True, [ex16, selT16], [pm_])
                        if kc >= 4 * i:
                            k.tt(mskS[:, kc, :], pm_[:], cm[:, kc - 4 * i, :], ALU.mult, [pm_, cm], [mskS])
                        else:
                            k.copy(mskS[:, kc, :], pm_[:], [pm_], [mskS], eng="act")
                    for r in range(4):
                        branch(i, g, r, ksT, vsa, list(range(nkc)), lambda kc: (mskS[:, kc, :], mskS, "pool"), 1)
                    for r in range(4):
                        chunks = list(range(max(0, 4 * i - 4), 4 * i + 4))
                        branch(i, g, r, kwT, vwa, chunks, lambda kc: (wm[:, kc - 4 * i + 4, :], wm, "dve"), 2)
                yt = yT16[i % 2]
                for c in range(8):
                    pass
                ytiles = []
                for sub in range(4):
                    k.act(junk[:], O[:, sub, :], AF.Square, [O], [junk, ssq], accum_out=ssq[:])
                    k.act(ssq[:], ssq[:], AF.Sqrt, [ssq, self.eps_t], [ssq], bias=self.eps_t[:], scale=1.0 / 1024.0)
                    k.recip(ssq[:], ssq[:], [ssq], [ssq])
                    k.stt(yn16[:], O[:, sub, :], ssq[:], gab[:], ALU.mult, ALU.mult, [O, ssq, gab], [yn16])
                    for half in range(2):
                        for cc in range(4):
                            c = half * 4 + cc
                            k.mm(pT[:, cc * 128:(cc + 1) * 128], yn16[:, c * 128:(c + 1) * 128], self.ident16[:], True, True, [yn16, self.ident16], [pT])
                        dst = k.sb(st, [128, 4, 128], BF16, "ytmp") if False else None
                        yb = yT16[cnt["y"] % 2]
                        cnt["y"] += 1
                        k.copy(yb[:], pT[:], [pT], [yb], eng=("act" if half == 0 else "dve"))
                        for cc in range(4):
                            c = half * 4 + cc
                            k.dma(yT_d[c][:, i * 512 + sub * 128:i * 512 + (sub + 1) * 128], yb[:, cc * 128:(cc + 1) * 128], [yb], [yT_r])
            if "O_dbg" in self.dbg:
                pass

    def p5_rnn(self):
        k, I = self.k, self.I
        sc = self.scr
        yT_d, yT_r = sc["yT"]
        xr_d, xr_r = sc["xrT"]
        xg_d, xg_r = sc["xgT"]
        with ExitStack() as st:
            cw = k.sb(st, [128, 8, 4], F32, "cw")
            cb = k.sb(st, [128, 8], F32, "cb")
            nba = k.sb(st, [128, 8], F32, "nba")
            nbi = k.sb(st, [128, 8], F32, "nbi")
            lam = k.sb(st, [128, 8], F32, "lam")
            clam = k.sb(st, [128, 8], F32, "clam")
            gr = k.sb(st, [128, 8], F32, "gr")
            wst = k.sb(st, [128, 2, 128], F32, "wst5")
            w16 = [k.sb(st, [128, 2, 128], BF16, f"w165{i}") for i in range(2)]
            k.dma(cw[:].rearrange("p a b -> p (a b)"), I["convw"], [], [cw])
            k.dma(cb[:], I["convb"], [], [cb])
            k.dma(nba[:], I["lba"], [], [nba])
            k.dma(nbi[:], I["lbi"], [], [nbi])
            k.dma(lam[:], I["lam"], [], [lam])
            k.dma(gr[:], I["gr"], [], [gr])
            k.ts(nba[:], nba[:], -1.0, None, ALU.mult, None, [nba], [nba])
            k.ts(nbi[:], nbi[:], -1.0, None, ALU.mult, None, [nbi], [nbi])
            k.act(clam[:], lam[:], AF.Exp, [lam], [clam], scale=-1.0)
            k.ts(clam[:], clam[:], 1.0, None, ALU.add, None, [clam], [clam])
            k.act(clam[:], clam[:], AF.Ln, [clam], [clam])
            k.ts(clam[:], clam[:], -8.0, None, ALU.mult, None, [clam], [clam])
            orn = k.sb(st, [128, 8, S], F32, "orn")
            xp = k.sb(st, [128, S + 4], F32, "xp")
            xg = k.sb(st, [128, S], F32, "xg")
            u = k.sb(st, [128, S], F32, "u")
            u16 = k.sb(st, [128, S], BF16, "u16")
            ra = k.sb(st, [128, S], F32, "ra")
            ig = k.sb(st, [128, S], F32, "ig")
            bb = k.sb(st, [128, S], F32, "bb")
            sq16 = k.sb(st, [128, S], BF16, "sq165")
            pg = [k.ps(st, [128, 512], F32, "pg") for _ in range(2)]
            pss = [k.ps(st, [128, 512], F32, "pss5") for _ in range(4)]
            k.memset(xp[:, 0:4], 0.0, [xp])
            ci = 0
            for n in range(8):
                k.dma(xp[:, 4:S + 4], xr_d[n], [xr_r], [xp])
                k.dma(xg[:], xg_d[n], [xg_r], [xg], eng="pool")
                wb = w16[n % 2]
                k.dma(wst[:, 0, :], I["wa"][n], [], [wst])
                k.dma(wst[:, 1, :], I["wi"][n], [], [wst])
                k.copy(wb[:], wst[:], [wst], [wb], eng="pool")
                k.ts(u[:], xp[:, 1:S + 1], cw[:, n, 0:1], cb[:, n:n + 1], ALU.mult, ALU.add, [xp, cw, cb], [u])
                for i_ in range(1, 4):
                    k.stt(u[:], xp[:, 1 + i_:S + 1 + i_], cw[:, n, i_:i_ + 1], u[:], ALU.mult, ALU.add, [xp, cw, u], [u])
                k.copy(u16[:], u[:], [u], [u16], eng="act")
                for which, dst, nb in ((0, ra, nba), (1, ig, nbi)):
                    for tg in range(4):
                        p = pg[ci % 2]
                        ci += 1
                        sl = slice(tg * 512, (tg + 1) * 512)
                        k.mm(p[:], wb[:, which, :], u16[:, sl], True, True, [wb, u16], [p])
                        k.act(dst[:, sl], p[:], AF.Exp, [p, nb], [dst], bias=nb[:, n:n + 1], scale=-1.0)
                    k.ts(dst[:], dst[:], 1.0, None, ALU.add, None, [dst], [dst], eng="pool")
                    k.recip(dst[:], dst[:], [dst], [dst])
                k.act(ra[:], ra[:], AF.Exp, [ra, clam], [ra], scale=clam[:, n:n + 1])
                k.tt(bb[:], ra[:], ra[:], ALU.mult, [ra], [bb])
                k.ts(bb[:], bb[:], -1.0, 1.0, ALU.mult, ALU.add, [bb], [bb])
                k.act(bb[:], bb[:], AF.Sqrt, [bb], [bb])
                k.tt(bb[:], bb[:], ig[:], ALU.mult, [bb, ig], [bb], eng="pool")
                k.tt(bb[:], bb[:], u[:], ALU.mult, [bb, u], [bb])
                k.fn("dve", lambda e, o=ig[:], a=ra[:], b=bb[:]: e.tensor_tensor_scan(o, a, b, 0.0, ALU.mult, ALU.add), [ra, bb, ig], [ig])
                k.act(xg[:], xg[:], AF.Gelu, [xg], [xg])
                k.tt(orn[:, n, :], xg[:], ig[:], ALU.mult, [xg, ig], [orn])
                k.act(sq16[:], orn[:, n, :], AF.Square, [orn], [sq16])
                for tg in range(4):
                    k.mm(pss[tg][:], self.ones16[:], sq16[:, tg * 512:(tg + 1) * 512], n == 0, n == 7, [self.ones16, sq16], [pss[tg]])
            rstd = ra
            for tg in range(4):
                sl = slice(tg * 512, (tg + 1) * 512)
                k.act(rstd[:, sl], pss[tg][:], AF.Sqrt, [pss[tg], self.eps_t], [rstd], bias=self.eps_t[:], scale=1.0 / 1024.0)
            k.recip(rstd[:], rstd[:], [rstd], [rstd])
            for n in range(8):
                k.stt(u16[:], orn[:, n, :], gr[:, n:n + 1], rstd[:], ALU.mult, ALU.mult, [orn, gr, rstd], [u16])
                k.dma(yT_d[8 + n], u16[:], [u16], [yT_r])

    def p6_out(self):
        k, I = self.k, self.I
        sc = self.scr
        yT_d, yT_r = sc["yT"]
        x1_d, x1_r = self.scratch("x1T", [16, 128, S], F32)
        h2_d, h2_r = self.scratch("h2T", [16, 128, S], BF16)
        xTv = I["xT"].rearrange("(k p) t -> k p t", p=128)
        wv = I["w_out"].rearrange("(k p) n -> p k n", p=128)
        with ExitStack() as st:
            rstd = k.sb(st, [128, S], F32, "rstd2")
            with ExitStack() as s1:
                yT = k.sb(s1, [128, 16, S], BF16, "yTs")
                wb = [k.sb(s1, [128, 16, 256], BF16, f"wo{i}") for i in range(2)]
                stg = [k.sb(s1, [128, 16, 128], F32, f"wos{i}") for i in range(2)]
                xb = [k.sb(s1, [128, S], F32, f"x6{i}") for i in range(2)]
                ob = [k.sb(s1, [128, S], F32, f"o6{i}") for i in range(2)]
                sq = [k.sb(s1, [128, S], BF16, f"sq6{i}") for i in range(2)]
                pz = [k.ps(s1, [128, 512], F32, "pz6") for _ in range(2)]
                pss = [k.ps(s1, [128, 512], F32, "pss6") for _ in range(4)]
                for c in range(16):
                    k.dma(yT[:, c, :], yT_d[c], [yT_r], [yT], eng=("sp" if c % 2 else "pool"))
                ci = 0
                for jg in range(8):
                    b = wb[jg % 2]
                    for h in range(2):
                        sg = stg[(jg * 2 + h) % 2]
                        k.dma(sg[:], wv[:, :, jg * 256 + h * 128:jg * 256 + (h + 1) * 128], [], [sg])
                        k.copy(b[:, :, h * 128:(h + 1) * 128], sg[:], [sg], [b], eng="pool")
                    for jj in range(2):
                        j = jg * 2 + jj
                        xt, ot, sqt = xb[j % 2], ob[j % 2], sq[j % 2]
                        k.dma(xt[:], xTv[j], [], [xt])
                        for tg in range(4):
                            p = pz[ci % 2]
                            ci += 1
                            sl = slice(tg * 512, (tg + 1) * 512)
                            for c in range(16):
                                k.mm(p[:], b[:, c, jj * 128:(jj + 1) * 128], yT[:, c, sl], c == 0, c == 15, [b, yT], [p])
                            k.stt(ot[:, sl], p[:], self.modT[:, 32 + j:33 + j], xt[:, sl], ALU.mult, ALU.add, [p, self.modT, xt], [ot])
                        k.dma(x1_d[j], ot[:], [ot], [x1_r])
                        k.act(sqt[:], ot[:], AF.Square, [ot], [sqt])
                        for tg in range(4):
                            k.mm(pss[tg][:], self.ones16[:], sqt[:, tg * 512:(tg + 1) * 512], j == 0, j == 15, [self.ones16, sqt], [pss[tg]])
                for tg in range(4):
                    sl = slice(tg * 512, (tg + 1) * 512)
                    k.act(rstd[:, sl], pss[tg][:], AF.Sqrt, [pss[tg], self.eps_t], [rstd], bias=self.eps_t[:], scale=1.0 / float(D))
                k.recip(rstd[:], rstd[:], [rstd], [rstd])
            k.barrier()
            with ExitStack() as s2:
                xb = [k.sb(s2, [128, S], F32, f"x6b{i}") for i in range(2)]
                tp = [k.sb(s2, [128, S], F32, f"t6b{i}") for i in range(2)]
                hb = [k.sb(s2, [128, S], BF16, f"h6b{i}") for i in range(2)]
                for j in range(16):
                    xt, tt_, ht = xb[j % 2], tp[j % 2], hb[j % 2]
                    k.dma(xt[:], x1_d[j], [x1_r], [xt])
                    k.stt(tt_[:], xt[:], self.G2[:, j:j + 1], rstd[:], ALU.mult, ALU.mult, [xt, self.G2, rstd], [tt_])
                    k.act(ht[:], tt_[:], AF.Identity, [tt_, self.modT], [ht], bias=self.modT[:, 48 + j:49 + j], scale=1.0)
                    k.dma(h2_d[j], ht[:], [ht], [h2_r])

    def p7_peer(self):
        k, I = self.k, self.I
        sc = self.scr
        h2_d, h2_r = sc["h2T"]
        x1_d, x1_r = sc["x1T"]
        qp_d, qp_r = self.scratch("qpT", [16, 128, S], BF16)
        wv = I["wq"].rearrange("(k p) n -> p k n", p=128)
        with ExitStack() as st:
            h2T = k.sb(st, [128, 16, S], BF16, "h2Ts")
            wb = [k.sb(st, [128, 16, 256], BF16, f"wq{i}") for i in range(2)]
            stg = [k.sb(st, [128, 16, 128], F32, f"wqs{i}") for i in range(2)]
            ob = [k.sb(st, [128, S], BF16, f"oq{i}") for i in range(2)]
            pz = [k.ps(st, [128, 512], F32, "pz7") for _ in range(2)]
            for c in range(16):
                k.dma(h2T[:, c, :], h2_d[c], [h2_r], [h2T], eng=("sp" if c % 2 else "pool"))
            ci = 0
            for jg in range(8):
                b = wb[jg % 2]
                for h in range(2):
                    sg = stg[(jg * 2 + h) % 2]
                    k.dma(sg[:], wv[:, :, jg * 256 + h * 128:jg * 256 + (h + 1) * 128], [], [sg])
                    k.copy(b[:, :, h * 128:(h + 1) * 128], sg[:], [sg], [b], eng="pool")
                for jj in range(2):
                    j = jg * 2 + jj
                    ot = ob[j % 2]
                    for tg in range(4):
                        p = pz[ci % 2]
                        ci += 1
                        sl = slice(tg * 512, (tg + 1) * 512)
                        for c in range(16):
                            k.mm(p[:], b[:, c, jj * 128:(jj + 1) * 128], h2T[:, c, sl], c == 0, c == 15, [b, h2T], [p])
                        k.copy(ot[:, sl], p[:], [p], [ot], eng=("act" if tg % 2 == 0 else "dve"))
                    k.dma(qp_d[j], ot[:], [ot], [qp_r])
        k.barrier()
        SLACK = 1.0 - 4e-6
        with ExitStack() as st:
            h2s = k.sb(st, [128, 16, 512], BF16, "h2s")
            E1s = k.sb(st, [128, 4, 8, 128], F32, "E1s")
            E2s = k.sb(st, [128, 4, 8, 128], F32, "E2s")
            dg16 = k.sb(st, [128, 4, 8, 128], BF16, "dg16")
            accT = k.sb(st, [128, 16, 512], F32, "accT")
            keys16 = k.sb(st, [128, 16, 128], BF16, "keys16")
            with ExitStack() as s0:
                k32 = k.sb(s0, [128, 16, 128], F32, "k32")
                k.dma(k32[:].rearrange("p a b -> p (a b)"), I["keysT"], [], [k32])
                k.copy(keys16[:], k32[:], [k32], [keys16])
            k.barrier()
            for su in range(4):
                tsl = slice(su * 512, (su + 1) * 512)
                k.dma(h2s[:], h2_d.rearrange("c p t -> p c t")[:, :, tsl], [h2_r], [h2s])
                with ExitStack() as sb_:
                    qps = k.sb(sb_, [128, 16, 512], BF16, "qps")
                    s_sb = k.sb(sb_, [128, 16, 128], F32, "s_sb")
                    wk = k.sb(sb_, [128, 256], F32, "wk7")
                    v16 = k.sb(sb_, [128, 16, 16], F32, "v16")
                    cand = k.sb(sb_, [128, 8, 256], F32, "cand")
                    c16 = k.sb(sb_, [128, 8, 16], F32, "c16")
                    en = k.sb(sb_, [128, 8, 16], F32, "en")
                    negm = k.sb(sb_, [128, 16], F32, "negm")
                    negM = k.sb(sb_, [128, 8], F32, "negM")
                    Z = k.sb(sb_, [128, 8], F32, "Z")
                    th = k.sb(sb_, [128, 8], F32, "th")
                    cf = k.sb(sb_, [128, 8], F32, "cf")
                    E1t = k.sb(sb_, [128, 8, 128], F32, "E1t")
                    pS_ = [k.ps(sb_, [128, 4, 128], F32, "pS7") for _ in range(2)]
                    k.dma(qps[:], qp_d.rearrange("c p t -> p c t")[:, :, tsl], [qp_r], [qps])
                    for tl in range(4):
                        for hg in range(4):
                            p = pS_[hg % 2]
                            for q4 in range(4):
                                hp = hg * 4 + q4
                                k.mm(p[:, q4, :], qps[:, hp, tl * 128:(tl + 1) * 128], keys16[:, hp, :], True, True, [qps, keys16], [p])
                            k.copy(s_sb[:, hg * 4:(hg + 1) * 4, :], p[:], [p], [s_sb], eng=("act" if hg % 2 == 0 else "dve"))
                        for hp in range(16):
                            k.fn("dve", lambda e, o=v16[:, hp, 0:8], a=s_sb[:, hp, :]: e.max(out=o, in_=a), [s_sb], [v16])
                            k.fn("dve", lambda e, o=wk[:, 0:128], a=v16[:, hp, 0:8], b=s_sb[:, hp, :]: e.match_replace(out=o, in_to_replace=a, in_values=b, imm_value=-1e30), [s_sb, v16], [wk])
                            k.fn("dve", lambda e, o=v16[:, hp, 8:16], a=wk[:, 0:128]: e.max(out=o, in_=a), [wk], [v16])
                        v16r = v16[:].rearrange("p (h two) i -> p h two i", two=2)
                        k.tt(cand[:].rearrange("p h (i j) -> p h i j", i=16),
                             v16r[:, :, 0, :].unsqueeze(3).to_broadcast([128, 8, 16, 16]),
                             v16r[:, :, 1, :].unsqueeze(2).to_broadcast([128, 8, 16, 16]), ALU.add, [v16], [cand])
                        for h in range(8):
                            k.fn("dve", lambda e, o=c16[:, h, 0:8], a=cand[:, h, :]: e.max(out=o, in_=a), [cand], [c16])
                            k.fn("dve", lambda e, o=wk[:], a=c16[:, h, 0:8], b=cand[:, h, :]: e.match_replace(out=o, in_to_replace=a, in_values=b, imm_value=-1e30), [cand, c16], [wk])
                            k.fn("dve", lambda e, o=c16[:, h, 8:16], a=wk[:]: e.max(out=o, in_=a), [wk], [c16])
                        k.ts(negm[:], v16[:, :, 0], -1.0, None, ALU.mult, None, [v16], [negm])
                        k.ts(negM[:], c16[:, :, 0], -1.0, None, ALU.mult, None, [c16], [negM])
                        for h in range(8):
                            k.act(en[:, h, :], c16[:, h, :], AF.Exp, [c16, negM], [en, Z], bias=negM[:, h:h + 1], scale=1.0, accum_out=Z[:, h:h + 1])
                        k.recip(Z[:], Z[:], [Z], [Z])
                        k.tt(th[:], en[:, :, 15], Z[:], ALU.mult, [en, Z], [th])
                        k.ts(th[:], th[:], SLACK, None, ALU.mult, None, [th], [th])
                        k.ts(cf[:], en[:, :, 15], SLACK, None, ALU.mult, None, [en], [cf])
                        k.recip(cf[:], cf[:], [cf], [cf])
                        for h in range(8):
                            k.act(E2s[:, tl, h, :], s_sb[:, 2 * h + 1, :], AF.Exp, [s_sb, negm], [E2s], bias=negm[:, 2 * h + 1:2 * h + 2], scale=1.0)
                            k.act(E1t[:, h, :], s_sb[:, 2 * h, :], AF.Exp, [s_sb, negm], [E1t], bias=negm[:, 2 * h:2 * h + 1], scale=1.0)
                            k.ts(dg16[:, tl, h, :], self.ident32[:], th[:, h:h + 1], None, ALU.mult, None, [self.ident32, th], [dg16], eng="pool")
                        k.tt(E1s[:, tl, :, :], E1t[:], cf[:].unsqueeze(2).to_broadcast([128, 8, 128]), ALU.mult, [E1t, cf], [E1s])
                k.barrier()
                if "peer_dbg" in self.dbg and su == 0:
                    for nm, tile_ in (("E1s", E1s), ("E2s", E2s)):
                        d, r = self.scratch(nm, [128, 4 * 8 * 128], F32)
                        k.dma(d, tile_[:].rearrange("p a b c -> p (a b c)"), [tile_], [r])
                with ExitStack() as sc_:
                    stgD = [k.sb(sc_, [128, 16, 128], F32, f"stgD{i}") for i in range(3)]
                    stgU = [k.sb(sc_, [128, 2048], F32, f"stgU{i}") for i in range(3)]
                    dn16 = [k.sb(sc_, [128, 16, 128], BF16, f"dn16{i}") for i in range(4)]
                    up16 = [k.sb(sc_, [128, 2048], BF16, f"up16{i}") for i in range(4)]
                    GA16 = [k.sb(sc_, [128, 512], BF16, f"GA{i}") for i in range(2)]
                    Pt = k.sb(sc_, [128, 2, 8, 128], F32, "Pt")
                    mE = [k.sb(sc_, [128, 2, 8, 128], BF16, f"mE{i}") for i in range(2)]
                    WA = [k.sb(sc_, [128, 512], BF16, f"WA{i}") for i in range(8)]
                    ev = [k.sb(sc_, [128, 512], F32, f"ev{i}") for i in range(2)]
                    pact = [k.ps(sc_, [128, 512], F32, "pact") for _ in range(2)]
                    pw = [k.ps(sc_, [128, 512], F32, "pw") for _ in range(2)]
                    po = [k.ps(sc_, [128, 512], F32, "po") for _ in range(2)]
                    cn = {"k": 0, "p": 0, "o": 0}
                    ngrp = 1 if "peer_short" in self.dbg else 32
                    for gi in range(ngrp):
                        was = []
                        for kl in range(4):
                            kap = gi * 4 + kl
                            sd, su_ = stgD[kap % 3], stgU[kap % 3]
                            dn, up = dn16[kl], up16[kl]
                            k.dma(sd[:].rearrange("p a b -> p (a b)"), I["downB"][kap * 128:(kap + 1) * 128, :], [], [sd], eng="sp")
                            k.dma(su_[:], I["up"][kap * 128:(kap + 1) * 128, :], [], [su_], eng="pool")
                            k.copy(dn[:], sd[:], [sd], [dn], eng="act")
                            k.copy(up[:], su_[:], [su_], [up], eng="act")
                        for kl in range(4):
                            kap = gi * 4 + kl
                            dn = dn16[kl]
                            pa_ = pact[cn["k"] % 2]
                            pw_ = pw[cn["k"] % 2]
                            ga = GA16[cn["k"] % 2]
                            wa = WA[cn["k"] % 8]
                            cn["k"] += 1
                            for c in range(16):
                                k.mm(pa_[:], dn[:, c, :], h2s[:, c, :], c == 0, c == 15, [dn, h2s], [pa_])
                            k.act(ga[:], pa_[:], AF.Gelu, [pa_], [ga])
                            for t2 in range(2):
                                me = mE[cn["p"] % 2]
                                cn["p"] += 1
                                k.tt(Pt[:], E2s[:, 2 * t2:2 * t2 + 2, :, :], E1s[:, 2 * t2:2 * t2 + 2, :, kap:kap + 1].to_broadcast([128, 2, 8, 128]), ALU.mult, [E2s, E1s], [Pt])
                                k.stt(me[:], Pt[:], 1.0, Pt[:], ALU.is_ge, ALU.mult, [Pt], [me])
                                for tq in range(2):
                                    tl = 2 * t2 + tq
                                    for h in range(8):
                                        k.mm(pw_[:, tl * 128:(tl + 1) * 128], me[:, tq, h, :], dg16[:, tl, h, :], h == 0, h == 7, [me, dg16], [pw_])
                            k.tt(wa[:], pw_[:], ga[:], ALU.mult, [pw_, ga], [wa])
                            was.append(wa)
                        for dc in range(16):
                            po_ = po[cn["o"] % 2]
                            e_ = ev[cn["o"] % 2]
                            cn["o"] += 1
                            for kl in range(4):
                                k.mm(po_[:], up16[kl][:, dc * 128:(dc + 1) * 128], was[kl][:], kl == 0, kl == 3, [up16[kl], was[kl]], [po_])
                            if gi == 0:
                                k.copy(accT[:, dc, :], po_[:], [po_], [accT], eng="act")
                            else:
                                k.copy(e_[:], po_[:], [po_], [e_], eng="act")
                                k.tt(accT[:, dc, :], accT[:, dc, :], e_[:], ALU.add, [accT, e_], [accT], eng="pool")
                    for dc in range(16):
                        x_ = ev[dc % 2]
                        k.dma(x_[:], x1_d[dc][:, tsl], [x1_r], [x_])
                        k.stt(x_[:], accT[:, dc, :], self.modT[:, 80 + dc:81 + dc], x_[:], ALU.mult, ALU.add, [accT, self.modT, x_], [x_])
                        k.dma(self.outT[dc * 128:(dc + 1) * 128, tsl], x_[:], [x_], [])
                k.barrier()


def _consts():
    f = np.float32
    half = 64
    freqs = (10000.0 ** (-np.arange(half, dtype=f) / f(half))).astype(f)
    ang = np.arange(S, dtype=f)[:, None] * freqs[None, :]
    cos = np.cos(ang).astype(f).T
    sin = np.sin(ang).astype(f).T
    cosT = np.concatenate([cos, cos], 0)
    sinT = np.concatenate([sin, sin], 0)
    rotm = np.zeros((128, 128), f)
    for m in range(64):
        rotm[m + 64, m] = -1.0
        rotm[m, m + 64] = 1.0
    ident = np.eye(128, dtype=f)
    t = np.arange(S)
    cst = np.arange(NCMP) * 16
    maskc = np.zeros((128, S), f)
    maskc[:NCMP] = ((cst + 31)[:, None] <= t[None, :]).astype(f)
    kk = np.arange(128)[:, None]
    tt = np.arange(512)[None, :]
    cm = np.stack([(128 * o + kk <= tt).astype(f) for o in range(4)], 1).reshape(128, 4 * 512)
    wm = np.stack([(((128 * rel + kk - tt) <= 0) & ((128 * rel + kk - tt) > -512)).astype(f)
                   for rel in range(-4, 4)], 1).reshape(128, 8 * 512)
    ex = np.zeros((32, 16, 128), f)
    for kc in range(16):
        for kq in range(128):
            ex[2 * kc + kq // 64, kc, kq] = 1.0
    ex = ex.reshape(32, 16 * 128)
    jb = np.arange(32)[None, :]
    cur = (t // 64)[:, None]
    forced = (jb == 0) | (jb == cur) | (jb == cur - 1)
    valid = (jb * 64) <= t[:, None]
    vm_ = (valid & ~forced).astype(f)
    fb_ = np.where(forced, 1e4, np.where(valid, 0.0, -1e4)).astype(f)
    vm = vm_.reshape(16, 128, 32).transpose(1, 0, 2).reshape(128, 512)
    fb = fb_.reshape(16, 128, 32).transpose(1, 0, 2).reshape(128, 512)
    sst = np.arange(32) * 64
    ov = np.maximum(np.minimum(cst[:, None] + 32, sst[None, :] + 64) - np.maximum(cst[:, None], sst[None, :]), 0)
    ovl = np.zeros((128, 32), f)
    ovl[:NCMP] = ov.astype(f) / 32.0
    return dict(cosT=cosT, sinT=sinT, rotm=rotm, ident=ident, maskc=maskc, cm=cm, wm=wm, ex=ex, vm=vm, fb=fb, ovl=ovl)


def prep_shared(inp):
    f = np.float32
    A = lambda v: np.ascontiguousarray(np.asarray(v, dtype=f))
    sh = {}
    sh["ada_w"] = A(inp["ada_w"][0])
    sh["ada_bT"] = A(inp["ada_b"][0].reshape(96, 128).T)
    sh["g1T"] = A(inp["norm_mix_g"][0].reshape(16, 128).T)
    sh["g2T"] = A(inp["norm_ffn_g"][0].reshape(16, 128).T)
    sh["w_in"] = A(inp["w_in"][0])
    sh["w_out"] = A(inp["w_out"][0])
    sh["wq"] = A(inp["peer_wq"][0])
    sh["qg"] = A(inp["q_norm_g"][0].reshape(128, 1))
    sh["kgT"] = A(inp["k_norm_g"][0].T)
    sh["kg0b"] = A(np.broadcast_to(inp["k_norm_g"][0, 0][None, :], (128, 128)))
    sh["pekT"] = A(inp["cmp_pe_k"][0].T)
    sh["pevT"] = A(inp["cmp_pe_v"][0].T)
    sh["cwk"] = A(inp["cmp_w_k"][0])
    sh["cwv"] = A(inp["cmp_w_v"][0])
    sh["gateb"] = A(np.broadcast_to(inp["gate_b"][0][None, :], (128, 24)))
    sh["convw"] = A(inp["conv_w"][0].reshape(4, 8, 128).transpose(2, 1, 0).reshape(128, 32))
    for nm, key in (("convb", "conv_b"), ("lba", "lru_ba"), ("lbi", "lru_bi"), ("lam", "lru_lam"), ("gr", "out_g_rnn")):
        sh[nm] = A(inp[key][0].reshape(8, 128).T)
    sh["gab"] = A(np.broadcast_to(inp["out_g_attn"][0][None, :], (128, 1024)))
    sh["wa"] = A(inp["lru_wa"][0])
    sh["wi"] = A(inp["lru_wi"][0])
    sh["keysT"] = A(inp["peer_keys"][0].transpose(3, 0, 1, 2).reshape(128, 16 * 128))
    sh["downB"] = A(inp["peer_down"][0].reshape(128, 128, 16, 128).transpose(0, 3, 2, 1).reshape(128 * 128, 16 * 128))
    sh["up"] = A(inp["peer_up"][0])
    sh.update(_consts())
    return sh


def prep_core(inp, b):
    f = np.float32
    return {"xT": np.ascontiguousarray(np.asarray(inp["x"][b], dtype=f).T),
            "c_col": np.ascontiguousarray(np.asarray(inp["c"][b], dtype=f).reshape(16, 128).T)}


def kernel(**inputs):
    inp = {k_: np.asarray(v) for k_, v in inputs.items()}
    sh = prep_shared(inp)
    nc = Prog().build()
    in_maps = []
    for b in range(8):
        m = dict(sh)
        m.update(prep_core(inp, b))
        in_maps.append(m)
    res = run_bass_kernel_spmd(nc, in_maps, core_ids=list(range(8)))
    out = np.stack([np.asarray(r["outT"]).T for r in res.results], 0)
    return np.ascontiguousarray(out.astype(np.float32))
```

```python
import numpy as np
from contextlib import ExitStack
import concourse.bass as bass
import concourse.mybir as mybir
from concourse.bass_utils import run_bass_kernel_spmd

F32 = mybir.dt.float32
BF16 = mybir.dt.bfloat16
AF = mybir.ActivationFunctionType
ALU = mybir.AluOpType
AX = mybir.AxisListType

D = 2048
S = 2048
NT = 16
NIN = 4632
C_Q, C_KC, C_VC, C_KS, C_VS, C_KW, C_VW, C_GL, C_XR, C_XG = 0, 1024, 1280, 1536, 1792, 2048, 2304, 2560, 2584, 3608
EPS = 1e-6
NCMP = 127
SCALE = 128 ** -0.5


class Res:
    __slots__ = ("name", "w", "r")

    def __init__(self, name=""):
        self.name = name
        self.w = None
        self.r = []


class Op:
    __slots__ = ("eng", "fn", "deps", "flag", "cval", "dma", "dsem", "dval")

    def __init__(self, eng, fn, dma=False):
        self.eng = eng
        self.fn = fn
        self.deps = []
        self.flag = False
        self.cval = 0
        self.dma = dma
        self.dsem = None
        self.dval = 0


class Sched:
    ENGS = ("pe", "act", "dve", "pool", "sp")

    def __init__(self, nc, n_dma_sems=16):
        self.nc = nc
        self.q = {e: [] for e in self.ENGS}
        self.n_dma_sems = n_dma_sems
        self.dma_rr = 0
        self.dma_last = [None] * n_dma_sems
        self.dma_cnt = [0] * n_dma_sems
        self.pending = {e: [] for e in self.ENGS}

    def _add(self, eng, fn, reads, writes, dma=False):
        op = Op(eng, fn, dma)
        deps = list(self.pending[eng])
        self.pending[eng] = []
        for r in reads:
            if r.w is not None:
                deps.append(r.w)
        for w in writes:
            if w.w is not None:
                deps.append(w.w)
            deps.extend(w.r)
        for r in reads:
            r.r.append(op)
        for w in writes:
            w.w = op
            w.r = []
        if dma:
            s = self.dma_rr
            self.dma_rr = (self.dma_rr + 1) % self.n_dma_sems
            prev = self.dma_last[s]
            if prev is not None:
                deps.append(prev)
            self.dma_last[s] = op
            self.dma_cnt[s] += 1
            op.dsem = s
            op.dval = 16 * self.dma_cnt[s]
        seen = set()
        for d in deps:
            if d is op or id(d) in seen:
                continue
            if (not d.dma) and d.eng == eng and eng == "pe":
                continue
            seen.add(id(d))
            op.deps.append(d)
            d.flag = True
        self.q[eng].append(op)
        return op

    def op(self, eng, fn, reads=(), writes=()):
        return self._add(eng, fn, list(reads), list(writes))

    def dma(self, eng, out, in_, reads=(), writes=()):
        return self._add(eng, lambda e: e.dma_start(out=out, in_=in_), list(reads), list(writes), dma=True)

    def barrier(self):
        lasts = []
        for e in self.ENGS:
            for op in reversed(self.q[e]):
                if not op.dma:
                    lasts.append(op)
                    break
        for s in range(self.n_dma_sems):
            if self.dma_last[s] is not None:
                lasts.append(self.dma_last[s])
        for e in self.ENGS:
            self.pending[e] = list(lasts)

    def emit(self):
        nc = self.nc
        with ExitStack() as st:
            esem = {e: st.enter_context(nc.semaphore(f"s_{e}")) for e in self.ENGS}
            dsem = [st.enter_context(nc.semaphore(f"d_{i}")) for i in range(self.n_dma_sems)]
            for e in self.ENGS:
                c = 0
                for op in self.q[e]:
                    if op.dma:
                        continue
                    if op.flag:
                        c += 1
                        op.cval = c
            block = st.enter_context(nc.Block())

            def run(ename, eobj):
                waited = {}
                for op in self.q[ename]:
                    need = {}
                    for d in op.deps:
                        if d.dma:
                            key, val = ("d", d.dsem), d.dval
                        else:
                            key, val = ("e", d.eng), d.cval
                        if val > need.get(key, 0):
                            need[key] = val
                    for key, val in need.items():
                        if waited.get(key, 0) >= val:
                            continue
                        waited[key] = val
                        sem = dsem[key[1]] if key[0] == "d" else esem[key[1]]
                        eobj.wait_ge(sem, val)
                    ins = op.fn(eobj)
                    if op.dma:
                        ins.then_inc(dsem[op.dsem], 16)
                    elif op.flag:
                        ins.then_inc(esem[ename], 1)
                if ename == "sp":
                    for s in range(self.n_dma_sems):
                        if self.dma_cnt[s] > 0:
                            eobj.wait_ge(dsem[s], 16 * self.dma_cnt[s])

            block.tensor(lambda e: run("pe", e))
            block.scalar(lambda e: run("act", e))
            block.vector(lambda e: run("dve", e))
            block.gpsimd(lambda e: run("pool", e))
            block.sync(lambda e: run("sp", e))


class T:
    __slots__ = ("t", "r")

    def __init__(self, t, name=""):
        self.t = t
        self.r = Res(name)

    def __getitem__(self, k):
        return self.t[k]


class K:
    def __init__(self, nc):
        self.nc = nc
        self.S = Sched(nc)
        self.uid = 0

    def sb(self, st, shape, dt, name=None):
        self.uid += 1
        n = f"{name or 't'}_{self.uid}"
        return T(st.enter_context(self.nc.sbuf_tensor(n, list(shape), dt)), n)

    def ps(self, st, shape, dt=F32, name=None):
        self.uid += 1
        n = f"{name or 'p'}_{self.uid}"
        return T(st.enter_context(self.nc.psum_tensor(n, list(shape), dt)), n)

    @staticmethod
    def _rs(xs):
        return [x.r if isinstance(x, T) else x for x in xs]

    def mm(self, out, lhsT, rhs, start, stop, R, W):
        self.S.op("pe", lambda e: e.matmul(out, lhsT, rhs, start=start, stop=stop), self._rs(R), self._rs(W))

    def act(self, out, in_, func, R, W, bias=None, scale=None, accum_out=None, eng="act"):
        kw = {}
        if bias is not None:
            kw["bias"] = bias
        if scale is not None:
            kw["scale"] = scale
        if accum_out is not None:
            kw["accum_out"] = accum_out
        self.S.op(eng, lambda e: e.activation(out, in_, func, **kw), self._rs(R), self._rs(W))

    def tt(self, out, in0, in1, op, R, W, eng="dve"):
        self.S.op(eng, lambda e: e.tensor_tensor(out, in0, in1, op), self._rs(R), self._rs(W))

    def ts(self, out, in0, s1, s2, op0, op1, R, W, eng="dve", accum_out=None):
        if accum_out is None:
            if op1 is None:
                self.S.op(eng, lambda e: e.tensor_scalar(out, in0, s1, None, op0), self._rs(R), self._rs(W))
            else:
                self.S.op(eng, lambda e: e.tensor_scalar(out, in0, s1, s2, op0, op1), self._rs(R), self._rs(W))
        else:
            self.S.op(eng, lambda e: e.tensor_scalar(out, in0, s1, s2, op0, op1, accum_out=accum_out),
                      self._rs(R), self._rs(W))

    def stt(self, out, in0, scalar, in1, op0, op1, R, W, eng="dve"):
        self.S.op(eng, lambda e: e.scalar_tensor_tensor(out, in0, scalar, in1, op0, op1), self._rs(R), self._rs(W))

    def copy(self, out, in_, R, W, eng="dve"):
        if eng == "act":
            self.S.op("act", lambda e: e.copy(out, in_), self._rs(R), self._rs(W))
        else:
            self.S.op(eng, lambda e: e.tensor_copy(out, in_), self._rs(R), self._rs(W))

    def memset(self, ap, val, W, eng="pool"):
        self.S.op(eng, lambda e: e.memset(ap, val), [], self._rs(W))

    def recip(self, out, in_, R, W):
        self.S.op("dve", lambda e: e.reciprocal(out, in_), self._rs(R), self._rs(W))

    def dma(self, out, in_, R, W, eng="sp"):
        self.S.dma(eng, out, in_, self._rs(R), self._rs(W))

    def fn(self, eng, f, R, W):
        self.S.op(eng, f, self._rs(R), self._rs(W))

    def barrier(self):
        self.S.barrier()


IN_SPECS = [
    ("xT", [D, S], F32), ("c_col", [128, 16], F32), ("ada_w", [D, 6 * D], F32), ("ada_bT", [128, 96], F32),
    ("g1T", [128, 16], F32), ("g2T", [128, 16], F32), ("w_in", [D, NIN], F32), ("w_out", [D, D], F32),
    ("wq", [D, D], F32), ("qg", [128, 1], F32), ("kgT", [128, 3], F32), ("kg0b", [128, 128], F32),
    ("pekT", [128, 32], F32), ("pevT", [128, 32], F32), ("cwk", [4096, 128], F32), ("cwv", [4096, 128], F32),
    ("gateb", [128, 24], F32), ("convw", [128, 32], F32), ("convb", [128, 8], F32), ("lba", [128, 8], F32),
    ("lbi", [128, 8], F32), ("lam", [128, 8], F32), ("gr", [128, 8], F32), ("gab", [128, 1024], F32),
    ("wa", [8, 128, 128], F32), ("wi", [8, 128, 128], F32), ("keysT", [128, 16 * 128], F32),
    ("downB", [128 * 128, 16 * 128], F32), ("up", [16384, D], F32),
    ("cosT", [128, S], F32), ("sinT", [128, S], F32), ("rotm", [128, 128], F32), ("ident", [128, 128], F32),
    ("maskc", [128, S], F32), ("cm", [128, 4 * 512], F32), ("wm", [128, 8 * 512], F32),
    ("ex", [32, 16 * 128], F32), ("vm", [128, 16 * 32], F32), ("fb", [128, 16 * 32], F32), ("ovl", [128, 32], F32),
]


class Prog:
    def __init__(self, stop_after=99, dbg=()):
        self.nc = nc = bass.Bass("TRN2", target_bir_lowering=False)
        self.k = K(nc)
        self.stop_after = stop_after
        self.dbg = set(dbg)
        for d_ in self.dbg:
            if d_.startswith("qkind="):
                self.qkind = d_.split("=")[1]
        self.I = {}
        for name, shape, dt in IN_SPECS:
            self.I[name] = nc.dram_tensor(name, shape, dt, kind="ExternalInput").ap()
        self.outT = nc.dram_tensor("outT", [D, S], F32, kind="ExternalOutput").ap()
        self.scr = {}

    def scratch(self, name, shape, dt):
        kind = "ExternalOutput" if name in self.dbg else "Internal"
        ap = self.nc.dram_tensor(name, list(shape), dt, kind=kind).ap()
        self.scr[name] = (ap, Res(name))
        return ap, self.scr[name][1]

    def build(self):
        k = self.k
        with ExitStack() as g:
            self.g = g
            self.modT = k.sb(g, [128, 96], F32, "modT")
            self.G1 = k.sb(g, [128, 16], F32, "G1")
            self.G2 = k.sb(g, [128, 16], F32, "G2")
            self.eps_t = k.sb(g, [128, 1], F32, "eps")
            self.ones16 = k.sb(g, [128, 128], BF16, "ones16")
            self.ident16 = k.sb(g, [128, 128], BF16, "ident16")
            k.memset(self.eps_t[:], EPS, [self.eps_t])
            k.memset(self.ones16[:], 1.0, [self.ones16])
            self.ident32 = k.sb(g, [128, 128], F32, "ident32")
            k.dma(self.ident32[:], self.I["ident"], [], [self.ident32])
            k.copy(self.ident16[:], self.ident32[:], [self.ident32], [self.ident16])
            names = ["p0_mod", "p12_proj", "p3_cmp", "p4_attn", "p5_rnn", "p6_out", "p7_peer"]
            phases = [getattr(self, n) for n in names if hasattr(self, n)]
            for i, ph in enumerate(phases):
                if i > self.stop_after:
                    break
                ph()
                k.barrier()
            k.S.emit()
        return self.nc

    def p0_mod(self):
        k, I = self.k, self.I
        with ExitStack() as st:
            cc = k.sb(st, [128, 16], F32, "cc")
            sc = k.sb(st, [128, 16], F32, "sc")
            abT = k.sb(st, [128, 96], F32, "abT")
            g1 = k.sb(st, [128, 16], F32, "g1")
            g2 = k.sb(st, [128, 16], F32, "g2")
            tmp = k.sb(st, [128, 16], F32, "tmp")
            wb = [k.sb(st, [128, 16, 512], F32, f"adaw{i}") for i in range(2)]
            pm = k.ps(st, [128, 96], F32, "pm")
            k.dma(cc[:], I["c_col"], [], [cc])
            k.dma(abT[:], I["ada_bT"], [], [abT])
            k.dma(g1[:], I["g1T"], [], [g1])
            k.dma(g2[:], I["g2T"], [], [g2])
            k.act(sc[:], cc[:], AF.Silu, [cc], [sc])
            wv = I["ada_w"].rearrange("(k p) n -> p k n", p=128)
            for gi in range(24):
                b = wb[gi % 2]
                k.dma(b[:], wv[:, :, gi * 512:(gi + 1) * 512], [], [b], eng=("sp" if gi % 2 == 0 else "pool"))
                for j in range(4):
                    col = gi * 4 + j
                    for kk in range(16):
                        k.mm(pm[:, col:col + 1], b[:, kk, j * 128:(j + 1) * 128], sc[:, kk:kk + 1],
                             kk == 0, kk == 15, [b, sc], [pm])
            k.tt(self.modT[:], pm[:], abT[:], ALU.add, [pm, abT], [self.modT])
            k.ts(tmp[:], self.modT[:, 16:32], 1.0, None, ALU.add, None, [self.modT], [tmp])
            k.tt(self.G1[:], tmp[:], g1[:], ALU.mult, [tmp, g1], [self.G1])
            k.ts(tmp[:], self.modT[:, 64:80], 1.0, None, ALU.add, None, [self.modT, self.G1], [tmp])
            k.tt(self.G2[:], tmp[:], g2[:], ALU.mult, [tmp, g2], [self.G2])
            if "modT" in self.dbg:
                d, r = self.scratch("modT", [128, 96], F32)
                k.dma(d, self.modT[:], [self.modT], [r])

    def rms_stats_fm(self, st, loader, nchunks, width, scale_div, name):
        k = self.k
        rstd = k.sb(st, [128, S], F32, name)
        with ExitStack() as s2:
            xb = [k.sb(s2, [128, S], F32, "xld") for _ in range(2)]
            sq = [k.sb(s2, [128, S], BF16, "sq") for _ in range(2)]
            pss = [k.ps(s2, [128, 512], F32, "pss") for _ in range(4)]
            for kk in range(nchunks):
                xt, sqt = xb[kk % 2], sq[kk % 2]
                loader(kk, xt)
                k.act(sqt[:], xt[:], AF.Square, [xt], [sqt])
                for tg in range(4):
                    k.mm(pss[tg][:], self.ones16[:], sqt[:, tg * 512:(tg + 1) * 512], kk == 0, kk == nchunks - 1,
                         [self.ones16, sqt], [pss[tg]])
            for tg in range(4):
                sl = slice(tg * 512, (tg + 1) * 512)
                k.act(rstd[:, sl], pss[tg][:], AF.Sqrt, [pss[tg], self.eps_t], [rstd], bias=self.eps_t[:], scale=1.0 / scale_div)
            k.recip(rstd[:], rstd[:], [rstd], [rstd])
        k.barrier()
        return rstd

    def p12_proj(self):
        k, I = self.k, self.I
        xTv = I["xT"].rearrange("(k p) t -> k p t", p=128)
        qT_d, qT_r = self.scratch("qT", [8, 128, S], BF16)
        kcT_d, kcT_r = self.scratch("kcT", [2, 128, S], BF16)
        vcT_d, vcT_r = self.scratch("vcT", [2, 128, S], BF16)
        ksT_d, ksT_r = self.scratch("ksT", [2, 128, S], BF16)
        kwT_d, kwT_r = self.scratch("kwT", [2, 128, S], BF16)
        vs_d, vs_r = self.scratch("vs", [S, 256], BF16)
        vw_d, vw_r = self.scratch("vw", [S, 256], BF16)
        gt_d, gt_r = self.scratch("gates", [128, 16 * 24], F32)
        xrT_d, xrT_r = self.scratch("xrT", [8, 128, S], F32)
        xgT_d, xgT_r = self.scratch("xgT", [8, 128, S], F32)
        with ExitStack() as st:
            hT = k.sb(st, [128, 16, S], BF16, "hT")
            with ExitStack() as s1:
                rstd = self.rms_stats_fm(s1, lambda kk, dst: k.dma(dst[:], xTv[kk], [], [dst]), 16, S, float(D), "rstd1")
                xb = [k.sb(s1, [128, S], F32, "xld2") for _ in range(2)]
                tmp = [k.sb(s1, [128, S], F32, "tmp") for _ in range(2)]
                for kk in range(16):
                    xt, tp = xb[kk % 2], tmp[kk % 2]
                    k.dma(xt[:], xTv[kk], [], [xt])
                    k.stt(tp[:], xt[:], self.G1[:, kk:kk + 1], rstd[:], ALU.mult, ALU.mult, [xt, self.G1, rstd], [tp])
                    k.act(hT[:, kk, :], tp[:], AF.Identity, [tp, self.modT], [hT], bias=self.modT[:, kk:kk + 1], scale=1.0)
            if "hT" in self.dbg:
                d, r = self.scratch("hT", [16, 128, S], BF16)
                k.dma(d.rearrange("k p t -> p k t"), hT[:], [hT], [r])
            k.barrier()
            if "stop_p1" in self.dbg:
                return
            with ExitStack() as s2:
                wb = [k.sb(s2, [128, 16, 544], BF16, f"win{i}") for i in range(2)]
                cosT = k.sb(s2, [128, S], F32, "cosT")
                sinT = k.sb(s2, [128, S], F32, "sinT")
                rot16 = k.sb(s2, [128, 128], BF16, "rot16")
                gq = k.sb(s2, [128, 4], F32, "gq")
                gateb = k.sb(s2, [128, 24], F32, "gateb")
                k.dma(cosT[:], I["cosT"], [], [cosT])
                k.dma(sinT[:], I["sinT"], [], [sinT])
                rot32 = k.sb(s2, [128, 128], F32, "rot32")
                k.dma(rot32[:], I["rotm"], [], [rot32])
                k.copy(rot16[:], rot32[:], [rot32], [rot16])
                k.dma(gq[:, 0:1], I["qg"], [], [gq])
                k.dma(gq[:, 1:4], I["kgT"], [], [gq])
                k.dma(gateb[:], I["gateb"], [], [gateb])
                pz = [k.ps(s2, [128, 512], F32, "pz") for _ in range(2)]
                pss = [k.ps(s2, [128, 512], F32, "pss2") for _ in range(2)]
                prot = [k.ps(s2, [128, 512], F32, "prot") for _ in range(2)]
                ptm = [k.ps(s2, [128, 512], F32, "ptm") for _ in range(2)]
                sq = [k.sb(s2, [128, 512], BF16, "sq2") for _ in range(2)]
                rs = [k.sb(s2, [128, 512], F32, "rs") for _ in range(2)]
                xn = [k.sb(s2, [128, 512], F32, "xn") for _ in range(2)]
                xn16 = [k.sb(s2, [128, 512], BF16, "xn16") for _ in range(2)]
                t1 = [k.sb(s2, [128, 512], F32, "t1") for _ in range(2)]
                t2 = [k.sb(s2, [128, 512], F32, "t2") for _ in range(2)]
                o16 = [k.sb(s2, [128, S], BF16, "o16") for _ in range(2)]
                o32 = [k.sb(s2, [128, S], F32, "o32") for _ in range(2)]
                vtm = k.sb(s2, [128, 16, 256], BF16, "vtm")
                gtm = k.sb(s2, [128, 16, 24], F32, "gtm")
                gtmp = k.sb(s2, [128, 24], F32, "gtmp")
                wv = I["w_in"].rearrange("(k p) n -> p k n", p=128)
                cnt = {"c": 0, "i": 0}

                def fm_chunk(b, co, kind, gcol, dst_ap, dst_res):
                    ci = cnt["c"]
                    cnt["c"] += 1
                    ob = (o32 if kind == "f32" else o16)[ci % 2]
                    for tg in range(4):
                        i = cnt["i"]
                        cnt["i"] += 1
                        p = pz[i % 2]
                        sl = slice(tg * 512, (tg + 1) * 512)
                        for kk in range(16):
                            k.mm(p[:], b[:, kk, co:co + 128], hT[:, kk, sl], kk == 0, kk == 15, [b, hT], [p])
                        if kind in ("f32", "bf16"):
                            k.copy(ob[:, sl], p[:], [p], [ob], eng=("act" if tg % 2 == 0 else "dve"))
                            continue
                        a, a16 = xn[i % 2], xn16[i % 2]
                        if kind == "normrope":
                            sqt, pst, rst = sq[i % 2], pss[i % 2], rs[i % 2]
                            k.act(sqt[:], p[:], AF.Square, [p], [sqt])
                            k.mm(pst[:], self.ones16[:], sqt[:], True, True, [self.ones16, sqt], [pst])
                            k.act(rst[:], pst[:], AF.Sqrt, [pst, self.eps_t], [rst], bias=self.eps_t[:], scale=1.0 / 128.0)
                            k.recip(rst[:], rst[:], [rst], [rst])
                            k.stt(a[:], p[:], gq[:, gcol:gcol + 1], rst[:], ALU.mult, ALU.mult, [p, gq, rst], [a])
                        else:
                            k.copy(a[:], p[:], [p], [a], eng="dve")
                        k.copy(a16[:], a[:], [a], [a16], eng="act")
                        pr = prot[i % 2]
                        k.mm(pr[:], rot16[:], a16[:], True, True, [rot16, a16], [pr])
                        k.tt(t1[i % 2][:], a[:], cosT[:, sl], ALU.mult, [a, cosT], [t1[i % 2]])
                        k.tt(t2[i % 2][:], pr[:], sinT[:, sl], ALU.mult, [pr, sinT], [t2[i % 2]])
                        k.tt(ob[:, sl], t1[i % 2][:], t2[i % 2][:], ALU.add, [t1[i % 2], t2[i % 2]], [ob], eng="pool")
                    k.dma(dst_ap, ob[:], [ob], [dst_res])

                stg = [k.sb(s2, [128, 16, 272], F32, f"stg{i}") for i in range(2)]
                lcnt = {"g": 0, "s": 0}

                def load_group(c0, c1):
                    b = wb[lcnt["g"] % 2]
                    lcnt["g"] += 1
                    w = c1 - c0
                    pieces = [(a, min(a + 256, w)) for a in range(0, w, 256)]
                    for (a0, a1) in pieces:
                        sg = stg[lcnt["s"] % 2]
                        lcnt["s"] += 1
                        k.dma(sg[:, :, 0:a1 - a0], wv[:, :, c0 + a0:c0 + a1], [], [sg], eng=("sp" if lcnt["s"] % 2 else "pool"))
                        k.copy(b[:, :, a0:a1], sg[:, :, 0:a1 - a0], [sg], [b], eng="pool")
                    return b

                def tm_block(b, co, width, post):
                    for tt_ in range(16):
                        p = ptm[tt_ % 2]
                        for kk in range(16):
                            k.mm(p[:, 0:width], hT[:, kk, tt_ * 128:(tt_ + 1) * 128], b[:, kk, co:co + width], kk == 0, kk == 15, [b, hT], [p])
                        post(tt_, p)

                b = load_group(0, 512)
                for j in range(4):
                    fm_chunk(b, j * 128, self.qkind if hasattr(self, "qkind") else "normrope", 0, qT_d[j], qT_r)
                b = load_group(512, 1024)
                for j in range(4):
                    fm_chunk(b, j * 128, self.qkind if hasattr(self, "qkind") else "normrope", 0, qT_d[4 + j], qT_r)
                if "stop_g0" in self.dbg:
                    return
                b = load_group(1024, 1536)
                for j in range(2):
                    fm_chunk(b, j * 128, "rope", 0, kcT_d[j], kcT_r)
                for j in range(2):
                    fm_chunk(b, 256 + j * 128, "bf16", 0, vcT_d[j], vcT_r)
                b = load_group(1536, 2048)
                for j in range(2):
                    fm_chunk(b, j * 128, "normrope", 2, ksT_d[j], ksT_r)
                tm_block(b, 256, 256, lambda tt_, p: k.copy(vtm[:, tt_, :], p[:, 0:256], [p], [vtm], eng=("act" if tt_ % 2 == 0 else "dve")))
                k.dma(vs_d.rearrange("(t p) c -> p t c", p=128), vtm[:], [vtm], [vs_r])
                if "stop_g1" in self.dbg:
                    return
                if "rep_g3" in self.dbg:
                    b = load_group(1536, 2048)
                    for j in range(2):
                        fm_chunk(b, j * 128, "normrope", 2, ksT_d[j], ksT_r)
                    tm_block(b, 256, 256, lambda tt_, p: k.copy(vtm[:, tt_, :], p[:, 0:256], [p], [vtm], eng=("act" if tt_ % 2 == 0 else "dve")))
                    k.dma(vs_d.rearrange("(t p) c -> p t c", p=128), vtm[:], [vtm], [vs_r])
                    return
                b = load_group(2048, 2560)
                for j in range(2):
                    fm_chunk(b, j * 128, "normrope", 3, kwT_d[j], kwT_r)
                tm_block(b, 256, 256, lambda tt_, p: k.copy(vtm[:, tt_, :], p[:, 0:256], [p], [vtm], eng=("act" if tt_ % 2 == 0 else "dve")))
                b = load_group(2560, 2584)

                def post_g(tt_, p):
                    k.tt(gtmp[:], p[:, 0:24], gateb[:], ALU.add, [p, gateb], [gtmp])
                    k.act(gtmp[:], gtmp[:], AF.Exp, [gtmp], [gtmp], scale=-1.0)
                    k.ts(gtmp[:], gtmp[:], 1.0, None, ALU.add, None, [gtmp], [gtmp])
                    k.recip(gtm[:, tt_, :], gtmp[:], [gtmp], [gtm])
                tm_block(b, 0, 24, post_g)
                k.dma(vw_d.rearrange("(t p) c -> p t c", p=128), vtm[:], [vtm], [vw_r])
                k.dma(gt_d, gtm[:].rearrange("p t c -> p (t c)"), [gtm], [gt_r])
                if "stop_g2" in self.dbg:
                    return
                for half in range(2):
                    b = load_group(C_XR + half * 512, C_XR + (half + 1) * 512)
                    for j in range(4):
                        fm_chunk(b, j * 128, "f32", 0, xrT_d[half * 4 + j], xrT_r)
                for half in range(2):
                    b = load_group(C_XG + half * 512, C_XG + (half + 1) * 512)
                    for j in range(4):
                        fm_chunk(b, j * 128, "f32", 0, xgT_d[half * 4 + j], xgT_r)

    def p3_cmp(self):
        k, I = self.k, self.I
        kcT_d = self.scr["kcT"][0]
        vcT_d = self.scr["vcT"][0]
        kcmpT_d, kcmpT_r = self.scratch("kcmpT", [128, 2, 128], BF16)
        vcmp_d, vcmp_r = self.scratch("vcmp", [128, 2, 162], BF16)
        with ExitStack() as st:
            src = k.sb(st, [128, 4, S], BF16, "cmpsrc")
            wst = k.sb(st, [128, 32, 128], F32, "wst")
            w16 = [k.sb(st, [128, 32, 128], BF16, f"w16{i}") for i in range(2)]
            pe32 = k.sb(st, [128, 2, 32], F32, "pe32")
            peB = [k.sb(st, [128, 32, 127], BF16, f"peB{i}") for i in range(2)]
            kg0b = k.sb(st, [128, 128], F32, "kg0b")
            ovl = k.sb(st, [128, 32], F32, "ovl")
            ss = k.sb(st, [128, 1], F32, "ss")
            junk = k.sb(st, [128, 128], F32, "junk")
            kn16 = k.sb(st, [128, 128], BF16, "kn16")
            kT16 = k.sb(st, [128, 2, 128], BF16, "kT16")
            va16 = k.sb(st, [128, 2, 162], BF16, "va16")
            pc = [k.ps(st, [128, 128], F32, "pc") for _ in range(2)]
            ptr = k.ps(st, [128, 128], F32, "ptr")
            for g in range(2):
                k.dma(src[:, g, :], kcT_d[g], [self.scr["kcT"][1]], [src])
                k.dma(src[:, 2 + g, :], vcT_d[g], [self.scr["vcT"][1]], [src])
            k.dma(pe32[:, 0, :], I["pekT"], [], [pe32])
            k.dma(pe32[:, 1, :], I["pevT"], [], [pe32])
            k.dma(kg0b[:], I["kg0b"], [], [kg0b])
            k.dma(ovl[:], I["ovl"], [], [ovl])
            k.memset(kT16[:], 0.0, [kT16])
            k.memset(va16[:], 0.0, [va16])
            for kv, wname in enumerate(("cwk", "cwv")):
                k.dma(wst[:], I[wname].rearrange("(l d) o -> d l o", d=128), [], [wst])
                k.copy(w16[kv][:], wst[:], [wst], [w16[kv]], eng="pool")
                k.copy(peB[kv][:], pe32[:, kv, :].unsqueeze(2).to_broadcast([128, 32, 127]), [pe32], [peB[kv]])
            for kv in range(2):
                for g in range(2):
                    p = pc[(kv * 2 + g) % 2]
                    for l in range(32):
                        k.mm(p[0:127, :], src[:, kv * 2 + g, l:l + 16 * 126 + 1:16], w16[kv][:, l, :], l == 0, False, [src, w16[kv]], [p])
                    for l in range(32):
                        k.mm(p[0:127, :], peB[kv][:, l, :], w16[kv][:, l, :], False, l == 31, [peB[kv], w16[kv]], [p])
                    if kv == 0:
                        k.act(junk[0:127, :], p[0:127, :], AF.Square, [p], [junk, ss], accum_out=ss[0:127, :])
                        k.act(ss[0:127, :], ss[0:127, :], AF.Sqrt, [ss, self.eps_t], [ss], bias=self.eps_t[0:127, :], scale=1.0 / 128.0)
                        k.recip(ss[0:127, :], ss[0:127, :], [ss], [ss])
                        k.stt(kn16[0:127, :], p[0:127, :], ss[0:127, :], kg0b[0:127, :], ALU.mult, ALU.mult, [p, ss, kg0b], [kn16])
                        k.mm(ptr[:, 0:127], kn16[0:127, :], self.ident16[0:127, 0:127], True, True, [kn16, self.ident16], [ptr])
                        k.copy(kT16[:, g, 0:127], ptr[:, 0:127], [ptr], [kT16])
                    else:
                        k.copy(va16[0:127, g, 0:128], p[0:127, :], [p], [va16])
                        k.memset(va16[0:127, g, 128:129], 1.0, [va16])
                        k.copy(va16[0:127, g, 129:161], ovl[0:127, :], [ovl], [va16])
            k.dma(kcmpT_d, kT16[:], [kT16], [kcmpT_r])
            k.dma(vcmp_d, va16[:], [va16], [vcmp_r])

    def p4_attn(self):
        k, I = self.k, self.I
        sc = self.scr
        yT_d, yT_r = self.scratch("yT", [16, 128, S], BF16)
        with ExitStack() as st:
            qT = k.sb(st, [128, 8, S], BF16, "qT")
            ksT = k.sb(st, [128, 2, S], BF16, "ksT")
            kwT = k.sb(st, [128, 2, S], BF16, "kwT")
            kcT = k.sb(st, [128, 2, 128], BF16, "kcT")
            vsa = k.sb(st, [128, 16, 2, 130], BF16, "vsa")
            vwa = k.sb(st, [128, 16, 2, 130], BF16, "vwa")
            vca = k.sb(st, [128, 2, 162], BF16, "vca")
            gts = k.sb(st, [128, 16, 24], F32, "gts")
            maskc = k.sb(st, [128, S], F32, "maskc")
            cm = k.sb(st, [128, 4, 512], F32, "cm")
            wm = k.sb(st, [128, 8, 512], F32, "wm")
            ex32 = k.sb(st, [32, 16, 128], F32, "ex32")
            ex16 = k.sb(st, [32, 16, 128], BF16, "ex16")
            vm = k.sb(st, [128, 16, 32], F32, "vm")
            fb = k.sb(st, [128, 16, 32], F32, "fb")
            gab = k.sb(st, [128, 1024], F32, "gab")
            for j in range(8):
                k.dma(qT[:, j, :], sc["qT"][0][j], [sc["qT"][1]], [qT], eng=("sp" if j % 2 else "pool"))
            for g in range(2):
                k.dma(ksT[:, g, :], sc["ksT"][0][g], [sc["ksT"][1]], [ksT])
                k.dma(kwT[:, g, :], sc["kwT"][0][g], [sc["kwT"][1]], [kwT])
            k.dma(kcT[:], sc["kcmpT"][0], [sc["kcmpT"][1]], [kcT])
            k.dma(vca[:], sc["vcmp"][0], [sc["vcmp"][1]], [vca])
            k.memset(vsa[:], 1.0, [vsa])
            k.memset(vwa[:], 1.0, [vwa])
            for g in range(2):
                k.dma(vsa[:, :, g, 0:128], sc["vs"][0].rearrange("(c p) x -> p c x", p=128)[:, :, g * 128:(g + 1) * 128], [sc["vs"][1]], [vsa])
                k.dma(vwa[:, :, g, 0:128], sc["vw"][0].rearrange("(c p) x -> p c x", p=128)[:, :, g * 128:(g + 1) * 128], [sc["vw"][1]], [vwa])
            k.dma(gts[:].rearrange("p t c -> p (t c)"), sc["gates"][0], [sc["gates"][1]], [gts])
            k.dma(maskc[:], I["maskc"], [], [maskc])
            k.dma(cm[:].rearrange("p a b -> p (a b)"), I["cm"], [], [cm])
            k.dma(wm[:].rearrange("p a b -> p (a b)"), I["wm"], [], [wm])
            k.dma(ex32[:].rearrange("p a b -> p (a b)"), I["ex"], [], [ex32])
            k.copy(ex16[:], ex32[:], [ex32], [ex16])
            k.dma(vm[:].rearrange("p a b -> p (a b)"), I["vm"], [], [vm])
            k.dma(fb[:].rearrange("p a b -> p (a b)"), I["fb"], [], [fb])
            k.dma(gab[:], I["gab"], [], [gab])
            O = k.sb(st, [128, 4, 1024], F32, "O")
            imp = [k.sb(st, [128, 32], F32, f"imp{i}") for i in range(4)]
            e16 = [k.sb(st, [128, 512], BF16, f"e16{i}") for i in range(3)]
            p16 = [k.sb(st, [128, 512], BF16, f"p16{i}") for i in range(3)]
            mskS = k.sb(st, [128, 16, 512], BF16, "mskS")
            selT16 = k.sb(st, [32, 512], BF16, "selT16")
            sel16 = [k.sb(st, [128, 32], BF16, f"sel16{i}") for i in range(2)]
            imp2 = [k.sb(st, [128, 32], F32, f"imp2{i}") for i in range(2)]
            wk = [k.sb(st, [128, 32], F32, f"wk{i}") for i in range(2)]
            m8 = [k.sb(st, [128, 16], F32, f"m8{i}") for i in range(2)]
            den = [k.sb(st, [128, 2], F32, f"den{i}") for i in range(4)]
            ssq = k.sb(st, [128, 1], F32, "ssq")
            junk = k.sb(st, [128, 1024], F32, "junk4")
            yn16 = k.sb(st, [128, 1024], BF16, "yn16")
            yT16 = [k.sb(st, [128, 512], BF16, f"yT16{i}") for i in range(2)]
            pS = [k.ps(st, [128, 512], F32, "pS") for _ in range(2)]
            pA = k.ps(st, [128, 512], F32, "pA")
            pACC = [k.ps(st, [128, 512], F32, "pACC") for _ in range(4)]
            pM = [k.ps(st, [128, 512], F32, "pM")] * 2
            pT = pA
            cnt = {"s": 0, "e": 0, "m": 0, "d": 0, "y": 0, "sel": 0}

            def finish_head(sub, hd, acc_ap, den_ap, tt_, gcol, first, Rp):
                dn = den[cnt["d"] % 4]
                cnt["d"] += 1
                k.ts(dn[:, 0:1], den_ap, 1e-30, None, ALU.max, None, Rp, [dn])
                k.recip(dn[:, 0:1], dn[:, 0:1], [dn], [dn])
                k.tt(dn[:, 1:2], dn[:, 0:1], gts[:, tt_, gcol:gcol + 1], ALU.mult, [dn, gts], [dn])
                osl = O[:, sub, hd * 128:(hd + 1) * 128]
                if first:
                    k.ts(osl, acc_ap, dn[:, 1:2], None, ALU.mult, None, Rp + [dn], [O])
                else:
                    k.stt(osl, acc_ap, dn[:, 1:2], osl, ALU.mult, ALU.add, Rp + [dn, O], [O])
                return dn

            def run_steps(i, steps):
                qsl = slice(i * 512, (i + 1) * 512)
                state = {}

                def s0(n):
                    keyT, vaug, g, r, kc, mask_of, first, last, gbranch = steps[n]
                    ps_ = pS[cnt["s"] % 2]
                    cnt["s"] += 1
                    k.mm(ps_[:], keyT[:, g, kc * 128:(kc + 1) * 128], qT[:, g * 4 + r, qsl], True, True, [keyT, qT], [ps_])
                    state[n] = ps_

                def s12(n):
                    keyT, vaug, g, r, kc, mask_of, first, last, gbranch = steps[n]
                    ps_ = state.pop(n)
                    hd = g * 4 + r
                    e = e16[cnt["e"] % 3]
                    p_ = p16[cnt["e"] % 3]
                    cnt["e"] += 1
                    k.act(e[:], ps_[:], AF.Exp, [ps_], [e], scale=SCALE)
                    mk, mkR, eng = mask_of(kc)
                    k.tt(p_[:], e[:], mk, ALU.mult, [e, mkR], [p_], eng=eng)
                    for sub in range(4):
                        acc = pACC[sub]
                        k.mm(acc[:, 0:129], p_[:, sub * 128:(sub + 1) * 128], vaug[:, kc, g, 0:129], first, last, [p_, vaug], [acc])
                    if last:
                        for sub in range(4):
                            acc = pACC[sub]
                            finish_head(sub, hd, acc[:, 0:128], acc[:, 128:129], i * 4 + sub, g * 12 + r * 3 + gbranch, False, [acc])

                s0(0)
                for n in range(len(steps)):
                    if n + 1 < len(steps):
                        s0(n + 1)
                    s12(n)

            for i in range(4):
                qsl = slice(i * 512, (i + 1) * 512)
                for g in range(2):
                    for r in range(4):
                        hd = g * 4 + r
                        ps_ = pS[cnt["s"] % 2]
                        cnt["s"] += 1
                        k.mm(ps_[0:127, :], kcT[:, g, 0:127], qT[:, hd, qsl], True, True, [kcT, qT], [ps_])
                        e = e16[cnt["e"] % 3]
                        p_ = p16[cnt["e"] % 3]
                        cnt["e"] += 1
                        k.act(e[0:127, :], ps_[0:127, :], AF.Exp, [ps_], [e], scale=SCALE)
                        k.tt(p_[0:127, :], e[0:127, :], maskc[0:127, qsl], ALU.mult, [e, maskc], [p_])
                        for sub in range(4):
                            k.mm(pA[:, 0:161], p_[0:127, sub * 128:(sub + 1) * 128], vca[0:127, g, 0:161], True, True, [p_, vca], [pA])
                            dn = finish_head(sub, hd, pA[:, 0:128], pA[:, 128:129], i * 4 + sub, g * 12 + r * 3, True, [pA])
                            if r == 0:
                                k.ts(imp[sub][:], pA[:, 129:161], dn[:, 0:1], None, ALU.mult, None, [pA, dn], [imp[sub]])
                            else:
                                k.stt(imp[sub][:], pA[:, 129:161], dn[:, 0:1], imp[sub][:], ALU.mult, ALU.add, [pA, dn, imp[sub]], [imp[sub]])
                    psel = pM[cnt["m"] % 2]
                    cnt["m"] += 1
                    for sub in range(4):
                        tt_ = i * 4 + sub
                        j = cnt["sel"] % 2
                        cnt["sel"] += 1
                        k.tt(imp2[j][:], imp[sub][:], vm[:, tt_, :], ALU.mult, [imp[sub], vm], [imp2[j]])
                        k.tt(imp2[j][:], imp2[j][:], fb[:, tt_, :], ALU.add, [imp2[j], fb], [imp2[j]])
                        k.fn("dve", lambda e, o=m8[j][:, 0:8], a=imp2[j][:]: e.max(out=o, in_=a), [imp2[j]], [m8[j]])
                        k.fn("dve", lambda e, o=wk[j][:], a=m8[j][:, 0:8], b=imp2[j][:]: e.match_replace(out=o, in_to_replace=a, in_values=b, imm_value=-1e30), [imp2[j], m8[j]], [wk[j]])
                        k.fn("dve", lambda e, o=m8[j][:, 8:16], a=wk[j][:]: e.max(out=o, in_=a), [wk[j]], [m8[j]])
                        k.ts(sel16[j][:], imp2[j][:], m8[j][:, 15:16], None, ALU.is_ge, None, [imp2[j], m8[j]], [sel16[j]])
                        k.mm(psel[0:32, sub * 128:(sub + 1) * 128], sel16[j][:], self.ident16[:], True, True, [sel16[j], self.ident16], [psel])
                    k.copy(selT16[:], psel[0:32, :], [psel], [selT16], eng="act")
                    nkc = 4 * i + 4
                    for kc in range(nkc):
                        pm_ = pM[cnt["m"] % 2]
                        cnt["m"] += 1
                        k.mm(pm_[:], ex16[:, kc, :], selT16[:], True, True, [ex16, selT16], [pm_])
                        if kc >= 4 * i:
                            k.tt(mskS[:, kc, :], pm_[:], cm[:, kc - 4 * i, :], ALU.mult, [pm_, cm], [mskS])
                        else:
                            k.copy(mskS[:, kc, :], pm_[:], [pm_], [mskS], eng="act")
                    steps = []
                    for r in range(4):
                        for kc in range(nkc):
                            steps.append((ksT, vsa, g, r, kc, (lambda kc_: (mskS[:, kc_, :], mskS, "pool")), kc == 0, kc == nkc - 1, 1))
                    for r in range(4):
                        chunks = list(range(max(0, 4 * i - 4), 4 * i + 4))
                        for kc in chunks:
                            steps.append((kwT, vwa, g, r, kc, (lambda kc_, i_=i: (wm[:, kc_ - 4 * i_ + 4, :], wm, "dve")), kc == chunks[0], kc == chunks[-1], 2))
                    run_steps(i, steps)
                yt = yT16[i % 2]
                for c in range(8):
                    pass
                ytiles = []
                for sub in range(4):
                    k.act(junk[:], O[:, sub, :], AF.Square, [O], [junk, ssq], accum_out=ssq[:])
                    k.act(ssq[:], ssq[:], AF.Sqrt, [ssq, self.eps_t], [ssq], bias=self.eps_t[:], scale=1.0 / 1024.0)
                    k.recip(ssq[:], ssq[:], [ssq], [ssq])
                    k.stt(yn16[:], O[:, sub, :], ssq[:], gab[:], ALU.mult, ALU.mult, [O, ssq, gab], [yn16])
                    for half in range(2):
                        for cc in range(4):
                            c = half * 4 + cc
                            k.mm(pT[:, cc * 128:(cc + 1) * 128], yn16[:, c * 128:(c + 1) * 128], self.ident16[:], True, True, [yn16, self.ident16], [pT])
                        dst = k.sb(st, [128, 4, 128], BF16, "ytmp") if False else None
                        yb = yT16[cnt["y"] % 2]
                        cnt["y"] += 1
                        k.copy(yb[:], pT[:], [pT], [yb], eng=("act" if half == 0 else "dve"))
                        for cc in range(4):
                            c = half * 4 + cc
                            k.dma(yT_d[c][:, i * 512 + sub * 128:i * 512 + (sub + 1) * 128], yb[:, cc * 128:(cc + 1) * 128], [yb], [yT_r])
            if "O_dbg" in self.dbg:
                pass

    def p5_rnn(self):
        k, I = self.k, self.I
        sc = self.scr
        yT_d, yT_r = sc["yT"]
        xr_d, xr_r = sc["xrT"]
        xg_d, xg_r = sc["xgT"]
        with ExitStack() as st:
            cw = k.sb(st, [128, 8, 4], F32, "cw")
            cb = k.sb(st, [128, 8], F32, "cb")
            nba = k.sb(st, [128, 8], F32, "nba")
            nbi = k.sb(st, [128, 8], F32, "nbi")
            lam = k.sb(st, [128, 8], F32, "lam")
            clam = k.sb(st, [128, 8], F32, "clam")
            gr = k.sb(st, [128, 8], F32, "gr")
            wst = k.sb(st, [128, 2, 128], F32, "wst5")
            w16 = [k.sb(st, [128, 2, 128], BF16, f"w165{i}") for i in range(2)]
            k.dma(cw[:].rearrange("p a b -> p (a b)"), I["convw"], [], [cw])
            k.dma(cb[:], I["convb"], [], [cb])
            k.dma(nba[:], I["lba"], [], [nba])
            k.dma(nbi[:], I["lbi"], [], [nbi])
            k.dma(lam[:], I["lam"], [], [lam])
            k.dma(gr[:], I["gr"], [], [gr])
            k.ts(nba[:], nba[:], -1.0, None, ALU.mult, None, [nba], [nba])
            k.ts(nbi[:], nbi[:], -1.0, None, ALU.mult, None, [nbi], [nbi])
            k.act(clam[:], lam[:], AF.Exp, [lam], [clam], scale=-1.0)
            k.ts(clam[:], clam[:], 1.0, None, ALU.add, None, [clam], [clam])
            k.act(clam[:], clam[:], AF.Ln, [clam], [clam])
            k.ts(clam[:], clam[:], -8.0, None, ALU.mult, None, [clam], [clam])
            orn = k.sb(st, [128, 8, S], F32, "orn")
            xp = k.sb(st, [128, S + 4], F32, "xp")
            xg = k.sb(st, [128, S], F32, "xg")
            u = k.sb(st, [128, S], F32, "u")
            u16 = k.sb(st, [128, S], BF16, "u16")
            ra = k.sb(st, [128, S], F32, "ra")
            ig = k.sb(st, [128, S], F32, "ig")
            bb = k.sb(st, [128, S], F32, "bb")
            sq16 = k.sb(st, [128, S], BF16, "sq165")
            pg = [k.ps(st, [128, 512], F32, "pg") for _ in range(2)]
            pss = [k.ps(st, [128, 512], F32, "pss5") for _ in range(4)]
            k.memset(xp[:, 0:4], 0.0, [xp])
            ci = 0
            for n in range(8):
                k.dma(xp[:, 4:S + 4], xr_d[n], [xr_r], [xp])
                k.dma(xg[:], xg_d[n], [xg_r], [xg], eng="pool")
                wb = w16[n % 2]
                k.dma(wst[:, 0, :], I["wa"][n], [], [wst])
                k.dma(wst[:, 1, :], I["wi"][n], [], [wst])
                k.copy(wb[:], wst[:], [wst], [wb], eng="pool")
                k.ts(u[:], xp[:, 1:S + 1], cw[:, n, 0:1], cb[:, n:n + 1], ALU.mult, ALU.add, [xp, cw, cb], [u])
                for i_ in range(1, 4):
                    k.stt(u[:], xp[:, 1 + i_:S + 1 + i_], cw[:, n, i_:i_ + 1], u[:], ALU.mult, ALU.add, [xp, cw, u], [u])
                k.copy(u16[:], u[:], [u], [u16], eng="act")
                for which, dst, nb in ((0, ra, nba), (1, ig, nbi)):
                    for tg in range(4):
                        p = pg[ci % 2]
                        ci += 1
                        sl = slice(tg * 512, (tg + 1) * 512)
                        k.mm(p[:], wb[:, which, :], u16[:, sl], True, True, [wb, u16], [p])
                        k.act(dst[:, sl], p[:], AF.Exp, [p, nb], [dst], bias=nb[:, n:n + 1], scale=-1.0)
                    k.ts(dst[:], dst[:], 1.0, None, ALU.add, None, [dst], [dst], eng="pool")
                    k.recip(dst[:], dst[:], [dst], [dst])
                k.act(ra[:], ra[:], AF.Exp, [ra, clam], [ra], scale=clam[:, n:n + 1])
                k.tt(bb[:], ra[:], ra[:], ALU.mult, [ra], [bb])
                k.ts(bb[:], bb[:], -1.0, 1.0, ALU.mult, ALU.add, [bb], [bb])
                k.act(bb[:], bb[:], AF.Sqrt, [bb], [bb])
                k.tt(bb[:], bb[:], ig[:], ALU.mult, [bb, ig], [bb], eng="pool")
                k.tt(bb[:], bb[:], u[:], ALU.mult, [bb, u], [bb])
                k.fn("dve", lambda e, o=ig[:], a=ra[:], b=bb[:]: e.tensor_tensor_scan(o, a, b, 0.0, ALU.mult, ALU.add), [ra, bb, ig], [ig])
                k.act(xg[:], xg[:], AF.Gelu, [xg], [xg])
                k.tt(orn[:, n, :], xg[:], ig[:], ALU.mult, [xg, ig], [orn])
                k.act(sq16[:], orn[:, n, :], AF.Square, [orn], [sq16])
                for tg in range(4):
                    k.mm(pss[tg][:], self.ones16[:], sq16[:, tg * 512:(tg + 1) * 512], n == 0, n == 7, [self.ones16, sq16], [pss[tg]])
            rstd = ra
            for tg in range(4):
                sl = slice(tg * 512, (tg + 1) * 512)
                k.act(rstd[:, sl], pss[tg][:], AF.Sqrt, [pss[tg], self.eps_t], [rstd], bias=self.eps_t[:], scale=1.0 / 1024.0)
            k.recip(rstd[:], rstd[:], [rstd], [rstd])
            for n in range(8):
                k.stt(u16[:], orn[:, n, :], gr[:, n:n + 1], rstd[:], ALU.mult, ALU.mult, [orn, gr, rstd], [u16])
                k.dma(yT_d[8 + n], u16[:], [u16], [yT_r])

    def p6_out(self):
        k, I = self.k, self.I
        sc = self.scr
        yT_d, yT_r = sc["yT"]
        x1_d, x1_r = self.scratch("x1T", [16, 128, S], F32)
        h2_d, h2_r = self.scratch("h2T", [16, 128, S], BF16)
        xTv = I["xT"].rearrange("(k p) t -> k p t", p=128)
        wv = I["w_out"].rearrange("(k p) n -> p k n", p=128)
        with ExitStack() as st:
            rstd = k.sb(st, [128, S], F32, "rstd2")
            with ExitStack() as s1:
                yT = k.sb(s1, [128, 16, S], BF16, "yTs")
                wb = [k.sb(s1, [128, 16, 256], BF16, f"wo{i}") for i in range(2)]
                stg = [k.sb(s1, [128, 16, 128], F32, f"wos{i}") for i in range(2)]
                xb = [k.sb(s1, [128, S], F32, f"x6{i}") for i in range(2)]
                ob = [k.sb(s1, [128, S], F32, f"o6{i}") for i in range(2)]
                sq = [k.sb(s1, [128, S], BF16, f"sq6{i}") for i in range(2)]
                pz = [k.ps(s1, [128, 512], F32, "pz6") for _ in range(2)]
                pss = [k.ps(s1, [128, 512], F32, "pss6") for _ in range(4)]
                for c in range(16):
                    k.dma(yT[:, c, :], yT_d[c], [yT_r], [yT], eng=("sp" if c % 2 else "pool"))
                ci = 0
                for jg in range(8):
                    b = wb[jg % 2]
                    for h in range(2):
                        sg = stg[(jg * 2 + h) % 2]
                        k.dma(sg[:], wv[:, :, jg * 256 + h * 128:jg * 256 + (h + 1) * 128], [], [sg])
                        k.copy(b[:, :, h * 128:(h + 1) * 128], sg[:], [sg], [b], eng="pool")
                    for jj in range(2):
                        j = jg * 2 + jj
                        xt, ot, sqt = xb[j % 2], ob[j % 2], sq[j % 2]
                        k.dma(xt[:], xTv[j], [], [xt])
                        for tg in range(4):
                            p = pz[ci % 2]
                            ci += 1
                            sl = slice(tg * 512, (tg + 1) * 512)
                            for c in range(16):
                                k.mm(p[:], b[:, c, jj * 128:(jj + 1) * 128], yT[:, c, sl], c == 0, c == 15, [b, yT], [p])
                            k.stt(ot[:, sl], p[:], self.modT[:, 32 + j:33 + j], xt[:, sl], ALU.mult, ALU.add, [p, self.modT, xt], [ot])
                        k.dma(x1_d[j], ot[:], [ot], [x1_r])
                        k.act(sqt[:], ot[:], AF.Square, [ot], [sqt])
                        for tg in range(4):
                            k.mm(pss[tg][:], self.ones16[:], sqt[:, tg * 512:(tg + 1) * 512], j == 0, j == 15, [self.ones16, sqt], [pss[tg]])
                for tg in range(4):
                    sl = slice(tg * 512, (tg + 1) * 512)
                    k.act(rstd[:, sl], pss[tg][:], AF.Sqrt, [pss[tg], self.eps_t], [rstd], bias=self.eps_t[:], scale=1.0 / float(D))
                k.recip(rstd[:], rstd[:], [rstd], [rstd])
            k.barrier()
            with ExitStack() as s2:
                xb = [k.sb(s2, [128, S], F32, f"x6b{i}") for i in range(2)]
                tp = [k.sb(s2, [128, S], F32, f"t6b{i}") for i in range(2)]
                hb = [k.sb(s2, [128, S], BF16, f"h6b{i}") for i in range(2)]
                for j in range(16):
                    xt, tt_, ht = xb[j % 2], tp[j % 2], hb[j % 2]
                    k.dma(xt[:], x1_d[j], [x1_r], [xt])
                    k.stt(tt_[:], xt[:], self.G2[:, j:j + 1], rstd[:], ALU.mult, ALU.mult, [xt, self.G2, rstd], [tt_])
                    k.act(ht[:], tt_[:], AF.Identity, [tt_, self.modT], [ht], bias=self.modT[:, 48 + j:49 + j], scale=1.0)
                    k.dma(h2_d[j], ht[:], [ht], [h2_r])

    def p7_peer(self):
        k, I = self.k, self.I
        sc = self.scr
        h2_d, h2_r = sc["h2T"]
        x1_d, x1_r = sc["x1T"]
        qp_d, qp_r = self.scratch("qpT", [16, 128, S], BF16)
        wv = I["wq"].rearrange("(k p) n -> p k n", p=128)
        with ExitStack() as st:
            h2T = k.sb(st, [128, 16, S], BF16, "h2Ts")
            wb = [k.sb(st, [128, 16, 256], BF16, f"wq{i}") for i in range(2)]
            stg = [k.sb(st, [128, 16, 128], F32, f"wqs{i}") for i in range(2)]
            ob = [k.sb(st, [128, S], BF16, f"oq{i}") for i in range(2)]
            pz = [k.ps(st, [128, 512], F32, "pz7") for _ in range(2)]
            for c in range(16):
                k.dma(h2T[:, c, :], h2_d[c], [h2_r], [h2T], eng=("sp" if c % 2 else "pool"))
            ci = 0
            for jg in range(8):
                b = wb[jg % 2]
                for h in range(2):
                    sg = stg[(jg * 2 + h) % 2]
                    k.dma(sg[:], wv[:, :, jg * 256 + h * 128:jg * 256 + (h + 1) * 128], [], [sg])
                    k.copy(b[:, :, h * 128:(h + 1) * 128], sg[:], [sg], [b], eng="pool")
                for jj in range(2):
                    j = jg * 2 + jj
                    ot = ob[j % 2]
                    for tg in range(4):
                        p = pz[ci % 2]
                        ci += 1
                        sl = slice(tg * 512, (tg + 1) * 512)
                        for c in range(16):
                            k.mm(p[:], b[:, c, jj * 128:(jj + 1) * 128], h2T[:, c, sl], c == 0, c == 15, [b, h2T], [p])
                        k.copy(ot[:, sl], p[:], [p], [ot], eng=("act" if tg % 2 == 0 else "dve"))
                    k.dma(qp_d[j], ot[:], [ot], [qp_r])
        k.barrier()
        SLACK = 1.0 - 4e-6
        with ExitStack() as st:
            h2s = k.sb(st, [128, 16, 512], BF16, "h2s")
            E1s = k.sb(st, [128, 4, 8, 128], F32, "E1s")
            E2s = k.sb(st, [128, 4, 8, 128], F32, "E2s")
            dg16 = k.sb(st, [128, 4, 8, 128], BF16, "dg16")
            accT = k.sb(st, [128, 16, 512], F32, "accT")
            keys16 = k.sb(st, [128, 16, 128], BF16, "keys16")
            with ExitStack() as s0:
                k32 = k.sb(s0, [128, 16, 128], F32, "k32")
                k.dma(k32[:].rearrange("p a b -> p (a b)"), I["keysT"], [], [k32])
                k.copy(keys16[:], k32[:], [k32], [keys16])
            k.barrier()
            for su in range(4):
                tsl = slice(su * 512, (su + 1) * 512)
                k.dma(h2s[:], h2_d.rearrange("c p t -> p c t")[:, :, tsl], [h2_r], [h2s])
                with ExitStack() as sb_:
                    qps = k.sb(sb_, [128, 16, 512], BF16, "qps")
                    s_sb = k.sb(sb_, [128, 16, 128], F32, "s_sb")
                    wk = k.sb(sb_, [128, 256], F32, "wk7")
                    v16 = k.sb(sb_, [128, 16, 16], F32, "v16")
                    cand = k.sb(sb_, [128, 8, 256], F32, "cand")
                    c16 = k.sb(sb_, [128, 8, 16], F32, "c16")
                    en = k.sb(sb_, [128, 8, 16], F32, "en")
                    negm = k.sb(sb_, [128, 16], F32, "negm")
                    negM = k.sb(sb_, [128, 8], F32, "negM")
                    Z = k.sb(sb_, [128, 8], F32, "Z")
                    th = k.sb(sb_, [128, 8], F32, "th")
                    cf = k.sb(sb_, [128, 8], F32, "cf")
                    E1t = k.sb(sb_, [128, 8, 128], F32, "E1t")
                    pS_ = [k.ps(sb_, [128, 4, 128], F32, "pS7") for _ in range(2)]
                    k.dma(qps[:], qp_d.rearrange("c p t -> p c t")[:, :, tsl], [qp_r], [qps])
                    for tl in range(4):
                        for hg in range(4):
                            p = pS_[hg % 2]
                            for q4 in range(4):
                                hp = hg * 4 + q4
                                k.mm(p[:, q4, :], qps[:, hp, tl * 128:(tl + 1) * 128], keys16[:, hp, :], True, True, [qps, keys16], [p])
                            k.copy(s_sb[:, hg * 4:(hg + 1) * 4, :], p[:], [p], [s_sb], eng=("act" if hg % 2 == 0 else "dve"))
                        for hp in range(16):
                            k.fn("dve", lambda e, o=v16[:, hp, 0:8], a=s_sb[:, hp, :]: e.max(out=o, in_=a), [s_sb], [v16])
                            k.fn("dve", lambda e, o=wk[:, 0:128], a=v16[:, hp, 0:8], b=s_sb[:, hp, :]: e.match_replace(out=o, in_to_replace=a, in_values=b, imm_value=-1e30), [s_sb, v16], [wk])
                            k.fn("dve", lambda e, o=v16[:, hp, 8:16], a=wk[:, 0:128]: e.max(out=o, in_=a), [wk], [v16])
                        v16r = v16[:].rearrange("p (h two) i -> p h two i", two=2)
                        k.tt(cand[:].rearrange("p h (i j) -> p h i j", i=16),
                             v16r[:, :, 0, :].unsqueeze(3).to_broadcast([128, 8, 16, 16]),
                             v16r[:, :, 1, :].unsqueeze(2).to_broadcast([128, 8, 16, 16]), ALU.add, [v16], [cand])
                        for h in range(8):
                            k.fn("dve", lambda e, o=c16[:, h, 0:8], a=cand[:, h, :]: e.max(out=o, in_=a), [cand], [c16])
                            k.fn("dve", lambda e, o=wk[:], a=c16[:, h, 0:8], b=cand[:, h, :]: e.match_replace(out=o, in_to_replace=a, in_values=b, imm_value=-1e30), [cand, c16], [wk])
                            k.fn("dve", lambda e, o=c16[:, h, 8:16], a=wk[:]: e.max(out=o, in_=a), [wk], [c16])
                        k.ts(negm[:], v16[:, :, 0], -1.0, None, ALU.mult, None, [v16], [negm])
                        k.ts(negM[:], c16[:, :, 0], -1.0, None, ALU.mult, None, [c16], [negM])
                        for h in range(8):
                            k.act(en[:, h, :], c16[:, h, :], AF.Exp, [c16, negM], [en, Z], bias=negM[:, h:h + 1], scale=1.0, accum_out=Z[:, h:h + 1])
                        k.recip(Z[:], Z[:], [Z], [Z])
                        k.tt(th[:], en[:, :, 15], Z[:], ALU.mult, [en, Z], [th])
                        k.ts(th[:], th[:], SLACK, None, ALU.mult, None, [th], [th])
                        k.ts(cf[:], en[:, :, 15], SLACK, None, ALU.mult, None, [en], [cf])
                        k.recip(cf[:], cf[:], [cf], [cf])
                        for h in range(8):
                            k.act(E2s[:, tl, h, :], s_sb[:, 2 * h + 1, :], AF.Exp, [s_sb, negm], [E2s], bias=negm[:, 2 * h + 1:2 * h + 2], scale=1.0)
                            k.act(E1t[:, h, :], s_sb[:, 2 * h, :], AF.Exp, [s_sb, negm], [E1t], bias=negm[:, 2 * h:2 * h + 1], scale=1.0)
                            k.ts(dg16[:, tl, h, :], self.ident32[:], th[:, h:h + 1], None, ALU.mult, None, [self.ident32, th], [dg16], eng="pool")
                        k.tt(E1s[:, tl, :, :], E1t[:], cf[:].unsqueeze(2).to_broadcast([128, 8, 128]), ALU.mult, [E1t, cf], [E1s])
                k.barrier()
                if "peer_dbg" in self.dbg and su == 0:
                    for nm, tile_ in (("E1s", E1s), ("E2s", E2s)):
                        d, r = self.scratch(nm, [128, 4 * 8 * 128], F32)
                        k.dma(d, tile_[:].rearrange("p a b c -> p (a b c)"), [tile_], [r])
                with ExitStack() as sc_:
                    stgD = [k.sb(sc_, [128, 16, 128], F32, f"stgD{i}") for i in range(2)]
                    stgU = [k.sb(sc_, [128, 2048], F32, f"stgU{i}") for i in range(2)]
                    dn16 = [k.sb(sc_, [128, 16, 128], BF16, f"dn16{i}") for i in range(4)]
                    up16 = [[k.sb(sc_, [128, 2048], BF16, f"up16{j}_{i}") for i in range(4)] for j in range(2)]
                    GA16 = [k.sb(sc_, [128, 512], BF16, f"GA{i}") for i in range(2)]
                    Pt = k.sb(sc_, [128, 2, 8, 128], F32, "Pt")
                    mE = [k.sb(sc_, [128, 2, 8, 128], BF16, f"mE{i}") for i in range(2)]
                    WA = [k.sb(sc_, [128, 512], BF16, f"WA{i}") for i in range(8)]
                    ev = [k.sb(sc_, [128, 512], F32, f"ev{i}") for i in range(2)]
                    pact = [k.ps(sc_, [128, 512], F32, "pact") for _ in range(2)]
                    pw = [k.ps(sc_, [128, 512], F32, "pw") for _ in range(2)]
                    po = [k.ps(sc_, [128, 512], F32, "po") for _ in range(2)]
                    cn = {"k": 0, "p": 0, "o": 0}
                    ngrp = 1 if "peer_short" in self.dbg else 32
                    pend_wa = []
                    pend_po = []

                    def flush_wa():
                        while pend_wa:
                            wa_, pw__, ga_ = pend_wa.pop(0)
                            k.tt(wa_[:], pw__[:], ga_[:], ALU.mult, [pw__, ga_], [wa_])

                    def flush_po():
                        while pend_po:
                            gi_, was_ = pend_po.pop(0)
                            for dc in range(16):
                                po_ = po[cn["o"] % 2]
                                e_ = ev[cn["o"] % 2]
                                cn["o"] += 1
                                for kl in range(4):
                                    k.mm(po_[:], up16[gi_ % 2][kl][:, dc * 128:(dc + 1) * 128], was_[kl][:], kl == 0, kl == 3, [up16[gi_ % 2][kl], was_[kl]], [po_])
                                if gi_ == 0:
                                    k.copy(accT[:, dc, :], po_[:], [po_], [accT], eng="act")
                                else:
                                    k.copy(e_[:], po_[:], [po_], [e_], eng="act")
                                    k.tt(accT[:, dc, :], accT[:, dc, :], e_[:], ALU.add, [accT, e_], [accT], eng="pool")

                    for gi in range(ngrp):
                        was = []
                        for kl in range(4):
                            kap = gi * 4 + kl
                            sd, su_ = stgD[kap % 2], stgU[kap % 2]
                            dn, up = dn16[kl], up16[gi % 2][kl]
                            k.dma(sd[:].rearrange("p a b -> p (a b)"), I["downB"][kap * 128:(kap + 1) * 128, :], [], [sd], eng="sp")
                            k.dma(su_[:], I["up"][kap * 128:(kap + 1) * 128, :], [], [su_], eng="pool")
                            k.copy(dn[:], sd[:], [sd], [dn], eng="act")
                            k.copy(up[:], su_[:], [su_], [up], eng="act")
                        for kl in range(4):
                            kap = gi * 4 + kl
                            dn = dn16[kl]
                            pa_ = pact[cn["k"] % 2]
                            pw_ = pw[cn["k"] % 2]
                            ga = GA16[cn["k"] % 2]
                            wa = WA[cn["k"] % 8]
                            cn["k"] += 1
                            for c in range(16):
                                k.mm(pa_[:], dn[:, c, :], h2s[:, c, :], c == 0, c == 15, [dn, h2s], [pa_])
                            k.act(ga[:], pa_[:], AF.Gelu, [pa_], [ga])
                            for t2 in range(2):
                                me = mE[cn["p"] % 2]
                                cn["p"] += 1
                                k.tt(Pt[:], E2s[:, 2 * t2:2 * t2 + 2, :, :], E1s[:, 2 * t2:2 * t2 + 2, :, kap:kap + 1].to_broadcast([128, 2, 8, 128]), ALU.mult, [E2s, E1s], [Pt])
                                k.stt(me[:], Pt[:], 1.0, Pt[:], ALU.is_ge, ALU.mult, [Pt], [me])
                                if t2 == 0:
                                    flush_wa()
                                for tq in range(2):
                                    tl = 2 * t2 + tq
                                    for h in range(8):
                                        k.mm(pw_[:, tl * 128:(tl + 1) * 128], me[:, tq, h, :], dg16[:, tl, h, :], h == 0, h == 7, [me, dg16], [pw_])
                            pend_wa.append((wa, pw_, ga))
                            was.append(wa)
                            if kl == 0:
                                flush_po()
                        pend_po.append((gi, was))
                    flush_wa()
                    flush_po()
                    for dc in range(16):
                        x_ = ev[dc % 2]
                        k.dma(x_[:], x1_d[dc][:, tsl], [x1_r], [x_])
                        k.stt(x_[:], accT[:, dc, :], self.modT[:, 80 + dc:81 + dc], x_[:], ALU.mult, ALU.add, [accT, self.modT, x_], [x_])
                        k.dma(self.outT[dc * 128:(dc + 1) * 128, tsl], x_[:], [x_], [])
                k.barrier()


def _consts():
    f = np.float32
    half = 64
    freqs = (10000.0 ** (-np.arange(half, dtype=f) / f(half))).astype(f)
    ang = np.arange(S, dtype=f)[:, None] * freqs[None, :]
    cos = np.cos(ang).astype(f).T
    sin = np.sin(ang).astype(f).T
    cosT = np.concatenate([cos, cos], 0)
    sinT = np.concatenate([sin, sin], 0)
    rotm = np.zeros((128, 128), f)
    for m in range(64):
        rotm[m + 64, m] = -1.0
        rotm[m, m + 64] = 1.0
    ident = np.eye(128, dtype=f)
    t = np.arange(S)
    cst = np.arange(NCMP) * 16
    maskc = np.zeros((128, S), f)
    maskc[:NCMP] = ((cst + 31)[:, None] <= t[None, :]).astype(f)
    kk = np.arange(128)[:, None]
    tt = np.arange(512)[None, :]
    cm = np.stack([(128 * o + kk <= tt).astype(f) for o in range(4)], 1).reshape(128, 4 * 512)
    wm = np.stack([(((128 * rel + kk - tt) <= 0) & ((128 * rel + kk - tt) > -512)).astype(f)
                   for rel in range(-4, 4)], 1).reshape(128, 8 * 512)
    ex = np.zeros((32, 16, 128), f)
    for kc in range(16):
        for kq in range(128):
            ex[2 * kc + kq // 64, kc, kq] = 1.0
    ex = ex.reshape(32, 16 * 128)
    jb = np.arange(32)[None, :]
    cur = (t // 64)[:, None]
    forced = (jb == 0) | (jb == cur) | (jb == cur - 1)
    valid = (jb * 64) <= t[:, None]
    vm_ = (valid & ~forced).astype(f)
    fb_ = np.where(forced, 1e4, np.where(valid, 0.0, -1e4)).astype(f)
    vm = vm_.reshape(16, 128, 32).transpose(1, 0, 2).reshape(128, 512)
    fb = fb_.reshape(16, 128, 32).transpose(1, 0, 2).reshape(128, 512)
    sst = np.arange(32) * 64
    ov = np.maximum(np.minimum(cst[:, None] + 32, sst[None, :] + 64) - np.maximum(cst[:, None], sst[None, :]), 0)
    ovl = np.zeros((128, 32), f)
    ovl[:NCMP] = ov.astype(f) / 32.0
    return dict(cosT=cosT, sinT=sinT, rotm=rotm, ident=ident, maskc=maskc, cm=cm, wm=wm, ex=ex, vm=vm, fb=fb, ovl=ovl)


def prep_shared(inp):
    f = np.float32
    A = lambda v: np.ascontiguousarray(np.asarray(v, dtype=f))
    sh = {}
    sh["ada_w"] = A(inp["ada_w"][0])
    sh["ada_bT"] = A(inp["ada_b"][0].reshape(96, 128).T)
    sh["g1T"] = A(inp["norm_mix_g"][0].reshape(16, 128).T)
    sh["g2T"] = A(inp["norm_ffn_g"][0].reshape(16, 128).T)
    sh["w_in"] = A(inp["w_in"][0])
    sh["w_out"] = A(inp["w_out"][0])
    sh["wq"] = A(inp["peer_wq"][0])
    sh["qg"] = A(inp["q_norm_g"][0].reshape(128, 1))
    sh["kgT"] = A(inp["k_norm_g"][0].T)
    sh["kg0b"] = A(np.broadcast_to(inp["k_norm_g"][0, 0][None, :], (128, 128)))
    sh["pekT"] = A(inp["cmp_pe_k"][0].T)
    sh["pevT"] = A(inp["cmp_pe_v"][0].T)
    sh["cwk"] = A(inp["cmp_w_k"][0])
    sh["cwv"] = A(inp["cmp_w_v"][0])
    sh["gateb"] = A(np.broadcast_to(inp["gate_b"][0][None, :], (128, 24)))
    sh["convw"] = A(inp["conv_w"][0].reshape(4, 8, 128).transpose(2, 1, 0).reshape(128, 32))
    for nm, key in (("convb", "conv_b"), ("lba", "lru_ba"), ("lbi", "lru_bi"), ("lam", "lru_lam"), ("gr", "out_g_rnn")):
        sh[nm] = A(inp[key][0].reshape(8, 128).T)
    sh["gab"] = A(np.broadcast_to(inp["out_g_attn"][0][None, :], (128, 1024)))
    sh["wa"] = A(inp["lru_wa"][0])
    sh["wi"] = A(inp["lru_wi"][0])
    sh["keysT"] = A(inp["peer_keys"][0].transpose(3, 0, 1, 2).reshape(128, 16 * 128))
    sh["downB"] = A(inp["peer_down"][0].reshape(128, 128, 16, 128).transpose(0, 3, 2, 1).reshape(128 * 128, 16 * 128))
    sh["up"] = A(inp["peer_up"][0])
    sh.update(_consts())
    return sh


def prep_core(inp, b):
    f = np.float32
    return {"xT": np.ascontiguousarray(np.asarray(inp["x"][b], dtype=f).T),
            "c_col": np.ascontiguousarray(np.asarray(inp["c"][b], dtype=f).reshape(16, 128).T)}


def kernel(**inputs):
    inp = {k_: np.asarray(v) for k_, v in inputs.items()}
    sh = prep_shared(inp)
    nc = Prog().build()
    in_maps = []
    for b in range(8):
        m = dict(sh)
        m.update(prep_core(inp, b))
        in_maps.append(m)
    res = run_bass_kernel_spmd(nc, in_maps, core_ids=list(range(8)))
    out = np.stack([np.asarray(r["outT"]).T for r in res.results], 0)
    return np.ascontiguousarray(out.astype(np.float32))
```

```python
import numpy as np
from contextlib import ExitStack
import concourse.bass as bass
import concourse.mybir as mybir
from concourse.bass_utils import run_bass_kernel_spmd

F32 = mybir.dt.float32
BF16 = mybir.dt.bfloat16
AF = mybir.ActivationFunctionType
ALU = mybir.AluOpType
AX = mybir.AxisListType

D = 2048
S = 2048
NT = 16
NIN = 4632
C_Q, C_KC, C_VC, C_KS, C_VS, C_KW, C_VW, C_GL, C_XR, C_XG = 0, 1024, 1280, 1536, 1792, 2048, 2304, 2560, 2584, 3608
EPS = 1e-6
NCMP = 127
SCALE = 128 ** -0.5


class Res:
    __slots__ = ("name", "w", "r")

    def __init__(self, name=""):
        self.name = name
        self.w = None
        self.r = []


class Op:
    __slots__ = ("eng", "fn", "deps", "flag", "cval", "dma", "dsem", "dval")

    def __init__(self, eng, fn, dma=False):
        self.eng = eng
        self.fn = fn
        self.deps = []
        self.flag = False
        self.cval = 0
        self.dma = dma
        self.dsem = None
        self.dval = 0


class Sched:
    ENGS = ("pe", "act", "dve", "pool", "sp")

    def __init__(self, nc, n_dma_sems=16):
        self.nc = nc
        self.q = {e: [] for e in self.ENGS}
        self.n_dma_sems = n_dma_sems
        self.dma_rr = 0
        self.dma_last = [None] * n_dma_sems
        self.dma_cnt = [0] * n_dma_sems
        self.pending = {e: [] for e in self.ENGS}

    def _add(self, eng, fn, reads, writes, dma=False):
        op = Op(eng, fn, dma)
        deps = list(self.pending[eng])
        self.pending[eng] = []
        for r in reads:
            if r.w is not None:
                deps.append(r.w)
        for w in writes:
            if w.w is not None:
                deps.append(w.w)
            deps.extend(w.r)
        for r in reads:
            r.r.append(op)
        for w in writes:
            w.w = op
            w.r = []
        if dma:
            s = self.dma_rr
            self.dma_rr = (self.dma_rr + 1) % self.n_dma_sems
            prev = self.dma_last[s]
            if prev is not None:
                deps.append(prev)
            self.dma_last[s] = op
            self.dma_cnt[s] += 1
            op.dsem = s
            op.dval = 16 * self.dma_cnt[s]
        seen = set()
        for d in deps:
            if d is op or id(d) in seen:
                continue
            if (not d.dma) and d.eng == eng and eng == "pe":
                continue
            seen.add(id(d))
            op.deps.append(d)
            d.flag = True
        self.q[eng].append(op)
        return op

    def op(self, eng, fn, reads=(), writes=()):
        return self._add(eng, fn, list(reads), list(writes))

    def dma(self, eng, out, in_, reads=(), writes=()):
        return self._add(eng, lambda e: e.dma_start(out=out, in_=in_), list(reads), list(writes), dma=True)

    def barrier(self):
        lasts = []
        for e in self.ENGS:
            for op in reversed(self.q[e]):
                if not op.dma:
                    lasts.append(op)
                    break
        for s in range(self.n_dma_sems):
            if self.dma_last[s] is not None:
                lasts.append(self.dma_last[s])
        for e in self.ENGS:
            self.pending[e] = list(lasts)

    def emit(self):
        nc = self.nc
        with ExitStack() as st:
            esem = {e: st.enter_context(nc.semaphore(f"s_{e}")) for e in self.ENGS}
            dsem = [st.enter_context(nc.semaphore(f"d_{i}")) for i in range(self.n_dma_sems)]
            for e in self.ENGS:
                c = 0
                for op in self.q[e]:
                    if op.dma:
                        continue
                    if op.flag:
                        c += 1
                        op.cval = c
            block = st.enter_context(nc.Block())

            def run(ename, eobj):
                waited = {}
                for op in self.q[ename]:
                    need = {}
                    for d in op.deps:
                        if d.dma:
                            key, val = ("d", d.dsem), d.dval
                        else:
                            key, val = ("e", d.eng), d.cval
                        if val > need.get(key, 0):
                            need[key] = val
                    for key, val in need.items():
                        if waited.get(key, 0) >= val:
                            continue
                        waited[key] = val
                        sem = dsem[key[1]] if key[0] == "d" else esem[key[1]]
                        eobj.wait_ge(sem, val)
                    ins = op.fn(eobj)
                    if op.dma:
                        ins.then_inc(dsem[op.dsem], 16)
                    elif op.flag:
                        ins.then_inc(esem[ename], 1)
                if ename == "sp":
                    for s in range(self.n_dma_sems):
                        if self.dma_cnt[s] > 0:
                            eobj.wait_ge(dsem[s], 16 * self.dma_cnt[s])

            block.tensor(lambda e: run("pe", e))
            block.scalar(lambda e: run("act", e))
            block.vector(lambda e: run("dve", e))
            block.gpsimd(lambda e: run("pool", e))
            block.sync(lambda e: run("sp", e))


class T:
    __slots__ = ("t", "r")

    def __init__(self, t, name=""):
        self.t = t
        self.r = Res(name)

    def __getitem__(self, k):
        return self.t[k]


class K:
    def __init__(self, nc):
        self.nc = nc
        self.S = Sched(nc)
        self.uid = 0

    def sb(self, st, shape, dt, name=None):
        self.uid += 1
        n = f"{name or 't'}_{self.uid}"
        return T(st.enter_context(self.nc.sbuf_tensor(n, list(shape), dt)), n)

    def ps(self, st, shape, dt=F32, name=None):
        self.uid += 1
        n = f"{name or 'p'}_{self.uid}"
        return T(st.enter_context(self.nc.psum_tensor(n, list(shape), dt)), n)

    @staticmethod
    def _rs(xs):
        return [x.r if isinstance(x, T) else x for x in xs]

    def mm(self, out, lhsT, rhs, start, stop, R, W):
        self.S.op("pe", lambda e: e.matmul(out, lhsT, rhs, start=start, stop=stop), self._rs(R), self._rs(W))

    def act(self, out, in_, func, R, W, bias=None, scale=None, accum_out=None, eng="act"):
        kw = {}
        if bias is not None:
            kw["bias"] = bias
        if scale is not None:
            kw["scale"] = scale
        if accum_out is not None:
            kw["accum_out"] = accum_out
        self.S.op(eng, lambda e: e.activation(out, in_, func, **kw), self._rs(R), self._rs(W))

    def tt(self, out, in0, in1, op, R, W, eng="dve"):
        self.S.op(eng, lambda e: e.tensor_tensor(out, in0, in1, op), self._rs(R), self._rs(W))

    def ts(self, out, in0, s1, s2, op0, op1, R, W, eng="dve", accum_out=None):
        if accum_out is None:
            if op1 is None:
                self.S.op(eng, lambda e: e.tensor_scalar(out, in0, s1, None, op0), self._rs(R), self._rs(W))
            else:
                self.S.op(eng, lambda e: e.tensor_scalar(out, in0, s1, s2, op0, op1), self._rs(R), self._rs(W))
        else:
            self.S.op(eng, lambda e: e.tensor_scalar(out, in0, s1, s2, op0, op1, accum_out=accum_out),
                      self._rs(R), self._rs(W))

    def stt(self, out, in0, scalar, in1, op0, op1, R, W, eng="dve"):
        self.S.op(eng, lambda e: e.scalar_tensor_tensor(out, in0, scalar, in1, op0, op1), self._rs(R), self._rs(W))

    def copy(self, out, in_, R, W, eng="dve"):
        if eng == "act":
            self.S.op("act", lambda e: e.copy(out, in_), self._rs(R), self._rs(W))
        else:
            self.S.op(eng, lambda e: e.tensor_copy(out, in_), self._rs(R), self._rs(W))

    def memset(self, ap, val, W, eng="pool"):
        self.S.op(eng, lambda e: e.memset(ap, val), [], self._rs(W))

    def recip(self, out, in_, R, W):
        self.S.op("dve", lambda e: e.reciprocal(out, in_), self._rs(R), self._rs(W))

    def dma(self, out, in_, R, W, eng="sp"):
        self.S.dma(eng, out, in_, self._rs(R), self._rs(W))

    def fn(self, eng, f, R, W):
        self.S.op(eng, f, self._rs(R), self._rs(W))

    def barrier(self):
        self.S.barrier()


IN_SPECS = [
    ("xT", [D, S], F32), ("c_col", [128, 16], F32), ("ada_w", [D, 6 * D], F32), ("ada_bT", [128, 96], F32),
    ("g1T", [128, 16], F32), ("g2T", [128, 16], F32), ("w_in", [D, NIN], F32), ("w_out", [D, D], F32),
    ("wq", [D, D], F32), ("qg", [128, 1], F32), ("kgT", [128, 3], F32), ("kg0b", [128, 128], F32),
    ("pekT", [128, 32], F32), ("pevT", [128, 32], F32), ("cwk", [4096, 128], F32), ("cwv", [4096, 128], F32),
    ("gateb", [128, 24], F32), ("convw", [128, 32], F32), ("convb", [128, 8], F32), ("lba", [128, 8], F32),
    ("lbi", [128, 8], F32), ("lam", [128, 8], F32), ("gr", [128, 8], F32), ("gab", [128, 1024], F32),
    ("wa", [8, 128, 128], F32), ("wi", [8, 128, 128], F32), ("keysT", [128, 16 * 128], F32),
    ("downB", [128 * 128, 16 * 128], F32), ("up", [16384, D], F32),
    ("cosT", [128, S], F32), ("sinT", [128, S], F32), ("rotm", [128, 128], F32), ("ident", [128, 128], F32),
    ("maskc", [128, S], F32), ("cm", [128, 4 * 512], F32), ("wm", [128, 8 * 512], F32),
    ("ex", [32, 16 * 128], F32), ("vm", [128, 16 * 32], F32), ("fb", [128, 16 * 32], F32), ("ovl", [128, 32], F32),
]


class Prog:
    def __init__(self, stop_after=99, dbg=()):
        self.nc = nc = bass.Bass("TRN2", target_bir_lowering=False)
        self.k = K(nc)
        self.stop_after = stop_after
        self.dbg = set(dbg)
        for d_ in self.dbg:
            if d_.startswith("qkind="):
                self.qkind = d_.split("=")[1]
        self.I = {}
        for name, shape, dt in IN_SPECS:
            self.I[name] = nc.dram_tensor(name, shape, dt, kind="ExternalInput").ap()
        self.outT = nc.dram_tensor("outT", [D, S], F32, kind="ExternalOutput").ap()
        self.scr = {}

    def scratch(self, name, shape, dt):
        kind = "ExternalOutput" if name in self.dbg else "Internal"
        ap = self.nc.dram_tensor(name, list(shape), dt, kind=kind).ap()
        self.scr[name] = (ap, Res(name))
        return ap, self.scr[name][1]

    def build(self):
        k = self.k
        with ExitStack() as g:
            self.g = g
            self.modT = k.sb(g, [128, 96], F32, "modT")
            self.G1 = k.sb(g, [128, 16], F32, "G1")
            self.G2 = k.sb(g, [128, 16], F32, "G2")
            self.eps_t = k.sb(g, [128, 1], F32, "eps")
            self.ones16 = k.sb(g, [128, 128], BF16, "ones16")
            self.ident16 = k.sb(g, [128, 128], BF16, "ident16")
            k.memset(self.eps_t[:], EPS, [self.eps_t])
            k.memset(self.ones16[:], 1.0, [self.ones16])
            self.ident32 = k.sb(g, [128, 128], F32, "ident32")
            k.dma(self.ident32[:], self.I["ident"], [], [self.ident32])
            k.copy(self.ident16[:], self.ident32[:], [self.ident32], [self.ident16])
            names = ["p0_mod", "p12_proj", "p3_cmp", "p4_attn", "p5_rnn", "p6_out", "p7_peer"]
            phases = [getattr(self, n) for n in names if hasattr(self, n)]
            for i, ph in enumerate(phases):
                if i > self.stop_after:
                    break
                ph()
                k.barrier()
            k.S.emit()
        return self.nc

    def p0_mod(self):
        k, I = self.k, self.I
        with ExitStack() as st:
            cc = k.sb(st, [128, 16], F32, "cc")
            sc = k.sb(st, [128, 16], F32, "sc")
            abT = k.sb(st, [128, 96], F32, "abT")
            g1 = k.sb(st, [128, 16], F32, "g1")
            g2 = k.sb(st, [128, 16], F32, "g2")
            tmp = k.sb(st, [128, 16], F32, "tmp")
            wb = [k.sb(st, [128, 16, 512], F32, f"adaw{i}") for i in range(2)]
            pm = k.ps(st, [128, 96], F32, "pm")
            k.dma(cc[:], I["c_col"], [], [cc])
            k.dma(abT[:], I["ada_bT"], [], [abT])
            k.dma(g1[:], I["g1T"], [], [g1])
            k.dma(g2[:], I["g2T"], [], [g2])
            k.act(sc[:], cc[:], AF.Silu, [cc], [sc])
            wv = I["ada_w"].rearrange("(k p) n -> p k n", p=128)
            for gi in range(24):
                b = wb[gi % 2]
                k.dma(b[:], wv[:, :, gi * 512:(gi + 1) * 512], [], [b], eng=("sp" if gi % 2 == 0 else "pool"))
                for j in range(4):
                    col = gi * 4 + j
                    for kk in range(16):
                        k.mm(pm[:, col:col + 1], b[:, kk, j * 128:(j + 1) * 128], sc[:, kk:kk + 1],
                             kk == 0, kk == 15, [b, sc], [pm])
            k.tt(self.modT[:], pm[:], abT[:], ALU.add, [pm, abT], [self.modT])
            k.ts(tmp[:], self.modT[:, 16:32], 1.0, None, ALU.add, None, [self.modT], [tmp])
            k.tt(self.G1[:], tmp[:], g1[:], ALU.mult, [tmp, g1], [self.G1])
            k.ts(tmp[:], self.modT[:, 64:80], 1.0, None, ALU.add, None, [self.modT, self.G1], [tmp])
            k.tt(self.G2[:], tmp[:], g2[:], ALU.mult, [tmp, g2], [self.G2])
            if "modT" in self.dbg:
                d, r = self.scratch("modT", [128, 96], F32)
                k.dma(d, self.modT[:], [self.modT], [r])

    def rms_stats_fm(self, st, loader, nchunks, width, scale_div, name):
        k = self.k
        rstd = k.sb(st, [128, S], F32, name)
        with ExitStack() as s2:
            xb = [k.sb(s2, [128, S], F32, "xld") for _ in range(2)]
            sq = [k.sb(s2, [128, S], BF16, "sq") for _ in range(2)]
            pss = [k.ps(s2, [128, 512], F32, "pss") for _ in range(4)]
            for kk in range(nchunks):
                xt, sqt = xb[kk % 2], sq[kk % 2]
                loader(kk, xt)
                k.act(sqt[:], xt[:], AF.Square, [xt], [sqt])
                for tg in range(4):
                    k.mm(pss[tg][:], self.ones16[:], sqt[:, tg * 512:(tg + 1) * 512], kk == 0, kk == nchunks - 1,
                         [self.ones16, sqt], [pss[tg]])
            for tg in range(4):
                sl = slice(tg * 512, (tg + 1) * 512)
                k.act(rstd[:, sl], pss[tg][:], AF.Sqrt, [pss[tg], self.eps_t], [rstd], bias=self.eps_t[:], scale=1.0 / scale_div)
            k.recip(rstd[:], rstd[:], [rstd], [rstd])
        k.barrier()
        return rstd

    def p12_proj(self):
        k, I = self.k, self.I
        xTv = I["xT"].rearrange("(k p) t -> k p t", p=128)
        qT_d, qT_r = self.scratch("qT", [8, 128, S], BF16)
        kcT_d, kcT_r = self.scratch("kcT", [2, 128, S], BF16)
        vcT_d, vcT_r = self.scratch("vcT", [2, 128, S], BF16)
        ksT_d, ksT_r = self.scratch("ksT", [2, 128, S], BF16)
        kwT_d, kwT_r = self.scratch("kwT", [2, 128, S], BF16)
        vs_d, vs_r = self.scratch("vs", [S, 256], BF16)
        vw_d, vw_r = self.scratch("vw", [S, 256], BF16)
        gt_d, gt_r = self.scratch("gates", [128, 16 * 24], F32)
        xrT_d, xrT_r = self.scratch("xrT", [8, 128, S], F32)
        xgT_d, xgT_r = self.scratch("xgT", [8, 128, S], F32)
        with ExitStack() as st:
            hT = k.sb(st, [128, 16, S], BF16, "hT")
            with ExitStack() as s1:
                rstd = self.rms_stats_fm(s1, lambda kk, dst: k.dma(dst[:], xTv[kk], [], [dst]), 16, S, float(D), "rstd1")
                xb = [k.sb(s1, [128, S], F32, "xld2") for _ in range(2)]
                tmp = [k.sb(s1, [128, S], F32, "tmp") for _ in range(2)]
                for kk in range(16):
                    xt, tp = xb[kk % 2], tmp[kk % 2]
                    k.dma(xt[:], xTv[kk], [], [xt])
                    k.stt(tp[:], xt[:], self.G1[:, kk:kk + 1], rstd[:], ALU.mult, ALU.mult, [xt, self.G1, rstd], [tp])
                    k.act(hT[:, kk, :], tp[:], AF.Identity, [tp, self.modT], [hT], bias=self.modT[:, kk:kk + 1], scale=1.0)
            if "hT" in self.dbg:
                d, r = self.scratch("hT", [16, 128, S], BF16)
                k.dma(d.rearrange("k p t -> p k t"), hT[:], [hT], [r])
            k.barrier()
            if "stop_p1" in self.dbg:
                return
            with ExitStack() as s2:
                wb = [k.sb(s2, [128, 16, 544], BF16, f"win{i}") for i in range(2)]
                cosT = k.sb(s2, [128, S], F32, "cosT")
                sinT = k.sb(s2, [128, S], F32, "sinT")
                rot16 = k.sb(s2, [128, 128], BF16, "rot16")
                gq = k.sb(s2, [128, 4], F32, "gq")
                gateb = k.sb(s2, [128, 24], F32, "gateb")
                k.dma(cosT[:], I["cosT"], [], [cosT])
                k.dma(sinT[:], I["sinT"], [], [sinT])
                rot32 = k.sb(s2, [128, 128], F32, "rot32")
                k.dma(rot32[:], I["rotm"], [], [rot32])
                k.copy(rot16[:], rot32[:], [rot32], [rot16])
                k.dma(gq[:, 0:1], I["qg"], [], [gq])
                k.dma(gq[:, 1:4], I["kgT"], [], [gq])
                k.dma(gateb[:], I["gateb"], [], [gateb])
                pz = [k.ps(s2, [128, 512], F32, "pz") for _ in range(4)]
                pss = [k.ps(s2, [128, 512], F32, "pss2") for _ in range(2)]
                prot = [k.ps(s2, [128, 512], F32, "prot") for _ in range(2)]
                ptm = pz[2:4]
                sq = [k.sb(s2, [128, 512], BF16, "sq2") for _ in range(2)]
                rs = [k.sb(s2, [128, 512], F32, "rs") for _ in range(2)]
                xn = [k.sb(s2, [128, 512], F32, "xn") for _ in range(2)]
                xn16 = [k.sb(s2, [128, 512], BF16, "xn16") for _ in range(2)]
                t1 = [k.sb(s2, [128, 512], F32, "t1") for _ in range(2)]
                t2 = [k.sb(s2, [128, 512], F32, "t2") for _ in range(2)]
                o16 = [k.sb(s2, [128, S], BF16, "o16") for _ in range(2)]
                o32 = [k.sb(s2, [128, S], F32, "o32") for _ in range(2)]
                vtm = k.sb(s2, [128, 16, 256], BF16, "vtm")
                gtm = k.sb(s2, [128, 16, 24], F32, "gtm")
                gtmp = k.sb(s2, [128, 24], F32, "gtmp")
                wv = I["w_in"].rearrange("(k p) n -> p k n", p=128)
                cnt = {"c": 0, "i": 0}

                pend = []

                def fm_chunk(b, co, kind, gcol, dst_ap, dst_res):
                    ci = cnt["c"]
                    cnt["c"] += 1
                    ob = (o32 if kind == "f32" else o16)[ci % 2]
                    for tg in range(4):
                        pend.append((b, co, kind, gcol, dst_ap, dst_res, ob, tg))

                def fm_s0(st_):
                    b, co, kind, gcol, dst_ap, dst_res, ob, tg = st_
                    i = cnt["i"]
                    cnt["i"] += 1
                    p = pz[i % 4]
                    sl = slice(tg * 512, (tg + 1) * 512)
                    for kk in range(16):
                        k.mm(p[:], b[:, kk, co:co + 128], hT[:, kk, sl], kk == 0, kk == 15, [b, hT], [p])
                    return (i, p)

                def fm_s1(st_, ip):
                    b, co, kind, gcol, dst_ap, dst_res, ob, tg = st_
                    i, p = ip
                    sl = slice(tg * 512, (tg + 1) * 512)
                    if kind in ("f32", "bf16"):
                        k.copy(ob[:, sl], p[:], [p], [ob], eng=("act" if tg % 2 == 0 else "dve"))
                    else:
                        a, a16 = xn[i % 2], xn16[i % 2]
                        if kind == "normrope":
                            sqt, pst, rst = sq[i % 2], pss[i % 2], rs[i % 2]
                            k.act(sqt[:], p[:], AF.Square, [p], [sqt])
                            k.mm(pst[:], self.ones16[:], sqt[:], True, True, [self.ones16, sqt], [pst])
                            k.act(rst[:], pst[:], AF.Sqrt, [pst, self.eps_t], [rst], bias=self.eps_t[:], scale=1.0 / 128.0)
                            k.recip(rst[:], rst[:], [rst], [rst])
                            k.stt(a[:], p[:], gq[:, gcol:gcol + 1], rst[:], ALU.mult, ALU.mult, [p, gq, rst], [a])
                        else:
                            k.copy(a[:], p[:], [p], [a], eng="dve")
                        k.copy(a16[:], a[:], [a], [a16], eng="act")
                        pr = prot[i % 2]
                        k.mm(pr[:], rot16[:], a16[:], True, True, [rot16, a16], [pr])
                        k.tt(t1[i % 2][:], a[:], cosT[:, sl], ALU.mult, [a, cosT], [t1[i % 2]])
                        k.tt(t2[i % 2][:], pr[:], sinT[:, sl], ALU.mult, [pr, sinT], [t2[i % 2]])
                        k.tt(ob[:, sl], t1[i % 2][:], t2[i % 2][:], ALU.add, [t1[i % 2], t2[i % 2]], [ob], eng="pool")
                    if tg == 3:
                        k.dma(dst_ap, ob[:], [ob], [dst_res])

                def fm_flush():
                    steps = list(pend)
                    del pend[:]
                    inflight = []
                    LA = 2
                    for n in range(min(LA, len(steps))):
                        inflight.append(fm_s0(steps[n]))
                    for n in range(len(steps)):
                        if n + LA < len(steps):
                            inflight.append(fm_s0(steps[n + LA]))
                        fm_s1(steps[n], inflight.pop(0))

                stg = [k.sb(s2, [128, 16, 272], F32, f"stg{i}") for i in range(2)]
                lcnt = {"g": 0, "s": 0}

                def load_group(c0, c1):
                    fm_flush()
                    b = wb[lcnt["g"] % 2]
                    lcnt["g"] += 1
                    w = c1 - c0
                    pieces = [(a, min(a + 256, w)) for a in range(0, w, 256)]
                    for (a0, a1) in pieces:
                        sg = stg[lcnt["s"] % 2]
                        lcnt["s"] += 1
                        k.dma(sg[:, :, 0:a1 - a0], wv[:, :, c0 + a0:c0 + a1], [], [sg], eng=("sp" if lcnt["s"] % 2 else "pool"))
                        k.copy(b[:, :, a0:a1], sg[:, :, 0:a1 - a0], [sg], [b], eng="pool")
                    return b

                def tm_block(b, co, width, post):
                    for tt_ in range(16):
                        p = ptm[tt_ % 2]
                        for kk in range(16):
                            k.mm(p[:, 0:width], hT[:, kk, tt_ * 128:(tt_ + 1) * 128], b[:, kk, co:co + width], kk == 0, kk == 15, [b, hT], [p])
                        post(tt_, p)

                b = load_group(0, 512)
                for j in range(4):
                    fm_chunk(b, j * 128, self.qkind if hasattr(self, "qkind") else "normrope", 0, qT_d[j], qT_r)
                b = load_group(512, 1024)
                for j in range(4):
                    fm_chunk(b, j * 128, self.qkind if hasattr(self, "qkind") else "normrope", 0, qT_d[4 + j], qT_r)
                if "stop_g0" in self.dbg:
                    fm_flush()
                    return
                b = load_group(1024, 1536)
                for j in range(2):
                    fm_chunk(b, j * 128, "rope", 0, kcT_d[j], kcT_r)
                for j in range(2):
                    fm_chunk(b, 256 + j * 128, "bf16", 0, vcT_d[j], vcT_r)
                b = load_group(1536, 2048)
                for j in range(2):
                    fm_chunk(b, j * 128, "normrope", 2, ksT_d[j], ksT_r)
                fm_flush()
                tm_block(b, 256, 256, lambda tt_, p: k.copy(vtm[:, tt_, :], p[:, 0:256], [p], [vtm], eng=("act" if tt_ % 2 == 0 else "dve")))
                k.dma(vs_d.rearrange("(t p) c -> p t c", p=128), vtm[:], [vtm], [vs_r])
                if "stop_g1" in self.dbg:
                    return
                if "rep_g3" in self.dbg:
                    b = load_group(1536, 2048)
                    for j in range(2):
                        fm_chunk(b, j * 128, "normrope", 2, ksT_d[j], ksT_r)
                    tm_block(b, 256, 256, lambda tt_, p: k.copy(vtm[:, tt_, :], p[:, 0:256], [p], [vtm], eng=("act" if tt_ % 2 == 0 else "dve")))
                    k.dma(vs_d.rearrange("(t p) c -> p t c", p=128), vtm[:], [vtm], [vs_r])
                    return
                b = load_group(2048, 2560)
                for j in range(2):
                    fm_chunk(b, j * 128, "normrope", 3, kwT_d[j], kwT_r)
                fm_flush()
                tm_block(b, 256, 256, lambda tt_, p: k.copy(vtm[:, tt_, :], p[:, 0:256], [p], [vtm], eng=("act" if tt_ % 2 == 0 else "dve")))
                b = load_group(2560, 2584)

                def post_g(tt_, p):
                    k.tt(gtmp[:], p[:, 0:24], gateb[:], ALU.add, [p, gateb], [gtmp])
                    k.act(gtmp[:], gtmp[:], AF.Exp, [gtmp], [gtmp], scale=-1.0)
                    k.ts(gtmp[:], gtmp[:], 1.0, None, ALU.add, None, [gtmp], [gtmp])
                    k.recip(gtm[:, tt_, :], gtmp[:], [gtmp], [gtm])
                tm_block(b, 0, 24, post_g)
                k.dma(vw_d.rearrange("(t p) c -> p t c", p=128), vtm[:], [vtm], [vw_r])
                k.dma(gt_d, gtm[:].rearrange("p t c -> p (t c)"), [gtm], [gt_r])
                if "stop_g2" in self.dbg:
                    return
                for half in range(2):
                    b = load_group(C_XR + half * 512, C_XR + (half + 1) * 512)
                    for j in range(4):
                        fm_chunk(b, j * 128, "f32", 0, xrT_d[half * 4 + j], xrT_r)
                for half in range(2):
                    b = load_group(C_XG + half * 512, C_XG + (half + 1) * 512)
                    for j in range(4):
                        fm_chunk(b, j * 128, "f32", 0, xgT_d[half * 4 + j], xgT_r)
                fm_flush()

    def p3_cmp(self):
        k, I = self.k, self.I
        kcT_d = self.scr["kcT"][0]
        vcT_d = self.scr["vcT"][0]
        kcmpT_d, kcmpT_r = self.scratch("kcmpT", [128, 2, 128], BF16)
        vcmp_d, vcmp_r = self.scratch("vcmp", [128, 2, 162], BF16)
        with ExitStack() as st:
            src = k.sb(st, [128, 4, S], BF16, "cmpsrc")
            wst = k.sb(st, [128, 32, 128], F32, "wst")
            w16 = [k.sb(st, [128, 32, 128], BF16, f"w16{i}") for i in range(2)]
            pe32 = k.sb(st, [128, 2, 32], F32, "pe32")
            peB = [k.sb(st, [128, 32, 127], BF16, f"peB{i}") for i in range(2)]
            kg0b = k.sb(st, [128, 128], F32, "kg0b")
            ovl = k.sb(st, [128, 32], F32, "ovl")
            ss = k.sb(st, [128, 1], F32, "ss")
            junk = k.sb(st, [128, 128], F32, "junk")
            kn16 = k.sb(st, [128, 128], BF16, "kn16")
            kT16 = k.sb(st, [128, 2, 128], BF16, "kT16")
            va16 = k.sb(st, [128, 2, 162], BF16, "va16")
            pc = [k.ps(st, [128, 128], F32, "pc") for _ in range(2)]
            ptr = k.ps(st, [128, 128], F32, "ptr")
            for g in range(2):
                k.dma(src[:, g, :], kcT_d[g], [self.scr["kcT"][1]], [src])
                k.dma(src[:, 2 + g, :], vcT_d[g], [self.scr["vcT"][1]], [src])
            k.dma(pe32[:, 0, :], I["pekT"], [], [pe32])
            k.dma(pe32[:, 1, :], I["pevT"], [], [pe32])
            k.dma(kg0b[:], I["kg0b"], [], [kg0b])
            k.dma(ovl[:], I["ovl"], [], [ovl])
            k.memset(kT16[:], 0.0, [kT16])
            k.memset(va16[:], 0.0, [va16])
            for kv, wname in enumerate(("cwk", "cwv")):
                k.dma(wst[:], I[wname].rearrange("(l d) o -> d l o", d=128), [], [wst])
                k.copy(w16[kv][:], wst[:], [wst], [w16[kv]], eng="pool")
                k.copy(peB[kv][:], pe32[:, kv, :].unsqueeze(2).to_broadcast([128, 32, 127]), [pe32], [peB[kv]])
            for kv in range(2):
                for g in range(2):
                    p = pc[(kv * 2 + g) % 2]
                    for l in range(32):
                        k.mm(p[0:127, :], src[:, kv * 2 + g, l:l + 16 * 126 + 1:16], w16[kv][:, l, :], l == 0, False, [src, w16[kv]], [p])
                    for l in range(32):
                        k.mm(p[0:127, :], peB[kv][:, l, :], w16[kv][:, l, :], False, l == 31, [peB[kv], w16[kv]], [p])
                    if kv == 0:
                        k.act(junk[0:127, :], p[0:127, :], AF.Square, [p], [junk, ss], accum_out=ss[0:127, :])
                        k.act(ss[0:127, :], ss[0:127, :], AF.Sqrt, [ss, self.eps_t], [ss], bias=self.eps_t[0:127, :], scale=1.0 / 128.0)
                        k.recip(ss[0:127, :], ss[0:127, :], [ss], [ss])
                        k.stt(kn16[0:127, :], p[0:127, :], ss[0:127, :], kg0b[0:127, :], ALU.mult, ALU.mult, [p, ss, kg0b], [kn16])
                        k.mm(ptr[:, 0:127], kn16[0:127, :], self.ident16[0:127, 0:127], True, True, [kn16, self.ident16], [ptr])
                        k.copy(kT16[:, g, 0:127], ptr[:, 0:127], [ptr], [kT16])
                    else:
                        k.copy(va16[0:127, g, 0:128], p[0:127, :], [p], [va16])
                        k.memset(va16[0:127, g, 128:129], 1.0, [va16])
                        k.copy(va16[0:127, g, 129:161], ovl[0:127, :], [ovl], [va16])
            k.dma(kcmpT_d, kT16[:], [kT16], [kcmpT_r])
            k.dma(vcmp_d, va16[:], [va16], [vcmp_r])

    def p4_attn(self):
        k, I = self.k, self.I
        sc = self.scr
        yT_d, yT_r = self.scratch("yT", [16, 128, S], BF16)
        with ExitStack() as st:
            qT = k.sb(st, [128, 8, S], BF16, "qT")
            ksT = k.sb(st, [128, 2, S], BF16, "ksT")
            kwT = k.sb(st, [128, 2, S], BF16, "kwT")
            kcT = k.sb(st, [128, 2, 128], BF16, "kcT")
            vsa = k.sb(st, [128, 16, 2, 130], BF16, "vsa")
            vwa = k.sb(st, [128, 16, 2, 130], BF16, "vwa")
            vca = k.sb(st, [128, 2, 162], BF16, "vca")
            gts = k.sb(st, [128, 16, 24], F32, "gts")
            maskc = k.sb(st, [128, S], F32, "maskc")
            cm = k.sb(st, [128, 4, 512], F32, "cm")
            wm = k.sb(st, [128, 8, 512], F32, "wm")
            ex32 = k.sb(st, [32, 16, 128], F32, "ex32")
            ex16 = k.sb(st, [32, 16, 128], BF16, "ex16")
            vm = k.sb(st, [128, 16, 32], F32, "vm")
            fb = k.sb(st, [128, 16, 32], F32, "fb")
            gab = k.sb(st, [128, 1024], F32, "gab")
            for j in range(8):
                k.dma(qT[:, j, :], sc["qT"][0][j], [sc["qT"][1]], [qT], eng=("sp" if j % 2 else "pool"))
            for g in range(2):
                k.dma(ksT[:, g, :], sc["ksT"][0][g], [sc["ksT"][1]], [ksT])
                k.dma(kwT[:, g, :], sc["kwT"][0][g], [sc["kwT"][1]], [kwT])
            k.dma(kcT[:], sc["kcmpT"][0], [sc["kcmpT"][1]], [kcT])
            k.dma(vca[:], sc["vcmp"][0], [sc["vcmp"][1]], [vca])
            k.memset(vsa[:], 1.0, [vsa])
            k.memset(vwa[:], 1.0, [vwa])
            for g in range(2):
                k.dma(vsa[:, :, g, 0:128], sc["vs"][0].rearrange("(c p) x -> p c x", p=128)[:, :, g * 128:(g + 1) * 128], [sc["vs"][1]], [vsa])
                k.dma(vwa[:, :, g, 0:128], sc["vw"][0].rearrange("(c p) x -> p c x", p=128)[:, :, g * 128:(g + 1) * 128], [sc["vw"][1]], [vwa])
            k.dma(gts[:].rearrange("p t c -> p (t c)"), sc["gates"][0], [sc["gates"][1]], [gts])
            k.dma(maskc[:], I["maskc"], [], [maskc])
            k.dma(cm[:].rearrange("p a b -> p (a b)"), I["cm"], [], [cm])
            k.dma(wm[:].rearrange("p a b -> p (a b)"), I["wm"], [], [wm])
            k.dma(ex32[:].rearrange("p a b -> p (a b)"), I["ex"], [], [ex32])
            k.copy(ex16[:], ex32[:], [ex32], [ex16])
            k.dma(vm[:].rearrange("p a b -> p (a b)"), I["vm"], [], [vm])
            k.dma(fb[:].rearrange("p a b -> p (a b)"), I["fb"], [], [fb])
            k.dma(gab[:], I["gab"], [], [gab])
            O = k.sb(st, [128, 4, 1024], F32, "O")
            imp = [k.sb(st, [128, 32], F32, f"imp{i}") for i in range(4)]
            e16 = [k.sb(st, [128, 512], BF16, f"e16{i}") for i in range(3)]
            p16 = [k.sb(st, [128, 512], BF16, f"p16{i}") for i in range(3)]
            mskS = k.sb(st, [128, 16, 512], BF16, "mskS")
            selT16 = k.sb(st, [32, 512], BF16, "selT16")
            sel16 = [k.sb(st, [128, 32], BF16, f"sel16{i}") for i in range(2)]
            imp2 = [k.sb(st, [128, 32], F32, f"imp2{i}") for i in range(2)]
            wk = [k.sb(st, [128, 32], F32, f"wk{i}") for i in range(2)]
            m8 = [k.sb(st, [128, 16], F32, f"m8{i}") for i in range(2)]
            den = [k.sb(st, [128, 2], F32, f"den{i}") for i in range(4)]
            ssq = k.sb(st, [128, 1], F32, "ssq")
            junk = k.sb(st, [128, 1024], F32, "junk4")
            yn16 = k.sb(st, [128, 1024], BF16, "yn16")
            yT16 = [k.sb(st, [128, 512], BF16, f"yT16{i}") for i in range(2)]
            pS = [k.ps(st, [128, 512], F32, "pS") for _ in range(2)]
            pA = k.ps(st, [128, 512], F32, "pA")
            pACC = [k.ps(st, [128, 512], F32, "pACC") for _ in range(4)]
            pM = [k.ps(st, [128, 512], F32, "pM")] * 2
            pT = pA
            cnt = {"s": 0, "e": 0, "m": 0, "d": 0, "y": 0, "sel": 0}

            def finish_head(sub, hd, acc_ap, den_ap, tt_, gcol, first, Rp):
                dn = den[cnt["d"] % 4]
                cnt["d"] += 1
                k.ts(dn[:, 0:1], den_ap, 1e-30, None, ALU.max, None, Rp, [dn])
                k.recip(dn[:, 0:1], dn[:, 0:1], [dn], [dn])
                k.tt(dn[:, 1:2], dn[:, 0:1], gts[:, tt_, gcol:gcol + 1], ALU.mult, [dn, gts], [dn])
                osl = O[:, sub, hd * 128:(hd + 1) * 128]
                if first:
                    k.ts(osl, acc_ap, dn[:, 1:2], None, ALU.mult, None, Rp + [dn], [O])
                else:
                    k.stt(osl, acc_ap, dn[:, 1:2], osl, ALU.mult, ALU.add, Rp + [dn, O], [O])
                return dn

            def run_steps(i, steps):
                qsl = slice(i * 512, (i + 1) * 512)
                state = {}

                def s0(n):
                    keyT, vaug, g, r, kc, mask_of, first, last, gbranch = steps[n]
                    ps_ = pS[cnt["s"] % 2]
                    cnt["s"] += 1
                    k.mm(ps_[:], keyT[:, g, kc * 128:(kc + 1) * 128], qT[:, g * 4 + r, qsl], True, True, [keyT, qT], [ps_])
                    state[n] = ps_

                def s12(n):
                    keyT, vaug, g, r, kc, mask_of, first, last, gbranch = steps[n]
                    ps_ = state.pop(n)
                    hd = g * 4 + r
                    e = e16[cnt["e"] % 3]
                    p_ = p16[cnt["e"] % 3]
                    cnt["e"] += 1
                    k.act(e[:], ps_[:], AF.Exp, [ps_], [e], scale=SCALE)
                    mk, mkR, eng = mask_of(kc)
                    k.tt(p_[:], e[:], mk, ALU.mult, [e, mkR], [p_], eng=eng)
                    for sub in range(4):
                        acc = pACC[sub]
                        k.mm(acc[:, 0:129], p_[:, sub * 128:(sub + 1) * 128], vaug[:, kc, g, 0:129], first, last, [p_, vaug], [acc])
                    if last:
                        for sub in range(4):
                            acc = pACC[sub]
                            finish_head(sub, hd, acc[:, 0:128], acc[:, 128:129], i * 4 + sub, g * 12 + r * 3 + gbranch, False, [acc])

                s0(0)
                for n in range(len(steps)):
                    if n + 1 < len(steps):
                        s0(n + 1)
                    s12(n)

            for i in range(4):
                qsl = slice(i * 512, (i + 1) * 512)
                for g in range(2):
                    for r in range(4):
                        hd = g * 4 + r
                        ps_ = pS[cnt["s"] % 2]
                        cnt["s"] += 1
                        k.mm(ps_[0:127, :], kcT[:, g, 0:127], qT[:, hd, qsl], True, True, [kcT, qT], [ps_])
                        e = e16[cnt["e"] % 3]
                        p_ = p16[cnt["e"] % 3]
                        cnt["e"] += 1
                        k.act(e[0:127, :], ps_[0:127, :], AF.Exp, [ps_], [e], scale=SCALE)
                        k.tt(p_[0:127, :], e[0:127, :], maskc[0:127, qsl], ALU.mult, [e, maskc], [p_])
                        for sub in range(4):
                            k.mm(pA[:, 0:161], p_[0:127, sub * 128:(sub + 1) * 128], vca[0:127, g, 0:161], True, True, [p_, vca], [pA])
                            dn = finish_head(sub, hd, pA[:, 0:128], pA[:, 128:129], i * 4 + sub, g * 12 + r * 3, True, [pA])
                            if r == 0:
                                k.ts(imp[sub][:], pA[:, 129:161], dn[:, 0:1], None, ALU.mult, None, [pA, dn], [imp[sub]])
                            else:
                                k.stt(imp[sub][:], pA[:, 129:161], dn[:, 0:1], imp[sub][:], ALU.mult, ALU.add, [pA, dn, imp[sub]], [imp[sub]])
                    psel = pM[cnt["m"] % 2]
                    cnt["m"] += 1
                    for sub in range(4):
                        tt_ = i * 4 + sub
                        j = cnt["sel"] % 2
                        cnt["sel"] += 1
                        k.tt(imp2[j][:], imp[sub][:], vm[:, tt_, :], ALU.mult, [imp[sub], vm], [imp2[j]])
                        k.tt(imp2[j][:], imp2[j][:], fb[:, tt_, :], ALU.add, [imp2[j], fb], [imp2[j]])
                        k.fn("dve", lambda e, o=m8[j][:, 0:8], a=imp2[j][:]: e.max(out=o, in_=a), [imp2[j]], [m8[j]])
                        k.fn("dve", lambda e, o=wk[j][:], a=m8[j][:, 0:8], b=imp2[j][:]: e.match_replace(out=o, in_to_replace=a, in_values=b, imm_value=-1e30), [imp2[j], m8[j]], [wk[j]])
                        k.fn("dve", lambda e, o=m8[j][:, 8:16], a=wk[j][:]: e.max(out=o, in_=a), [wk[j]], [m8[j]])
                        k.ts(sel16[j][:], imp2[j][:], m8[j][:, 15:16], None, ALU.is_ge, None, [imp2[j], m8[j]], [sel16[j]])
                        k.mm(psel[0:32, sub * 128:(sub + 1) * 128], sel16[j][:], self.ident16[:], True, True, [sel16[j], self.ident16], [psel])
                    k.copy(selT16[:], psel[0:32, :], [psel], [selT16], eng="act")
                    nkc = 4 * i + 4
                    for kc in range(nkc):
                        pm_ = pM[cnt["m"] % 2]
                        cnt["m"] += 1
                        k.mm(pm_[:], ex16[:, kc, :], selT16[:], True, True, [ex16, selT16], [pm_])
                        if kc >= 4 * i:
                            k.tt(mskS[:, kc, :], pm_[:], cm[:, kc - 4 * i, :], ALU.mult, [pm_, cm], [mskS])
                        else:
                            k.copy(mskS[:, kc, :], pm_[:], [pm_], [mskS], eng="act")
                    steps = []
                    for r in range(4):
                        for kc in range(nkc):
                            steps.append((ksT, vsa, g, r, kc, (lambda kc_: (mskS[:, kc_, :], mskS, "pool")), kc == 0, kc == nkc - 1, 1))
                    for r in range(4):
                        chunks = list(range(max(0, 4 * i - 4), 4 * i + 4))
                        for kc in chunks:
                            steps.append((kwT, vwa, g, r, kc, (lambda kc_, i_=i: (wm[:, kc_ - 4 * i_ + 4, :], wm, "dve")), kc == chunks[0], kc == chunks[-1], 2))
                    run_steps(i, steps)
                yt = yT16[i % 2]
                for c in range(8):
                    pass
                ytiles = []
                for sub in range(4):
                    k.act(junk[:], O[:, sub, :], AF.Square, [O], [junk, ssq], accum_out=ssq[:])
                    k.act(ssq[:], ssq[:], AF.Sqrt, [ssq, self.eps_t], [ssq], bias=self.eps_t[:], scale=1.0 / 1024.0)
                    k.recip(ssq[:], ssq[:], [ssq], [ssq])
                    k.stt(yn16[:], O[:, sub, :], ssq[:], gab[:], ALU.mult, ALU.mult, [O, ssq, gab], [yn16])
                    for half in range(2):
                        for cc in range(4):
                            c = half * 4 + cc
                            k.mm(pT[:, cc * 128:(cc + 1) * 128], yn16[:, c * 128:(c + 1) * 128], self.ident16[:], True, True, [yn16, self.ident16], [pT])
                        dst = k.sb(st, [128, 4, 128], BF16, "ytmp") if False else None
                        yb = yT16[cnt["y"] % 2]
                        cnt["y"] += 1
                        k.copy(yb[:], pT[:], [pT], [yb], eng=("act" if half == 0 else "dve"))
                        for cc in range(4):
                            c = half * 4 + cc
                            k.dma(yT_d[c][:, i * 512 + sub * 128:i * 512 + (sub + 1) * 128], yb[:, cc * 128:(cc + 1) * 128], [yb], [yT_r])
            if "O_dbg" in self.dbg:
                pass

    def p5_rnn(self):
        k, I = self.k, self.I
        sc = self.scr
        yT_d, yT_r = sc["yT"]
        xr_d, xr_r = sc["xrT"]
        xg_d, xg_r = sc["xgT"]
        with ExitStack() as st:
            cw = k.sb(st, [128, 8, 4], F32, "cw")
            cb = k.sb(st, [128, 8], F32, "cb")
            nba = k.sb(st, [128, 8], F32, "nba")
            nbi = k.sb(st, [128, 8], F32, "nbi")
            lam = k.sb(st, [128, 8], F32, "lam")
            clam = k.sb(st, [128, 8], F32, "clam")
            gr = k.sb(st, [128, 8], F32, "gr")
            wst = k.sb(st, [128, 2, 128], F32, "wst5")
            w16 = [k.sb(st, [128, 2, 128], BF16, f"w165{i}") for i in range(2)]
            k.dma(cw[:].rearrange("p a b -> p (a b)"), I["convw"], [], [cw])
            k.dma(cb[:], I["convb"], [], [cb])
            k.dma(nba[:], I["lba"], [], [nba])
            k.dma(nbi[:], I["lbi"], [], [nbi])
            k.dma(lam[:], I["lam"], [], [lam])
            k.dma(gr[:], I["gr"], [], [gr])
            k.ts(nba[:], nba[:], -1.0, None, ALU.mult, None, [nba], [nba])
            k.ts(nbi[:], nbi[:], -1.0, None, ALU.mult, None, [nbi], [nbi])
            k.act(clam[:], lam[:], AF.Exp, [lam], [clam], scale=-1.0)
            k.ts(clam[:], clam[:], 1.0, None, ALU.add, None, [clam], [clam])
            k.act(clam[:], clam[:], AF.Ln, [clam], [clam])
            k.ts(clam[:], clam[:], -8.0, None, ALU.mult, None, [clam], [clam])
            orn = k.sb(st, [128, 8, S], F32, "orn")
            xp = k.sb(st, [128, S + 4], F32, "xp")
            xg = k.sb(st, [128, S], F32, "xg")
            u = k.sb(st, [128, S], F32, "u")
            u16 = k.sb(st, [128, S], BF16, "u16")
            ra = k.sb(st, [128, S], F32, "ra")
            ig = k.sb(st, [128, S], F32, "ig")
            bb = k.sb(st, [128, S], F32, "bb")
            sq16 = k.sb(st, [128, S], BF16, "sq165")
            pg = [k.ps(st, [128, 512], F32, "pg") for _ in range(2)]
            pss = [k.ps(st, [128, 512], F32, "pss5") for _ in range(4)]
            k.memset(xp[:, 0:4], 0.0, [xp])
            ci = 0
            for n in range(8):
                k.dma(xp[:, 4:S + 4], xr_d[n], [xr_r], [xp])
                k.dma(xg[:], xg_d[n], [xg_r], [xg], eng="pool")
                wb = w16[n % 2]
                k.dma(wst[:, 0, :], I["wa"][n], [], [wst])
                k.dma(wst[:, 1, :], I["wi"][n], [], [wst])
                k.copy(wb[:], wst[:], [wst], [wb], eng="pool")
                k.ts(u[:], xp[:, 1:S + 1], cw[:, n, 0:1], cb[:, n:n + 1], ALU.mult, ALU.add, [xp, cw, cb], [u])
                for i_ in range(1, 4):
                    k.stt(u[:], xp[:, 1 + i_:S + 1 + i_], cw[:, n, i_:i_ + 1], u[:], ALU.mult, ALU.add, [xp, cw, u], [u])
                k.copy(u16[:], u[:], [u], [u16], eng="act")
                for which, dst, nb in ((0, ra, nba), (1, ig, nbi)):
                    for tg in range(4):
                        p = pg[ci % 2]
                        ci += 1
                        sl = slice(tg * 512, (tg + 1) * 512)
                        k.mm(p[:], wb[:, which, :], u16[:, sl], True, True, [wb, u16], [p])
                        k.act(dst[:, sl], p[:], AF.Exp, [p, nb], [dst], bias=nb[:, n:n + 1], scale=-1.0)
                    k.ts(dst[:], dst[:], 1.0, None, ALU.add, None, [dst], [dst], eng="pool")
                    k.recip(dst[:], dst[:], [dst], [dst])
                k.act(ra[:], ra[:], AF.Exp, [ra, clam], [ra], scale=clam[:, n:n + 1])
                k.tt(bb[:], ra[:], ra[:], ALU.mult, [ra], [bb])
                k.ts(bb[:], bb[:], -1.0, 1.0, ALU.mult, ALU.add, [bb], [bb])
                k.act(bb[:], bb[:], AF.Sqrt, [bb], [bb])
                k.tt(bb[:], bb[:], ig[:], ALU.mult, [bb, ig], [bb], eng="pool")
                k.tt(bb[:], bb[:], u[:], ALU.mult, [bb, u], [bb])
                k.fn("dve", lambda e, o=ig[:], a=ra[:], b=bb[:]: e.tensor_tensor_scan(o, a, b, 0.0, ALU.mult, ALU.add), [ra, bb, ig], [ig])
                k.act(xg[:], xg[:], AF.Gelu, [xg], [xg])
                k.tt(orn[:, n, :], xg[:], ig[:], ALU.mult, [xg, ig], [orn])
                k.act(sq16[:], orn[:, n, :], AF.Square, [orn], [sq16])
                for tg in range(4):
                    k.mm(pss[tg][:], self.ones16[:], sq16[:, tg * 512:(tg + 1) * 512], n == 0, n == 7, [self.ones16, sq16], [pss[tg]])
            rstd = ra
            for tg in range(4):
                sl = slice(tg * 512, (tg + 1) * 512)
                k.act(rstd[:, sl], pss[tg][:], AF.Sqrt, [pss[tg], self.eps_t], [rstd], bias=self.eps_t[:], scale=1.0 / 1024.0)
            k.recip(rstd[:], rstd[:], [rstd], [rstd])
            for n in range(8):
                k.stt(u16[:], orn[:, n, :], gr[:, n:n + 1], rstd[:], ALU.mult, ALU.mult, [orn, gr, rstd], [u16])
                k.dma(yT_d[8 + n], u16[:], [u16], [yT_r])

    def p6_out(self):
        k, I = self.k, self.I
        sc = self.scr
        yT_d, yT_r = sc["yT"]
        x1_d, x1_r = self.scratch("x1T", [16, 128, S], F32)
        h2_d, h2_r = self.scratch("h2T", [16, 128, S], BF16)
        xTv = I["xT"].rearrange("(k p) t -> k p t", p=128)
        wv = I["w_out"].rearrange("(k p) n -> p k n", p=128)
        with ExitStack() as st:
            rstd = k.sb(st, [128, S], F32, "rstd2")
            with ExitStack() as s1:
                yT = k.sb(s1, [128, 16, S], BF16, "yTs")
                wb = [k.sb(s1, [128, 16, 256], BF16, f"wo{i}") for i in range(2)]
                stg = [k.sb(s1, [128, 16, 128], F32, f"wos{i}") for i in range(2)]
                xb = [k.sb(s1, [128, S], F32, f"x6{i}") for i in range(2)]
                ob = [k.sb(s1, [128, S], F32, f"o6{i}") for i in range(2)]
                sq = [k.sb(s1, [128, S], BF16, f"sq6{i}") for i in range(2)]
                pz = [k.ps(s1, [128, 512], F32, "pz6") for _ in range(2)]
                pss = [k.ps(s1, [128, 512], F32, "pss6") for _ in range(4)]
                for c in range(16):
                    k.dma(yT[:, c, :], yT_d[c], [yT_r], [yT], eng=("sp" if c % 2 else "pool"))
                ci = 0
                for jg in range(8):
                    b = wb[jg % 2]
                    for h in range(2):
                        sg = stg[(jg * 2 + h) % 2]
                        k.dma(sg[:], wv[:, :, jg * 256 + h * 128:jg * 256 + (h + 1) * 128], [], [sg])
                        k.copy(b[:, :, h * 128:(h + 1) * 128], sg[:], [sg], [b], eng="pool")
                    for jj in range(2):
                        j = jg * 2 + jj
                        xt, ot, sqt = xb[j % 2], ob[j % 2], sq[j % 2]
                        k.dma(xt[:], xTv[j], [], [xt])
                        for tg in range(4):
                            p = pz[ci % 2]
                            ci += 1
                            sl = slice(tg * 512, (tg + 1) * 512)
                            for c in range(16):
                                k.mm(p[:], b[:, c, jj * 128:(jj + 1) * 128], yT[:, c, sl], c == 0, c == 15, [b, yT], [p])
                            k.stt(ot[:, sl], p[:], self.modT[:, 32 + j:33 + j], xt[:, sl], ALU.mult, ALU.add, [p, self.modT, xt], [ot])
                        k.dma(x1_d[j], ot[:], [ot], [x1_r])
                        k.act(sqt[:], ot[:], AF.Square, [ot], [sqt])
                        for tg in range(4):
                            k.mm(pss[tg][:], self.ones16[:], sqt[:, tg * 512:(tg + 1) * 512], j == 0, j == 15, [self.ones16, sqt], [pss[tg]])
                for tg in range(4):
                    sl = slice(tg * 512, (tg + 1) * 512)
                    k.act(rstd[:, sl], pss[tg][:], AF.Sqrt, [pss[tg], self.eps_t], [rstd], bias=self.eps_t[:], scale=1.0 / float(D))
                k.recip(rstd[:], rstd[:], [rstd], [rstd])
            k.barrier()
            with ExitStack() as s2:
                xb = [k.sb(s2, [128, S], F32, f"x6b{i}") for i in range(2)]
                tp = [k.sb(s2, [128, S], F32, f"t6b{i}") for i in range(2)]
                hb = [k.sb(s2, [128, S], BF16, f"h6b{i}") for i in range(2)]
                for j in range(16):
                    xt, tt_, ht = xb[j % 2], tp[j % 2], hb[j % 2]
                    k.dma(xt[:], x1_d[j], [x1_r], [xt])
                    k.stt(tt_[:], xt[:], self.G2[:, j:j + 1], rstd[:], ALU.mult, ALU.mult, [xt, self.G2, rstd], [tt_])
                    k.act(ht[:], tt_[:], AF.Identity, [tt_, self.modT], [ht], bias=self.modT[:, 48 + j:49 + j], scale=1.0)
                    k.dma(h2_d[j], ht[:], [ht], [h2_r])

    def p7_peer(self):
        k, I = self.k, self.I
        sc = self.scr
        h2_d, h2_r = sc["h2T"]
        x1_d, x1_r = sc["x1T"]
        qp_d, qp_r = self.scratch("qpT", [16, 128, S], BF16)
        wv = I["wq"].rearrange("(k p) n -> p k n", p=128)
        with ExitStack() as st:
            h2T = k.sb(st, [128, 16, S], BF16, "h2Ts")
            wb = [k.sb(st, [128, 16, 256], BF16, f"wq{i}") for i in range(2)]
            stg = [k.sb(st, [128, 16, 128], F32, f"wqs{i}") for i in range(2)]
            ob = [k.sb(st, [128, S], BF16, f"oq{i}") for i in range(2)]
            pz = [k.ps(st, [128, 512], F32, "pz7") for _ in range(2)]
            for c in range(16):
                k.dma(h2T[:, c, :], h2_d[c], [h2_r], [h2T], eng=("sp" if c % 2 else "pool"))
            ci = 0
            for jg in range(8):
                b = wb[jg % 2]
                for h in range(2):
                    sg = stg[(jg * 2 + h) % 2]
                    k.dma(sg[:], wv[:, :, jg * 256 + h * 128:jg * 256 + (h + 1) * 128], [], [sg])
                    k.copy(b[:, :, h * 128:(h + 1) * 128], sg[:], [sg], [b], eng="pool")
                for jj in range(2):
                    j = jg * 2 + jj
                    ot = ob[j % 2]
                    for tg in range(4):
                        p = pz[ci % 2]
                        ci += 1
                        sl = slice(tg * 512, (tg + 1) * 512)
                        for c in range(16):
                            k.mm(p[:], b[:, c, jj * 128:(jj + 1) * 128], h2T[:, c, sl], c == 0, c == 15, [b, h2T], [p])
                        k.copy(ot[:, sl], p[:], [p], [ot], eng=("act" if tg % 2 == 0 else "dve"))
                    k.dma(qp_d[j], ot[:], [ot], [qp_r])
        k.barrier()
        SLACK = 1.0 - 4e-6
        with ExitStack() as st:
            h2s = k.sb(st, [128, 16, 512], BF16, "h2s")
            E1s = k.sb(st, [128, 4, 8, 128], F32, "E1s")
            E2s = k.sb(st, [128, 4, 8, 128], F32, "E2s")
            dg16 = k.sb(st, [128, 4, 8, 128], BF16, "dg16")
            accT = k.sb(st, [128, 16, 512], F32, "accT")
            keys16 = k.sb(st, [128, 16, 128], BF16, "keys16")
            with ExitStack() as s0:
                k32 = k.sb(s0, [128, 16, 128], F32, "k32")
                k.dma(k32[:].rearrange("p a b -> p (a b)"), I["keysT"], [], [k32])
                k.copy(keys16[:], k32[:], [k32], [keys16])
            k.barrier()
            for su in range(4):
                tsl = slice(su * 512, (su + 1) * 512)
                k.dma(h2s[:], h2_d.rearrange("c p t -> p c t")[:, :, tsl], [h2_r], [h2s])
                with ExitStack() as sb_:
                    qps = k.sb(sb_, [128, 16, 512], BF16, "qps")
                    s_sb = k.sb(sb_, [128, 16, 128], F32, "s_sb")
                    wk = k.sb(sb_, [128, 16, 128], F32, "wk7")
                    wk2 = k.sb(sb_, [128, 8, 256], F32, "wk72")
                    v16R = [Res() for _ in range(16)]
                    wkR = [Res() for _ in range(16)]
                    c16R = [Res() for _ in range(8)]
                    wk2R = [Res() for _ in range(8)]
                    v16 = k.sb(sb_, [128, 16, 16], F32, "v16")
                    cand = k.sb(sb_, [128, 8, 256], F32, "cand")
                    c16 = k.sb(sb_, [128, 8, 16], F32, "c16")
                    en = k.sb(sb_, [128, 8, 16], F32, "en")
                    negm = k.sb(sb_, [128, 16], F32, "negm")
                    negM = k.sb(sb_, [128, 8], F32, "negM")
                    Z = k.sb(sb_, [128, 8], F32, "Z")
                    th = k.sb(sb_, [128, 8], F32, "th")
                    cf = k.sb(sb_, [128, 8], F32, "cf")
                    E1t = k.sb(sb_, [128, 8, 128], F32, "E1t")
                    pS_ = [k.ps(sb_, [128, 4, 128], F32, "pS7") for _ in range(2)]
                    k.dma(qps[:], qp_d.rearrange("c p t -> p c t")[:, :, tsl], [qp_r], [qps])
                    for tl in range(4):
                        for hg in range(4):
                            p = pS_[hg % 2]
                            for q4 in range(4):
                                hp = hg * 4 + q4
                                k.mm(p[:, q4, :], qps[:, hp, tl * 128:(tl + 1) * 128], keys16[:, hp, :], True, True, [qps, keys16], [p])
                            k.copy(s_sb[:, hg * 4:(hg + 1) * 4, :], p[:], [p], [s_sb], eng=("act" if hg % 2 == 0 else "dve"))
                        for hp in range(16):
                            k.fn("dve", lambda e, o=v16[:, hp, 0:8], a=s_sb[:, hp, :]: e.max(out=o, in_=a), [s_sb], [v16R[hp]])
                        for hp in range(16):
                            k.fn("dve", lambda e, o=wk[:, hp, :], a=v16[:, hp, 0:8], b=s_sb[:, hp, :]: e.match_replace(out=o, in_to_replace=a, in_values=b, imm_value=-1e30), [s_sb, v16R[hp]], [wkR[hp]])
                        for hp in range(16):
                            k.fn("dve", lambda e, o=v16[:, hp, 8:16], a=wk[:, hp, :]: e.max(out=o, in_=a), [wkR[hp]], [v16R[hp]])
                        v16r = v16[:].rearrange("p (h two) i -> p h two i", two=2)
                        k.tt(cand[:].rearrange("p h (i j) -> p h i j", i=16),
                             v16r[:, :, 0, :].unsqueeze(3).to_broadcast([128, 8, 16, 16]),
                             v16r[:, :, 1, :].unsqueeze(2).to_broadcast([128, 8, 16, 16]), ALU.add, v16R, [cand])
                        for h in range(8):
                            k.fn("dve", lambda e, o=c16[:, h, 0:8], a=cand[:, h, :]: e.max(out=o, in_=a), [cand], [c16R[h]])
                        for h in range(8):
                            k.fn("dve", lambda e, o=wk2[:, h, :], a=c16[:, h, 0:8], b=cand[:, h, :]: e.match_replace(out=o, in_to_replace=a, in_values=b, imm_value=-1e30), [cand, c16R[h]], [wk2R[h]])
                        for h in range(8):
                            k.fn("dve", lambda e, o=c16[:, h, 8:16], a=wk2[:, h, :]: e.max(out=o, in_=a), [wk2R[h]], [c16R[h]])
                        k.ts(negm[:], v16[:, :, 0], -1.0, None, ALU.mult, None, v16R, [negm])
                        k.ts(negM[:], c16[:, :, 0], -1.0, None, ALU.mult, None, c16R, [negM])
                        for h in range(8):
                            k.act(en[:, h, :], c16[:, h, :], AF.Exp, [c16R[h], negM], [en, Z], bias=negM[:, h:h + 1], scale=1.0, accum_out=Z[:, h:h + 1])
                        k.recip(Z[:], Z[:], [Z], [Z])
                        k.tt(th[:], en[:, :, 15], Z[:], ALU.mult, [en, Z], [th])
                        k.ts(th[:], th[:], SLACK, None, ALU.mult, None, [th], [th])
                        k.ts(cf[:], en[:, :, 15], SLACK, None, ALU.mult, None, [en], [cf])
                        k.recip(cf[:], cf[:], [cf], [cf])
                        for h in range(8):
                            k.act(E2s[:, tl, h, :], s_sb[:, 2 * h + 1, :], AF.Exp, [s_sb, negm], [E2s], bias=negm[:, 2 * h + 1:2 * h + 2], scale=1.0)
                            k.act(E1t[:, h, :], s_sb[:, 2 * h, :], AF.Exp, [s_sb, negm], [E1t], bias=negm[:, 2 * h:2 * h + 1], scale=1.0)
                            k.ts(dg16[:, tl, h, :], self.ident32[:], th[:, h:h + 1], None, ALU.mult, None, [self.ident32, th], [dg16], eng="pool")
                        k.tt(E1s[:, tl, :, :], E1t[:], cf[:].unsqueeze(2).to_broadcast([128, 8, 128]), ALU.mult, [E1t, cf], [E1s])
                k.barrier()
                if "peer_dbg" in self.dbg and su == 0:
                    for nm, tile_ in (("E1s", E1s), ("E2s", E2s)):
                        d, r = self.scratch(nm, [128, 4 * 8 * 128], F32)
                        k.dma(d, tile_[:].rearrange("p a b c -> p (a b c)"), [tile_], [r])
                with ExitStack() as sc_:
                    stgD = [k.sb(sc_, [128, 16, 128], F32, f"stgD{i}") for i in range(2)]
                    stgU = [k.sb(sc_, [128, 2048], F32, f"stgU{i}") for i in range(2)]
                    dn16 = [k.sb(sc_, [128, 16, 128], BF16, f"dn16{i}") for i in range(4)]
                    up16 = [[k.sb(sc_, [128, 2048], BF16, f"up16{j}_{i}") for i in range(4)] for j in range(2)]
                    GA16 = [k.sb(sc_, [128, 512], BF16, f"GA{i}") for i in range(2)]
                    Pt = [k.sb(sc_, [128, 8, 128], F32, f"Pt{i}") for i in range(2)]
                    mE = [k.sb(sc_, [128, 8, 128], BF16, f"mE{i}") for i in range(4)]
                    WA = [k.sb(sc_, [128, 512], BF16, f"WA{i}") for i in range(8)]
                    ev = [k.sb(sc_, [128, 512], F32, f"ev{i}") for i in range(2)]
                    pact = [k.ps(sc_, [128, 512], F32, "pact") for _ in range(2)]
                    pw = [k.ps(sc_, [128, 512], F32, "pw") for _ in range(2)]
                    po = [k.ps(sc_, [128, 512], F32, "po") for _ in range(2)]
                    cn = {"k": 0, "p": 0, "o": 0}
                    ngrp = 1 if "peer_short" in self.dbg else 32
                    pend_wa = []
                    pend_po = []

                    def flush_wa():
                        while pend_wa:
                            wa_, pw__, ga_ = pend_wa.pop(0)
                            k.tt(wa_[:], pw__[:], ga_[:], ALU.mult, [pw__, ga_], [wa_])

                    def flush_po():
                        while pend_po:
                            gi_, was_ = pend_po.pop(0)
                            for dc in range(16):
                                po_ = po[cn["o"] % 2]
                                e_ = ev[cn["o"] % 2]
                                cn["o"] += 1
                                for kl in range(4):
                                    k.mm(po_[:], up16[gi_ % 2][kl][:, dc * 128:(dc + 1) * 128], was_[kl][:], kl == 0, kl == 3, [up16[gi_ % 2][kl], was_[kl]], [po_])
                                if gi_ == 0:
                                    k.copy(accT[:, dc, :], po_[:], [po_], [accT], eng="act")
                                else:
                                    k.copy(e_[:], po_[:], [po_], [e_], eng="act")
                                    k.tt(accT[:, dc, :], accT[:, dc, :], e_[:], ALU.add, [accT, e_], [accT], eng="pool")

                    for gi in range(ngrp):
                        was = []
                        for kl in range(4):
                            kap = gi * 4 + kl
                            sd, su_ = stgD[kap % 2], stgU[kap % 2]
                            dn, up = dn16[kl], up16[gi % 2][kl]
                            k.dma(sd[:].rearrange("p a b -> p (a b)"), I["downB"][kap * 128:(kap + 1) * 128, :], [], [sd], eng="sp")
                            k.dma(su_[:], I["up"][kap * 128:(kap + 1) * 128, :], [], [su_], eng="pool")
                            k.copy(dn[:], sd[:], [sd], [dn], eng="act")
                            k.copy(up[:], su_[:], [su_], [up], eng="act")
                        for kl in range(4):
                            kap = gi * 4 + kl
                            dn = dn16[kl]
                            pa_ = pact[cn["k"] % 2]
                            pw_ = pw[cn["k"] % 2]
                            ga = GA16[cn["k"] % 2]
                            wa = WA[cn["k"] % 8]
                            cn["k"] += 1
                            for c in range(16):
                                k.mm(pa_[:], dn[:, c, :], h2s[:, c, :], c == 0, c == 15, [dn, h2s], [pa_])
                            k.act(ga[:], pa_[:], AF.Gelu, [pa_], [ga])
                            for t2 in range(2):
                                tls = (2 * t2, 2 * t2 + 1)
                                mes = []
                                for tl in tls:
                                    k.tt(Pt[tl % 2][:], E2s[:, tl, :, :], E1s[:, tl, :, kap:kap + 1].to_broadcast([128, 8, 128]), ALU.mult, [E2s, E1s], [Pt[tl % 2]])
                                for tl in tls:
                                    me = mE[cn["p"] % 4]
                                    cn["p"] += 1
                                    k.stt(me[:], Pt[tl % 2][:], 1.0, Pt[tl % 2][:], ALU.is_ge, ALU.mult, [Pt[tl % 2]], [me])
                                    mes.append(me)
                                if t2 == 0:
                                    flush_wa()
                                for tl, me in zip(tls, mes):
                                    for h in range(8):
                                        k.mm(pw_[:, tl * 128:(tl + 1) * 128], me[:, h, :], dg16[:, tl, h, :], h == 0, h == 7, [me, dg16], [pw_])
                            pend_wa.append((wa, pw_, ga))
                            was.append(wa)
                            if kl == 0:
                                flush_po()
                        pend_po.append((gi, was))
                    flush_wa()
                    flush_po()
                    for dc in range(16):
                        x_ = ev[dc % 2]
                        k.dma(x_[:], x1_d[dc][:, tsl], [x1_r], [x_])
                        k.stt(x_[:], accT[:, dc, :], self.modT[:, 80 + dc:81 + dc], x_[:], ALU.mult, ALU.add, [accT, self.modT, x_], [x_])
                        k.dma(self.outT[dc * 128:(dc + 1) * 128, tsl], x_[:], [x_], [])
                k.barrier()


def _consts():
    f = np.float32
    half = 64
    freqs = (10000.0 ** (-np.arange(half, dtype=f) / f(half))).astype(f)
    ang = np.arange(S, dtype=f)[:, None] * freqs[None, :]
    cos = np.cos(ang).astype(f).T
    sin = np.sin(ang).astype(f).T
    cosT = np.concatenate([cos, cos], 0)
    sinT = np.concatenate([sin, sin], 0)
    rotm = np.zeros((128, 128), f)
    for m in range(64):
        rotm[m + 64, m] = -1.0
        rotm[m, m + 64] = 1.0
    ident = np.eye(128, dtype=f)
    t = np.arange(S)
    cst = np.arange(NCMP) * 16
    maskc = np.zeros((128, S), f)
    maskc[:NCMP] = ((cst + 31)[:, None] <= t[None, :]).astype(f)
    kk = np.arange(128)[:, None]
    tt = np.arange(512)[None, :]
    cm = np.stack([(128 * o + kk <= tt).astype(f) for o in range(4)], 1).reshape(128, 4 * 512)
    wm = np.stack([(((128 * rel + kk - tt) <= 0) & ((128 * rel + kk - tt) > -512)).astype(f)
                   for rel in range(-4, 4)], 1).reshape(128, 8 * 512)
    ex = np.zeros((32, 16, 128), f)
    for kc in range(16):
        for kq in range(128):
            ex[2 * kc + kq // 64, kc, kq] = 1.0
    ex = ex.reshape(32, 16 * 128)
    jb = np.arange(32)[None, :]
    cur = (t // 64)[:, None]
    forced = (jb == 0) | (jb == cur) | (jb == cur - 1)
    valid = (jb * 64) <= t[:, None]
    vm_ = (valid & ~forced).astype(f)
    fb_ = np.where(forced, 1e4, np.where(valid, 0.0, -1e4)).astype(f)
    vm = vm_.reshape(16, 128, 32).transpose(1, 0, 2).reshape(128, 512)
    fb = fb_.reshape(16, 128, 32).transpose(1, 0, 2).reshape(128, 512)
    sst = np.arange(32) * 64
    ov = np.maximum(np.minimum(cst[:, None] + 32, sst[None, :] + 64) - np.maximum(cst[:, None], sst[None, :]), 0)
    ovl = np.zeros((128, 32), f)
    ovl[:NCMP] = ov.astype(f) / 32.0
    return dict(cosT=cosT, sinT=sinT, rotm=rotm, ident=ident, maskc=maskc, cm=cm, wm=wm, ex=ex, vm=vm, fb=fb, ovl=ovl)


def prep_shared(inp):
    f = np.float32
    A = lambda v: np.ascontiguousarray(np.asarray(v, dtype=f))
    sh = {}
    sh["ada_w"] = A(inp["ada_w"][0])
    sh["ada_bT"] = A(inp["ada_b"][0].reshape(96, 128).T)
    sh["g1T"] = A(inp["norm_mix_g"][0].reshape(16, 128).T)
    sh["g2T"] = A(inp["norm_ffn_g"][0].reshape(16, 128).T)
    sh["w_in"] = A(inp["w_in"][0])
    sh["w_out"] = A(inp["w_out"][0])
    sh["wq"] = A(inp["peer_wq"][0])
    sh["qg"] = A(inp["q_norm_g"][0].reshape(128, 1))
    sh["kgT"] = A(inp["k_norm_g"][0].T)
    sh["kg0b"] = A(np.broadcast_to(inp["k_norm_g"][0, 0][None, :], (128, 128)))
    sh["pekT"] = A(inp["cmp_pe_k"][0].T)
    sh["pevT"] = A(inp["cmp_pe_v"][0].T)
    sh["cwk"] = A(inp["cmp_w_k"][0])
    sh["cwv"] = A(inp["cmp_w_v"][0])
    sh["gateb"] = A(np.broadcast_to(inp["gate_b"][0][None, :], (128, 24)))
    sh["convw"] = A(inp["conv_w"][0].reshape(4, 8, 128).transpose(2, 1, 0).reshape(128, 32))
    for nm, key in (("convb", "conv_b"), ("lba", "lru_ba"), ("lbi", "lru_bi"), ("lam", "lru_lam"), ("gr", "out_g_rnn")):
        sh[nm] = A(inp[key][0].reshape(8, 128).T)
    sh["gab"] = A(np.broadcast_to(inp["out_g_attn"][0][None, :], (128, 1024)))
    sh["wa"] = A(inp["lru_wa"][0])
    sh["wi"] = A(inp["lru_wi"][0])
    sh["keysT"] = A(inp["peer_keys"][0].transpose(3, 0, 1, 2).reshape(128, 16 * 128))
    sh["downB"] = A(inp["peer_down"][0].reshape(128, 128, 16, 128).transpose(0, 3, 2, 1).reshape(128 * 128, 16 * 128))
    sh["up"] = A(inp["peer_up"][0])
    sh.update(_consts())
    return sh


def prep_core(inp, b):
    f = np.float32
    return {"xT": np.ascontiguousarray(np.asarray(inp["x"][b], dtype=f).T),
            "c_col": np.ascontiguousarray(np.asarray(inp["c"][b], dtype=f).reshape(16, 128).T)}


def kernel(**inputs):
    inp = {k_: np.asarray(v) for k_, v in inputs.items()}
    sh = prep_shared(inp)
    nc = Prog().build()
    in_maps = []
    for b in range(8):
        m = dict(sh)
        m.update(prep_core(inp, b))
        in_maps.append(m)
    res = run_bass_kernel_spmd(nc, in_maps, core_ids=list(range(8)))
    out = np.stack([np.asarray(r["outT"]).T for r in res.results], 0)
    return np.ascontiguousarray(out.astype(np.float32))
```

```python
import numpy as np
from contextlib import ExitStack
import concourse.bass as bass
import concourse.mybir as mybir
from concourse.bass_utils import run_bass_kernel_spmd

F32 = mybir.dt.float32
BF16 = mybir.dt.bfloat16
AF = mybir.ActivationFunctionType
ALU = mybir.AluOpType
AX = mybir.AxisListType

D = 2048
S = 2048
NT = 16
NIN = 4632
C_Q, C_KC, C_VC, C_KS, C_VS, C_KW, C_VW, C_GL, C_XR, C_XG = 0, 1024, 1280, 1536, 1792, 2048, 2304, 2560, 2584, 3608
EPS = 1e-6
NCMP = 127
SCALE = 128 ** -0.5


class Res:
    __slots__ = ("name", "w", "r")

    def __init__(self, name=""):
        self.name = name
        self.w = None
        self.r = []


class Op:
    __slots__ = ("eng", "fn", "deps", "flag", "cval", "dma", "dsem", "dval")

    def __init__(self, eng, fn, dma=False):
        self.eng = eng
        self.fn = fn
        self.deps = []
        self.flag = False
        self.cval = 0
        self.dma = dma
        self.dsem = None
        self.dval = 0


class Sched:
    ENGS = ("pe", "act", "dve", "pool", "sp")

    def __init__(self, nc, n_dma_sems=16):
        self.nc = nc
        self.q = {e: [] for e in self.ENGS}
        self.n_dma_sems = n_dma_sems
        self.dma_rr = 0
        self.dma_last = [None] * n_dma_sems
        self.dma_cnt = [0] * n_dma_sems
        self.pending = {e: [] for e in self.ENGS}

    def _add(self, eng, fn, reads, writes, dma=False):
        op = Op(eng, fn, dma)
        deps = list(self.pending[eng])
        self.pending[eng] = []
        for r in reads:
            if r.w is not None:
                deps.append(r.w)
        for w in writes:
            if w.w is not None:
                deps.append(w.w)
            deps.extend(w.r)
        for r in reads:
            r.r.append(op)
        for w in writes:
            w.w = op
            w.r = []
        if dma:
            s = self.dma_rr
            self.dma_rr = (self.dma_rr + 1) % self.n_dma_sems
            prev = self.dma_last[s]
            if prev is not None:
                deps.append(prev)
            self.dma_last[s] = op
            self.dma_cnt[s] += 1
            op.dsem = s
            op.dval = 16 * self.dma_cnt[s]
        seen = set()
        for d in deps:
            if d is op or id(d) in seen:
                continue
            if (not d.dma) and d.eng == eng and eng == "pe":
                continue
            seen.add(id(d))
            op.deps.append(d)
            d.flag = True
        self.q[eng].append(op)
        return op

    def op(self, eng, fn, reads=(), writes=()):
        return self._add(eng, fn, list(reads), list(writes))

    def dma(self, eng, out, in_, reads=(), writes=()):
        return self._add(eng, lambda e: e.dma_start(out=out, in_=in_), list(reads), list(writes), dma=True)

    def barrier(self):
        lasts = []
        for e in self.ENGS:
            for op in reversed(self.q[e]):
                if not op.dma:
                    lasts.append(op)
                    break
        for s in range(self.n_dma_sems):
            if self.dma_last[s] is not None:
                lasts.append(self.dma_last[s])
        for e in self.ENGS:
            self.pending[e] = list(lasts)

    def emit(self):
        nc = self.nc
        with ExitStack() as st:
            esem = {e: st.enter_context(nc.semaphore(f"s_{e}")) for e in self.ENGS}
            dsem = [st.enter_context(nc.semaphore(f"d_{i}")) for i in range(self.n_dma_sems)]
            for e in self.ENGS:
                c = 0
                for op in self.q[e]:
                    if op.dma:
                        continue
                    if op.flag:
                        c += 1
                        op.cval = c
            block = st.enter_context(nc.Block())

            def run(ename, eobj):
                waited = {}
                for op in self.q[ename]:
                    need = {}
                    for d in op.deps:
                        if d.dma:
                            key, val = ("d", d.dsem), d.dval
                        else:
                            key, val = ("e", d.eng), d.cval
                        if val > need.get(key, 0):
                            need[key] = val
                    for key, val in need.items():
                        if waited.get(key, 0) >= val:
                            continue
                        waited[key] = val
                        sem = dsem[key[1]] if key[0] == "d" else esem[key[1]]
                        eobj.wait_ge(sem, val)
                    ins = op.fn(eobj)
                    if op.dma:
                        ins.then_inc(dsem[op.dsem], 16)
                    elif op.flag:
                        ins.then_inc(esem[ename], 1)
                if ename == "sp":
                    for s in range(self.n_dma_sems):
                        if self.dma_cnt[s] > 0:
                            eobj.wait_ge(dsem[s], 16 * self.dma_cnt[s])

            block.tensor(lambda e: run("pe", e))
            block.scalar(lambda e: run("act", e))
            block.vector(lambda e: run("dve", e))
            block.gpsimd(lambda e: run("pool", e))
            block.sync(lambda e: run("sp", e))


class T:
    __slots__ = ("t", "r")

    def __init__(self, t, name=""):
        self.t = t
        self.r = Res(name)

    def __getitem__(self, k):
        return self.t[k]


class K:
    def __init__(self, nc):
        self.nc = nc
        self.S = Sched(nc)
        self.uid = 0

    def sb(self, st, shape, dt, name=None):
        self.uid += 1
        n = f"{name or 't'}_{self.uid}"
        return T(st.enter_context(self.nc.sbuf_tensor(n, list(shape), dt)), n)

    def ps(self, st, shape, dt=F32, name=None):
        self.uid += 1
        n = f"{name or 'p'}_{self.uid}"
        return T(st.enter_context(self.nc.psum_tensor(n, list(shape), dt)), n)

    @staticmethod
    def _rs(xs):
        return [x.r if isinstance(x, T) else x for x in xs]

    def mm(self, out, lhsT, rhs, start, stop, R, W):
        self.S.op("pe", lambda e: e.matmul(out, lhsT, rhs, start=start, stop=stop), self._rs(R), self._rs(W))

    def act(self, out, in_, func, R, W, bias=None, scale=None, accum_out=None, eng="act"):
        kw = {}
        if bias is not None:
            kw["bias"] = bias
        if scale is not None:
            kw["scale"] = scale
        if accum_out is not None:
            kw["accum_out"] = accum_out
        self.S.op(eng, lambda e: e.activation(out, in_, func, **kw), self._rs(R), self._rs(W))

    def tt(self, out, in0, in1, op, R, W, eng="dve"):
        self.S.op(eng, lambda e: e.tensor_tensor(out, in0, in1, op), self._rs(R), self._rs(W))

    def ts(self, out, in0, s1, s2, op0, op1, R, W, eng="dve", accum_out=None):
        if accum_out is None:
            if op1 is None:
                self.S.op(eng, lambda e: e.tensor_scalar(out, in0, s1, None, op0), self._rs(R), self._rs(W))
            else:
                self.S.op(eng, lambda e: e.tensor_scalar(out, in0, s1, s2, op0, op1), self._rs(R), self._rs(W))
        else:
            self.S.op(eng, lambda e: e.tensor_scalar(out, in0, s1, s2, op0, op1, accum_out=accum_out),
                      self._rs(R), self._rs(W))

    def stt(self, out, in0, scalar, in1, op0, op1, R, W, eng="dve"):
        self.S.op(eng, lambda e: e.scalar_tensor_tensor(out, in0, scalar, in1, op0, op1), self._rs(R), self._rs(W))

    def copy(self, out, in_, R, W, eng="dve"):
        if eng == "act":
            self.S.op("act", lambda e: e.copy(out, in_), self._rs(R), self._rs(W))
        else:
            self.S.op(eng, lambda e: e.tensor_copy(out, in_), self._rs(R), self._rs(W))

    def memset(self, ap, val, W, eng="pool"):
        self.S.op(eng, lambda e: e.memset(ap, val), [], self._rs(W))

    def recip(self, out, in_, R, W):
        self.S.op("dve", lambda e: e.reciprocal(out, in_), self._rs(R), self._rs(W))

    def dma(self, out, in_, R, W, eng="sp"):
        self.S.dma(eng, out, in_, self._rs(R), self._rs(W))

    def fn(self, eng, f, R, W):
        self.S.op(eng, f, self._rs(R), self._rs(W))

    def barrier(self):
        self.S.barrier()


IN_SPECS = [
    ("xT", [D, S], F32), ("c_col", [128, 16], F32), ("ada_w", [D, 6 * D], F32), ("ada_bT", [128, 96], F32),
    ("g1T", [128, 16], F32), ("g2T", [128, 16], F32), ("w_in", [D, NIN], F32), ("w_out", [D, D], F32),
    ("wq", [D, D], F32), ("qg", [128, 1], F32), ("kgT", [128, 3], F32), ("kg0b", [128, 128], F32),
    ("pekT", [128, 32], F32), ("pevT", [128, 32], F32), ("cwk", [4096, 128], F32), ("cwv", [4096, 128], F32),
    ("gateb", [128, 24], F32), ("convw", [128, 32], F32), ("convb", [128, 8], F32), ("lba", [128, 8], F32),
    ("lbi", [128, 8], F32), ("lam", [128, 8], F32), ("gr", [128, 8], F32), ("gab", [128, 1024], F32),
    ("wa", [8, 128, 128], F32), ("wi", [8, 128, 128], F32), ("keysT", [128, 16 * 128], F32),
    ("downB", [128 * 128, 16 * 128], F32), ("up", [16384, D], F32),
    ("cosT", [128, S], F32), ("sinT", [128, S], F32), ("rotm", [128, 128], F32), ("ident", [128, 128], F32),
    ("maskc", [128, S], F32), ("cm", [128, 4 * 512], F32), ("wm", [128, 8 * 512], F32),
    ("ex", [32, 16 * 128], F32), ("vm", [128, 16 * 32], F32), ("fb", [128, 16 * 32], F32), ("ovl", [128, 32], F32),
]


class Prog:
    def __init__(self, stop_after=99, dbg=()):
        self.nc = nc = bass.Bass("TRN2", target_bir_lowering=False)
        self.k = K(nc)
        self.stop_after = stop_after
        self.dbg = set(dbg)
        for d_ in self.dbg:
            if d_.startswith("qkind="):
                self.qkind = d_.split("=")[1]
        self.I = {}
        for name, shape, dt in IN_SPECS:
            self.I[name] = nc.dram_tensor(name, shape, dt, kind="ExternalInput").ap()
        self.outT = nc.dram_tensor("outT", [D, S], F32, kind="ExternalOutput").ap()
        self.scr = {}

    def scratch(self, name, shape, dt):
        kind = "ExternalOutput" if name in self.dbg else "Internal"
        ap = self.nc.dram_tensor(name, list(shape), dt, kind=kind).ap()
        self.scr[name] = (ap, Res(name))
        return ap, self.scr[name][1]

    def build(self):
        k = self.k
        with ExitStack() as g:
            self.g = g
            self.modT = k.sb(g, [128, 96], F32, "modT")
            self.G1 = k.sb(g, [128, 16], F32, "G1")
            self.G2 = k.sb(g, [128, 16], F32, "G2")
            self.eps_t = k.sb(g, [128, 1], F32, "eps")
            self.ones16 = k.sb(g, [128, 128], BF16, "ones16")
            self.ident16 = k.sb(g, [128, 128], BF16, "ident16")
            k.memset(self.eps_t[:], EPS, [self.eps_t])
            k.memset(self.ones16[:], 1.0, [self.ones16])
            self.ident32 = k.sb(g, [128, 128], F32, "ident32")
            k.dma(self.ident32[:], self.I["ident"], [], [self.ident32])
            k.copy(self.ident16[:], self.ident32[:], [self.ident32], [self.ident16])
            names = ["p0_mod", "p12_proj", "p3_cmp", "p4_attn", "p5_rnn", "p6_out", "p7_peer"]
            phases = [getattr(self, n) for n in names if hasattr(self, n)]
            for i, ph in enumerate(phases):
                if i > self.stop_after:
                    break
                ph()
                k.barrier()
            k.S.emit()
        return self.nc

    def p0_mod(self):
        k, I = self.k, self.I
        with ExitStack() as st:
            cc = k.sb(st, [128, 16], F32, "cc")
            sc = k.sb(st, [128, 16], F32, "sc")
            abT = k.sb(st, [128, 96], F32, "abT")
            g1 = k.sb(st, [128, 16], F32, "g1")
            g2 = k.sb(st, [128, 16], F32, "g2")
            tmp = k.sb(st, [128, 16], F32, "tmp")
            wb = [k.sb(st, [128, 16, 512], F32, f"adaw{i}") for i in range(2)]
            wb16 = [k.sb(st, [128, 16, 512], BF16, f"adaw16{i}") for i in range(2)]
            sc16 = k.sb(st, [128, 16], BF16, "sc16")
            pm = k.ps(st, [128, 96], F32, "pm")
            k.dma(cc[:], I["c_col"], [], [cc])
            k.dma(abT[:], I["ada_bT"], [], [abT])
            k.dma(g1[:], I["g1T"], [], [g1])
            k.dma(g2[:], I["g2T"], [], [g2])
            k.act(sc[:], cc[:], AF.Silu, [cc], [sc])
            k.copy(sc16[:], sc[:], [sc], [sc16])
            wv = I["ada_w"].rearrange("(k p) n -> p k n", p=128)
            cast_eng = ("act", "dve", "pool", "dve")
            for gi in range(24):
                b = wb[gi % 2]
                b16 = wb16[gi % 2]
                k.dma(b[:], wv[:, :, gi * 512:(gi + 1) * 512], [], [b], eng=("sp" if gi % 2 == 0 else "pool"))
                for q in range(4):
                    k.copy(b16[:, q * 4:(q + 1) * 4, :], b[:, q * 4:(q + 1) * 4, :], [b], [b16], eng=cast_eng[q])
                for j in range(4):
                    col = gi * 4 + j
                    for kk in range(16):
                        k.mm(pm[:, col:col + 1], b16[:, kk, j * 128:(j + 1) * 128], sc16[:, kk:kk + 1],
                             kk == 0, kk == 15, [b16, sc16], [pm])
            k.tt(self.modT[:], pm[:], abT[:], ALU.add, [pm, abT], [self.modT])
            k.ts(tmp[:], self.modT[:, 16:32], 1.0, None, ALU.add, None, [self.modT], [tmp])
            k.tt(self.G1[:], tmp[:], g1[:], ALU.mult, [tmp, g1], [self.G1])
            k.ts(tmp[:], self.modT[:, 64:80], 1.0, None, ALU.add, None, [self.modT, self.G1], [tmp])
            k.tt(self.G2[:], tmp[:], g2[:], ALU.mult, [tmp, g2], [self.G2])
            if "modT" in self.dbg:
                d, r = self.scratch("modT", [128, 96], F32)
                k.dma(d, self.modT[:], [self.modT], [r])

    def rms_stats_fm(self, st, loader, nchunks, width, scale_div, name):
        k = self.k
        rstd = k.sb(st, [128, S], F32, name)
        with ExitStack() as s2:
            xb = [k.sb(s2, [128, S], F32, "xld") for _ in range(2)]
            sq = [k.sb(s2, [128, S], BF16, "sq") for _ in range(2)]
            pss = [k.ps(s2, [128, 512], F32, "pss") for _ in range(4)]
            for kk in range(nchunks):
                xt, sqt = xb[kk % 2], sq[kk % 2]
                loader(kk, xt)
                k.act(sqt[:], xt[:], AF.Square, [xt], [sqt])
                for tg in range(4):
                    k.mm(pss[tg][:], self.ones16[:], sqt[:, tg * 512:(tg + 1) * 512], kk == 0, kk == nchunks - 1,
                         [self.ones16, sqt], [pss[tg]])
            for tg in range(4):
                sl = slice(tg * 512, (tg + 1) * 512)
                k.act(rstd[:, sl], pss[tg][:], AF.Sqrt, [pss[tg], self.eps_t], [rstd], bias=self.eps_t[:], scale=1.0 / scale_div)
            k.recip(rstd[:], rstd[:], [rstd], [rstd])
        k.barrier()
        return rstd

    def p12_proj(self):
        k, I = self.k, self.I
        xTv = I["xT"].rearrange("(k p) t -> k p t", p=128)
        qT_d, qT_r = self.scratch("qT", [8, 128, S], BF16)
        kcT_d, kcT_r = self.scratch("kcT", [2, 128, S], BF16)
        vcT_d, vcT_r = self.scratch("vcT", [2, 128, S], BF16)
        ksT_d, ksT_r = self.scratch("ksT", [2, 128, S], BF16)
        kwT_d, kwT_r = self.scratch("kwT", [2, 128, S], BF16)
        vs_d, vs_r = self.scratch("vs", [S, 256], BF16)
        vw_d, vw_r = self.scratch("vw", [S, 256], BF16)
        gt_d, gt_r = self.scratch("gates", [128, 16 * 24], F32)
        xrT_d, xrT_r = self.scratch("xrT", [8, 128, S], F32)
        xgT_d, xgT_r = self.scratch("xgT", [8, 128, S], F32)
        with ExitStack() as st:
            hT = k.sb(st, [128, 16, S], BF16, "hT")
            with ExitStack() as s1:
                rstd = self.rms_stats_fm(s1, lambda kk, dst: k.dma(dst[:], xTv[kk], [], [dst]), 16, S, float(D), "rstd1")
                xb = [k.sb(s1, [128, S], F32, "xld2") for _ in range(2)]
                tmp = [k.sb(s1, [128, S], F32, "tmp") for _ in range(2)]
                for kk in range(16):
                    xt, tp = xb[kk % 2], tmp[kk % 2]
                    k.dma(xt[:], xTv[kk], [], [xt])
                    k.stt(tp[:], xt[:], self.G1[:, kk:kk + 1], rstd[:], ALU.mult, ALU.mult, [xt, self.G1, rstd], [tp])
                    k.act(hT[:, kk, :], tp[:], AF.Identity, [tp, self.modT], [hT], bias=self.modT[:, kk:kk + 1], scale=1.0)
            if "hT" in self.dbg:
                d, r = self.scratch("hT", [16, 128, S], BF16)
                k.dma(d.rearrange("k p t -> p k t"), hT[:], [hT], [r])
            k.barrier()
            if "stop_p1" in self.dbg:
                return
            with ExitStack() as s2:
                wb = [k.sb(s2, [128, 16, 544], BF16, f"win{i}") for i in range(2)]
                cosT = k.sb(s2, [128, S], F32, "cosT")
                sinT = k.sb(s2, [128, S], F32, "sinT")
                rot16 = k.sb(s2, [128, 128], BF16, "rot16")
                gq = k.sb(s2, [128, 4], F32, "gq")
                gateb = k.sb(s2, [128, 24], F32, "gateb")
                k.dma(cosT[:], I["cosT"], [], [cosT])
                k.dma(sinT[:], I["sinT"], [], [sinT])
                rot32 = k.sb(s2, [128, 128], F32, "rot32")
                k.dma(rot32[:], I["rotm"], [], [rot32])
                k.copy(rot16[:], rot32[:], [rot32], [rot16])
                k.dma(gq[:, 0:1], I["qg"], [], [gq])
                k.dma(gq[:, 1:4], I["kgT"], [], [gq])
                k.dma(gateb[:], I["gateb"], [], [gateb])
                pz = [k.ps(s2, [128, 512], F32, "pz") for _ in range(4)]
                pss = [k.ps(s2, [128, 512], F32, "pss2") for _ in range(2)]
                prot = [k.ps(s2, [128, 512], F32, "prot") for _ in range(2)]
                ptm = pz[2:4]
                sq = [k.sb(s2, [128, 512], BF16, "sq2") for _ in range(2)]
                rs = [k.sb(s2, [128, 512], F32, "rs") for _ in range(2)]
                xn = [k.sb(s2, [128, 512], F32, "xn") for _ in range(2)]
                xn16 = [k.sb(s2, [128, 512], BF16, "xn16") for _ in range(2)]
                t1 = [k.sb(s2, [128, 512], F32, "t1") for _ in range(2)]
                t2 = [k.sb(s2, [128, 512], F32, "t2") for _ in range(2)]
                o16 = [k.sb(s2, [128, S], BF16, "o16") for _ in range(2)]
                o32 = [k.sb(s2, [128, S], F32, "o32") for _ in range(2)]
                vtm = k.sb(s2, [128, 16, 256], BF16, "vtm")
                gtm = k.sb(s2, [128, 16, 24], F32, "gtm")
                gtmp = k.sb(s2, [128, 24], F32, "gtmp")
                wv = I["w_in"].rearrange("(k p) n -> p k n", p=128)
                cnt = {"c": 0, "i": 0}

                pend = []

                def fm_chunk(b, co, kind, gcol, dst_ap, dst_res):
                    ci = cnt["c"]
                    cnt["c"] += 1
                    ob = (o32 if kind == "f32" else o16)[ci % 2]
                    for tg in range(4):
                        pend.append((b, co, kind, gcol, dst_ap, dst_res, ob, tg))

                def fm_s0(st_):
                    b, co, kind, gcol, dst_ap, dst_res, ob, tg = st_
                    i = cnt["i"]
                    cnt["i"] += 1
                    p = pz[i % 4]
                    sl = slice(tg * 512, (tg + 1) * 512)
                    for kk in range(16):
                        k.mm(p[:], b[:, kk, co:co + 128], hT[:, kk, sl], kk == 0, kk == 15, [b, hT], [p])
                    return (i, p)

                def fm_s1(st_, ip):
                    b, co, kind, gcol, dst_ap, dst_res, ob, tg = st_
                    i, p = ip
                    sl = slice(tg * 512, (tg + 1) * 512)
                    if kind in ("f32", "bf16"):
                        k.copy(ob[:, sl], p[:], [p], [ob], eng=("act" if tg % 2 == 0 else "dve"))
                    else:
                        a, a16 = xn[i % 2], xn16[i % 2]
                        if kind == "normrope":
                            sqt, pst, rst = sq[i % 2], pss[i % 2], rs[i % 2]
                            k.act(sqt[:], p[:], AF.Square, [p], [sqt])
                            k.mm(pst[:], self.ones16[:], sqt[:], True, True, [self.ones16, sqt], [pst])
                            k.act(rst[:], pst[:], AF.Sqrt, [pst, self.eps_t], [rst], bias=self.eps_t[:], scale=1.0 / 128.0)
                            k.recip(rst[:], rst[:], [rst], [rst])
                            k.stt(a[:], p[:], gq[:, gcol:gcol + 1], rst[:], ALU.mult, ALU.mult, [p, gq, rst], [a])
                        else:
                            k.copy(a[:], p[:], [p], [a], eng="dve")
                        k.copy(a16[:], a[:], [a], [a16], eng="act")
                        pr = prot[i % 2]
                        k.mm(pr[:], rot16[:], a16[:], True, True, [rot16, a16], [pr])
                        k.tt(t1[i % 2][:], a[:], cosT[:, sl], ALU.mult, [a, cosT], [t1[i % 2]])
                        k.tt(t2[i % 2][:], pr[:], sinT[:, sl], ALU.mult, [pr, sinT], [t2[i % 2]])
                        k.tt(ob[:, sl], t1[i % 2][:], t2[i % 2][:], ALU.add, [t1[i % 2], t2[i % 2]], [ob], eng="pool")
                    if tg == 3:
                        k.dma(dst_ap, ob[:], [ob], [dst_res])

                def fm_flush():
                    steps = list(pend)
                    del pend[:]
                    inflight = []
                    LA = 2
                    for n in range(min(LA, len(steps))):
                        inflight.append(fm_s0(steps[n]))
                    for n in range(len(steps)):
                        if n + LA < len(steps):
                            inflight.append(fm_s0(steps[n + LA]))
                        fm_s1(steps[n], inflight.pop(0))

                stg = [k.sb(s2, [128, 16, 272], F32, f"stg{i}") for i in range(2)]
                lcnt = {"g": 0, "s": 0}

                def load_group(c0, c1):
                    fm_flush()
                    b = wb[lcnt["g"] % 2]
                    lcnt["g"] += 1
                    w = c1 - c0
                    pieces = [(a, min(a + 256, w)) for a in range(0, w, 256)]
                    for (a0, a1) in pieces:
                        sg = stg[lcnt["s"] % 2]
                        lcnt["s"] += 1
                        k.dma(sg[:, :, 0:a1 - a0], wv[:, :, c0 + a0:c0 + a1], [], [sg], eng=("sp" if lcnt["s"] % 2 else "pool"))
                        k.copy(b[:, :, a0:a1], sg[:, :, 0:a1 - a0], [sg], [b], eng="pool")
                    return b

                def tm_block(b, co, width, post):
                    for tt_ in range(16):
                        p = ptm[tt_ % 2]
                        for kk in range(16):
                            k.mm(p[:, 0:width], hT[:, kk, tt_ * 128:(tt_ + 1) * 128], b[:, kk, co:co + width], kk == 0, kk == 15, [b, hT], [p])
                        post(tt_, p)

                b = load_group(0, 512)
                for j in range(4):
                    fm_chunk(b, j * 128, self.qkind if hasattr(self, "qkind") else "normrope", 0, qT_d[j], qT_r)
                b = load_group(512, 1024)
                for j in range(4):
                    fm_chunk(b, j * 128, self.qkind if hasattr(self, "qkind") else "normrope", 0, qT_d[4 + j], qT_r)
                if "stop_g0" in self.dbg:
                    fm_flush()
                    return
                b = load_group(1024, 1536)
                for j in range(2):
                    fm_chunk(b, j * 128, "rope", 0, kcT_d[j], kcT_r)
                for j in range(2):
                    fm_chunk(b, 256 + j * 128, "bf16", 0, vcT_d[j], vcT_r)
                b = load_group(1536, 2048)
                for j in range(2):
                    fm_chunk(b, j * 128, "normrope", 2, ksT_d[j], ksT_r)
                fm_flush()
                tm_block(b, 256, 256, lambda tt_, p: k.copy(vtm[:, tt_, :], p[:, 0:256], [p], [vtm], eng=("act" if tt_ % 2 == 0 else "dve")))
                k.dma(vs_d.rearrange("(t p) c -> p t c", p=128), vtm[:], [vtm], [vs_r])
                if "stop_g1" in self.dbg:
                    return
                if "rep_g3" in self.dbg:
                    b = load_group(1536, 2048)
                    for j in range(2):
                        fm_chunk(b, j * 128, "normrope", 2, ksT_d[j], ksT_r)
                    tm_block(b, 256, 256, lambda tt_, p: k.copy(vtm[:, tt_, :], p[:, 0:256], [p], [vtm], eng=("act" if tt_ % 2 == 0 else "dve")))
                    k.dma(vs_d.rearrange("(t p) c -> p t c", p=128), vtm[:], [vtm], [vs_r])
                    return
                b = load_group(2048, 2560)
                for j in range(2):
                    fm_chunk(b, j * 128, "normrope", 3, kwT_d[j], kwT_r)
                fm_flush()
                tm_block(b, 256, 256, lambda tt_, p: k.copy(vtm[:, tt_, :], p[:, 0:256], [p], [vtm], eng=("act" if tt_ % 2 == 0 else "dve")))
                b = load_group(2560, 2584)

                def post_g(tt_, p):
                    k.tt(gtmp[:], p[:, 0:24], gateb[:], ALU.add, [p, gateb], [gtmp])
                    k.act(gtmp[:], gtmp[:], AF.Exp, [gtmp], [gtmp], scale=-1.0)
                    k.ts(gtmp[:], gtmp[:], 1.0, None, ALU.add, None, [gtmp], [gtmp])
                    k.recip(gtm[:, tt_, :], gtmp[:], [gtmp], [gtm])
                tm_block(b, 0, 24, post_g)
                k.dma(vw_d.rearrange("(t p) c -> p t c", p=128), vtm[:], [vtm], [vw_r])
                k.dma(gt_d, gtm[:].rearrange("p t c -> p (t c)"), [gtm], [gt_r])
                if "stop_g2" in self.dbg:
                    return
                for half in range(2):
                    b = load_group(C_XR + half * 512, C_XR + (half + 1) * 512)
                    for j in range(4):
                        fm_chunk(b, j * 128, "f32", 0, xrT_d[half * 4 + j], xrT_r)
                for half in range(2):
                    b = load_group(C_XG + half * 512, C_XG + (half + 1) * 512)
                    for j in range(4):
                        fm_chunk(b, j * 128, "f32", 0, xgT_d[half * 4 + j], xgT_r)
                fm_flush()

    def p3_cmp(self):
        k, I = self.k, self.I
        kcT_d = self.scr["kcT"][0]
        vcT_d = self.scr["vcT"][0]
        kcmpT_d, kcmpT_r = self.scratch("kcmpT", [128, 2, 128], BF16)
        vcmp_d, vcmp_r = self.scratch("vcmp", [128, 2, 162], BF16)
        with ExitStack() as st:
            src = k.sb(st, [128, 4, S], BF16, "cmpsrc")
            wst = k.sb(st, [128, 32, 128], F32, "wst")
            w16 = [k.sb(st, [128, 32, 128], BF16, f"w16{i}") for i in range(2)]
            pe32 = k.sb(st, [128, 2, 32], F32, "pe32")
            peB = [k.sb(st, [128, 32, 127], BF16, f"peB{i}") for i in range(2)]
            kg0b = k.sb(st, [128, 128], F32, "kg0b")
            ovl = k.sb(st, [128, 32], F32, "ovl")
            ss = k.sb(st, [128, 1], F32, "ss")
            junk = k.sb(st, [128, 128], F32, "junk")
            kn16 = k.sb(st, [128, 128], BF16, "kn16")
            kT16 = k.sb(st, [128, 2, 128], BF16, "kT16")
            va16 = k.sb(st, [128, 2, 162], BF16, "va16")
            pc = [k.ps(st, [128, 128], F32, "pc") for _ in range(2)]
            ptr = k.ps(st, [128, 128], F32, "ptr")
            for g in range(2):
                k.dma(src[:, g, :], kcT_d[g], [self.scr["kcT"][1]], [src])
                k.dma(src[:, 2 + g, :], vcT_d[g], [self.scr["vcT"][1]], [src])
            k.dma(pe32[:, 0, :], I["pekT"], [], [pe32])
            k.dma(pe32[:, 1, :], I["pevT"], [], [pe32])
            k.dma(kg0b[:], I["kg0b"], [], [kg0b])
            k.dma(ovl[:], I["ovl"], [], [ovl])
            k.memset(kT16[:], 0.0, [kT16])
            k.memset(va16[:], 0.0, [va16])
            for kv, wname in enumerate(("cwk", "cwv")):
                k.dma(wst[:], I[wname].rearrange("(l d) o -> d l o", d=128), [], [wst])
                k.copy(w16[kv][:], wst[:], [wst], [w16[kv]], eng="pool")
                k.copy(peB[kv][:], pe32[:, kv, :].unsqueeze(2).to_broadcast([128, 32, 127]), [pe32], [peB[kv]])
            for kv in range(2):
                for g in range(2):
                    p = pc[(kv * 2 + g) % 2]
                    for l in range(32):
                        k.mm(p[0:127, :], src[:, kv * 2 + g, l:l + 16 * 126 + 1:16], w16[kv][:, l, :], l == 0, False, [src, w16[kv]], [p])
                    for l in range(32):
                        k.mm(p[0:127, :], peB[kv][:, l, :], w16[kv][:, l, :], False, l == 31, [peB[kv], w16[kv]], [p])
                    if kv == 0:
                        k.act(junk[0:127, :], p[0:127, :], AF.Square, [p], [junk, ss], accum_out=ss[0:127, :])
                        k.act(ss[0:127, :], ss[0:127, :], AF.Sqrt, [ss, self.eps_t], [ss], bias=self.eps_t[0:127, :], scale=1.0 / 128.0)
                        k.recip(ss[0:127, :], ss[0:127, :], [ss], [ss])
                        k.stt(kn16[0:127, :], p[0:127, :], ss[0:127, :], kg0b[0:127, :], ALU.mult, ALU.mult, [p, ss, kg0b], [kn16])
                        k.mm(ptr[:, 0:127], kn16[0:127, :], self.ident16[0:127, 0:127], True, True, [kn16, self.ident16], [ptr])
                        k.copy(kT16[:, g, 0:127], ptr[:, 0:127], [ptr], [kT16])
                    else:
                        k.copy(va16[0:127, g, 0:128], p[0:127, :], [p], [va16])
                        k.memset(va16[0:127, g, 128:129], 1.0, [va16])
                        k.copy(va16[0:127, g, 129:161], ovl[0:127, :], [ovl], [va16])
            k.dma(kcmpT_d, kT16[:], [kT16], [kcmpT_r])
            k.dma(vcmp_d, va16[:], [va16], [vcmp_r])

    def p4_attn(self):
        k, I = self.k, self.I
        sc = self.scr
        yT_d, yT_r = self.scratch("yT", [16, 128, S], BF16)
        with ExitStack() as st:
            qT = k.sb(st, [128, 8, S], BF16, "qT")
            ksT = k.sb(st, [128, 2, S], BF16, "ksT")
            kwT = k.sb(st, [128, 2, S], BF16, "kwT")
            kcT = k.sb(st, [128, 2, 128], BF16, "kcT")
            vsa = k.sb(st, [128, 16, 2, 130], BF16, "vsa")
            vwa = k.sb(st, [128, 16, 2, 130], BF16, "vwa")
            vca = k.sb(st, [128, 2, 162], BF16, "vca")
            gts = k.sb(st, [128, 16, 24], F32, "gts")
            maskc = k.sb(st, [128, S], F32, "maskc")
            cm = k.sb(st, [128, 4, 512], F32, "cm")
            wm = k.sb(st, [128, 8, 512], F32, "wm")
            ex32 = k.sb(st, [32, 16, 128], F32, "ex32")
            ex16 = k.sb(st, [32, 16, 128], BF16, "ex16")
            vm = k.sb(st, [128, 16, 32], F32, "vm")
            fb = k.sb(st, [128, 16, 32], F32, "fb")
            gab = k.sb(st, [128, 1024], F32, "gab")
            for j in range(8):
                k.dma(qT[:, j, :], sc["qT"][0][j], [sc["qT"][1]], [qT], eng=("sp" if j % 2 else "pool"))
            for g in range(2):
                k.dma(ksT[:, g, :], sc["ksT"][0][g], [sc["ksT"][1]], [ksT])
                k.dma(kwT[:, g, :], sc["kwT"][0][g], [sc["kwT"][1]], [kwT])
            k.dma(kcT[:], sc["kcmpT"][0], [sc["kcmpT"][1]], [kcT])
            k.dma(vca[:], sc["vcmp"][0], [sc["vcmp"][1]], [vca])
            k.memset(vsa[:], 1.0, [vsa])
            k.memset(vwa[:], 1.0, [vwa])
            for g in range(2):
                k.dma(vsa[:, :, g, 0:128], sc["vs"][0].rearrange("(c p) x -> p c x", p=128)[:, :, g * 128:(g + 1) * 128], [sc["vs"][1]], [vsa])
                k.dma(vwa[:, :, g, 0:128], sc["vw"][0].rearrange("(c p) x -> p c x", p=128)[:, :, g * 128:(g + 1) * 128], [sc["vw"][1]], [vwa])
            k.dma(gts[:].rearrange("p t c -> p (t c)"), sc["gates"][0], [sc["gates"][1]], [gts])
            k.dma(maskc[:], I["maskc"], [], [maskc])
            k.dma(cm[:].rearrange("p a b -> p (a b)"), I["cm"], [], [cm])
            k.dma(wm[:].rearrange("p a b -> p (a b)"), I["wm"], [], [wm])
            k.dma(ex32[:].rearrange("p a b -> p (a b)"), I["ex"], [], [ex32])
            k.copy(ex16[:], ex32[:], [ex32], [ex16])
            k.dma(vm[:].rearrange("p a b -> p (a b)"), I["vm"], [], [vm])
            k.dma(fb[:].rearrange("p a b -> p (a b)"), I["fb"], [], [fb])
            k.dma(gab[:], I["gab"], [], [gab])
            O = k.sb(st, [128, 4, 1024], F32, "O")
            Osub = [Res() for _ in range(4)]
            imp = [k.sb(st, [128, 32], F32, f"imp{i}") for i in range(4)]
            e16 = [k.sb(st, [128, 512], BF16, f"e16{i}") for i in range(3)]
            p16 = [k.sb(st, [128, 512], BF16, f"p16{i}") for i in range(3)]
            mskS = k.sb(st, [128, 16, 512], BF16, "mskS")
            selT16 = k.sb(st, [32, 512], BF16, "selT16")
            sel16 = [k.sb(st, [128, 32], BF16, f"sel16{i}") for i in range(2)]
            imp2 = [k.sb(st, [128, 32], F32, f"imp2{i}") for i in range(2)]
            wk = [k.sb(st, [128, 32], F32, f"wk{i}") for i in range(2)]
            m8 = [k.sb(st, [128, 16], F32, f"m8{i}") for i in range(2)]
            den = [k.sb(st, [128, 2], F32, f"den{i}") for i in range(4)]
            ssq = k.sb(st, [128, 1], F32, "ssq")
            junk = k.sb(st, [128, 1024], F32, "junk4")
            yn16 = k.sb(st, [128, 1024], BF16, "yn16")
            yT16 = [k.sb(st, [128, 512], BF16, f"yT16{i}") for i in range(2)]
            pS = [k.ps(st, [128, 512], F32, "pS") for _ in range(2)]
            pA = k.ps(st, [128, 512], F32, "pA")
            pACC = [k.ps(st, [128, 512], F32, "pACC") for _ in range(4)]
            pM = [k.ps(st, [128, 512], F32, "pM")] * 2
            pT = pA
            cnt = {"s": 0, "e": 0, "m": 0, "d": 0, "y": 0, "sel": 0}

            def finish_head(sub, hd, acc_ap, den_ap, tt_, gcol, first, Rp):
                dn = den[cnt["d"] % 4]
                cnt["d"] += 1
                k.ts(dn[:, 0:1], den_ap, 1e-30, None, ALU.max, None, Rp, [dn])
                k.recip(dn[:, 0:1], dn[:, 0:1], [dn], [dn])
                k.tt(dn[:, 1:2], dn[:, 0:1], gts[:, tt_, gcol:gcol + 1], ALU.mult, [dn, gts], [dn])
                osl = O[:, sub, hd * 128:(hd + 1) * 128]
                if first:
                    k.ts(osl, acc_ap, dn[:, 1:2], None, ALU.mult, None, Rp + [dn], [Osub[sub]])
                else:
                    k.stt(osl, acc_ap, dn[:, 1:2], osl, ALU.mult, ALU.add, Rp + [dn, Osub[sub]], [Osub[sub]])
                return dn

            def run_steps(i, steps):
                qsl = slice(i * 512, (i + 1) * 512)
                state = {}

                def s0(n):
                    keyT, vaug, g, r, kc, mask_of, first, last, gbranch = steps[n]
                    ps_ = pS[cnt["s"] % 2]
                    cnt["s"] += 1
                    k.mm(ps_[:], keyT[:, g, kc * 128:(kc + 1) * 128], qT[:, g * 4 + r, qsl], True, True, [keyT, qT], [ps_])
                    state[n] = ps_

                def s12(n):
                    keyT, vaug, g, r, kc, mask_of, first, last, gbranch = steps[n]
                    ps_ = state.pop(n)
                    hd = g * 4 + r
                    e = e16[cnt["e"] % 3]
                    p_ = p16[cnt["e"] % 3]
                    cnt["e"] += 1
                    k.act(e[:], ps_[:], AF.Exp, [ps_], [e], scale=SCALE)
                    mk, mkR, eng = mask_of(kc)
                    k.tt(p_[:], e[:], mk, ALU.mult, [e, mkR], [p_], eng=eng)
                    for sub in range(4):
                        acc = pACC[sub]
                        k.mm(acc[:, 0:129], p_[:, sub * 128:(sub + 1) * 128], vaug[:, kc, g, 0:129], first, last, [p_, vaug], [acc])
                    if last:
                        gcol = g * 12 + r * 3 + gbranch
                        dns = [den[sub] for sub in range(4)]
                        for sub in range(4):
                            k.ts(dns[sub][:, 0:1], pACC[sub][:, 128:129], 1e-30, None, ALU.max, None, [pACC[sub]], [dns[sub]])
                        for sub in range(4):
                            k.recip(dns[sub][:, 0:1], dns[sub][:, 0:1], [dns[sub]], [dns[sub]])
                        for sub in range(4):
                            k.tt(dns[sub][:, 1:2], dns[sub][:, 0:1], gts[:, i * 4 + sub, gcol:gcol + 1], ALU.mult, [dns[sub], gts], [dns[sub]])
                        for sub in range(4):
                            osl = O[:, sub, hd * 128:(hd + 1) * 128]
                            k.stt(osl, pACC[sub][:, 0:128], dns[sub][:, 1:2], osl, ALU.mult, ALU.add, [pACC[sub], dns[sub], Osub[sub]], [Osub[sub]])

                s0(0)
                for n in range(len(steps)):
                    if n + 1 < len(steps):
                        s0(n + 1)
                    s12(n)

            for i in range(4):
                qsl = slice(i * 512, (i + 1) * 512)
                for g in range(2):
                    for r in range(4):
                        hd = g * 4 + r
                        ps_ = pS[cnt["s"] % 2]
                        cnt["s"] += 1
                        k.mm(ps_[0:127, :], kcT[:, g, 0:127], qT[:, hd, qsl], True, True, [kcT, qT], [ps_])
                        e = e16[cnt["e"] % 3]
                        p_ = p16[cnt["e"] % 3]
                        cnt["e"] += 1
                        k.act(e[0:127, :], ps_[0:127, :], AF.Exp, [ps_], [e], scale=SCALE)
                        k.tt(p_[0:127, :], e[0:127, :], maskc[0:127, qsl], ALU.mult, [e, maskc], [p_])
                        for sub in range(4):
                            k.mm(pA[:, 0:161], p_[0:127, sub * 128:(sub + 1) * 128], vca[0:127, g, 0:161], True, True, [p_, vca], [pA])
                            dn = finish_head(sub, hd, pA[:, 0:128], pA[:, 128:129], i * 4 + sub, g * 12 + r * 3, True, [pA])
                            if r == 0:
                                k.ts(imp[sub][:], pA[:, 129:161], dn[:, 0:1], None, ALU.mult, None, [pA, dn], [imp[sub]])
                            else:
                                k.stt(imp[sub][:], pA[:, 129:161], dn[:, 0:1], imp[sub][:], ALU.mult, ALU.add, [pA, dn, imp[sub]], [imp[sub]])
                    psel = pM[cnt["m"] % 2]
                    cnt["m"] += 1
                    for sub in range(4):
                        tt_ = i * 4 + sub
                        j = cnt["sel"] % 2
                        cnt["sel"] += 1
                        k.tt(imp2[j][:], imp[sub][:], vm[:, tt_, :], ALU.mult, [imp[sub], vm], [imp2[j]])
                        k.tt(imp2[j][:], imp2[j][:], fb[:, tt_, :], ALU.add, [imp2[j], fb], [imp2[j]])
                        k.fn("dve", lambda e, o=m8[j][:, 0:8], a=imp2[j][:]: e.max(out=o, in_=a), [imp2[j]], [m8[j]])
                        k.fn("dve", lambda e, o=wk[j][:], a=m8[j][:, 0:8], b=imp2[j][:]: e.match_replace(out=o, in_to_replace=a, in_values=b, imm_value=-1e30), [imp2[j], m8[j]], [wk[j]])
                        k.fn("dve", lambda e, o=m8[j][:, 8:16], a=wk[j][:]: e.max(out=o, in_=a), [wk[j]], [m8[j]])
                        k.ts(sel16[j][:], imp2[j][:], m8[j][:, 15:16], None, ALU.is_ge, None, [imp2[j], m8[j]], [sel16[j]])
                        k.mm(psel[0:32, sub * 128:(sub + 1) * 128], sel16[j][:], self.ident16[:], True, True, [sel16[j], self.ident16], [psel])
                    k.copy(selT16[:], psel[0:32, :], [psel], [selT16], eng="act")
                    nkc = 4 * i + 4
                    for kc in range(nkc):
                        pm_ = pM[cnt["m"] % 2]
                        cnt["m"] += 1
                        k.mm(pm_[:], ex16[:, kc, :], selT16[:], True, True, [ex16, selT16], [pm_])
                        if kc >= 4 * i:
                            k.tt(mskS[:, kc, :], pm_[:], cm[:, kc - 4 * i, :], ALU.mult, [pm_, cm], [mskS])
                        else:
                            k.copy(mskS[:, kc, :], pm_[:], [pm_], [mskS], eng="act")
                    steps = []
                    for r in range(4):
                        for kc in range(nkc):
                            steps.append((ksT, vsa, g, r, kc, (lambda kc_: (mskS[:, kc_, :], mskS, "pool")), kc == 0, kc == nkc - 1, 1))
                    for r in range(4):
                        chunks = list(range(max(0, 4 * i - 4), 4 * i + 4))
                        for kc in chunks:
                            steps.append((kwT, vwa, g, r, kc, (lambda kc_, i_=i: (wm[:, kc_ - 4 * i_ + 4, :], wm, "dve")), kc == chunks[0], kc == chunks[-1], 2))
                    run_steps(i, steps)
                yt = yT16[i % 2]
                for c in range(8):
                    pass
                ytiles = []
                for sub in range(4):
                    k.act(junk[:], O[:, sub, :], AF.Square, [Osub[sub]], [junk, ssq], accum_out=ssq[:])
                    k.act(ssq[:], ssq[:], AF.Sqrt, [ssq, self.eps_t], [ssq], bias=self.eps_t[:], scale=1.0 / 1024.0)
                    k.recip(ssq[:], ssq[:], [ssq], [ssq])
                    k.stt(yn16[:], O[:, sub, :], ssq[:], gab[:], ALU.mult, ALU.mult, [Osub[sub], ssq, gab], [yn16])
                    for half in range(2):
                        for cc in range(4):
                            c = half * 4 + cc
                            k.mm(pT[:, cc * 128:(cc + 1) * 128], yn16[:, c * 128:(c + 1) * 128], self.ident16[:], True, True, [yn16, self.ident16], [pT])
                        dst = k.sb(st, [128, 4, 128], BF16, "ytmp") if False else None
                        yb = yT16[cnt["y"] % 2]
                        cnt["y"] += 1
                        k.copy(yb[:], pT[:], [pT], [yb], eng=("act" if half == 0 else "dve"))
                        for cc in range(4):
                            c = half * 4 + cc
                            k.dma(yT_d[c][:, i * 512 + sub * 128:i * 512 + (sub + 1) * 128], yb[:, cc * 128:(cc + 1) * 128], [yb], [yT_r])
            if "O_dbg" in self.dbg:
                pass

    def p5_rnn(self):
        k, I = self.k, self.I
        sc = self.scr
        yT_d, yT_r = sc["yT"]
        xr_d, xr_r = sc["xrT"]
        xg_d, xg_r = sc["xgT"]
        with ExitStack() as st:
            cw = k.sb(st, [128, 8, 4], F32, "cw")
            cb = k.sb(st, [128, 8], F32, "cb")
            nba = k.sb(st, [128, 8], F32, "nba")
            nbi = k.sb(st, [128, 8], F32, "nbi")
            lam = k.sb(st, [128, 8], F32, "lam")
            clam = k.sb(st, [128, 8], F32, "clam")
            gr = k.sb(st, [128, 8], F32, "gr")
            wst = k.sb(st, [128, 2, 128], F32, "wst5")
            w16 = [k.sb(st, [128, 2, 128], BF16, f"w165{i}") for i in range(2)]
            k.dma(cw[:].rearrange("p a b -> p (a b)"), I["convw"], [], [cw])
            k.dma(cb[:], I["convb"], [], [cb])
            k.dma(nba[:], I["lba"], [], [nba])
            k.dma(nbi[:], I["lbi"], [], [nbi])
            k.dma(lam[:], I["lam"], [], [lam])
            k.dma(gr[:], I["gr"], [], [gr])
            k.ts(nba[:], nba[:], -1.0, None, ALU.mult, None, [nba], [nba])
            k.ts(nbi[:], nbi[:], -1.0, None, ALU.mult, None, [nbi], [nbi])
            k.act(clam[:], lam[:], AF.Exp, [lam], [clam], scale=-1.0)
            k.ts(clam[:], clam[:], 1.0, None, ALU.add, None, [clam], [clam])
            k.act(clam[:], clam[:], AF.Ln, [clam], [clam])
            k.ts(clam[:], clam[:], -8.0, None, ALU.mult, None, [clam], [clam])
            orn = k.sb(st, [128, 8, S], F32, "orn")
            xp = k.sb(st, [128, S + 4], F32, "xp")
            xg = k.sb(st, [128, S], F32, "xg")
            u = k.sb(st, [128, S], F32, "u")
            u16 = k.sb(st, [128, S], BF16, "u16")
            ra = k.sb(st, [128, S], F32, "ra")
            ig = k.sb(st, [128, S], F32, "ig")
            bb = k.sb(st, [128, S], F32, "bb")
            sq16 = k.sb(st, [128, S], BF16, "sq165")
            pg = [k.ps(st, [128, 512], F32, "pg") for _ in range(2)]
            pss = [k.ps(st, [128, 512], F32, "pss5") for _ in range(4)]
            k.memset(xp[:, 0:4], 0.0, [xp])
            ci = 0
            for n in range(8):
                k.dma(xp[:, 4:S + 4], xr_d[n], [xr_r], [xp])
                k.dma(xg[:], xg_d[n], [xg_r], [xg], eng="pool")
                wb = w16[n % 2]
                k.dma(wst[:, 0, :], I["wa"][n], [], [wst])
                k.dma(wst[:, 1, :], I["wi"][n], [], [wst])
                k.copy(wb[:], wst[:], [wst], [wb], eng="pool")
                k.ts(u[:], xp[:, 1:S + 1], cw[:, n, 0:1], cb[:, n:n + 1], ALU.mult, ALU.add, [xp, cw, cb], [u])
                for i_ in range(1, 4):
                    k.stt(u[:], xp[:, 1 + i_:S + 1 + i_], cw[:, n, i_:i_ + 1], u[:], ALU.mult, ALU.add, [xp, cw, u], [u])
                k.copy(u16[:], u[:], [u], [u16], eng="act")
                for which, dst, nb in ((0, ra, nba), (1, ig, nbi)):
                    for tg in range(4):
                        p = pg[ci % 2]
                        ci += 1
                        sl = slice(tg * 512, (tg + 1) * 512)
                        k.mm(p[:], wb[:, which, :], u16[:, sl], True, True, [wb, u16], [p])
                        k.act(dst[:, sl], p[:], AF.Exp, [p, nb], [dst], bias=nb[:, n:n + 1], scale=-1.0)
                    k.ts(dst[:], dst[:], 1.0, None, ALU.add, None, [dst], [dst], eng="pool")
                    k.recip(dst[:], dst[:], [dst], [dst])
                k.act(ra[:], ra[:], AF.Exp, [ra, clam], [ra], scale=clam[:, n:n + 1])
                k.tt(bb[:], ra[:], ra[:], ALU.mult, [ra], [bb])
                k.ts(bb[:], bb[:], -1.0, 1.0, ALU.mult, ALU.add, [bb], [bb])
                k.act(bb[:], bb[:], AF.Sqrt, [bb], [bb])
                k.tt(bb[:], bb[:], ig[:], ALU.mult, [bb, ig], [bb], eng="pool")
                k.tt(bb[:], bb[:], u[:], ALU.mult, [bb, u], [bb])
                k.fn("dve", lambda e, o=ig[:], a=ra[:], b=bb[:]: e.tensor_tensor_scan(o, a, b, 0.0, ALU.mult, ALU.add), [ra, bb, ig], [ig])
                k.act(xg[:], xg[:], AF.Gelu, [xg], [xg])
                k.tt(orn[:, n, :], xg[:], ig[:], ALU.mult, [xg, ig], [orn])
                k.act(sq16[:], orn[:, n, :], AF.Square, [orn], [sq16])
                for tg in range(4):
                    k.mm(pss[tg][:], self.ones16[:], sq16[:, tg * 512:(tg + 1) * 512], n == 0, n == 7, [self.ones16, sq16], [pss[tg]])
            rstd = ra
            for tg in range(4):
                sl = slice(tg * 512, (tg + 1) * 512)
                k.act(rstd[:, sl], pss[tg][:], AF.Sqrt, [pss[tg], self.eps_t], [rstd], bias=self.eps_t[:], scale=1.0 / 1024.0)
            k.recip(rstd[:], rstd[:], [rstd], [rstd])
            for n in range(8):
                k.stt(u16[:], orn[:, n, :], gr[:, n:n + 1], rstd[:], ALU.mult, ALU.mult, [orn, gr, rstd], [u16])
                k.dma(yT_d[8 + n], u16[:], [u16], [yT_r])

    def p6_out(self):
        k, I = self.k, self.I
        sc = self.scr
        yT_d, yT_r = sc["yT"]
        x1_d, x1_r = self.scratch("x1T", [16, 128, S], F32)
        h2_d, h2_r = self.scratch("h2T", [16, 128, S], BF16)
        xTv = I["xT"].rearrange("(k p) t -> k p t", p=128)
        wv = I["w_out"].rearrange("(k p) n -> p k n", p=128)
        with ExitStack() as st:
            rstd = k.sb(st, [128, S], F32, "rstd2")
            with ExitStack() as s1:
                yT = k.sb(s1, [128, 16, S], BF16, "yTs")
                wb = [k.sb(s1, [128, 16, 256], BF16, f"wo{i}") for i in range(2)]
                stg = [k.sb(s1, [128, 16, 128], F32, f"wos{i}") for i in range(2)]
                xb = [k.sb(s1, [128, S], F32, f"x6{i}") for i in range(2)]
                ob = [k.sb(s1, [128, S], F32, f"o6{i}") for i in range(2)]
                sq = [k.sb(s1, [128, S], BF16, f"sq6{i}") for i in range(2)]
                pz = [k.ps(s1, [128, 512], F32, "pz6") for _ in range(2)]
                pss = [k.ps(s1, [128, 512], F32, "pss6") for _ in range(4)]
                for c in range(16):
                    k.dma(yT[:, c, :], yT_d[c], [yT_r], [yT], eng=("sp" if c % 2 else "pool"))
                ci = 0
                for jg in range(8):
                    b = wb[jg % 2]
                    for h in range(2):
                        sg = stg[(jg * 2 + h) % 2]
                        k.dma(sg[:], wv[:, :, jg * 256 + h * 128:jg * 256 + (h + 1) * 128], [], [sg])
                        k.copy(b[:, :, h * 128:(h + 1) * 128], sg[:], [sg], [b], eng="pool")
                    for jj in range(2):
                        j = jg * 2 + jj
                        xt, ot, sqt = xb[j % 2], ob[j % 2], sq[j % 2]
                        k.dma(xt[:], xTv[j], [], [xt])
                        for tg in range(4):
                            p = pz[ci % 2]
                            ci += 1
                            sl = slice(tg * 512, (tg + 1) * 512)
                            for c in range(16):
                                k.mm(p[:], b[:, c, jj * 128:(jj + 1) * 128], yT[:, c, sl], c == 0, c == 15, [b, yT], [p])
                            k.stt(ot[:, sl], p[:], self.modT[:, 32 + j:33 + j], xt[:, sl], ALU.mult, ALU.add, [p, self.modT, xt], [ot])
                        k.dma(x1_d[j], ot[:], [ot], [x1_r])
                        k.act(sqt[:], ot[:], AF.Square, [ot], [sqt])
                        for tg in range(4):
                            k.mm(pss[tg][:], self.ones16[:], sqt[:, tg * 512:(tg + 1) * 512], j == 0, j == 15, [self.ones16, sqt], [pss[tg]])
                for tg in range(4):
                    sl = slice(tg * 512, (tg + 1) * 512)
                    k.act(rstd[:, sl], pss[tg][:], AF.Sqrt, [pss[tg], self.eps_t], [rstd], bias=self.eps_t[:], scale=1.0 / float(D))
                k.recip(rstd[:], rstd[:], [rstd], [rstd])
            k.barrier()
            with ExitStack() as s2:
                xb = [k.sb(s2, [128, S], F32, f"x6b{i}") for i in range(2)]
                tp = [k.sb(s2, [128, S], F32, f"t6b{i}") for i in range(2)]
                hb = [k.sb(s2, [128, S], BF16, f"h6b{i}") for i in range(2)]
                for j in range(16):
                    xt, tt_, ht = xb[j % 2], tp[j % 2], hb[j % 2]
                    k.dma(xt[:], x1_d[j], [x1_r], [xt])
                    k.stt(tt_[:], xt[:], self.G2[:, j:j + 1], rstd[:], ALU.mult, ALU.mult, [xt, self.G2, rstd], [tt_])
                    k.act(ht[:], tt_[:], AF.Identity, [tt_, self.modT], [ht], bias=self.modT[:, 48 + j:49 + j], scale=1.0)
                    k.dma(h2_d[j], ht[:], [ht], [h2_r])

    def p7_peer(self):
        k, I = self.k, self.I
        sc = self.scr
        h2_d, h2_r = sc["h2T"]
        x1_d, x1_r = sc["x1T"]
        qp_d, qp_r = self.scratch("qpT", [16, 128, S], BF16)
        wv = I["wq"].rearrange("(k p) n -> p k n", p=128)
        with ExitStack() as st:
            h2T = k.sb(st, [128, 16, S], BF16, "h2Ts")
            wb = [k.sb(st, [128, 16, 256], BF16, f"wq{i}") for i in range(2)]
            stg = [k.sb(st, [128, 16, 128], F32, f"wqs{i}") for i in range(2)]
            ob = [k.sb(st, [128, S], BF16, f"oq{i}") for i in range(2)]
            pz = [k.ps(st, [128, 512], F32, "pz7") for _ in range(2)]
            for c in range(16):
                k.dma(h2T[:, c, :], h2_d[c], [h2_r], [h2T], eng=("sp" if c % 2 else "pool"))
            ci = 0
            for jg in range(8):
                b = wb[jg % 2]
                for h in range(2):
                    sg = stg[(jg * 2 + h) % 2]
                    k.dma(sg[:], wv[:, :, jg * 256 + h * 128:jg * 256 + (h + 1) * 128], [], [sg])
                    k.copy(b[:, :, h * 128:(h + 1) * 128], sg[:], [sg], [b], eng="pool")
                for jj in range(2):
                    j = jg * 2 + jj
                    ot = ob[j % 2]
                    for tg in range(4):
                        p = pz[ci % 2]
                        ci += 1
                        sl = slice(tg * 512, (tg + 1) * 512)
                        for c in range(16):
                            k.mm(p[:], b[:, c, jj * 128:(jj + 1) * 128], h2T[:, c, sl], c == 0, c == 15, [b, h2T], [p])
                        k.copy(ot[:, sl], p[:], [p], [ot], eng=("act" if tg % 2 == 0 else "dve"))
                    k.dma(qp_d[j], ot[:], [ot], [qp_r])
        k.barrier()
        SLACK = 1.0 - 4e-6
        with ExitStack() as st:
            h2s = k.sb(st, [128, 16, 512], BF16, "h2s")
            E1s = k.sb(st, [128, 4, 8, 128], F32, "E1s")
            E2s = k.sb(st, [128, 4, 8, 128], F32, "E2s")
            dg16 = k.sb(st, [128, 4, 8, 128], BF16, "dg16")
            accT = k.sb(st, [128, 16, 512], F32, "accT")
            keys16 = k.sb(st, [128, 16, 128], BF16, "keys16")
            with ExitStack() as s0:
                k32 = k.sb(s0, [128, 16, 128], F32, "k32")
                k.dma(k32[:].rearrange("p a b -> p (a b)"), I["keysT"], [], [k32])
                k.copy(keys16[:], k32[:], [k32], [keys16])
            k.barrier()
            for su in range(4):
                tsl = slice(su * 512, (su + 1) * 512)
                k.dma(h2s[:], h2_d.rearrange("c p t -> p c t")[:, :, tsl], [h2_r], [h2s])
                with ExitStack() as sb_:
                    qps = k.sb(sb_, [128, 16, 512], BF16, "qps")
                    s_sb = k.sb(sb_, [128, 16, 128], F32, "s_sb")
                    wk = k.sb(sb_, [128, 16, 128], F32, "wk7")
                    wk2 = k.sb(sb_, [128, 8, 256], F32, "wk72")
                    v16R = [Res() for _ in range(16)]
                    wkR = [Res() for _ in range(16)]
                    c16R = [Res() for _ in range(8)]
                    wk2R = [Res() for _ in range(8)]
                    v16 = k.sb(sb_, [128, 16, 16], F32, "v16")
                    cand = k.sb(sb_, [128, 8, 256], F32, "cand")
                    c16 = k.sb(sb_, [128, 8, 16], F32, "c16")
                    en = k.sb(sb_, [128, 8, 16], F32, "en")
                    negm = k.sb(sb_, [128, 16], F32, "negm")
                    negM = k.sb(sb_, [128, 8], F32, "negM")
                    Z = k.sb(sb_, [128, 8], F32, "Z")
                    th = k.sb(sb_, [128, 8], F32, "th")
                    cf = k.sb(sb_, [128, 8], F32, "cf")
                    E1t = k.sb(sb_, [128, 8, 128], F32, "E1t")
                    pS_ = [k.ps(sb_, [128, 4, 128], F32, "pS7") for _ in range(2)]
                    k.dma(qps[:], qp_d.rearrange("c p t -> p c t")[:, :, tsl], [qp_r], [qps])
                    for tl in range(4):
                        for hg in range(4):
                            p = pS_[hg % 2]
                            for q4 in range(4):
                                hp = hg * 4 + q4
                                k.mm(p[:, q4, :], qps[:, hp, tl * 128:(tl + 1) * 128], keys16[:, hp, :], True, True, [qps, keys16], [p])
                            k.copy(s_sb[:, hg * 4:(hg + 1) * 4, :], p[:], [p], [s_sb], eng=("act" if hg % 2 == 0 else "dve"))
                        for hp in range(16):
                            k.fn("dve", lambda e, o=v16[:, hp, 0:8], a=s_sb[:, hp, :]: e.max(out=o, in_=a), [s_sb], [v16R[hp]])
                        for hp in range(16):
                            k.fn("dve", lambda e, o=wk[:, hp, :], a=v16[:, hp, 0:8], b=s_sb[:, hp, :]: e.match_replace(out=o, in_to_replace=a, in_values=b, imm_value=-1e30), [s_sb, v16R[hp]], [wkR[hp]])
                        for hp in range(16):
                            k.fn("dve", lambda e, o=v16[:, hp, 8:16], a=wk[:, hp, :]: e.max(out=o, in_=a), [wkR[hp]], [v16R[hp]])
                        v16r = v16[:].rearrange("p (h two) i -> p h two i", two=2)
                        k.tt(cand[:].rearrange("p h (i j) -> p h i j", i=16),
                             v16r[:, :, 0, :].unsqueeze(3).to_broadcast([128, 8, 16, 16]),
                             v16r[:, :, 1, :].unsqueeze(2).to_broadcast([128, 8, 16, 16]), ALU.add, v16R, [cand])
                        for h in range(8):
                            k.fn("dve", lambda e, o=c16[:, h, 0:8], a=cand[:, h, :]: e.max(out=o, in_=a), [cand], [c16R[h]])
                        for h in range(8):
                            k.fn("dve", lambda e, o=wk2[:, h, :], a=c16[:, h, 0:8], b=cand[:, h, :]: e.match_replace(out=o, in_to_replace=a, in_values=b, imm_value=-1e30), [cand, c16R[h]], [wk2R[h]])
                        for h in range(8):
                            k.fn("dve", lambda e, o=c16[:, h, 8:16], a=wk2[:, h, :]: e.max(out=o, in_=a), [wk2R[h]], [c16R[h]])
                        k.ts(negm[:], v16[:, :, 0], -1.0, None, ALU.mult, None, v16R, [negm])
                        k.ts(negM[:], c16[:, :, 0], -1.0, None, ALU.mult, None, c16R, [negM])
                        for h in range(8):
                            k.act(en[:, h, :], c16[:, h, :], AF.Exp, [c16R[h], negM], [en, Z], bias=negM[:, h:h + 1], scale=1.0, accum_out=Z[:, h:h + 1])
                        k.recip(Z[:], Z[:], [Z], [Z])
                        k.tt(th[:], en[:, :, 15], Z[:], ALU.mult, [en, Z], [th])
                        k.ts(th[:], th[:], SLACK, None, ALU.mult, None, [th], [th])
                        k.ts(cf[:], en[:, :, 15], SLACK, None, ALU.mult, None, [en], [cf])
                        k.recip(cf[:], cf[:], [cf], [cf])
                        for h in range(8):
                            k.act(E2s[:, tl, h, :], s_sb[:, 2 * h + 1, :], AF.Exp, [s_sb, negm], [E2s], bias=negm[:, 2 * h + 1:2 * h + 2], scale=1.0)
                            k.act(E1t[:, h, :], s_sb[:, 2 * h, :], AF.Exp, [s_sb, negm], [E1t], bias=negm[:, 2 * h:2 * h + 1], scale=1.0)
                            k.ts(dg16[:, tl, h, :], self.ident32[:], th[:, h:h + 1], None, ALU.mult, None, [self.ident32, th], [dg16], eng="pool")
                        k.tt(E1s[:, tl, :, :], E1t[:], cf[:].unsqueeze(2).to_broadcast([128, 8, 128]), ALU.mult, [E1t, cf], [E1s])
                k.barrier()
                if "peer_dbg" in self.dbg and su == 0:
                    for nm, tile_ in (("E1s", E1s), ("E2s", E2s)):
                        d, r = self.scratch(nm, [128, 4 * 8 * 128], F32)
                        k.dma(d, tile_[:].rearrange("p a b c -> p (a b c)"), [tile_], [r])
                with ExitStack() as sc_:
                    stgD = [k.sb(sc_, [128, 16, 128], F32, f"stgD{i}") for i in range(2)]
                    stgU = [k.sb(sc_, [128, 1024], F32, f"stgU{i}") for i in range(2)]
                    dn16 = [k.sb(sc_, [128, 16, 128], BF16, f"dn16{i}") for i in range(4)]
                    up16 = [[k.sb(sc_, [128, 2048], BF16, f"up16{j}_{i}") for i in range(4)] for j in range(2)]
                    GA16 = [k.sb(sc_, [128, 512], BF16, f"GA{i}") for i in range(2)]
                    Pt = [k.sb(sc_, [128, 8, 128], F32, f"Pt{i}") for i in range(4)]
                    mE = [k.sb(sc_, [128, 8, 128], BF16, f"mE{i}") for i in range(4)]
                    WA = [k.sb(sc_, [128, 512], BF16, f"WA{i}") for i in range(8)]
                    ev = [k.sb(sc_, [128, 512], F32, f"ev{i}") for i in range(2)]
                    pact = [k.ps(sc_, [128, 512], F32, "pact") for _ in range(2)]
                    pw = [k.ps(sc_, [128, 512], F32, "pw") for _ in range(2)]
                    po = [k.ps(sc_, [128, 512], F32, "po") for _ in range(2)]
                    cn = {"k": 0, "p": 0, "o": 0}
                    ngrp = 1 if "peer_short" in self.dbg else 32
                    pend_wa = []
                    pend_po = []

                    def flush_wa():
                        while pend_wa:
                            wa_, pw__, ga_ = pend_wa.pop(0)
                            k.tt(wa_[:], pw__[:], ga_[:], ALU.mult, [pw__, ga_], [wa_])

                    def flush_po():
                        while pend_po:
                            gi_, was_ = pend_po.pop(0)
                            for dc in range(16):
                                po_ = po[cn["o"] % 2]
                                e_ = ev[cn["o"] % 2]
                                cn["o"] += 1
                                for kl in range(4):
                                    k.mm(po_[:], up16[gi_ % 2][kl][:, dc * 128:(dc + 1) * 128], was_[kl][:], kl == 0, kl == 3, [up16[gi_ % 2][kl], was_[kl]], [po_])
                                if gi_ == 0:
                                    k.copy(accT[:, dc, :], po_[:], [po_], [accT], eng="act")
                                else:
                                    k.copy(e_[:], po_[:], [po_], [e_], eng="act")
                                    k.tt(accT[:, dc, :], accT[:, dc, :], e_[:], ALU.add, [accT, e_], [accT], eng="pool")

                    nk = ngrp * 4
                    seq = [(kap, tl) for kap in range(nk) for tl in range(4)]
                    LA = 2

                    def load_w(kap2):
                        gi2, kl2 = kap2 // 4, kap2 % 4
                        sd = stgD[kap2 % 2]
                        dn, up = dn16[kl2], up16[gi2 % 2][kl2]
                        k.dma(sd[:].rearrange("p a b -> p (a b)"), I["downB"][kap2 * 128:(kap2 + 1) * 128, :], [], [sd], eng="sp")
                        k.copy(dn[:], sd[:], [sd], [dn], eng="act")
                        for hf in range(2):
                            su_ = stgU[(kap2 * 2 + hf) % 2]
                            k.dma(su_[:], I["up"][kap2 * 128:(kap2 + 1) * 128, hf * 1024:(hf + 1) * 1024], [], [su_], eng="sp")
                            k.copy(up[:, hf * 1024:(hf + 1) * 1024], su_[:], [su_], [up], eng="act")

                    def p1(n):
                        kap, tl = seq[n]
                        k.tt(Pt[n % 4][:], E2s[:, tl, :, :], E1s[:, tl, :, kap:kap + 1].to_broadcast([128, 8, 128]), ALU.mult, [E2s, E1s], [Pt[n % 4]])

                    for n in range(min(LA, len(seq))):
                        p1(n)
                    cur = {}
                    for n, (kap, tl) in enumerate(seq):
                        gi, kl = kap // 4, kap % 4
                        if tl == 0:
                            if kl == 0:
                                cur["was"] = []
                                if gi == 0:
                                    for kl2 in range(4):
                                        load_w(kl2)
                            dn = dn16[kl]
                            pa_ = pact[cn["k"] % 2]
                            pw_ = pw[cn["k"] % 2]
                            ga = GA16[cn["k"] % 2]
                            wa = WA[cn["k"] % 8]
                            cn["k"] += 1
                            cur.update(pw=pw_, ga=ga, wa=wa)
                            for c in range(16):
                                k.mm(pa_[:], dn[:, c, :], h2s[:, c, :], c == 0, c == 15, [dn, h2s], [pa_])
                            k.act(ga[:], pa_[:], AF.Gelu, [pa_], [ga])
                            if kl == 0:
                                flush_wa()
                                flush_po()
                            if kap + 4 < nk:
                                load_w(kap + 4)
                        if n + LA < len(seq):
                            p1(n + LA)
                        me = mE[n % 4]
                        k.stt(me[:], Pt[n % 4][:], 1.0, Pt[n % 4][:], ALU.is_ge, ALU.mult, [Pt[n % 4]], [me])
                        if tl == 1:
                            flush_wa()
                        pw_ = cur["pw"]
                        for h in range(8):
                            k.mm(pw_[:, tl * 128:(tl + 1) * 128], me[:, h, :], dg16[:, tl, h, :], h == 0, h == 7, [me, dg16], [pw_])
                        if tl == 3:
                            pend_wa.append((cur["wa"], pw_, cur["ga"]))
                            cur["was"].append(cur["wa"])
                            if kl == 3:
                                pend_po.append((gi, cur["was"]))
                    flush_wa()
                    flush_po()
                    for dc in range(16):
                        x_ = ev[dc % 2]
                        k.dma(x_[:], x1_d[dc][:, tsl], [x1_r], [x_])
                        k.stt(x_[:], accT[:, dc, :], self.modT[:, 80 + dc:81 + dc], x_[:], ALU.mult, ALU.add, [accT, self.modT, x_], [x_])
                        k.dma(self.outT[dc * 128:(dc + 1) * 128, tsl], x_[:], [x_], [])
                k.barrier()


def _consts():
    f = np.float32
    half = 64
    freqs = (10000.0 ** (-np.arange(half, dtype=f) / f(half))).astype(f)
    ang = np.arange(S, dtype=f)[:, None] * freqs[None, :]
    cos = np.cos(ang).astype(f).T
    sin = np.sin(ang).astype(f).T
    cosT = np.concatenate([cos, cos], 0)
    sinT = np.concatenate([sin, sin], 0)
    rotm = np.zeros((128, 128), f)
    for m in range(64):
        rotm[m + 64, m] = -1.0
        rotm[m, m + 64] = 1.0
    ident = np.eye(128, dtype=f)
    t = np.arange(S)
    cst = np.arange(NCMP) * 16
    maskc = np.zeros((128, S), f)
    maskc[:NCMP] = ((cst + 31)[:, None] <= t[None, :]).astype(f)
    kk = np.arange(128)[:, None]
    tt = np.arange(512)[None, :]
    cm = np.stack([(128 * o + kk <= tt).astype(f) for o in range(4)], 1).reshape(128, 4 * 512)
    wm = np.stack([(((128 * rel + kk - tt) <= 0) & ((128 * rel + kk - tt) > -512)).astype(f)
                   for rel in range(-4, 4)], 1).reshape(128, 8 * 512)
    ex = np.zeros((32, 16, 128), f)
    for kc in range(16):
        for kq in range(128):
            ex[2 * kc + kq // 64, kc, kq] = 1.0
    ex = ex.reshape(32, 16 * 128)
    jb = np.arange(32)[None, :]
    cur = (t // 64)[:, None]
    forced = (jb == 0) | (jb == cur) | (jb == cur - 1)
    valid = (jb * 64) <= t[:, None]
    vm_ = (valid & ~forced).astype(f)
    fb_ = np.where(forced, 1e4, np.where(valid, 0.0, -1e4)).astype(f)
    vm = vm_.reshape(16, 128, 32).transpose(1, 0, 2).reshape(128, 512)
    fb = fb_.reshape(16, 128, 32).transpose(1, 0, 2).reshape(128, 512)
    sst = np.arange(32) * 64
    ov = np.maximum(np.minimum(cst[:, None] + 32, sst[None, :] + 64) - np.maximum(cst[:, None], sst[None, :]), 0)
    ovl = np.zeros((128, 32), f)
    ovl[:NCMP] = ov.astype(f) / 32.0
    return dict(cosT=cosT, sinT=sinT, rotm=rotm, ident=ident, maskc=maskc, cm=cm, wm=wm, ex=ex, vm=vm, fb=fb, ovl=ovl)


def prep_shared(inp):
    f = np.float32
    A = lambda v: np.ascontiguousarray(np.asarray(v, dtype=f))
    sh = {}
    sh["ada_w"] = A(inp["ada_w"][0])
    sh["ada_bT"] = A(inp["ada_b"][0].reshape(96, 128).T)
    sh["g1T"] = A(inp["norm_mix_g"][0].reshape(16, 128).T)
    sh["g2T"] = A(inp["norm_ffn_g"][0].reshape(16, 128).T)
    sh["w_in"] = A(inp["w_in"][0])
    sh["w_out"] = A(inp["w_out"][0])
    sh["wq"] = A(inp["peer_wq"][0])
    sh["qg"] = A(inp["q_norm_g"][0].reshape(128, 1))
    sh["kgT"] = A(inp["k_norm_g"][0].T)
    sh["kg0b"] = A(np.broadcast_to(inp["k_norm_g"][0, 0][None, :], (128, 128)))
    sh["pekT"] = A(inp["cmp_pe_k"][0].T)
    sh["pevT"] = A(inp["cmp_pe_v"][0].T)
    sh["cwk"] = A(inp["cmp_w_k"][0])
    sh["cwv"] = A(inp["cmp_w_v"][0])
    sh["gateb"] = A(np.broadcast_to(inp["gate_b"][0][None, :], (128, 24)))
    sh["convw"] = A(inp["conv_w"][0].reshape(4, 8, 128).transpose(2, 1, 0).reshape(128, 32))
    for nm, key in (("convb", "conv_b"), ("lba", "lru_ba"), ("lbi", "lru_bi"), ("lam", "lru_lam"), ("gr", "out_g_rnn")):
        sh[nm] = A(inp[key][0].reshape(8, 128).T)
    sh["gab"] = A(np.broadcast_to(inp["out_g_attn"][0][None, :], (128, 1024)))
    sh["wa"] = A(inp["lru_wa"][0])
    sh["wi"] = A(inp["lru_wi"][0])
    sh["keysT"] = A(inp["peer_keys"][0].transpose(3, 0, 1, 2).reshape(128, 16 * 128))
    sh["downB"] = A(inp["peer_down"][0].reshape(128, 128, 16, 128).transpose(0, 3, 2, 1).reshape(128 * 128, 16 * 128))
    sh["up"] = A(inp["peer_up"][0])
    sh.update(_consts())
    return sh


def prep_core(inp, b):
    f = np.float32
    return {"xT": np.ascontiguousarray(np.asarray(inp["x"][b], dtype=f).T),
            "c_col": np.ascontiguousarray(np.asarray(inp["c"][b], dtype=f).reshape(16, 128).T)}


def kernel(**inputs):
    inp = {k_: np.asarray(v) for k_, v in inputs.items()}
    sh = prep_shared(inp)
    nc = Prog().build()
    in_maps = []
    for b in range(8):
        m = dict(sh)
        m.update(prep_core(inp, b))
        in_maps.append(m)
    res = run_bass_kernel_spmd(nc, in_maps, core_ids=list(range(8)))
    out = np.stack([np.asarray(r["outT"]).T for r in res.results], 0)
    return np.ascontiguousarray(out.astype(np.float32))
```

```python
import numpy as np
from contextlib import ExitStack
import concourse.bass as bass
import concourse.mybir as mybir
from concourse.bass_utils import run_bass_kernel_spmd

F32 = mybir.dt.float32
BF16 = mybir.dt.bfloat16
AF = mybir.ActivationFunctionType
ALU = mybir.AluOpType
AX = mybir.AxisListType

D = 2048
S = 2048
NT = 16
NIN = 4632
C_Q, C_KC, C_VC, C_KS, C_VS, C_KW, C_VW, C_GL, C_XR, C_XG = 0, 1024, 1280, 1536, 1792, 2048, 2304, 2560, 2584, 3608
EPS = 1e-6
NCMP = 127
SCALE = 128 ** -0.5


class Res:
    __slots__ = ("name", "w", "r")

    def __init__(self, name=""):
        self.name = name
        self.w = None
        self.r = []


class Op:
    __slots__ = ("eng", "fn", "deps", "flag", "cval", "dma", "dsem", "dval")

    def __init__(self, eng, fn, dma=False):
        self.eng = eng
        self.fn = fn
        self.deps = []
        self.flag = False
        self.cval = 0
        self.dma = dma
        self.dsem = None
        self.dval = 0


class Sched:
    ENGS = ("pe", "act", "dve", "pool", "sp")

    def __init__(self, nc, n_dma_sems=16):
        self.nc = nc
        self.q = {e: [] for e in self.ENGS}
        self.n_dma_sems = n_dma_sems
        self.dma_rr = 0
        self.dma_last = [None] * n_dma_sems
        self.dma_cnt = [0] * n_dma_sems
        self.pending = {e: [] for e in self.ENGS}

    def _add(self, eng, fn, reads, writes, dma=False):
        op = Op(eng, fn, dma)
        deps = list(self.pending[eng])
        self.pending[eng] = []
        for r in reads:
            if r.w is not None:
                deps.append(r.w)
        for w in writes:
            if w.w is not None:
                deps.append(w.w)
            deps.extend(w.r)
        for r in reads:
            r.r.append(op)
        for w in writes:
            w.w = op
            w.r = []
        if dma:
            s = self.dma_rr
            self.dma_rr = (self.dma_rr + 1) % self.n_dma_sems
            prev = self.dma_last[s]
            if prev is not None:
                deps.append(prev)
            self.dma_last[s] = op
            self.dma_cnt[s] += 1
            op.dsem = s
            op.dval = 16 * self.dma_cnt[s]
        seen = set()
        for d in deps:
            if d is op or id(d) in seen:
                continue
            if (not d.dma) and d.eng == eng and eng == "pe":
                continue
            seen.add(id(d))
            op.deps.append(d)
            d.flag = True
        self.q[eng].append(op)
        return op

    def op(self, eng, fn, reads=(), writes=()):
        return self._add(eng, fn, list(reads), list(writes))

    def dma(self, eng, out, in_, reads=(), writes=()):
        return self._add(eng, lambda e: e.dma_start(out=out, in_=in_), list(reads), list(writes), dma=True)

    def barrier(self):
        lasts = []
        for e in self.ENGS:
            for op in reversed(self.q[e]):
                if not op.dma:
                    lasts.append(op)
                    break
        for s in range(self.n_dma_sems):
            if self.dma_last[s] is not None:
                lasts.append(self.dma_last[s])
        for e in self.ENGS:
            self.pending[e] = list(lasts)

    def emit(self):
        nc = self.nc
        with ExitStack() as st:
            esem = {e: st.enter_context(nc.semaphore(f"s_{e}")) for e in self.ENGS}
            dsem = [st.enter_context(nc.semaphore(f"d_{i}")) for i in range(self.n_dma_sems)]
            for e in self.ENGS:
                c = 0
                for op in self.q[e]:
                    if op.dma:
                        continue
                    if op.flag:
                        c += 1
                        op.cval = c
            block = st.enter_context(nc.Block())

            def run(ename, eobj):
                waited = {}
                for op in self.q[ename]:
                    need = {}
                    for d in op.deps:
                        if d.dma:
                            key, val = ("d", d.dsem), d.dval
                        else:
                            key, val = ("e", d.eng), d.cval
                        if val > need.get(key, 0):
                            need[key] = val
                    for key, val in need.items():
                        if waited.get(key, 0) >= val:
                            continue
                        waited[key] = val
                        sem = dsem[key[1]] if key[0] == "d" else esem[key[1]]
                        eobj.wait_ge(sem, val)
                    ins = op.fn(eobj)
                    if op.dma:
                        ins.then_inc(dsem[op.dsem], 16)
                    elif op.flag:
                        ins.then_inc(esem[ename], 1)
                if ename == "sp":
                    for s in range(self.n_dma_sems):
                        if self.dma_cnt[s] > 0:
                            eobj.wait_ge(dsem[s], 16 * self.dma_cnt[s])

            block.tensor(lambda e: run("pe", e))
            block.scalar(lambda e: run("act", e))
            block.vector(lambda e: run("dve", e))
            block.gpsimd(lambda e: run("pool", e))
            block.sync(lambda e: run("sp", e))


class T:
    __slots__ = ("t", "r")

    def __init__(self, t, name=""):
        self.t = t
        self.r = Res(name)

    def __getitem__(self, k):
        return self.t[k]


class K:
    def __init__(self, nc):
        self.nc = nc
        self.S = Sched(nc)
        self.uid = 0

    def sb(self, st, shape, dt, name=None):
        self.uid += 1
        n = f"{name or 't'}_{self.uid}"
        return T(st.enter_context(self.nc.sbuf_tensor(n, list(shape), dt)), n)

    def ps(self, st, shape, dt=F32, name=None):
        self.uid += 1
        n = f"{name or 'p'}_{self.uid}"
        return T(st.enter_context(self.nc.psum_tensor(n, list(shape), dt)), n)

    @staticmethod
    def _rs(xs):
        return [x.r if isinstance(x, T) else x for x in xs]

    def mm(self, out, lhsT, rhs, start, stop, R, W):
        self.S.op("pe", lambda e: e.matmul(out, lhsT, rhs, start=start, stop=stop), self._rs(R), self._rs(W))

    def act(self, out, in_, func, R, W, bias=None, scale=None, accum_out=None, eng="act"):
        kw = {}
        if bias is not None:
            kw["bias"] = bias
        if scale is not None:
            kw["scale"] = scale
        if accum_out is not None:
            kw["accum_out"] = accum_out
        self.S.op(eng, lambda e: e.activation(out, in_, func, **kw), self._rs(R), self._rs(W))

    def tt(self, out, in0, in1, op, R, W, eng="dve"):
        self.S.op(eng, lambda e: e.tensor_tensor(out, in0, in1, op), self._rs(R), self._rs(W))

    def ts(self, out, in0, s1, s2, op0, op1, R, W, eng="dve", accum_out=None):
        if accum_out is None:
            if op1 is None:
                self.S.op(eng, lambda e: e.tensor_scalar(out, in0, s1, None, op0), self._rs(R), self._rs(W))
            else:
                self.S.op(eng, lambda e: e.tensor_scalar(out, in0, s1, s2, op0, op1), self._rs(R), self._rs(W))
        else:
            self.S.op(eng, lambda e: e.tensor_scalar(out, in0, s1, s2, op0, op1, accum_out=accum_out),
                      self._rs(R), self._rs(W))

    def stt(self, out, in0, scalar, in1, op0, op1, R, W, eng="dve"):
        self.S.op(eng, lambda e: e.scalar_tensor_tensor(out, in0, scalar, in1, op0, op1), self._rs(R), self._rs(W))

    def copy(self, out, in_, R, W, eng="dve"):
        if eng == "act":
            self.S.op("act", lambda e: e.copy(out, in_), self._rs(R), self._rs(W))
        else:
            self.S.op(eng, lambda e: e.tensor_copy(out, in_), self._rs(R), self._rs(W))

    def memset(self, ap, val, W, eng="pool"):
        self.S.op(eng, lambda e: e.memset(ap, val), [], self._rs(W))

    def recip(self, out, in_, R, W):
        self.S.op("dve", lambda e: e.reciprocal(out, in_), self._rs(R), self._rs(W))

    def dma(self, out, in_, R, W, eng="sp"):
        self.S.dma(eng, out, in_, self._rs(R), self._rs(W))

    def fn(self, eng, f, R, W):
        self.S.op(eng, f, self._rs(R), self._rs(W))

    def barrier(self):
        self.S.barrier()


IN_SPECS = [
    ("xT", [D, S], F32), ("c_col", [128, 16], F32), ("ada_w", [D, 6 * D], F32), ("ada_bT", [128, 96], F32),
    ("g1T", [128, 16], F32), ("g2T", [128, 16], F32), ("w_in", [D, NIN], F32), ("w_out", [D, D], F32),
    ("wq", [D, D], F32), ("qg", [128, 1], F32), ("kgT", [128, 3], F32), ("kg0b", [128, 128], F32),
    ("pekT", [128, 32], F32), ("pevT", [128, 32], F32), ("cwk", [4096, 128], F32), ("cwv", [4096, 128], F32),
    ("gateb", [128, 24], F32), ("convw", [128, 32], F32), ("convb", [128, 8], F32), ("lba", [128, 8], F32),
    ("lbi", [128, 8], F32), ("lam", [128, 8], F32), ("gr", [128, 8], F32), ("gab", [128, 1024], F32),
    ("wa", [8, 128, 128], F32), ("wi", [8, 128, 128], F32), ("keysT", [128, 16 * 128], F32),
    ("downB", [128 * 128, 16 * 128], F32), ("up", [16384, D], F32),
    ("cosT", [128, S], F32), ("sinT", [128, S], F32), ("rotm", [128, 128], F32), ("ident", [128, 128], F32),
    ("maskc", [128, S], F32), ("cm", [128, 4 * 512], F32), ("wm", [128, 8 * 512], F32),
    ("ex", [32, 16 * 128], F32), ("vm", [128, 16 * 32], F32), ("fb", [128, 16 * 32], F32), ("ovl", [128, 32], F32),
]


class Prog:
    def __init__(self, stop_after=99, dbg=()):
        self.nc = nc = bass.Bass("TRN2", target_bir_lowering=False)
        self.k = K(nc)
        self.stop_after = stop_after
        self.dbg = set(dbg)
        for d_ in self.dbg:
            if d_.startswith("qkind="):
                self.qkind = d_.split("=")[1]
        self.I = {}
        for name, shape, dt in IN_SPECS:
            self.I[name] = nc.dram_tensor(name, shape, dt, kind="ExternalInput").ap()
        self.outT = nc.dram_tensor("outT", [D, S], F32, kind="ExternalOutput").ap()
        self.scr = {}

    def scratch(self, name, shape, dt):
        kind = "ExternalOutput" if name in self.dbg else "Internal"
        ap = self.nc.dram_tensor(name, list(shape), dt, kind=kind).ap()
        self.scr[name] = (ap, Res(name))
        return ap, self.scr[name][1]

    def build(self):
        k = self.k
        with ExitStack() as g:
            self.g = g
            self.modT = k.sb(g, [128, 96], F32, "modT")
            self.G1 = k.sb(g, [128, 16], F32, "G1")
            self.G2 = k.sb(g, [128, 16], F32, "G2")
            self.eps_t = k.sb(g, [128, 1], F32, "eps")
            self.ones16 = k.sb(g, [128, 128], BF16, "ones16")
            self.ident16 = k.sb(g, [128, 128], BF16, "ident16")
            k.memset(self.eps_t[:], EPS, [self.eps_t])
            k.memset(self.ones16[:], 1.0, [self.ones16])
            self.ident32 = k.sb(g, [128, 128], F32, "ident32")
            k.dma(self.ident32[:], self.I["ident"], [], [self.ident32])
            k.copy(self.ident16[:], self.ident32[:], [self.ident32], [self.ident16])
            names = ["p0_mod", "p12_proj", "p3_cmp", "p4_attn", "p5_rnn", "p6_out", "p7_peer"]
            phases = [getattr(self, n) for n in names if hasattr(self, n)]
            for i, ph in enumerate(phases):
                if i > self.stop_after:
                    break
                ph()
                k.barrier()
            k.S.emit()
        return self.nc

    def p0_mod(self):
        k, I = self.k, self.I
        with ExitStack() as st:
            cc = k.sb(st, [128, 16], F32, "cc")
            sc = k.sb(st, [128, 16], F32, "sc")
            abT = k.sb(st, [128, 96], F32, "abT")
            g1 = k.sb(st, [128, 16], F32, "g1")
            g2 = k.sb(st, [128, 16], F32, "g2")
            tmp = k.sb(st, [128, 16], F32, "tmp")
            wb = [k.sb(st, [128, 16, 512], F32, f"adaw{i}") for i in range(2)]
            wb16 = [k.sb(st, [128, 16, 512], BF16, f"adaw16{i}") for i in range(2)]
            sc16 = k.sb(st, [128, 16], BF16, "sc16")
            pm = k.ps(st, [128, 96], F32, "pm")
            k.dma(cc[:], I["c_col"], [], [cc])
            k.dma(abT[:], I["ada_bT"], [], [abT])
            k.dma(g1[:], I["g1T"], [], [g1])
            k.dma(g2[:], I["g2T"], [], [g2])
            k.act(sc[:], cc[:], AF.Silu, [cc], [sc])
            k.copy(sc16[:], sc[:], [sc], [sc16])
            wv = I["ada_w"].rearrange("(k p) n -> p k n", p=128)
            cast_eng = ("act", "dve", "pool", "dve")
            for gi in range(24):
                b = wb[gi % 2]
                b16 = wb16[gi % 2]
                k.dma(b[:], wv[:, :, gi * 512:(gi + 1) * 512], [], [b], eng=("sp" if gi % 2 == 0 else "pool"))
                for q in range(4):
                    k.copy(b16[:, q * 4:(q + 1) * 4, :], b[:, q * 4:(q + 1) * 4, :], [b], [b16], eng=cast_eng[q])
                for j in range(4):
                    col = gi * 4 + j
                    for kk in range(16):
                        k.mm(pm[:, col:col + 1], b16[:, kk, j * 128:(j + 1) * 128], sc16[:, kk:kk + 1],
                             kk == 0, kk == 15, [b16, sc16], [pm])
            k.tt(self.modT[:], pm[:], abT[:], ALU.add, [pm, abT], [self.modT])
            k.ts(tmp[:], self.modT[:, 16:32], 1.0, None, ALU.add, None, [self.modT], [tmp])
            k.tt(self.G1[:], tmp[:], g1[:], ALU.mult, [tmp, g1], [self.G1])
            k.ts(tmp[:], self.modT[:, 64:80], 1.0, None, ALU.add, None, [self.modT, self.G1], [tmp])
            k.tt(self.G2[:], tmp[:], g2[:], ALU.mult, [tmp, g2], [self.G2])
            if "modT" in self.dbg:
                d, r = self.scratch("modT", [128, 96], F32)
                k.dma(d, self.modT[:], [self.modT], [r])

    def rms_stats_fm(self, st, loader, nchunks, width, scale_div, name):
        k = self.k
        rstd = k.sb(st, [128, S], F32, name)
        with ExitStack() as s2:
            xb = [k.sb(s2, [128, S], F32, "xld") for _ in range(2)]
            sq = [k.sb(s2, [128, S], BF16, "sq") for _ in range(2)]
            pss = [k.ps(s2, [128, 512], F32, "pss") for _ in range(4)]
            for kk in range(nchunks):
                xt, sqt = xb[kk % 2], sq[kk % 2]
                loader(kk, xt)
                k.act(sqt[:], xt[:], AF.Square, [xt], [sqt])
                for tg in range(4):
                    k.mm(pss[tg][:], self.ones16[:], sqt[:, tg * 512:(tg + 1) * 512], kk == 0, kk == nchunks - 1,
                         [self.ones16, sqt], [pss[tg]])
            for tg in range(4):
                sl = slice(tg * 512, (tg + 1) * 512)
                k.act(rstd[:, sl], pss[tg][:], AF.Sqrt, [pss[tg], self.eps_t], [rstd], bias=self.eps_t[:], scale=1.0 / scale_div)
            k.recip(rstd[:], rstd[:], [rstd], [rstd])
        k.barrier()
        return rstd

    def p12_proj(self):
        k, I = self.k, self.I
        xTv = I["xT"].rearrange("(k p) t -> k p t", p=128)
        qT_d, qT_r = self.scratch("qT", [8, 128, S], BF16)
        kcT_d, kcT_r = self.scratch("kcT", [2, 128, S], BF16)
        vcT_d, vcT_r = self.scratch("vcT", [2, 128, S], BF16)
        ksT_d, ksT_r = self.scratch("ksT", [2, 128, S], BF16)
        kwT_d, kwT_r = self.scratch("kwT", [2, 128, S], BF16)
        vs_d, vs_r = self.scratch("vs", [S, 256], BF16)
        vw_d, vw_r = self.scratch("vw", [S, 256], BF16)
        gt_d, gt_r = self.scratch("gates", [128, 16 * 24], F32)
        xrT_d, xrT_r = self.scratch("xrT", [8, 128, S], F32)
        xgT_d, xgT_r = self.scratch("xgT", [8, 128, S], F32)
        with ExitStack() as st:
            hT = k.sb(st, [128, 16, S], BF16, "hT")
            with ExitStack() as s1:
                rstd = self.rms_stats_fm(s1, lambda kk, dst: k.dma(dst[:], xTv[kk], [], [dst]), 16, S, float(D), "rstd1")
                xb = [k.sb(s1, [128, S], F32, "xld2") for _ in range(2)]
                tmp = [k.sb(s1, [128, S], F32, "tmp") for _ in range(2)]
                for kk in range(16):
                    xt, tp = xb[kk % 2], tmp[kk % 2]
                    k.dma(xt[:], xTv[kk], [], [xt])
                    k.stt(tp[:], xt[:], self.G1[:, kk:kk + 1], rstd[:], ALU.mult, ALU.mult, [xt, self.G1, rstd], [tp])
                    k.act(hT[:, kk, :], tp[:], AF.Identity, [tp, self.modT], [hT], bias=self.modT[:, kk:kk + 1], scale=1.0)
            if "hT" in self.dbg:
                d, r = self.scratch("hT", [16, 128, S], BF16)
                k.dma(d.rearrange("k p t -> p k t"), hT[:], [hT], [r])
            k.barrier()
            if "stop_p1" in self.dbg:
                return
            with ExitStack() as s2:
                wb = [k.sb(s2, [128, 16, 544], BF16, f"win{i}") for i in range(2)]
                cosT = k.sb(s2, [128, S], F32, "cosT")
                sinT = k.sb(s2, [128, S], F32, "sinT")
                rot16 = k.sb(s2, [128, 128], BF16, "rot16")
                gq = k.sb(s2, [128, 4], F32, "gq")
                gateb = k.sb(s2, [128, 24], F32, "gateb")
                k.dma(cosT[:], I["cosT"], [], [cosT])
                k.dma(sinT[:], I["sinT"], [], [sinT])
                rot32 = k.sb(s2, [128, 128], F32, "rot32")
                k.dma(rot32[:], I["rotm"], [], [rot32])
                k.copy(rot16[:], rot32[:], [rot32], [rot16])
                k.dma(gq[:, 0:1], I["qg"], [], [gq])
                k.dma(gq[:, 1:4], I["kgT"], [], [gq])
                k.dma(gateb[:], I["gateb"], [], [gateb])
                pz = [k.ps(s2, [128, 512], F32, "pz") for _ in range(4)]
                pss = [k.ps(s2, [128, 512], F32, "pss2") for _ in range(2)]
                prot = [k.ps(s2, [128, 512], F32, "prot") for _ in range(2)]
                ptm = pz[2:4]
                sq = [k.sb(s2, [128, 512], BF16, "sq2") for _ in range(2)]
                rs = [k.sb(s2, [128, 512], F32, "rs") for _ in range(2)]
                xn = [k.sb(s2, [128, 512], F32, "xn") for _ in range(2)]
                xn16 = [k.sb(s2, [128, 512], BF16, "xn16") for _ in range(2)]
                t1 = [k.sb(s2, [128, 512], F32, "t1") for _ in range(2)]
                t2 = [k.sb(s2, [128, 512], F32, "t2") for _ in range(2)]
                o16 = [k.sb(s2, [128, S], BF16, "o16") for _ in range(2)]
                o32 = [k.sb(s2, [128, S], F32, "o32") for _ in range(2)]
                vtm = k.sb(s2, [128, 16, 256], BF16, "vtm")
                gtm = k.sb(s2, [128, 16, 24], F32, "gtm")
                gtmp = k.sb(s2, [128, 24], F32, "gtmp")
                wv = I["w_in"].rearrange("(k p) n -> p k n", p=128)
                cnt = {"c": 0, "i": 0}

                pend = []

                def fm_chunk(b, co, kind, gcol, dst_ap, dst_res):
                    ci = cnt["c"]
                    cnt["c"] += 1
                    ob = (o32 if kind == "f32" else o16)[ci % 2]
                    for tg in range(4):
                        pend.append((b, co, kind, gcol, dst_ap, dst_res, ob, tg))

                def fm_s0(st_):
                    b, co, kind, gcol, dst_ap, dst_res, ob, tg = st_
                    i = cnt["i"]
                    cnt["i"] += 1
                    p = pz[i % 4]
                    sl = slice(tg * 512, (tg + 1) * 512)
                    for kk in range(16):
                        k.mm(p[:], b[:, kk, co:co + 128], hT[:, kk, sl], kk == 0, kk == 15, [b, hT], [p])
                    return (i, p)

                def fm_s1(st_, ip):
                    b, co, kind, gcol, dst_ap, dst_res, ob, tg = st_
                    i, p = ip
                    sl = slice(tg * 512, (tg + 1) * 512)
                    if kind in ("f32", "bf16"):
                        k.copy(ob[:, sl], p[:], [p], [ob], eng=("act" if tg % 2 == 0 else "dve"))
                    else:
                        a, a16 = xn[i % 2], xn16[i % 2]
                        if kind == "normrope":
                            sqt, pst, rst = sq[i % 2], pss[i % 2], rs[i % 2]
                            k.act(sqt[:], p[:], AF.Square, [p], [sqt])
                            k.mm(pst[:], self.ones16[:], sqt[:], True, True, [self.ones16, sqt], [pst])
                            k.act(rst[:], pst[:], AF.Sqrt, [pst, self.eps_t], [rst], bias=self.eps_t[:], scale=1.0 / 128.0)
                            k.recip(rst[:], rst[:], [rst], [rst])
                            k.stt(a[:], p[:], gq[:, gcol:gcol + 1], rst[:], ALU.mult, ALU.mult, [p, gq, rst], [a])
                        else:
                            k.copy(a[:], p[:], [p], [a], eng="dve")
                        k.copy(a16[:], a[:], [a], [a16], eng="act")
                        pr = prot[i % 2]
                        k.mm(pr[:], rot16[:], a16[:], True, True, [rot16, a16], [pr])
                        k.tt(t1[i % 2][:], a[:], cosT[:, sl], ALU.mult, [a, cosT], [t1[i % 2]])
                        k.tt(t2[i % 2][:], pr[:], sinT[:, sl], ALU.mult, [pr, sinT], [t2[i % 2]])
                        k.tt(ob[:, sl], t1[i % 2][:], t2[i % 2][:], ALU.add, [t1[i % 2], t2[i % 2]], [ob], eng="pool")
                    if tg == 3:
                        k.dma(dst_ap, ob[:], [ob], [dst_res])

                def fm_flush():
                    steps = list(pend)
                    del pend[:]
                    inflight = []
                    LA = 2
                    for n in range(min(LA, len(steps))):
                        inflight.append(fm_s0(steps[n]))
                    for n in range(len(steps)):
                        if n + LA < len(steps):
                            inflight.append(fm_s0(steps[n + LA]))
                        fm_s1(steps[n], inflight.pop(0))

                stg = [k.sb(s2, [128, 16, 272], F32, f"stg{i}") for i in range(2)]
                lcnt = {"g": 0, "s": 0}

                def load_group(c0, c1):
                    fm_flush()
                    b = wb[lcnt["g"] % 2]
                    lcnt["g"] += 1
                    w = c1 - c0
                    pieces = [(a, min(a + 256, w)) for a in range(0, w, 256)]
                    for (a0, a1) in pieces:
                        sg = stg[lcnt["s"] % 2]
                        lcnt["s"] += 1
                        k.dma(sg[:, :, 0:a1 - a0], wv[:, :, c0 + a0:c0 + a1], [], [sg], eng=("sp" if lcnt["s"] % 2 else "pool"))
                        k.copy(b[:, :, a0:a1], sg[:, :, 0:a1 - a0], [sg], [b], eng="pool")
                    return b

                def tm_block(b, co, width, post):
                    for tt_ in range(16):
                        p = ptm[tt_ % 2]
                        for kk in range(16):
                            k.mm(p[:, 0:width], hT[:, kk, tt_ * 128:(tt_ + 1) * 128], b[:, kk, co:co + width], kk == 0, kk == 15, [b, hT], [p])
                        post(tt_, p)

                b = load_group(0, 512)
                for j in range(4):
                    fm_chunk(b, j * 128, self.qkind if hasattr(self, "qkind") else "normrope", 0, qT_d[j], qT_r)
                b = load_group(512, 1024)
                for j in range(4):
                    fm_chunk(b, j * 128, self.qkind if hasattr(self, "qkind") else "normrope", 0, qT_d[4 + j], qT_r)
                if "stop_g0" in self.dbg:
                    fm_flush()
                    return
                b = load_group(1024, 1536)
                for j in range(2):
                    fm_chunk(b, j * 128, "rope", 0, kcT_d[j], kcT_r)
                for j in range(2):
                    fm_chunk(b, 256 + j * 128, "bf16", 0, vcT_d[j], vcT_r)
                b = load_group(1536, 2048)
                for j in range(2):
                    fm_chunk(b, j * 128, "normrope", 2, ksT_d[j], ksT_r)
                fm_flush()
                tm_block(b, 256, 256, lambda tt_, p: k.copy(vtm[:, tt_, :], p[:, 0:256], [p], [vtm], eng=("act" if tt_ % 2 == 0 else "dve")))
                k.dma(vs_d.rearrange("(t p) c -> p t c", p=128), vtm[:], [vtm], [vs_r])
                if "stop_g1" in self.dbg:
                    return
                if "rep_g3" in self.dbg:
                    b = load_group(1536, 2048)
                    for j in range(2):
                        fm_chunk(b, j * 128, "normrope", 2, ksT_d[j], ksT_r)
                    tm_block(b, 256, 256, lambda tt_, p: k.copy(vtm[:, tt_, :], p[:, 0:256], [p], [vtm], eng=("act" if tt_ % 2 == 0 else "dve")))
                    k.dma(vs_d.rearrange("(t p) c -> p t c", p=128), vtm[:], [vtm], [vs_r])
                    return
                b = load_group(2048, 2560)
                for j in range(2):
                    fm_chunk(b, j * 128, "normrope", 3, kwT_d[j], kwT_r)
                fm_flush()
                tm_block(b, 256, 256, lambda tt_, p: k.copy(vtm[:, tt_, :], p[:, 0:256], [p], [vtm], eng=("act" if tt_ % 2 == 0 else "dve")))
                b = load_group(2560, 2584)

                def post_g(tt_, p):
                    k.tt(gtmp[:], p[:, 0:24], gateb[:], ALU.add, [p, gateb], [gtmp])
                    k.act(gtmp[:], gtmp[:], AF.Exp, [gtmp], [gtmp], scale=-1.0)
                    k.ts(gtmp[:], gtmp[:], 1.0, None, ALU.add, None, [gtmp], [gtmp])
                    k.recip(gtm[:, tt_, :], gtmp[:], [gtmp], [gtm])
                tm_block(b, 0, 24, post_g)
                k.dma(vw_d.rearrange("(t p) c -> p t c", p=128), vtm[:], [vtm], [vw_r])
                k.dma(gt_d, gtm[:].rearrange("p t c -> p (t c)"), [gtm], [gt_r])
                if "stop_g2" in self.dbg:
                    return
                for half in range(2):
                    b = load_group(C_XR + half * 512, C_XR + (half + 1) * 512)
                    for j in range(4):
                        fm_chunk(b, j * 128, "f32", 0, xrT_d[half * 4 + j], xrT_r)
                for half in range(2):
                    b = load_group(C_XG + half * 512, C_XG + (half + 1) * 512)
                    for j in range(4):
                        fm_chunk(b, j * 128, "f32", 0, xgT_d[half * 4 + j], xgT_r)
                fm_flush()

    def p3_cmp(self):
        k, I = self.k, self.I
        kcT_d = self.scr["kcT"][0]
        vcT_d = self.scr["vcT"][0]
        kcmpT_d, kcmpT_r = self.scratch("kcmpT", [128, 2, 128], BF16)
        vcmp_d, vcmp_r = self.scratch("vcmp", [128, 2, 162], BF16)
        with ExitStack() as st:
            src = k.sb(st, [128, 4, S], BF16, "cmpsrc")
            wst = k.sb(st, [128, 32, 128], F32, "wst")
            w16 = [k.sb(st, [128, 32, 128], BF16, f"w16{i}") for i in range(2)]
            pe32 = k.sb(st, [128, 2, 32], F32, "pe32")
            peB = [k.sb(st, [128, 32, 127], BF16, f"peB{i}") for i in range(2)]
            kg0b = k.sb(st, [128, 128], F32, "kg0b")
            ovl = k.sb(st, [128, 32], F32, "ovl")
            ss = k.sb(st, [128, 1], F32, "ss")
            junk = k.sb(st, [128, 128], F32, "junk")
            kn16 = k.sb(st, [128, 128], BF16, "kn16")
            kT16 = k.sb(st, [128, 2, 128], BF16, "kT16")
            va16 = k.sb(st, [128, 2, 162], BF16, "va16")
            pc = [k.ps(st, [128, 128], F32, "pc") for _ in range(2)]
            ptr = k.ps(st, [128, 128], F32, "ptr")
            for g in range(2):
                k.dma(src[:, g, :], kcT_d[g], [self.scr["kcT"][1]], [src])
                k.dma(src[:, 2 + g, :], vcT_d[g], [self.scr["vcT"][1]], [src])
            k.dma(pe32[:, 0, :], I["pekT"], [], [pe32])
            k.dma(pe32[:, 1, :], I["pevT"], [], [pe32])
            k.dma(kg0b[:], I["kg0b"], [], [kg0b])
            k.dma(ovl[:], I["ovl"], [], [ovl])
            k.memset(kT16[:], 0.0, [kT16])
            k.memset(va16[:], 0.0, [va16])
            for kv, wname in enumerate(("cwk", "cwv")):
                k.dma(wst[:], I[wname].rearrange("(l d) o -> d l o", d=128), [], [wst])
                k.copy(w16[kv][:], wst[:], [wst], [w16[kv]], eng="pool")
                k.copy(peB[kv][:], pe32[:, kv, :].unsqueeze(2).to_broadcast([128, 32, 127]), [pe32], [peB[kv]])
            for kv in range(2):
                for g in range(2):
                    p = pc[(kv * 2 + g) % 2]
                    for l in range(32):
                        k.mm(p[0:127, :], src[:, kv * 2 + g, l:l + 16 * 126 + 1:16], w16[kv][:, l, :], l == 0, False, [src, w16[kv]], [p])
                    for l in range(32):
                        k.mm(p[0:127, :], peB[kv][:, l, :], w16[kv][:, l, :], False, l == 31, [peB[kv], w16[kv]], [p])
                    if kv == 0:
                        k.act(junk[0:127, :], p[0:127, :], AF.Square, [p], [junk, ss], accum_out=ss[0:127, :])
                        k.act(ss[0:127, :], ss[0:127, :], AF.Sqrt, [ss, self.eps_t], [ss], bias=self.eps_t[0:127, :], scale=1.0 / 128.0)
                        k.recip(ss[0:127, :], ss[0:127, :], [ss], [ss])
                        k.stt(kn16[0:127, :], p[0:127, :], ss[0:127, :], kg0b[0:127, :], ALU.mult, ALU.mult, [p, ss, kg0b], [kn16])
                        k.mm(ptr[:, 0:127], kn16[0:127, :], self.ident16[0:127, 0:127], True, True, [kn16, self.ident16], [ptr])
                        k.copy(kT16[:, g, 0:127], ptr[:, 0:127], [ptr], [kT16])
                    else:
                        k.copy(va16[0:127, g, 0:128], p[0:127, :], [p], [va16])
                        k.memset(va16[0:127, g, 128:129], 1.0, [va16])
                        k.copy(va16[0:127, g, 129:161], ovl[0:127, :], [ovl], [va16])
            k.dma(kcmpT_d, kT16[:], [kT16], [kcmpT_r])
            k.dma(vcmp_d, va16[:], [va16], [vcmp_r])

    def p4_attn(self):
        k, I = self.k, self.I
        sc = self.scr
        yT_d, yT_r = self.scratch("yT", [16, 128, S], BF16)
        with ExitStack() as st:
            qT = k.sb(st, [128, 8, S], BF16, "qT")
            ksT = k.sb(st, [128, 2, S], BF16, "ksT")
            kwT = k.sb(st, [128, 2, S], BF16, "kwT")
            kcT = k.sb(st, [128, 2, 128], BF16, "kcT")
            vsa = k.sb(st, [128, 16, 2, 130], BF16, "vsa")
            vwa = k.sb(st, [128, 16, 2, 130], BF16, "vwa")
            vca = k.sb(st, [128, 2, 162], BF16, "vca")
            gts = k.sb(st, [128, 16, 24], F32, "gts")
            maskc = k.sb(st, [128, S], F32, "maskc")
            cm = k.sb(st, [128, 4, 512], F32, "cm")
            wm = k.sb(st, [128, 8, 512], F32, "wm")
            ex32 = k.sb(st, [32, 16, 128], F32, "ex32")
            ex16 = k.sb(st, [32, 16, 128], BF16, "ex16")
            vm = k.sb(st, [128, 16, 32], F32, "vm")
            fb = k.sb(st, [128, 16, 32], F32, "fb")
            gab = k.sb(st, [128, 1024], F32, "gab")
            for j in range(8):
                k.dma(qT[:, j, :], sc["qT"][0][j], [sc["qT"][1]], [qT], eng=("sp" if j % 2 else "pool"))
            for g in range(2):
                k.dma(ksT[:, g, :], sc["ksT"][0][g], [sc["ksT"][1]], [ksT])
                k.dma(kwT[:, g, :], sc["kwT"][0][g], [sc["kwT"][1]], [kwT])
            k.dma(kcT[:], sc["kcmpT"][0], [sc["kcmpT"][1]], [kcT])
            k.dma(vca[:], sc["vcmp"][0], [sc["vcmp"][1]], [vca])
            k.memset(vsa[:], 1.0, [vsa])
            k.memset(vwa[:], 1.0, [vwa])
            for g in range(2):
                k.dma(vsa[:, :, g, 0:128], sc["vs"][0].rearrange("(c p) x -> p c x", p=128)[:, :, g * 128:(g + 1) * 128], [sc["vs"][1]], [vsa])
                k.dma(vwa[:, :, g, 0:128], sc["vw"][0].rearrange("(c p) x -> p c x", p=128)[:, :, g * 128:(g + 1) * 128], [sc["vw"][1]], [vwa])
            k.dma(gts[:].rearrange("p t c -> p (t c)"), sc["gates"][0], [sc["gates"][1]], [gts])
            k.dma(maskc[:], I["maskc"], [], [maskc])
            k.dma(cm[:].rearrange("p a b -> p (a b)"), I["cm"], [], [cm])
            k.dma(wm[:].rearrange("p a b -> p (a b)"), I["wm"], [], [wm])
            k.dma(ex32[:].rearrange("p a b -> p (a b)"), I["ex"], [], [ex32])
            k.copy(ex16[:], ex32[:], [ex32], [ex16])
            k.dma(vm[:].rearrange("p a b -> p (a b)"), I["vm"], [], [vm])
            k.dma(fb[:].rearrange("p a b -> p (a b)"), I["fb"], [], [fb])
            k.dma(gab[:], I["gab"], [], [gab])
            O = k.sb(st, [128, 4, 1024], F32, "O")
            Osub = [Res() for _ in range(4)]
            imp = [k.sb(st, [128, 32], F32, f"imp{i}") for i in range(4)]
            e16 = [k.sb(st, [128, 512], BF16, f"e16{i}") for i in range(3)]
            p16 = [k.sb(st, [128, 512], BF16, f"p16{i}") for i in range(3)]
            mskS = k.sb(st, [128, 16, 512], BF16, "mskS")
            selT16 = k.sb(st, [32, 512], BF16, "selT16")
            sel16 = [k.sb(st, [128, 32], BF16, f"sel16{i}") for i in range(2)]
            imp2 = [k.sb(st, [128, 32], F32, f"imp2{i}") for i in range(2)]
            wk = [k.sb(st, [128, 32], F32, f"wk{i}") for i in range(2)]
            m8 = [k.sb(st, [128, 16], F32, f"m8{i}") for i in range(2)]
            den = [k.sb(st, [128, 2], F32, f"den{i}") for i in range(4)]
            ssq = k.sb(st, [128, 1], F32, "ssq")
            junk = k.sb(st, [128, 1024], F32, "junk4")
            yn16 = k.sb(st, [128, 1024], BF16, "yn16")
            yT16 = [k.sb(st, [128, 512], BF16, f"yT16{i}") for i in range(2)]
            pS = [k.ps(st, [128, 512], F32, "pS") for _ in range(2)]
            pA = k.ps(st, [128, 512], F32, "pA")
            pACC = [k.ps(st, [128, 512], F32, "pACC") for _ in range(4)]
            pM = [k.ps(st, [128, 512], F32, "pM")] * 2
            pT = pA
            cnt = {"s": 0, "e": 0, "m": 0, "d": 0, "y": 0, "sel": 0}

            def finish_head(sub, hd, acc_ap, den_ap, tt_, gcol, first, Rp):
                dn = den[cnt["d"] % 4]
                cnt["d"] += 1
                k.ts(dn[:, 0:1], den_ap, 1e-30, None, ALU.max, None, Rp, [dn])
                k.recip(dn[:, 0:1], dn[:, 0:1], [dn], [dn])
                k.tt(dn[:, 1:2], dn[:, 0:1], gts[:, tt_, gcol:gcol + 1], ALU.mult, [dn, gts], [dn])
                osl = O[:, sub, hd * 128:(hd + 1) * 128]
                if first:
                    k.ts(osl, acc_ap, dn[:, 1:2], None, ALU.mult, None, Rp + [dn], [Osub[sub]])
                else:
                    k.stt(osl, acc_ap, dn[:, 1:2], osl, ALU.mult, ALU.add, Rp + [dn, Osub[sub]], [Osub[sub]])
                return dn

            def run_steps(i, steps):
                qsl = slice(i * 512, (i + 1) * 512)
                state = {}

                def s0(n):
                    keyT, vaug, g, r, kc, mask_of, first, last, gbranch = steps[n]
                    ps_ = pS[cnt["s"] % 2]
                    cnt["s"] += 1
                    k.mm(ps_[:], keyT[:, g, kc * 128:(kc + 1) * 128], qT[:, g * 4 + r, qsl], True, True, [keyT, qT], [ps_])
                    state[n] = ps_

                def s12(n):
                    keyT, vaug, g, r, kc, mask_of, first, last, gbranch = steps[n]
                    ps_ = state.pop(n)
                    hd = g * 4 + r
                    e = e16[cnt["e"] % 3]
                    p_ = p16[cnt["e"] % 3]
                    cnt["e"] += 1
                    k.act(e[:], ps_[:], AF.Exp, [ps_], [e], scale=SCALE)
                    mk, mkR, eng = mask_of(kc)
                    k.tt(p_[:], e[:], mk, ALU.mult, [e, mkR], [p_], eng=eng)
                    for sub in range(4):
                        acc = pACC[sub]
                        k.mm(acc[:, 0:129], p_[:, sub * 128:(sub + 1) * 128], vaug[:, kc, g, 0:129], first, last, [p_, vaug], [acc])
                    if last:
                        gcol = g * 12 + r * 3 + gbranch
                        dns = [den[sub] for sub in range(4)]
                        for sub in range(4):
                            k.ts(dns[sub][:, 0:1], pACC[sub][:, 128:129], 1e-30, None, ALU.max, None, [pACC[sub]], [dns[sub]])
                        for sub in range(4):
                            k.recip(dns[sub][:, 0:1], dns[sub][:, 0:1], [dns[sub]], [dns[sub]])
                        for sub in range(4):
                            k.tt(dns[sub][:, 1:2], dns[sub][:, 0:1], gts[:, i * 4 + sub, gcol:gcol + 1], ALU.mult, [dns[sub], gts], [dns[sub]])
                        for sub in range(4):
                            osl = O[:, sub, hd * 128:(hd + 1) * 128]
                            k.stt(osl, pACC[sub][:, 0:128], dns[sub][:, 1:2], osl, ALU.mult, ALU.add, [pACC[sub], dns[sub], Osub[sub]], [Osub[sub]])

                s0(0)
                for n in range(len(steps)):
                    if n + 1 < len(steps):
                        s0(n + 1)
                    s12(n)

            for i in range(4):
                qsl = slice(i * 512, (i + 1) * 512)
                for g in range(2):
                    for r in range(4):
                        hd = g * 4 + r
                        ps_ = pS[cnt["s"] % 2]
                        cnt["s"] += 1
                        k.mm(ps_[0:127, :], kcT[:, g, 0:127], qT[:, hd, qsl], True, True, [kcT, qT], [ps_])
                        e = e16[cnt["e"] % 3]
                        p_ = p16[cnt["e"] % 3]
                        cnt["e"] += 1
                        k.act(e[0:127, :], ps_[0:127, :], AF.Exp, [ps_], [e], scale=SCALE)
                        k.tt(p_[0:127, :], e[0:127, :], maskc[0:127, qsl], ALU.mult, [e, maskc], [p_])
                        for sub in range(4):
                            k.mm(pA[:, 0:161], p_[0:127, sub * 128:(sub + 1) * 128], vca[0:127, g, 0:161], True, True, [p_, vca], [pA])
                            dn = finish_head(sub, hd, pA[:, 0:128], pA[:, 128:129], i * 4 + sub, g * 12 + r * 3, True, [pA])
                            if r == 0:
                                k.ts(imp[sub][:], pA[:, 129:161], dn[:, 0:1], None, ALU.mult, None, [pA, dn], [imp[sub]])
                            else:
                                k.stt(imp[sub][:], pA[:, 129:161], dn[:, 0:1], imp[sub][:], ALU.mult, ALU.add, [pA, dn, imp[sub]], [imp[sub]])
                    psel = pM[cnt["m"] % 2]
                    cnt["m"] += 1
                    for sub in range(4):
                        tt_ = i * 4 + sub
                        j = cnt["sel"] % 2
                        cnt["sel"] += 1
                        k.tt(imp2[j][:], imp[sub][:], vm[:, tt_, :], ALU.mult, [imp[sub], vm], [imp2[j]])
                        k.tt(imp2[j][:], imp2[j][:], fb[:, tt_, :], ALU.add, [imp2[j], fb], [imp2[j]])
                        k.fn("dve", lambda e, o=m8[j][:, 0:8], a=imp2[j][:]: e.max(out=o, in_=a), [imp2[j]], [m8[j]])
                        k.fn("dve", lambda e, o=wk[j][:], a=m8[j][:, 0:8], b=imp2[j][:]: e.match_replace(out=o, in_to_replace=a, in_values=b, imm_value=-1e30), [imp2[j], m8[j]], [wk[j]])
                        k.fn("dve", lambda e, o=m8[j][:, 8:16], a=wk[j][:]: e.max(out=o, in_=a), [wk[j]], [m8[j]])
                        k.ts(sel16[j][:], imp2[j][:], m8[j][:, 15:16], None, ALU.is_ge, None, [imp2[j], m8[j]], [sel16[j]])
                        k.mm(psel[0:32, sub * 128:(sub + 1) * 128], sel16[j][:], self.ident16[:], True, True, [sel16[j], self.ident16], [psel])
                    k.copy(selT16[:], psel[0:32, :], [psel], [selT16], eng="act")
                    nkc = 4 * i + 4
                    for kc in range(nkc):
                        pm_ = pM[cnt["m"] % 2]
                        cnt["m"] += 1
                        k.mm(pm_[:], ex16[:, kc, :], selT16[:], True, True, [ex16, selT16], [pm_])
                        if kc >= 4 * i:
                            k.tt(mskS[:, kc, :], pm_[:], cm[:, kc - 4 * i, :], ALU.mult, [pm_, cm], [mskS])
                        else:
                            k.copy(mskS[:, kc, :], pm_[:], [pm_], [mskS], eng="act")
                    steps = []
                    for r in range(4):
                        for kc in range(nkc):
                            steps.append((ksT, vsa, g, r, kc, (lambda kc_: (mskS[:, kc_, :], mskS, "pool")), kc == 0, kc == nkc - 1, 1))
                    for r in range(4):
                        chunks = list(range(max(0, 4 * i - 4), 4 * i + 4))
                        for kc in chunks:
                            steps.append((kwT, vwa, g, r, kc, (lambda kc_, i_=i: (wm[:, kc_ - 4 * i_ + 4, :], wm, "dve")), kc == chunks[0], kc == chunks[-1], 2))
                    run_steps(i, steps)
                yt = yT16[i % 2]
                for c in range(8):
                    pass
                ytiles = []
                for sub in range(4):
                    k.act(junk[:], O[:, sub, :], AF.Square, [Osub[sub]], [junk, ssq], accum_out=ssq[:])
                    k.act(ssq[:], ssq[:], AF.Sqrt, [ssq, self.eps_t], [ssq], bias=self.eps_t[:], scale=1.0 / 1024.0)
                    k.recip(ssq[:], ssq[:], [ssq], [ssq])
                    k.stt(yn16[:], O[:, sub, :], ssq[:], gab[:], ALU.mult, ALU.mult, [Osub[sub], ssq, gab], [yn16])
                    for half in range(2):
                        for cc in range(4):
                            c = half * 4 + cc
                            k.mm(pT[:, cc * 128:(cc + 1) * 128], yn16[:, c * 128:(c + 1) * 128], self.ident16[:], True, True, [yn16, self.ident16], [pT])
                        dst = k.sb(st, [128, 4, 128], BF16, "ytmp") if False else None
                        yb = yT16[cnt["y"] % 2]
                        cnt["y"] += 1
                        k.copy(yb[:], pT[:], [pT], [yb], eng=("act" if half == 0 else "dve"))
                        for cc in range(4):
                            c = half * 4 + cc
                            k.dma(yT_d[c][:, i * 512 + sub * 128:i * 512 + (sub + 1) * 128], yb[:, cc * 128:(cc + 1) * 128], [yb], [yT_r])
            if "O_dbg" in self.dbg:
                pass

    def p5_rnn(self):
        k, I = self.k, self.I
        sc = self.scr
        yT_d, yT_r = sc["yT"]
        xr_d, xr_r = sc["xrT"]
        xg_d, xg_r = sc["xgT"]
        with ExitStack() as st:
            cw = k.sb(st, [128, 8, 4], F32, "cw")
            cb = k.sb(st, [128, 8], F32, "cb")
            nba = k.sb(st, [128, 8], F32, "nba")
            nbi = k.sb(st, [128, 8], F32, "nbi")
            lam = k.sb(st, [128, 8], F32, "lam")
            clam = k.sb(st, [128, 8], F32, "clam")
            gr = k.sb(st, [128, 8], F32, "gr")
            wst = k.sb(st, [128, 2, 128], F32, "wst5")
            w16 = [k.sb(st, [128, 2, 128], BF16, f"w165{i}") for i in range(2)]
            k.dma(cw[:].rearrange("p a b -> p (a b)"), I["convw"], [], [cw])
            k.dma(cb[:], I["convb"], [], [cb])
            k.dma(nba[:], I["lba"], [], [nba])
            k.dma(nbi[:], I["lbi"], [], [nbi])
            k.dma(lam[:], I["lam"], [], [lam])
            k.dma(gr[:], I["gr"], [], [gr])
            k.ts(nba[:], nba[:], -1.0, None, ALU.mult, None, [nba], [nba])
            k.ts(nbi[:], nbi[:], -1.0, None, ALU.mult, None, [nbi], [nbi])
            k.act(clam[:], lam[:], AF.Exp, [lam], [clam], scale=-1.0)
            k.ts(clam[:], clam[:], 1.0, None, ALU.add, None, [clam], [clam])
            k.act(clam[:], clam[:], AF.Ln, [clam], [clam])
            k.ts(clam[:], clam[:], -8.0, None, ALU.mult, None, [clam], [clam])
            orn = k.sb(st, [128, 8, S], F32, "orn")
            xp = k.sb(st, [128, S + 4], F32, "xp")
            xg = k.sb(st, [128, S], F32, "xg")
            u = k.sb(st, [128, S], F32, "u")
            u16 = k.sb(st, [128, S], BF16, "u16")
            ra = k.sb(st, [128, S], F32, "ra")
            ig = k.sb(st, [128, S], F32, "ig")
            bb = k.sb(st, [128, S], F32, "bb")
            sq16 = k.sb(st, [128, S], BF16, "sq165")
            pg = [k.ps(st, [128, 512], F32, "pg") for _ in range(2)]
            pss = [k.ps(st, [128, 512], F32, "pss5") for _ in range(4)]
            k.memset(xp[:, 0:4], 0.0, [xp])
            ci = 0
            for n in range(8):
                k.dma(xp[:, 4:S + 4], xr_d[n], [xr_r], [xp])
                k.dma(xg[:], xg_d[n], [xg_r], [xg], eng="pool")
                wb = w16[n % 2]
                k.dma(wst[:, 0, :], I["wa"][n], [], [wst])
                k.dma(wst[:, 1, :], I["wi"][n], [], [wst])
                k.copy(wb[:], wst[:], [wst], [wb], eng="pool")
                k.ts(u[:], xp[:, 1:S + 1], cw[:, n, 0:1], cb[:, n:n + 1], ALU.mult, ALU.add, [xp, cw, cb], [u])
                for i_ in range(1, 4):
                    k.stt(u[:], xp[:, 1 + i_:S + 1 + i_], cw[:, n, i_:i_ + 1], u[:], ALU.mult, ALU.add, [xp, cw, u], [u])
                k.copy(u16[:], u[:], [u], [u16], eng="act")
                for which, dst, nb in ((0, ra, nba), (1, ig, nbi)):
                    for tg in range(4):
                        p = pg[ci % 2]
                        ci += 1
                        sl = slice(tg * 512, (tg + 1) * 512)
                        k.mm(p[:], wb[:, which, :], u16[:, sl], True, True, [wb, u16], [p])
                        k.act(dst[:, sl], p[:], AF.Exp, [p, nb], [dst], bias=nb[:, n:n + 1], scale=-1.0)
                    k.ts(dst[:], dst[:], 1.0, None, ALU.add, None, [dst], [dst], eng="pool")
                    k.recip(dst[:], dst[:], [dst], [dst])
                k.act(ra[:], ra[:], AF.Exp, [ra, clam], [ra], scale=clam[:, n:n + 1])
                k.tt(bb[:], ra[:], ra[:], ALU.mult, [ra], [bb])
                k.ts(bb[:], bb[:], -1.0, 1.0, ALU.mult, ALU.add, [bb], [bb])
                k.act(bb[:], bb[:], AF.Sqrt, [bb], [bb])
                k.tt(bb[:], bb[:], ig[:], ALU.mult, [bb, ig], [bb], eng="pool")
                k.tt(bb[:], bb[:], u[:], ALU.mult, [bb, u], [bb])
                k.fn("dve", lambda e, o=ig[:], a=ra[:], b=bb[:]: e.tensor_tensor_scan(o, a, b, 0.0, ALU.mult, ALU.add), [ra, bb, ig], [ig])
                k.act(xg[:], xg[:], AF.Gelu, [xg], [xg])
                k.tt(orn[:, n, :], xg[:], ig[:], ALU.mult, [xg, ig], [orn])
                k.act(sq16[:], orn[:, n, :], AF.Square, [orn], [sq16])
                for tg in range(4):
                    k.mm(pss[tg][:], self.ones16[:], sq16[:, tg * 512:(tg + 1) * 512], n == 0, n == 7, [self.ones16, sq16], [pss[tg]])
            rstd = ra
            for tg in range(4):
                sl = slice(tg * 512, (tg + 1) * 512)
                k.act(rstd[:, sl], pss[tg][:], AF.Sqrt, [pss[tg], self.eps_t], [rstd], bias=self.eps_t[:], scale=1.0 / 1024.0)
            k.recip(rstd[:], rstd[:], [rstd], [rstd])
            for n in range(8):
                k.stt(u16[:], orn[:, n, :], gr[:, n:n + 1], rstd[:], ALU.mult, ALU.mult, [orn, gr, rstd], [u16])
                k.dma(yT_d[8 + n], u16[:], [u16], [yT_r])

    def p6_out(self):
        k, I = self.k, self.I
        sc = self.scr
        yT_d, yT_r = sc["yT"]
        x1_d, x1_r = self.scratch("x1T", [16, 128, S], F32)
        h2_d, h2_r = self.scratch("h2T", [16, 128, S], BF16)
        xTv = I["xT"].rearrange("(k p) t -> k p t", p=128)
        wv = I["w_out"].rearrange("(k p) n -> p k n", p=128)
        with ExitStack() as st:
            rstd = k.sb(st, [128, S], F32, "rstd2")
            with ExitStack() as s1:
                yT = k.sb(s1, [128, 16, S], BF16, "yTs")
                wb = [k.sb(s1, [128, 16, 256], BF16, f"wo{i}") for i in range(2)]
                stg = [k.sb(s1, [128, 16, 128], F32, f"wos{i}") for i in range(2)]
                xb = [k.sb(s1, [128, S], F32, f"x6{i}") for i in range(2)]
                ob = [k.sb(s1, [128, S], F32, f"o6{i}") for i in range(2)]
                sq = [k.sb(s1, [128, S], BF16, f"sq6{i}") for i in range(2)]
                pz = [k.ps(s1, [128, 512], F32, "pz6") for _ in range(2)]
                pss = [k.ps(s1, [128, 512], F32, "pss6") for _ in range(4)]
                for c in range(16):
                    k.dma(yT[:, c, :], yT_d[c], [yT_r], [yT], eng=("sp" if c % 2 else "pool"))
                ci = 0
                for jg in range(8):
                    b = wb[jg % 2]
                    for h in range(2):
                        sg = stg[(jg * 2 + h) % 2]
                        k.dma(sg[:], wv[:, :, jg * 256 + h * 128:jg * 256 + (h + 1) * 128], [], [sg])
                        k.copy(b[:, :, h * 128:(h + 1) * 128], sg[:], [sg], [b], eng="pool")
                    for jj in range(2):
                        j = jg * 2 + jj
                        xt, ot, sqt = xb[j % 2], ob[j % 2], sq[j % 2]
                        k.dma(xt[:], xTv[j], [], [xt])
                        for tg in range(4):
                            p = pz[ci % 2]
                            ci += 1
                            sl = slice(tg * 512, (tg + 1) * 512)
                            for c in range(16):
                                k.mm(p[:], b[:, c, jj * 128:(jj + 1) * 128], yT[:, c, sl], c == 0, c == 15, [b, yT], [p])
                            k.stt(ot[:, sl], p[:], self.modT[:, 32 + j:33 + j], xt[:, sl], ALU.mult, ALU.add, [p, self.modT, xt], [ot])
                        k.dma(x1_d[j], ot[:], [ot], [x1_r])
                        k.act(sqt[:], ot[:], AF.Square, [ot], [sqt])
                        for tg in range(4):
                            k.mm(pss[tg][:], self.ones16[:], sqt[:, tg * 512:(tg + 1) * 512], j == 0, j == 15, [self.ones16, sqt], [pss[tg]])
                for tg in range(4):
                    sl = slice(tg * 512, (tg + 1) * 512)
                    k.act(rstd[:, sl], pss[tg][:], AF.Sqrt, [pss[tg], self.eps_t], [rstd], bias=self.eps_t[:], scale=1.0 / float(D))
                k.recip(rstd[:], rstd[:], [rstd], [rstd])
            k.barrier()
            with ExitStack() as s2:
                xb = [k.sb(s2, [128, S], F32, f"x6b{i}") for i in range(2)]
                tp = [k.sb(s2, [128, S], F32, f"t6b{i}") for i in range(2)]
                hb = [k.sb(s2, [128, S], BF16, f"h6b{i}") for i in range(2)]
                for j in range(16):
                    xt, tt_, ht = xb[j % 2], tp[j % 2], hb[j % 2]
                    k.dma(xt[:], x1_d[j], [x1_r], [xt])
                    k.stt(tt_[:], xt[:], self.G2[:, j:j + 1], rstd[:], ALU.mult, ALU.mult, [xt, self.G2, rstd], [tt_])
                    k.act(ht[:], tt_[:], AF.Identity, [tt_, self.modT], [ht], bias=self.modT[:, 48 + j:49 + j], scale=1.0)
                    k.dma(h2_d[j], ht[:], [ht], [h2_r])

    def p7_peer(self):
        k, I = self.k, self.I
        sc = self.scr
        h2_d, h2_r = sc["h2T"]
        x1_d, x1_r = sc["x1T"]
        qp_d, qp_r = self.scratch("qpT", [16, 128, S], BF16)
        wv = I["wq"].rearrange("(k p) n -> p k n", p=128)
        with ExitStack() as st:
            h2T = k.sb(st, [128, 16, S], BF16, "h2Ts")
            wb = [k.sb(st, [128, 16, 256], BF16, f"wq{i}") for i in range(2)]
            stg = [k.sb(st, [128, 16, 128], F32, f"wqs{i}") for i in range(2)]
            ob = [k.sb(st, [128, S], BF16, f"oq{i}") for i in range(2)]
            pz = [k.ps(st, [128, 512], F32, "pz7") for _ in range(2)]
            for c in range(16):
                k.dma(h2T[:, c, :], h2_d[c], [h2_r], [h2T], eng=("sp" if c % 2 else "pool"))
            ci = 0
            for jg in range(8):
                b = wb[jg % 2]
                for h in range(2):
                    sg = stg[(jg * 2 + h) % 2]
                    k.dma(sg[:], wv[:, :, jg * 256 + h * 128:jg * 256 + (h + 1) * 128], [], [sg])
                    k.copy(b[:, :, h * 128:(h + 1) * 128], sg[:], [sg], [b], eng="pool")
                for jj in range(2):
                    j = jg * 2 + jj
                    ot = ob[j % 2]
                    for tg in range(4):
                        p = pz[ci % 2]
                        ci += 1
                        sl = slice(tg * 512, (tg + 1) * 512)
                        for c in range(16):
                            k.mm(p[:], b[:, c, jj * 128:(jj + 1) * 128], h2T[:, c, sl], c == 0, c == 15, [b, h2T], [p])
                        k.copy(ot[:, sl], p[:], [p], [ot], eng=("act" if tg % 2 == 0 else "dve"))
                    k.dma(qp_d[j], ot[:], [ot], [qp_r])
        k.barrier()
        SLACK = 1.0 - 4e-6
        with ExitStack() as st:
            h2s = k.sb(st, [128, 16, 512], BF16, "h2s")
            E1s = k.sb(st, [128, 4, 8, 128], F32, "E1s")
            E2s = k.sb(st, [128, 4, 8, 128], F32, "E2s")
            dg16 = k.sb(st, [128, 4, 8, 128], BF16, "dg16")
            accT = k.sb(st, [128, 16, 512], F32, "accT")
            keys16 = k.sb(st, [128, 16, 128], BF16, "keys16")
            with ExitStack() as s0:
                k32 = k.sb(s0, [128, 16, 128], F32, "k32")
                k.dma(k32[:].rearrange("p a b -> p (a b)"), I["keysT"], [], [k32])
                k.copy(keys16[:], k32[:], [k32], [keys16])
            k.barrier()
            for su in range(4):
                tsl = slice(su * 512, (su + 1) * 512)
                k.dma(h2s[:], h2_d.rearrange("c p t -> p c t")[:, :, tsl], [h2_r], [h2s])
                with ExitStack() as sb_:
                    qps = k.sb(sb_, [128, 16, 512], BF16, "qps")
                    s_sb = k.sb(sb_, [128, 16, 128], F32, "s_sb")
                    wk = k.sb(sb_, [128, 16, 128], F32, "wk7")
                    wk2 = k.sb(sb_, [128, 8, 256], F32, "wk72")
                    v16R = [Res() for _ in range(16)]
                    wkR = [Res() for _ in range(16)]
                    c16R = [Res() for _ in range(8)]
                    wk2R = [Res() for _ in range(8)]
                    v16 = k.sb(sb_, [128, 16, 16], F32, "v16")
                    cand = k.sb(sb_, [128, 8, 256], F32, "cand")
                    c16 = k.sb(sb_, [128, 8, 16], F32, "c16")
                    en = k.sb(sb_, [128, 8, 16], F32, "en")
                    negm = k.sb(sb_, [128, 16], F32, "negm")
                    negM = k.sb(sb_, [128, 8], F32, "negM")
                    Z = k.sb(sb_, [128, 8], F32, "Z")
                    th = k.sb(sb_, [128, 8], F32, "th")
                    cf = k.sb(sb_, [128, 8], F32, "cf")
                    E1t = k.sb(sb_, [128, 8, 128], F32, "E1t")
                    pS_ = [k.ps(sb_, [128, 4, 128], F32, "pS7") for _ in range(2)]
                    k.dma(qps[:], qp_d.rearrange("c p t -> p c t")[:, :, tsl], [qp_r], [qps])
                    for tl in range(4):
                        for hg in range(4):
                            p = pS_[hg % 2]
                            for q4 in range(4):
                                hp = hg * 4 + q4
                                k.mm(p[:, q4, :], qps[:, hp, tl * 128:(tl + 1) * 128], keys16[:, hp, :], True, True, [qps, keys16], [p])
                            k.copy(s_sb[:, hg * 4:(hg + 1) * 4, :], p[:], [p], [s_sb], eng=("act" if hg % 2 == 0 else "dve"))
                        for hp in range(16):
                            k.fn("dve", lambda e, o=v16[:, hp, 0:8], a=s_sb[:, hp, :]: e.max(out=o, in_=a), [s_sb], [v16R[hp]])
                        for hp in range(16):
                            k.fn("dve", lambda e, o=wk[:, hp, :], a=v16[:, hp, 0:8], b=s_sb[:, hp, :]: e.match_replace(out=o, in_to_replace=a, in_values=b, imm_value=-1e30), [s_sb, v16R[hp]], [wkR[hp]])
                        for hp in range(16):
                            k.fn("dve", lambda e, o=v16[:, hp, 8:16], a=wk[:, hp, :]: e.max(out=o, in_=a), [wkR[hp]], [v16R[hp]])
                        v16r = v16[:].rearrange("p (h two) i -> p h two i", two=2)
                        k.tt(cand[:].rearrange("p h (i j) -> p h i j", i=16),
                             v16r[:, :, 0, :].unsqueeze(3).to_broadcast([128, 8, 16, 16]),
                             v16r[:, :, 1, :].unsqueeze(2).to_broadcast([128, 8, 16, 16]), ALU.add, v16R, [cand])
                        for h in range(8):
                            k.fn("dve", lambda e, o=c16[:, h, 0:8], a=cand[:, h, :]: e.max(out=o, in_=a), [cand], [c16R[h]])
                        for h in range(8):
                            k.fn("dve", lambda e, o=wk2[:, h, :], a=c16[:, h, 0:8], b=cand[:, h, :]: e.match_replace(out=o, in_to_replace=a, in_values=b, imm_value=-1e30), [cand, c16R[h]], [wk2R[h]])
                        for h in range(8):
                            k.fn("dve", lambda e, o=c16[:, h, 8:16], a=wk2[:, h, :]: e.max(out=o, in_=a), [wk2R[h]], [c16R[h]])
                        k.ts(negm[:], v16[:, :, 0], -1.0, None, ALU.mult, None, v16R, [negm])
                        k.ts(negM[:], c16[:, :, 0], -1.0, None, ALU.mult, None, c16R, [negM])
                        for h in range(8):
                            k.act(en[:, h, :], c16[:, h, :], AF.Exp, [c16R[h], negM], [en, Z], bias=negM[:, h:h + 1], scale=1.0, accum_out=Z[:, h:h + 1])
                        k.recip(Z[:], Z[:], [Z], [Z])
                        k.tt(th[:], en[:, :, 15], Z[:], ALU.mult, [en, Z], [th])
                        k.ts(th[:], th[:], SLACK, None, ALU.mult, None, [th], [th])
                        k.ts(cf[:], en[:, :, 15], SLACK, None, ALU.mult, None, [en], [cf])
                        k.recip(cf[:], cf[:], [cf], [cf])
                        for h in range(8):
                            k.act(E2s[:, tl, h, :], s_sb[:, 2 * h + 1, :], AF.Exp, [s_sb, negm], [E2s], bias=negm[:, 2 * h + 1:2 * h + 2], scale=1.0)
                            k.act(E1t[:, h, :], s_sb[:, 2 * h, :], AF.Exp, [s_sb, negm], [E1t], bias=negm[:, 2 * h:2 * h + 1], scale=1.0)
                            k.ts(dg16[:, tl, h, :], self.ident32[:], th[:, h:h + 1], None, ALU.mult, None, [self.ident32, th], [dg16], eng="pool")
                        k.tt(E1s[:, tl, :, :], E1t[:], cf[:].unsqueeze(2).to_broadcast([128, 8, 128]), ALU.mult, [E1t, cf], [E1s])
                k.barrier()
                if "peer_dbg" in self.dbg and su == 0:
                    for nm, tile_ in (("E1s", E1s), ("E2s", E2s)):
                        d, r = self.scratch(nm, [128, 4 * 8 * 128], F32)
                        k.dma(d, tile_[:].rearrange("p a b c -> p (a b c)"), [tile_], [r])
                with ExitStack() as sc_:
                    stgD = [k.sb(sc_, [128, 16, 128], F32, f"stgD{i}") for i in range(2)]
                    stgU = [k.sb(sc_, [128, 1024], F32, f"stgU{i}") for i in range(2)]
                    dn16 = [k.sb(sc_, [128, 16, 128], BF16, f"dn16{i}") for i in range(4)]
                    up16 = [[k.sb(sc_, [128, 2048], BF16, f"up16{j}_{i}") for i in range(4)] for j in range(2)]
                    GA16 = [k.sb(sc_, [128, 512], BF16, f"GA{i}") for i in range(2)]
                    Pt = [k.sb(sc_, [128, 8, 128], F32, f"Pt{i}") for i in range(4)]
                    mE = [k.sb(sc_, [128, 8, 128], BF16, f"mE{i}") for i in range(4)]
                    WA = [k.sb(sc_, [128, 512], BF16, f"WA{i}") for i in range(8)]
                    ev = [k.sb(sc_, [128, 512], F32, f"ev{i}") for i in range(2)]
                    pact = [k.ps(sc_, [128, 512], F32, "pact") for _ in range(2)]
                    pw = [k.ps(sc_, [128, 512], F32, "pw") for _ in range(2)]
                    po = [k.ps(sc_, [128, 512], F32, "po") for _ in range(2)]
                    cn = {"k": 0, "p": 0, "o": 0}
                    ngrp = 1 if "peer_short" in self.dbg else 32
                    pend_wa = []
                    pend_po = []

                    def flush_wa():
                        while pend_wa:
                            wa_, pw__, ga_ = pend_wa.pop(0)
                            k.tt(wa_[:], pw__[:], ga_[:], ALU.mult, [pw__, ga_], [wa_])

                    def drain_po(ndc):
                        while pend_po and ndc != 0:
                            gi_, was_, dcs = pend_po[0]
                            dc = dcs.pop(0)
                            po_ = po[cn["o"] % 2]
                            cn["o"] += 1
                            for kl in range(4):
                                k.mm(po_[:], up16[gi_ % 2][kl][:, dc * 128:(dc + 1) * 128], was_[kl][:], kl == 0, kl == 3, [up16[gi_ % 2][kl], was_[kl]], [po_])
                            if gi_ == 0:
                                k.copy(accT[:, dc, :], po_[:], [po_], [accR[dc]], eng="act")
                            else:
                                pend_add.append((dc, po_))
                            if not dcs:
                                pend_po.pop(0)
                            ndc -= 1
                            if len(pend_add) >= 2 and ndc != 0:
                                flush_add()

                    accR = [Res() for _ in range(16)]
                    pend_add = []

                    def flush_add():
                        while pend_add:
                            dc, po_ = pend_add.pop(0)
                            k.tt(accT[:, dc, :], accT[:, dc, :], po_[:], ALU.add, [accR[dc], po_], [accR[dc]])
                    nk = ngrp * 4
                    seq = [(kap, tl) for kap in range(nk) for tl in range(4)]
                    LA = 2

                    def load_dn(kap2):
                        kl2 = kap2 % 4
                        sd = stgD[kap2 % 2]
                        k.dma(sd[:].rearrange("p a b -> p (a b)"), I["downB"][kap2 * 128:(kap2 + 1) * 128, :], [], [sd], eng="sp")
                        k.copy(dn16[kl2][:], sd[:], [sd], [dn16[kl2]], eng="act")

                    def load_up(kap2):
                        gi2, kl2 = kap2 // 4, kap2 % 4
                        up = up16[gi2 % 2][kl2]
                        for hf in range(2):
                            su_ = stgU[(kap2 * 2 + hf) % 2]
                            k.dma(su_[:], I["up"][kap2 * 128:(kap2 + 1) * 128, hf * 1024:(hf + 1) * 1024], [], [su_], eng="sp")
                            k.copy(up[:, hf * 1024:(hf + 1) * 1024], su_[:], [su_], [up], eng="act")

                    def p1(n):
                        kap, tl = seq[n]
                        k.tt(Pt[n % 4][:], E2s[:, tl, :, :], E1s[:, tl, :, kap:kap + 1].to_broadcast([128, 8, 128]), ALU.mult, [E2s, E1s], [Pt[n % 4]])

                    for n in range(min(LA, len(seq))):
                        p1(n)
                    cur = {}
                    for n, (kap, tl) in enumerate(seq):
                        gi, kl = kap // 4, kap % 4
                        if tl == 0:
                            if kl == 0:
                                cur["was"] = []
                                if gi == 0:
                                    for kl2 in range(4):
                                        load_dn(kl2)
                            dn = dn16[kl]
                            pa_ = pact[cn["k"] % 2]
                            pw_ = pw[cn["k"] % 2]
                            ga = GA16[cn["k"] % 2]
                            wa = WA[cn["k"] % 8]
                            cn["k"] += 1
                            cur.update(pw=pw_, ga=ga, wa=wa)
                            for c in range(16):
                                k.mm(pa_[:], dn[:, c, :], h2s[:, c, :], c == 0, c == 15, [dn, h2s], [pa_])
                            k.act(ga[:], pa_[:], AF.Gelu, [pa_], [ga])
                            if kap + 4 < nk:
                                load_dn(kap + 4)
                            load_up(kap)
                        if n + LA < len(seq):
                            p1(n + LA)
                        me = mE[n % 4]
                        k.stt(me[:], Pt[n % 4][:], 1.0, Pt[n % 4][:], ALU.is_ge, ALU.mult, [Pt[n % 4]], [me])
                        flush_add()
                        if tl == 1:
                            flush_wa()
                        pw_ = cur["pw"]
                        for h in range(8):
                            k.mm(pw_[:, tl * 128:(tl + 1) * 128], me[:, h, :], dg16[:, tl, h, :], h == 0, h == 7, [me, dg16], [pw_])
                        if tl in (1, 3):
                            drain_po(2)
                        if tl == 3:
                            pend_wa.append((cur["wa"], pw_, cur["ga"]))
                            cur["was"].append(cur["wa"])
                            if kl == 3:
                                assert not pend_po
                                pend_po.append((gi, cur["was"], list(range(16))))
                    flush_add()
                    flush_wa()
                    drain_po(-1)
                    flush_add()
                    for dc in range(16):
                        x_ = ev[dc % 2]
                        k.dma(x_[:], x1_d[dc][:, tsl], [x1_r], [x_])
                        k.stt(x_[:], accT[:, dc, :], self.modT[:, 80 + dc:81 + dc], x_[:], ALU.mult, ALU.add, [accR[dc], self.modT, x_], [x_])
                        k.dma(self.outT[dc * 128:(dc + 1) * 128, tsl], x_[:], [x_], [])
                k.barrier()


def _consts():
    f = np.float32
    half = 64
    freqs = (10000.0 ** (-np.arange(half, dtype=f) / f(half))).astype(f)
    ang = np.arange(S, dtype=f)[:, None] * freqs[None, :]
    cos = np.cos(ang).astype(f).T
    sin = np.sin(ang).astype(f).T
    cosT = np.concatenate([cos, cos], 0)
    sinT = np.concatenate([sin, sin], 0)
    rotm = np.zeros((128, 128), f)
    for m in range(64):
        rotm[m + 64, m] = -1.0
        rotm[m, m + 64] = 1.0
    ident = np.eye(128, dtype=f)
    t = np.arange(S)
    cst = np.arange(NCMP) * 16
    maskc = np.zeros((128, S), f)
    maskc[:NCMP] = ((cst + 31)[:, None] <= t[None, :]).astype(f)
    kk = np.arange(128)[:, None]
    tt = np.arange(512)[None, :]
    cm = np.stack([(128 * o + kk <= tt).astype(f) for o in range(4)], 1).reshape(128, 4 * 512)
    wm = np.stack([(((128 * rel + kk - tt) <= 0) & ((128 * rel + kk - tt) > -512)).astype(f)
                   for rel in range(-4, 4)], 1).reshape(128, 8 * 512)
    ex = np.zeros((32, 16, 128), f)
    for kc in range(16):
        for kq in range(128):
            ex[2 * kc + kq // 64, kc, kq] = 1.0
    ex = ex.reshape(32, 16 * 128)
    jb = np.arange(32)[None, :]
    cur = (t // 64)[:, None]
    forced = (jb == 0) | (jb == cur) | (jb == cur - 1)
    valid = (jb * 64) <= t[:, None]
    vm_ = (valid & ~forced).astype(f)
    fb_ = np.where(forced, 1e4, np.where(valid, 0.0, -1e4)).astype(f)
    vm = vm_.reshape(16, 128, 32).transpose(1, 0, 2).reshape(128, 512)
    fb = fb_.reshape(16, 128, 32).transpose(1, 0, 2).reshape(128, 512)
    sst = np.arange(32) * 64
    ov = np.maximum(np.minimum(cst[:, None] + 32, sst[None, :] + 64) - np.maximum(cst[:, None], sst[None, :]), 0)
    ovl = np.zeros((128, 32), f)
    ovl[:NCMP] = ov.astype(f) / 32.0
    return dict(cosT=cosT, sinT=sinT, rotm=rotm, ident=ident, maskc=maskc, cm=cm, wm=wm, ex=ex, vm=vm, fb=fb, ovl=ovl)


def prep_shared(inp):
    f = np.float32
    A = lambda v: np.ascontiguousarray(np.asarray(v, dtype=f))
    sh = {}
    sh["ada_w"] = A(inp["ada_w"][0])
    sh["ada_bT"] = A(inp["ada_b"][0].reshape(96, 128).T)
    sh["g1T"] = A(inp["norm_mix_g"][0].reshape(16, 128).T)
    sh["g2T"] = A(inp["norm_ffn_g"][0].reshape(16, 128).T)
    sh["w_in"] = A(inp["w_in"][0])
    sh["w_out"] = A(inp["w_out"][0])
    sh["wq"] = A(inp["peer_wq"][0])
    sh["qg"] = A(inp["q_norm_g"][0].reshape(128, 1))
    sh["kgT"] = A(inp["k_norm_g"][0].T)
    sh["kg0b"] = A(np.broadcast_to(inp["k_norm_g"][0, 0][None, :], (128, 128)))
    sh["pekT"] = A(inp["cmp_pe_k"][0].T)
    sh["pevT"] = A(inp["cmp_pe_v"][0].T)
    sh["cwk"] = A(inp["cmp_w_k"][0])
    sh["cwv"] = A(inp["cmp_w_v"][0])
    sh["gateb"] = A(np.broadcast_to(inp["gate_b"][0][None, :], (128, 24)))
    sh["convw"] = A(inp["conv_w"][0].reshape(4, 8, 128).transpose(2, 1, 0).reshape(128, 32))
    for nm, key in (("convb", "conv_b"), ("lba", "lru_ba"), ("lbi", "lru_bi"), ("lam", "lru_lam"), ("gr", "out_g_rnn")):
        sh[nm] = A(inp[key][0].reshape(8, 128).T)
    sh["gab"] = A(np.broadcast_to(inp["out_g_attn"][0][None, :], (128, 1024)))
    sh["wa"] = A(inp["lru_wa"][0])
    sh["wi"] = A(inp["lru_wi"][0])
    sh["keysT"] = A(inp["peer_keys"][0].transpose(3, 0, 1, 2).reshape(128, 16 * 128))
    sh["downB"] = A(inp["peer_down"][0].reshape(128, 128, 16, 128).transpose(0, 3, 2, 1).reshape(128 * 128, 16 * 128))
    sh["up"] = A(inp["peer_up"][0])
    sh.update(_consts())
    return sh


def prep_core(inp, b):
    f = np.float32
    return {"xT": np.ascontiguousarray(np.asarray(inp["x"][b], dtype=f).T),
            "c_col": np.ascontiguousarray(np.asarray(inp["c"][b], dtype=f).reshape(16, 128).T)}


def kernel(**inputs):
    inp = {k_: np.asarray(v) for k_, v in inputs.items()}
    sh = prep_shared(inp)
    nc = Prog().build()
    in_maps = []
    for b in range(8):
        m = dict(sh)
        m.update(prep_core(inp, b))
        in_maps.append(m)
    res = run_bass_kernel_spmd(nc, in_maps, core_ids=list(range(8)))
    out = np.stack([np.asarray(r["outT"]).T for r in res.results], 0)
    return np.ascontiguousarray(out.astype(np.float32))
```

```python
import numpy as np
from contextlib import ExitStack
import concourse.bass as bass
import concourse.mybir as mybir
from concourse.bass_utils import run_bass_kernel_spmd

F32 = mybir.dt.float32
BF16 = mybir.dt.bfloat16
AF = mybir.ActivationFunctionType
ALU = mybir.AluOpType
AX = mybir.AxisListType

D = 2048
S = 2048
NT = 16
NIN = 4632
C_Q, C_KC, C_VC, C_KS, C_VS, C_KW, C_VW, C_GL, C_XR, C_XG = 0, 1024, 1280, 1536, 1792, 2048, 2304, 2560, 2584, 3608
EPS = 1e-6
NCMP = 127
SCALE = 128 ** -0.5


class Res:
    __slots__ = ("name", "w", "r")

    def __init__(self, name=""):
        self.name = name
        self.w = None
        self.r = []


class Op:
    __slots__ = ("eng", "fn", "deps", "flag", "cval", "dma", "dsem", "dval")

    def __init__(self, eng, fn, dma=False):
        self.eng = eng
        self.fn = fn
        self.deps = []
        self.flag = False
        self.cval = 0
        self.dma = dma
        self.dsem = None
        self.dval = 0


class Sched:
    ENGS = ("pe", "act", "dve", "pool", "sp")

    def __init__(self, nc, n_dma_sems=16):
        self.nc = nc
        self.q = {e: [] for e in self.ENGS}
        self.n_dma_sems = n_dma_sems
        self.dma_rr = 0
        self.dma_last = [None] * n_dma_sems
        self.dma_cnt = [0] * n_dma_sems
        self.pending = {e: [] for e in self.ENGS}

    def _add(self, eng, fn, reads, writes, dma=False):
        op = Op(eng, fn, dma)
        deps = list(self.pending[eng])
        self.pending[eng] = []
        for r in reads:
            if r.w is not None:
                deps.append(r.w)
        for w in writes:
            if w.w is not None:
                deps.append(w.w)
            deps.extend(w.r)
        for r in reads:
            r.r.append(op)
        for w in writes:
            w.w = op
            w.r = []
        if dma:
            s = self.dma_rr
            self.dma_rr = (self.dma_rr + 1) % self.n_dma_sems
            prev = self.dma_last[s]
            if prev is not None:
                deps.append(prev)
            self.dma_last[s] = op
            self.dma_cnt[s] += 1
            op.dsem = s
            op.dval = 16 * self.dma_cnt[s]
        seen = set()
        for d in deps:
            if d is op or id(d) in seen:
                continue
            if (not d.dma) and d.eng == eng and eng == "pe":
                continue
            seen.add(id(d))
            op.deps.append(d)
            d.flag = True
        self.q[eng].append(op)
        return op

    def op(self, eng, fn, reads=(), writes=()):
        return self._add(eng, fn, list(reads), list(writes))

    def dma(self, eng, out, in_, reads=(), writes=()):
        return self._add(eng, lambda e: e.dma_start(out=out, in_=in_), list(reads), list(writes), dma=True)

    def barrier(self):
        lasts = []
        for e in self.ENGS:
            for op in reversed(self.q[e]):
                if not op.dma:
                    lasts.append(op)
                    break
        for s in range(self.n_dma_sems):
            if self.dma_last[s] is not None:
                lasts.append(self.dma_last[s])
        for e in self.ENGS:
            self.pending[e] = list(lasts)

    def emit(self):
        nc = self.nc
        with ExitStack() as st:
            esem = {e: st.enter_context(nc.semaphore(f"s_{e}")) for e in self.ENGS}
            dsem = [st.enter_context(nc.semaphore(f"d_{i}")) for i in range(self.n_dma_sems)]
            for e in self.ENGS:
                c = 0
                for op in self.q[e]:
                    if op.dma:
                        continue
                    if op.flag:
                        c += 1
                        op.cval = c
            block = st.enter_context(nc.Block())

            def run(ename, eobj):
                waited = {}
                for op in self.q[ename]:
                    need = {}
                    for d in op.deps:
                        if d.dma:
                            key, val = ("d", d.dsem), d.dval
                        else:
                            key, val = ("e", d.eng), d.cval
                        if val > need.get(key, 0):
                            need[key] = val
                    for key, val in need.items():
                        if waited.get(key, 0) >= val:
                            continue
                        waited[key] = val
                        sem = dsem[key[1]] if key[0] == "d" else esem[key[1]]
                        eobj.wait_ge(sem, val)
                    ins = op.fn(eobj)
                    if op.dma:
                        ins.then_inc(dsem[op.dsem], 16)
                    elif op.flag:
                        ins.then_inc(esem[ename], 1)
                if ename == "sp":
                    for s in range(self.n_dma_sems):
                        if self.dma_cnt[s] > 0:
                            eobj.wait_ge(dsem[s], 16 * self.dma_cnt[s])

            block.tensor(lambda e: run("pe", e))
            block.scalar(lambda e: run("act", e))
            block.vector(lambda e: run("dve", e))
            block.gpsimd(lambda e: run("pool", e))
            block.sync(lambda e: run("sp", e))


class T:
    __slots__ = ("t", "r")

    def __init__(self, t, name=""):
        self.t = t
        self.r = Res(name)

    def __getitem__(self, k):
        return self.t[k]


class K:
    def __init__(self, nc):
        self.nc = nc
        self.S = Sched(nc)
        self.uid = 0

    def sb(self, st, shape, dt, name=None):
        self.uid += 1
        n = f"{name or 't'}_{self.uid}"
        return T(st.enter_context(self.nc.sbuf_tensor(n, list(shape), dt)), n)

    def ps(self, st, shape, dt=F32, name=None):
        self.uid += 1
        n = f"{name or 'p'}_{self.uid}"
        return T(st.enter_context(self.nc.psum_tensor(n, list(shape), dt)), n)

    @staticmethod
    def _rs(xs):
        return [x.r if isinstance(x, T) else x for x in xs]

    def mm(self, out, lhsT, rhs, start, stop, R, W):
        self.S.op("pe", lambda e: e.matmul(out, lhsT, rhs, start=start, stop=stop), self._rs(R), self._rs(W))

    def act(self, out, in_, func, R, W, bias=None, scale=None, accum_out=None, eng="act"):
        kw = {}
        if bias is not None:
            kw["bias"] = bias
        if scale is not None:
            kw["scale"] = scale
        if accum_out is not None:
            kw["accum_out"] = accum_out
        self.S.op(eng, lambda e: e.activation(out, in_, func, **kw), self._rs(R), self._rs(W))

    def tt(self, out, in0, in1, op, R, W, eng="dve"):
        self.S.op(eng, lambda e: e.tensor_tensor(out, in0, in1, op), self._rs(R), self._rs(W))

    def ts(self, out, in0, s1, s2, op0, op1, R, W, eng="dve", accum_out=None):
        if accum_out is None:
            if op1 is None:
                self.S.op(eng, lambda e: e.tensor_scalar(out, in0, s1, None, op0), self._rs(R), self._rs(W))
            else:
                self.S.op(eng, lambda e: e.tensor_scalar(out, in0, s1, s2, op0, op1), self._rs(R), self._rs(W))
        else:
            self.S.op(eng, lambda e: e.tensor_scalar(out, in0, s1, s2, op0, op1, accum_out=accum_out),
                      self._rs(R), self._rs(W))

    def stt(self, out, in0, scalar, in1, op0, op1, R, W, eng="dve"):
        self.S.op(eng, lambda e: e.scalar_tensor_tensor(out, in0, scalar, in1, op0, op1), self._rs(R), self._rs(W))

    def copy(self, out, in_, R, W, eng="dve"):
        if eng == "act":
            self.S.op("act", lambda e: e.copy(out, in_), self._rs(R), self._rs(W))
        else:
            self.S.op(eng, lambda e: e.tensor_copy(out, in_), self._rs(R), self._rs(W))

    def memset(self, ap, val, W, eng="pool"):
        self.S.op(eng, lambda e: e.memset(ap, val), [], self._rs(W))

    def recip(self, out, in_, R, W):
        self.S.op("dve", lambda e: e.reciprocal(out, in_), self._rs(R), self._rs(W))

    def dma(self, out, in_, R, W, eng="sp"):
        self.S.dma("sp", out, in_, self._rs(R), self._rs(W))

    def fn(self, eng, f, R, W):
        self.S.op(eng, f, self._rs(R), self._rs(W))

    def barrier(self):
        self.S.barrier()


IN_SPECS = [
    ("xT", [D, S], F32), ("c_col", [128, 16], F32), ("ada_w", [D, 6 * D], F32), ("ada_bT", [128, 96], F32),
    ("g1T", [128, 16], F32), ("g2T", [128, 16], F32), ("w_in", [D, NIN], F32), ("w_out", [D, D], F32),
    ("wq", [D, D], F32), ("qg", [128, 1], F32), ("kgT", [128, 3], F32), ("kg0b", [128, 128], F32),
    ("pekT", [128, 32], F32), ("pevT", [128, 32], F32), ("cwk", [4096, 128], F32), ("cwv", [4096, 128], F32),
    ("gateb", [128, 24], F32), ("convw", [128, 32], F32), ("convb", [128, 8], F32), ("lba", [128, 8], F32),
    ("lbi", [128, 8], F32), ("lam", [128, 8], F32), ("gr", [128, 8], F32), ("gab", [128, 1024], F32),
    ("wa", [8, 128, 128], F32), ("wi", [8, 128, 128], F32), ("keysT", [128, 16 * 128], F32),
    ("downB", [128 * 128, 16 * 128], F32), ("up", [16384, D], F32),
    ("cosT", [128, S], F32), ("sinT", [128, S], F32), ("rotm", [128, 128], F32), ("ident", [128, 128], F32),
    ("maskc", [128, S], F32), ("cm", [128, 4 * 512], F32), ("wm", [128, 8 * 512], F32),
    ("ex", [32, 16 * 128], F32), ("vm", [128, 16 * 32], F32), ("fb", [128, 16 * 32], F32), ("ovl", [128, 32], F32),
]


class Prog:
    def __init__(self, stop_after=99, dbg=()):
        self.nc = nc = bass.Bass("TRN2", target_bir_lowering=False)
        self.k = K(nc)
        self.stop_after = stop_after
        self.dbg = set(dbg)
        for d_ in self.dbg:
            if d_.startswith("qkind="):
                self.qkind = d_.split("=")[1]
        self.I = {}
        for name, shape, dt in IN_SPECS:
            self.I[name] = nc.dram_tensor(name, shape, dt, kind="ExternalInput").ap()
        self.outT = nc.dram_tensor("outT", [D, S], F32, kind="ExternalOutput").ap()
        self.scr = {}

    def scratch(self, name, shape, dt):
        kind = "ExternalOutput" if name in self.dbg else "Internal"
        ap = self.nc.dram_tensor(name, list(shape), dt, kind=kind).ap()
        self.scr[name] = (ap, Res(name))
        return ap, self.scr[name][1]

    def build(self):
        k = self.k
        with ExitStack() as g:
            self.g = g
            self.modT = k.sb(g, [128, 96], F32, "modT")
            self.G1 = k.sb(g, [128, 16], F32, "G1")
            self.G2 = k.sb(g, [128, 16], F32, "G2")
            self.eps_t = k.sb(g, [128, 1], F32, "eps")
            self.ones16 = k.sb(g, [128, 128], BF16, "ones16")
            self.ident16 = k.sb(g, [128, 128], BF16, "ident16")
            k.memset(self.eps_t[:], EPS, [self.eps_t])
            k.memset(self.ones16[:], 1.0, [self.ones16])
            self.ident32 = k.sb(g, [128, 128], F32, "ident32")
            k.dma(self.ident32[:], self.I["ident"], [], [self.ident32])
            k.copy(self.ident16[:], self.ident32[:], [self.ident32], [self.ident16])
            names = ["p0_mod", "p12_proj", "p3_cmp", "p4_attn", "p5_rnn", "p6_out", "p7_peer"]
            phases = [getattr(self, n) for n in names if hasattr(self, n)]
            for i, ph in enumerate(phases):
                if i > self.stop_after:
                    break
                ph()
                k.barrier()
            k.S.emit()
        return self.nc

    def p0_mod(self):
        k, I = self.k, self.I
        with ExitStack() as st:
            cc = k.sb(st, [128, 16], F32, "cc")
            sc = k.sb(st, [128, 16], F32, "sc")
            abT = k.sb(st, [128, 96], F32, "abT")
            g1 = k.sb(st, [128, 16], F32, "g1")
            g2 = k.sb(st, [128, 16], F32, "g2")
            tmp = k.sb(st, [128, 16], F32, "tmp")
            wb = [k.sb(st, [128, 16, 512], F32, f"adaw{i}") for i in range(2)]
            wb16 = [k.sb(st, [128, 16, 512], BF16, f"adaw16{i}") for i in range(2)]
            sc16 = k.sb(st, [128, 16], BF16, "sc16")
            pm = k.ps(st, [128, 96], F32, "pm")
            k.dma(cc[:], I["c_col"], [], [cc])
            k.dma(abT[:], I["ada_bT"], [], [abT])
            k.dma(g1[:], I["g1T"], [], [g1])
            k.dma(g2[:], I["g2T"], [], [g2])
            k.act(sc[:], cc[:], AF.Silu, [cc], [sc])
            k.copy(sc16[:], sc[:], [sc], [sc16])
            wv = I["ada_w"].rearrange("(k p) n -> p k n", p=128)
            cast_eng = ("act", "dve", "pool", "dve")
            for gi in range(24):
                b = wb[gi % 2]
                b16 = wb16[gi % 2]
                k.dma(b[:], wv[:, :, gi * 512:(gi + 1) * 512], [], [b], eng=("sp" if gi % 2 == 0 else "pool"))
                for q in range(4):
                    k.copy(b16[:, q * 4:(q + 1) * 4, :], b[:, q * 4:(q + 1) * 4, :], [b], [b16], eng=cast_eng[q])
                for j in range(4):
                    col = gi * 4 + j
                    for kk in range(16):
                        k.mm(pm[:, col:col + 1], b16[:, kk, j * 128:(j + 1) * 128], sc16[:, kk:kk + 1],
                             kk == 0, kk == 15, [b16, sc16], [pm])
            k.tt(self.modT[:], pm[:], abT[:], ALU.add, [pm, abT], [self.modT])
            k.ts(tmp[:], self.modT[:, 16:32], 1.0, None, ALU.add, None, [self.modT], [tmp])
            k.tt(self.G1[:], tmp[:], g1[:], ALU.mult, [tmp, g1], [self.G1])
            k.ts(tmp[:], self.modT[:, 64:80], 1.0, None, ALU.add, None, [self.modT, self.G1], [tmp])
            k.tt(self.G2[:], tmp[:], g2[:], ALU.mult, [tmp, g2], [self.G2])
            if "modT" in self.dbg:
                d, r = self.scratch("modT", [128, 96], F32)
                k.dma(d, self.modT[:], [self.modT], [r])

    def rms_stats_fm(self, st, loader, nchunks, width, scale_div, name):
        k = self.k
        rstd = k.sb(st, [128, S], F32, name)
        with ExitStack() as s2:
            xb = [k.sb(s2, [128, S], F32, "xld") for _ in range(2)]
            sq = [k.sb(s2, [128, S], BF16, "sq") for _ in range(2)]
            pss = [k.ps(s2, [128, 512], F32, "pss") for _ in range(4)]
            for kk in range(nchunks):
                xt, sqt = xb[kk % 2], sq[kk % 2]
                loader(kk, xt)
                k.act(sqt[:], xt[:], AF.Square, [xt], [sqt])
                for tg in range(4):
                    k.mm(pss[tg][:], self.ones16[:], sqt[:, tg * 512:(tg + 1) * 512], kk == 0, kk == nchunks - 1,
                         [self.ones16, sqt], [pss[tg]])
            for tg in range(4):
                sl = slice(tg * 512, (tg + 1) * 512)
                k.act(rstd[:, sl], pss[tg][:], AF.Sqrt, [pss[tg], self.eps_t], [rstd], bias=self.eps_t[:], scale=1.0 / scale_div)
            k.recip(rstd[:], rstd[:], [rstd], [rstd])
        k.barrier()
        return rstd

    def p12_proj(self):
        k, I = self.k, self.I
        xTv = I["xT"].rearrange("(k p) t -> k p t", p=128)
        qT_d, qT_r = self.scratch("qT", [8, 128, S], BF16)
        kcT_d, kcT_r = self.scratch("kcT", [2, 128, S], BF16)
        vcT_d, vcT_r = self.scratch("vcT", [2, 128, S], BF16)
        ksT_d, ksT_r = self.scratch("ksT", [2, 128, S], BF16)
        kwT_d, kwT_r = self.scratch("kwT", [2, 128, S], BF16)
        vs_d, vs_r = self.scratch("vs", [S, 256], BF16)
        vw_d, vw_r = self.scratch("vw", [S, 256], BF16)
        gt_d, gt_r = self.scratch("gates", [128, 16 * 24], F32)
        xrT_d, xrT_r = self.scratch("xrT", [8, 128, S], F32)
        xgT_d, xgT_r = self.scratch("xgT", [8, 128, S], F32)
        with ExitStack() as st:
            hT = k.sb(st, [128, 16, S], BF16, "hT")
            with ExitStack() as s1:
                rstd = self.rms_stats_fm(s1, lambda kk, dst: k.dma(dst[:], xTv[kk], [], [dst]), 16, S, float(D), "rstd1")
                xb = [k.sb(s1, [128, S], F32, "xld2") for _ in range(2)]
                tmp = [k.sb(s1, [128, S], F32, "tmp") for _ in range(2)]
                for kk in range(16):
                    xt, tp = xb[kk % 2], tmp[kk % 2]
                    k.dma(xt[:], xTv[kk], [], [xt])
                    k.stt(tp[:], xt[:], self.G1[:, kk:kk + 1], rstd[:], ALU.mult, ALU.mult, [xt, self.G1, rstd], [tp])
                    k.act(hT[:, kk, :], tp[:], AF.Identity, [tp, self.modT], [hT], bias=self.modT[:, kk:kk + 1], scale=1.0)
            if "hT" in self.dbg:
                d, r = self.scratch("hT", [16, 128, S], BF16)
                k.dma(d.rearrange("k p t -> p k t"), hT[:], [hT], [r])
            k.barrier()
            if "stop_p1" in self.dbg:
                return
            with ExitStack() as s2:
                wb = [k.sb(s2, [128, 16, 544], BF16, f"win{i}") for i in range(2)]
                cosT = k.sb(s2, [128, S], F32, "cosT")
                sinT = k.sb(s2, [128, S], F32, "sinT")
                rot16 = k.sb(s2, [128, 128], BF16, "rot16")
                gq = k.sb(s2, [128, 4], F32, "gq")
                gateb = k.sb(s2, [128, 24], F32, "gateb")
                k.dma(cosT[:], I["cosT"], [], [cosT])
                k.dma(sinT[:], I["sinT"], [], [sinT])
                rot32 = k.sb(s2, [128, 128], F32, "rot32")
                k.dma(rot32[:], I["rotm"], [], [rot32])
                k.copy(rot16[:], rot32[:], [rot32], [rot16])
                k.dma(gq[:, 0:1], I["qg"], [], [gq])
                k.dma(gq[:, 1:4], I["kgT"], [], [gq])
                k.dma(gateb[:], I["gateb"], [], [gateb])
                pz = [k.ps(s2, [128, 512], F32, "pz") for _ in range(4)]
                pss = [k.ps(s2, [128, 512], F32, "pss2") for _ in range(2)]
                prot = [k.ps(s2, [128, 512], F32, "prot") for _ in range(2)]
                ptm = pz[2:4]
                sq = [k.sb(s2, [128, 512], BF16, "sq2") for _ in range(2)]
                rs = [k.sb(s2, [128, 512], F32, "rs") for _ in range(2)]
                xn = [k.sb(s2, [128, 512], F32, "xn") for _ in range(2)]
                xn16 = [k.sb(s2, [128, 512], BF16, "xn16") for _ in range(2)]
                t1 = [k.sb(s2, [128, 512], F32, "t1") for _ in range(2)]
                t2 = [k.sb(s2, [128, 512], F32, "t2") for _ in range(2)]
                o16 = [k.sb(s2, [128, S], BF16, "o16") for _ in range(2)]
                o32 = [k.sb(s2, [128, S], F32, "o32") for _ in range(2)]
                vtm = k.sb(s2, [128, 16, 256], BF16, "vtm")
                gtm = k.sb(s2, [128, 16, 24], F32, "gtm")
                gtmp = k.sb(s2, [128, 24], F32, "gtmp")
                wv = I["w_in"].rearrange("(k p) n -> p k n", p=128)
                cnt = {"c": 0, "i": 0}

                pend = []

                def fm_chunk(b, co, kind, gcol, dst_ap, dst_res):
                    ci = cnt["c"]
                    cnt["c"] += 1
                    ob = (o32 if kind == "f32" else o16)[ci % 2]
                    for tg in range(4):
                        pend.append((b, co, kind, gcol, dst_ap, dst_res, ob, tg))

                def fm_s0(st_):
                    b, co, kind, gcol, dst_ap, dst_res, ob, tg = st_
                    i = cnt["i"]
                    cnt["i"] += 1
                    p = pz[i % 4]
                    sl = slice(tg * 512, (tg + 1) * 512)
                    for kk in range(16):
                        k.mm(p[:], b[:, kk, co:co + 128], hT[:, kk, sl], kk == 0, kk == 15, [b, hT], [p])
                    return (i, p)

                def fm_s1(st_, ip):
                    b, co, kind, gcol, dst_ap, dst_res, ob, tg = st_
                    i, p = ip
                    sl = slice(tg * 512, (tg + 1) * 512)
                    if kind in ("f32", "bf16"):
                        k.copy(ob[:, sl], p[:], [p], [ob], eng=("act" if tg % 2 == 0 else "dve"))
                    else:
                        a, a16 = xn[i % 2], xn16[i % 2]
                        if kind == "normrope":
                            sqt, pst, rst = sq[i % 2], pss[i % 2], rs[i % 2]
                            k.act(sqt[:], p[:], AF.Square, [p], [sqt])
                            k.mm(pst[:], self.ones16[:], sqt[:], True, True, [self.ones16, sqt], [pst])
                            k.act(rst[:], pst[:], AF.Sqrt, [pst, self.eps_t], [rst], bias=self.eps_t[:], scale=1.0 / 128.0)
                            k.recip(rst[:], rst[:], [rst], [rst])
                            k.stt(a[:], p[:], gq[:, gcol:gcol + 1], rst[:], ALU.mult, ALU.mult, [p, gq, rst], [a])
                        else:
                            k.copy(a[:], p[:], [p], [a], eng="dve")
                        k.copy(a16[:], a[:], [a], [a16], eng="act")
                        pr = prot[i % 2]
                        k.mm(pr[:], rot16[:], a16[:], True, True, [rot16, a16], [pr])
                        k.tt(t1[i % 2][:], a[:], cosT[:, sl], ALU.mult, [a, cosT], [t1[i % 2]])
                        k.tt(t2[i % 2][:], pr[:], sinT[:, sl], ALU.mult, [pr, sinT], [t2[i % 2]])
                        k.tt(ob[:, sl], t1[i % 2][:], t2[i % 2][:], ALU.add, [t1[i % 2], t2[i % 2]], [ob], eng="pool")
                    if tg == 3:
                        k.dma(dst_ap, ob[:], [ob], [dst_res])

                def fm_flush():
                    steps = list(pend)
                    del pend[:]
                    inflight = []
                    LA = 2
                    for n in range(min(LA, len(steps))):
                        inflight.append(fm_s0(steps[n]))
                    for n in range(len(steps)):
                        if n + LA < len(steps):
                            inflight.append(fm_s0(steps[n + LA]))
                        fm_s1(steps[n], inflight.pop(0))

                stg = [k.sb(s2, [128, 16, 272], F32, f"stg{i}") for i in range(2)]
                lcnt = {"g": 0, "s": 0}

                def load_group(c0, c1):
                    fm_flush()
                    b = wb[lcnt["g"] % 2]
                    lcnt["g"] += 1
                    w = c1 - c0
                    pieces = [(a, min(a + 256, w)) for a in range(0, w, 256)]
                    for (a0, a1) in pieces:
                        sg = stg[lcnt["s"] % 2]
                        lcnt["s"] += 1
                        k.dma(sg[:, :, 0:a1 - a0], wv[:, :, c0 + a0:c0 + a1], [], [sg], eng=("sp" if lcnt["s"] % 2 else "pool"))
                        k.copy(b[:, :, a0:a1], sg[:, :, 0:a1 - a0], [sg], [b], eng="pool")
                    return b

                def tm_block(b, co, width, post):
                    for tt_ in range(16):
                        p = ptm[tt_ % 2]
                        for kk in range(16):
                            k.mm(p[:, 0:width], hT[:, kk, tt_ * 128:(tt_ + 1) * 128], b[:, kk, co:co + width], kk == 0, kk == 15, [b, hT], [p])
                        post(tt_, p)

                b = load_group(0, 512)
                for j in range(4):
                    fm_chunk(b, j * 128, self.qkind if hasattr(self, "qkind") else "normrope", 0, qT_d[j], qT_r)
                b = load_group(512, 1024)
                for j in range(4):
                    fm_chunk(b, j * 128, self.qkind if hasattr(self, "qkind") else "normrope", 0, qT_d[4 + j], qT_r)
                if "stop_g0" in self.dbg:
                    fm_flush()
                    return
                b = load_group(1024, 1536)
                for j in range(2):
                    fm_chunk(b, j * 128, "rope", 0, kcT_d[j], kcT_r)
                for j in range(2):
                    fm_chunk(b, 256 + j * 128, "bf16", 0, vcT_d[j], vcT_r)
                b = load_group(1536, 2048)
                for j in range(2):
                    fm_chunk(b, j * 128, "normrope", 2, ksT_d[j], ksT_r)
                fm_flush()
                tm_block(b, 256, 256, lambda tt_, p: k.copy(vtm[:, tt_, :], p[:, 0:256], [p], [vtm], eng=("act" if tt_ % 2 == 0 else "dve")))
                k.dma(vs_d.rearrange("(t p) c -> p t c", p=128), vtm[:], [vtm], [vs_r])
                if "stop_g1" in self.dbg:
                    return
                if "rep_g3" in self.dbg:
                    b = load_group(1536, 2048)
                    for j in range(2):
                        fm_chunk(b, j * 128, "normrope", 2, ksT_d[j], ksT_r)
                    tm_block(b, 256, 256, lambda tt_, p: k.copy(vtm[:, tt_, :], p[:, 0:256], [p], [vtm], eng=("act" if tt_ % 2 == 0 else "dve")))
                    k.dma(vs_d.rearrange("(t p) c -> p t c", p=128), vtm[:], [vtm], [vs_r])
                    return
                b = load_group(2048, 2560)
                for j in range(2):
                    fm_chunk(b, j * 128, "normrope", 3, kwT_d[j], kwT_r)
                fm_flush()
                tm_block(b, 256, 256, lambda tt_, p: k.copy(vtm[:, tt_, :], p[:, 0:256], [p], [vtm], eng=("act" if tt_ % 2 == 0 else "dve")))
                b = load_group(2560, 2584)

                def post_g(tt_, p):
                    k.tt(gtmp[:], p[:, 0:24], gateb[:], ALU.add, [p, gateb], [gtmp])
                    k.act(gtmp[:], gtmp[:], AF.Exp, [gtmp], [gtmp], scale=-1.0)
                    k.ts(gtmp[:], gtmp[:], 1.0, None, ALU.add, None, [gtmp], [gtmp])
                    k.recip(gtm[:, tt_, :], gtmp[:], [gtmp], [gtm])
                tm_block(b, 0, 24, post_g)
                k.dma(vw_d.rearrange("(t p) c -> p t c", p=128), vtm[:], [vtm], [vw_r])
                k.dma(gt_d, gtm[:].rearrange("p t c -> p (t c)"), [gtm], [gt_r])
                if "stop_g2" in self.dbg:
                    return
                for half in range(2):
                    b = load_group(C_XR + half * 512, C_XR + (half + 1) * 512)
                    for j in range(4):
                        fm_chunk(b, j * 128, "f32", 0, xrT_d[half * 4 + j], xrT_r)
                for half in range(2):
                    b = load_group(C_XG + half * 512, C_XG + (half + 1) * 512)
                    for j in range(4):
                        fm_chunk(b, j * 128, "f32", 0, xgT_d[half * 4 + j], xgT_r)
                fm_flush()

    def p3_cmp(self):
        k, I = self.k, self.I
        kcT_d = self.scr["kcT"][0]
        vcT_d = self.scr["vcT"][0]
        kcmpT_d, kcmpT_r = self.scratch("kcmpT", [128, 2, 128], BF16)
        vcmp_d, vcmp_r = self.scratch("vcmp", [128, 2, 162], BF16)
        with ExitStack() as st:
            src = k.sb(st, [128, 4, S], BF16, "cmpsrc")
            wst = k.sb(st, [128, 32, 128], F32, "wst")
            w16 = [k.sb(st, [128, 32, 128], BF16, f"w16{i}") for i in range(2)]
            pe32 = k.sb(st, [128, 2, 32], F32, "pe32")
            peB = [k.sb(st, [128, 32, 127], BF16, f"peB{i}") for i in range(2)]
            kg0b = k.sb(st, [128, 128], F32, "kg0b")
            ovl = k.sb(st, [128, 32], F32, "ovl")
            ss = k.sb(st, [128, 1], F32, "ss")
            junk = k.sb(st, [128, 128], F32, "junk")
            kn16 = k.sb(st, [128, 128], BF16, "kn16")
            kT16 = k.sb(st, [128, 2, 128], BF16, "kT16")
            va16 = k.sb(st, [128, 2, 162], BF16, "va16")
            pc = [k.ps(st, [128, 128], F32, "pc") for _ in range(2)]
            ptr = k.ps(st, [128, 128], F32, "ptr")
            for g in range(2):
                k.dma(src[:, g, :], kcT_d[g], [self.scr["kcT"][1]], [src])
                k.dma(src[:, 2 + g, :], vcT_d[g], [self.scr["vcT"][1]], [src])
            k.dma(pe32[:, 0, :], I["pekT"], [], [pe32])
            k.dma(pe32[:, 1, :], I["pevT"], [], [pe32])
            k.dma(kg0b[:], I["kg0b"], [], [kg0b])
            k.dma(ovl[:], I["ovl"], [], [ovl])
            k.memset(kT16[:], 0.0, [kT16])
            k.memset(va16[:], 0.0, [va16])
            for kv, wname in enumerate(("cwk", "cwv")):
                k.dma(wst[:], I[wname].rearrange("(l d) o -> d l o", d=128), [], [wst])
                k.copy(w16[kv][:], wst[:], [wst], [w16[kv]], eng="pool")
                k.copy(peB[kv][:], pe32[:, kv, :].unsqueeze(2).to_broadcast([128, 32, 127]), [pe32], [peB[kv]])
            for kv in range(2):
                for g in range(2):
                    p = pc[(kv * 2 + g) % 2]
                    for l in range(32):
                        k.mm(p[0:127, :], src[:, kv * 2 + g, l:l + 16 * 126 + 1:16], w16[kv][:, l, :], l == 0, False, [src, w16[kv]], [p])
                    for l in range(32):
                        k.mm(p[0:127, :], peB[kv][:, l, :], w16[kv][:, l, :], False, l == 31, [peB[kv], w16[kv]], [p])
                    if kv == 0:
                        k.act(junk[0:127, :], p[0:127, :], AF.Square, [p], [junk, ss], accum_out=ss[0:127, :])
                        k.act(ss[0:127, :], ss[0:127, :], AF.Sqrt, [ss, self.eps_t], [ss], bias=self.eps_t[0:127, :], scale=1.0 / 128.0)
                        k.recip(ss[0:127, :], ss[0:127, :], [ss], [ss])
                        k.stt(kn16[0:127, :], p[0:127, :], ss[0:127, :], kg0b[0:127, :], ALU.mult, ALU.mult, [p, ss, kg0b], [kn16])
                        k.mm(ptr[:, 0:127], kn16[0:127, :], self.ident16[0:127, 0:127], True, True, [kn16, self.ident16], [ptr])
                        k.copy(kT16[:, g, 0:127], ptr[:, 0:127], [ptr], [kT16])
                    else:
                        k.copy(va16[0:127, g, 0:128], p[0:127, :], [p], [va16])
                        k.memset(va16[0:127, g, 128:129], 1.0, [va16])
                        k.copy(va16[0:127, g, 129:161], ovl[0:127, :], [ovl], [va16])
            k.dma(kcmpT_d, kT16[:], [kT16], [kcmpT_r])
            k.dma(vcmp_d, va16[:], [va16], [vcmp_r])

    def p4_attn(self):
        k, I = self.k, self.I
        sc = self.scr
        yT_d, yT_r = self.scratch("yT", [16, 128, S], BF16)
        with ExitStack() as st:
            qT = k.sb(st, [128, 8, S], BF16, "qT")
            ksT = k.sb(st, [128, 2, S], BF16, "ksT")
            kwT = k.sb(st, [128, 2, S], BF16, "kwT")
            kcT = k.sb(st, [128, 2, 128], BF16, "kcT")
            vsa = k.sb(st, [128, 16, 2, 130], BF16, "vsa")
            vwa = k.sb(st, [128, 16, 2, 130], BF16, "vwa")
            vca = k.sb(st, [128, 2, 162], BF16, "vca")
            gts = k.sb(st, [128, 16, 24], F32, "gts")
            maskc = k.sb(st, [128, S], F32, "maskc")
            cm = k.sb(st, [128, 4, 512], F32, "cm")
            wm = k.sb(st, [128, 8, 512], F32, "wm")
            ex32 = k.sb(st, [32, 16, 128], F32, "ex32")
            ex16 = k.sb(st, [32, 16, 128], BF16, "ex16")
            vm = k.sb(st, [128, 16, 32], F32, "vm")
            fb = k.sb(st, [128, 16, 32], F32, "fb")
            gab = k.sb(st, [128, 1024], F32, "gab")
            for j in range(8):
                k.dma(qT[:, j, :], sc["qT"][0][j], [sc["qT"][1]], [qT], eng=("sp" if j % 2 else "pool"))
            for g in range(2):
                k.dma(ksT[:, g, :], sc["ksT"][0][g], [sc["ksT"][1]], [ksT])
                k.dma(kwT[:, g, :], sc["kwT"][0][g], [sc["kwT"][1]], [kwT])
            k.dma(kcT[:], sc["kcmpT"][0], [sc["kcmpT"][1]], [kcT])
            k.dma(vca[:], sc["vcmp"][0], [sc["vcmp"][1]], [vca])
            k.memset(vsa[:], 1.0, [vsa])
            k.memset(vwa[:], 1.0, [vwa])
            for g in range(2):
                k.dma(vsa[:, :, g, 0:128], sc["vs"][0].rearrange("(c p) x -> p c x", p=128)[:, :, g * 128:(g + 1) * 128], [sc["vs"][1]], [vsa])
                k.dma(vwa[:, :, g, 0:128], sc["vw"][0].rearrange("(c p) x -> p c x", p=128)[:, :, g * 128:(g + 1) * 128], [sc["vw"][1]], [vwa])
            k.dma(gts[:].rearrange("p t c -> p (t c)"), sc["gates"][0], [sc["gates"][1]], [gts])
            k.dma(maskc[:], I["maskc"], [], [maskc])
            k.dma(cm[:].rearrange("p a b -> p (a b)"), I["cm"], [], [cm])
            k.dma(wm[:].rearrange("p a b -> p (a b)"), I["wm"], [], [wm])
            k.dma(ex32[:].rearrange("p a b -> p (a b)"), I["ex"], [], [ex32])
            k.copy(ex16[:], ex32[:], [ex32], [ex16])
            k.dma(vm[:].rearrange("p a b -> p (a b)"), I["vm"], [], [vm])
            k.dma(fb[:].rearrange("p a b -> p (a b)"), I["fb"], [], [fb])
            k.dma(gab[:], I["gab"], [], [gab])
            O = k.sb(st, [128, 4, 1024], F32, "O")
            Osub = [Res() for _ in range(4)]
            imp = [k.sb(st, [128, 32], F32, f"imp{i}") for i in range(4)]
            e16 = [k.sb(st, [128, 512], BF16, f"e16{i}") for i in range(3)]
            p16 = [k.sb(st, [128, 512], BF16, f"p16{i}") for i in range(3)]
            mskS = k.sb(st, [128, 16, 512], BF16, "mskS")
            selT16 = k.sb(st, [32, 512], BF16, "selT16")
            sel16 = [k.sb(st, [128, 32], BF16, f"sel16{i}") for i in range(2)]
            imp2 = [k.sb(st, [128, 32], F32, f"imp2{i}") for i in range(2)]
            wk = [k.sb(st, [128, 32], F32, f"wk{i}") for i in range(2)]
            m8 = [k.sb(st, [128, 16], F32, f"m8{i}") for i in range(2)]
            den = [k.sb(st, [128, 2], F32, f"den{i}") for i in range(4)]
            ssq = k.sb(st, [128, 1], F32, "ssq")
            junk = k.sb(st, [128, 1024], F32, "junk4")
            yn16 = k.sb(st, [128, 1024], BF16, "yn16")
            yT16 = [k.sb(st, [128, 512], BF16, f"yT16{i}") for i in range(2)]
            pS = [k.ps(st, [128, 512], F32, "pS") for _ in range(2)]
            pA = k.ps(st, [128, 512], F32, "pA")
            pACC = [k.ps(st, [128, 512], F32, "pACC") for _ in range(4)]
            pM = [k.ps(st, [128, 512], F32, "pM")] * 2
            pT = pA
            cnt = {"s": 0, "e": 0, "m": 0, "d": 0, "y": 0, "sel": 0}

            def finish_head(sub, hd, acc_ap, den_ap, tt_, gcol, first, Rp):
                dn = den[cnt["d"] % 4]
                cnt["d"] += 1
                k.ts(dn[:, 0:1], den_ap, 1e-30, None, ALU.max, None, Rp, [dn])
                k.recip(dn[:, 0:1], dn[:, 0:1], [dn], [dn])
                k.tt(dn[:, 1:2], dn[:, 0:1], gts[:, tt_, gcol:gcol + 1], ALU.mult, [dn, gts], [dn])
                osl = O[:, sub, hd * 128:(hd + 1) * 128]
                if first:
                    k.ts(osl, acc_ap, dn[:, 1:2], None, ALU.mult, None, Rp + [dn], [Osub[sub]])
                else:
                    k.stt(osl, acc_ap, dn[:, 1:2], osl, ALU.mult, ALU.add, Rp + [dn, Osub[sub]], [Osub[sub]])
                return dn

            def run_steps(i, steps):
                qsl = slice(i * 512, (i + 1) * 512)
                state = {}

                def s0(n):
                    keyT, vaug, g, r, kc, mask_of, first, last, gbranch = steps[n]
                    ps_ = pS[cnt["s"] % 2]
                    cnt["s"] += 1
                    k.mm(ps_[:], keyT[:, g, kc * 128:(kc + 1) * 128], qT[:, g * 4 + r, qsl], True, True, [keyT, qT], [ps_])
                    state[n] = ps_

                def s12(n):
                    keyT, vaug, g, r, kc, mask_of, first, last, gbranch = steps[n]
                    ps_ = state.pop(n)
                    hd = g * 4 + r
                    e = e16[cnt["e"] % 3]
                    p_ = p16[cnt["e"] % 3]
                    cnt["e"] += 1
                    k.act(e[:], ps_[:], AF.Exp, [ps_], [e], scale=SCALE)
                    mk, mkR, eng = mask_of(kc)
                    k.tt(p_[:], e[:], mk, ALU.mult, [e, mkR], [p_], eng=eng)
                    for sub in range(4):
                        acc = pACC[sub]
                        k.mm(acc[:, 0:129], p_[:, sub * 128:(sub + 1) * 128], vaug[:, kc, g, 0:129], first, last, [p_, vaug], [acc])
                    if last:
                        gcol = g * 12 + r * 3 + gbranch
                        dns = [den[sub] for sub in range(4)]
                        for sub in range(4):
                            k.ts(dns[sub][:, 0:1], pACC[sub][:, 128:129], 1e-30, None, ALU.max, None, [pACC[sub]], [dns[sub]])
                        for sub in range(4):
                            k.recip(dns[sub][:, 0:1], dns[sub][:, 0:1], [dns[sub]], [dns[sub]])
                        for sub in range(4):
                            k.tt(dns[sub][:, 1:2], dns[sub][:, 0:1], gts[:, i * 4 + sub, gcol:gcol + 1], ALU.mult, [dns[sub], gts], [dns[sub]])
                        for sub in range(4):
                            osl = O[:, sub, hd * 128:(hd + 1) * 128]
                            k.stt(osl, pACC[sub][:, 0:128], dns[sub][:, 1:2], osl, ALU.mult, ALU.add, [pACC[sub], dns[sub], Osub[sub]], [Osub[sub]])

                s0(0)
                for n in range(len(steps)):
                    if n + 1 < len(steps):
                        s0(n + 1)
                    s12(n)

            for i in range(4):
                qsl = slice(i * 512, (i + 1) * 512)
                for g in range(2):
                    for r in range(4):
                        hd = g * 4 + r
                        ps_ = pS[cnt["s"] % 2]
                        cnt["s"] += 1
                        k.mm(ps_[0:127, :], kcT[:, g, 0:127], qT[:, hd, qsl], True, True, [kcT, qT], [ps_])
                        e = e16[cnt["e"] % 3]
                        p_ = p16[cnt["e"] % 3]
                        cnt["e"] += 1
                        k.act(e[0:127, :], ps_[0:127, :], AF.Exp, [ps_], [e], scale=SCALE)
                        k.tt(p_[0:127, :], e[0:127, :], maskc[0:127, qsl], ALU.mult, [e, maskc], [p_])
                        for sub in range(4):
                            k.mm(pA[:, 0:161], p_[0:127, sub * 128:(sub + 1) * 128], vca[0:127, g, 0:161], True, True, [p_, vca], [pA])
                            dn = finish_head(sub, hd, pA[:, 0:128], pA[:, 128:129], i * 4 + sub, g * 12 + r * 3, True, [pA])
                            if r == 0:
                                k.ts(imp[sub][:], pA[:, 129:161], dn[:, 0:1], None, ALU.mult, None, [pA, dn], [imp[sub]])
                            else:
                                k.stt(imp[sub][:], pA[:, 129:161], dn[:, 0:1], imp[sub][:], ALU.mult, ALU.add, [pA, dn, imp[sub]], [imp[sub]])
                    psel = pM[cnt["m"] % 2]
                    cnt["m"] += 1
                    for sub in range(4):
                        tt_ = i * 4 + sub
                        j = cnt["sel"] % 2
                        cnt["sel"] += 1
                        k.tt(imp2[j][:], imp[sub][:], vm[:, tt_, :], ALU.mult, [imp[sub], vm], [imp2[j]])
                        k.tt(imp2[j][:], imp2[j][:], fb[:, tt_, :], ALU.add, [imp2[j], fb], [imp2[j]])
                        k.fn("dve", lambda e, o=m8[j][:, 0:8], a=imp2[j][:]: e.max(out=o, in_=a), [imp2[j]], [m8[j]])
                        k.fn("dve", lambda e, o=wk[j][:], a=m8[j][:, 0:8], b=imp2[j][:]: e.match_replace(out=o, in_to_replace=a, in_values=b, imm_value=-1e30), [imp2[j], m8[j]], [wk[j]])
                        k.fn("dve", lambda e, o=m8[j][:, 8:16], a=wk[j][:]: e.max(out=o, in_=a), [wk[j]], [m8[j]])
                        k.ts(sel16[j][:], imp2[j][:], m8[j][:, 15:16], None, ALU.is_ge, None, [imp2[j], m8[j]], [sel16[j]])
                        k.mm(psel[0:32, sub * 128:(sub + 1) * 128], sel16[j][:], self.ident16[:], True, True, [sel16[j], self.ident16], [psel])
                    k.copy(selT16[:], psel[0:32, :], [psel], [selT16], eng="act")
                    nkc = 4 * i + 4
                    for kc in range(nkc):
                        pm_ = pM[cnt["m"] % 2]
                        cnt["m"] += 1
                        k.mm(pm_[:], ex16[:, kc, :], selT16[:], True, True, [ex16, selT16], [pm_])
                        if kc >= 4 * i:
                            k.tt(mskS[:, kc, :], pm_[:], cm[:, kc - 4 * i, :], ALU.mult, [pm_, cm], [mskS])
                        else:
                            k.copy(mskS[:, kc, :], pm_[:], [pm_], [mskS], eng="act")
                    steps = []
                    for r in range(4):
                        for kc in range(nkc):
                            steps.append((ksT, vsa, g, r, kc, (lambda kc_: (mskS[:, kc_, :], mskS, "pool")), kc == 0, kc == nkc - 1, 1))
                    for r in range(4):
                        chunks = list(range(max(0, 4 * i - 4), 4 * i + 4))
                        for kc in chunks:
                            steps.append((kwT, vwa, g, r, kc, (lambda kc_, i_=i: (wm[:, kc_ - 4 * i_ + 4, :], wm, "dve")), kc == chunks[0], kc == chunks[-1], 2))
                    run_steps(i, steps)
                yt = yT16[i % 2]
                for c in range(8):
                    pass
                ytiles = []
                for sub in range(4):
                    k.act(junk[:], O[:, sub, :], AF.Square, [Osub[sub]], [junk, ssq], accum_out=ssq[:])
                    k.act(ssq[:], ssq[:], AF.Sqrt, [ssq, self.eps_t], [ssq], bias=self.eps_t[:], scale=1.0 / 1024.0)
                    k.recip(ssq[:], ssq[:], [ssq], [ssq])
                    k.stt(yn16[:], O[:, sub, :], ssq[:], gab[:], ALU.mult, ALU.mult, [Osub[sub], ssq, gab], [yn16])
                    for half in range(2):
                        for cc in range(4):
                            c = half * 4 + cc
                            k.mm(pT[:, cc * 128:(cc + 1) * 128], yn16[:, c * 128:(c + 1) * 128], self.ident16[:], True, True, [yn16, self.ident16], [pT])
                        dst = k.sb(st, [128, 4, 128], BF16, "ytmp") if False else None
                        yb = yT16[cnt["y"] % 2]
                        cnt["y"] += 1
                        k.copy(yb[:], pT[:], [pT], [yb], eng=("act" if half == 0 else "dve"))
                        for cc in range(4):
                            c = half * 4 + cc
                            k.dma(yT_d[c][:, i * 512 + sub * 128:i * 512 + (sub + 1) * 128], yb[:, cc * 128:(cc + 1) * 128], [yb], [yT_r])
            if "O_dbg" in self.dbg:
                pass

    def p5_rnn(self):
        k, I = self.k, self.I
        sc = self.scr
        yT_d, yT_r = sc["yT"]
        xr_d, xr_r = sc["xrT"]
        xg_d, xg_r = sc["xgT"]
        with ExitStack() as st:
            cw = k.sb(st, [128, 8, 4], F32, "cw")
            cb = k.sb(st, [128, 8], F32, "cb")
            nba = k.sb(st, [128, 8], F32, "nba")
            nbi = k.sb(st, [128, 8], F32, "nbi")
            lam = k.sb(st, [128, 8], F32, "lam")
            clam = k.sb(st, [128, 8], F32, "clam")
            gr = k.sb(st, [128, 8], F32, "gr")
            wst = k.sb(st, [128, 2, 128], F32, "wst5")
            w16 = [k.sb(st, [128, 2, 128], BF16, f"w165{i}") for i in range(2)]
            k.dma(cw[:].rearrange("p a b -> p (a b)"), I["convw"], [], [cw])
            k.dma(cb[:], I["convb"], [], [cb])
            k.dma(nba[:], I["lba"], [], [nba])
            k.dma(nbi[:], I["lbi"], [], [nbi])
            k.dma(lam[:], I["lam"], [], [lam])
            k.dma(gr[:], I["gr"], [], [gr])
            k.ts(nba[:], nba[:], -1.0, None, ALU.mult, None, [nba], [nba])
            k.ts(nbi[:], nbi[:], -1.0, None, ALU.mult, None, [nbi], [nbi])
            k.act(clam[:], lam[:], AF.Exp, [lam], [clam], scale=-1.0)
            k.ts(clam[:], clam[:], 1.0, None, ALU.add, None, [clam], [clam])
            k.act(clam[:], clam[:], AF.Ln, [clam], [clam])
            k.ts(clam[:], clam[:], -8.0, None, ALU.mult, None, [clam], [clam])
            orn = k.sb(st, [128, 8, S], F32, "orn")
            xp = k.sb(st, [128, S + 4], F32, "xp")
            xg = k.sb(st, [128, S], F32, "xg")
            u = k.sb(st, [128, S], F32, "u")
            u16 = k.sb(st, [128, S], BF16, "u16")
            ra = k.sb(st, [128, S], F32, "ra")
            ig = k.sb(st, [128, S], F32, "ig")
            bb = k.sb(st, [128, S], F32, "bb")
            sq16 = k.sb(st, [128, S], BF16, "sq165")
            pg = [k.ps(st, [128, 512], F32, "pg") for _ in range(2)]
            pss = [k.ps(st, [128, 512], F32, "pss5") for _ in range(4)]
            k.memset(xp[:, 0:4], 0.0, [xp])
            ci = 0
            for n in range(8):
                k.dma(xp[:, 4:S + 4], xr_d[n], [xr_r], [xp])
                k.dma(xg[:], xg_d[n], [xg_r], [xg], eng="pool")
                wb = w16[n % 2]
                k.dma(wst[:, 0, :], I["wa"][n], [], [wst])
                k.dma(wst[:, 1, :], I["wi"][n], [], [wst])
                k.copy(wb[:], wst[:], [wst], [wb], eng="pool")
                k.ts(u[:], xp[:, 1:S + 1], cw[:, n, 0:1], cb[:, n:n + 1], ALU.mult, ALU.add, [xp, cw, cb], [u])
                for i_ in range(1, 4):
                    k.stt(u[:], xp[:, 1 + i_:S + 1 + i_], cw[:, n, i_:i_ + 1], u[:], ALU.mult, ALU.add, [xp, cw, u], [u])
                k.copy(u16[:], u[:], [u], [u16], eng="act")
                for which, dst, nb in ((0, ra, nba), (1, ig, nbi)):
                    for tg in range(4):
                        p = pg[ci % 2]
                        ci += 1
                        sl = slice(tg * 512, (tg + 1) * 512)
                        k.mm(p[:], wb[:, which, :], u16[:, sl], True, True, [wb, u16], [p])
                        k.act(dst[:, sl], p[:], AF.Exp, [p, nb], [dst], bias=nb[:, n:n + 1], scale=-1.0)
                    k.ts(dst[:], dst[:], 1.0, None, ALU.add, None, [dst], [dst], eng="pool")
                    k.recip(dst[:], dst[:], [dst], [dst])
                k.act(ra[:], ra[:], AF.Exp, [ra, clam], [ra], scale=clam[:, n:n + 1])
                k.tt(bb[:], ra[:], ra[:], ALU.mult, [ra], [bb])
                k.ts(bb[:], bb[:], -1.0, 1.0, ALU.mult, ALU.add, [bb], [bb])
                k.act(bb[:], bb[:], AF.Sqrt, [bb], [bb])
                k.tt(bb[:], bb[:], ig[:], ALU.mult, [bb, ig], [bb], eng="pool")
                k.tt(bb[:], bb[:], u[:], ALU.mult, [bb, u], [bb])
                k.fn("dve", lambda e, o=ig[:], a=ra[:], b=bb[:]: e.tensor_tensor_scan(o, a, b, 0.0, ALU.mult, ALU.add), [ra, bb, ig], [ig])
                k.act(xg[:], xg[:], AF.Gelu, [xg], [xg])
                k.tt(orn[:, n, :], xg[:], ig[:], ALU.mult, [xg, ig], [orn])
                k.act(sq16[:], orn[:, n, :], AF.Square, [orn], [sq16])
                for tg in range(4):
                    k.mm(pss[tg][:], self.ones16[:], sq16[:, tg * 512:(tg + 1) * 512], n == 0, n == 7, [self.ones16, sq16], [pss[tg]])
            rstd = ra
            for tg in range(4):
                sl = slice(tg * 512, (tg + 1) * 512)
                k.act(rstd[:, sl], pss[tg][:], AF.Sqrt, [pss[tg], self.eps_t], [rstd], bias=self.eps_t[:], scale=1.0 / 1024.0)
            k.recip(rstd[:], rstd[:], [rstd], [rstd])
            for n in range(8):
                k.stt(u16[:], orn[:, n, :], gr[:, n:n + 1], rstd[:], ALU.mult, ALU.mult, [orn, gr, rstd], [u16])
                k.dma(yT_d[8 + n], u16[:], [u16], [yT_r])

    def p6_out(self):
        k, I = self.k, self.I
        sc = self.scr
        yT_d, yT_r = sc["yT"]
        x1_d, x1_r = self.scratch("x1T", [16, 128, S], F32)
        h2_d, h2_r = self.scratch("h2T", [16, 128, S], BF16)
        xTv = I["xT"].rearrange("(k p) t -> k p t", p=128)
        wv = I["w_out"].rearrange("(k p) n -> p k n", p=128)
        with ExitStack() as st:
            rstd = k.sb(st, [128, S], F32, "rstd2")
            with ExitStack() as s1:
                yT = k.sb(s1, [128, 16, S], BF16, "yTs")
                wb = [k.sb(s1, [128, 16, 256], BF16, f"wo{i}") for i in range(2)]
                stg = [k.sb(s1, [128, 16, 128], F32, f"wos{i}") for i in range(2)]
                xb = [k.sb(s1, [128, S], F32, f"x6{i}") for i in range(2)]
                ob = [k.sb(s1, [128, S], F32, f"o6{i}") for i in range(2)]
                sq = [k.sb(s1, [128, S], BF16, f"sq6{i}") for i in range(2)]
                pz = [k.ps(s1, [128, 512], F32, "pz6") for _ in range(2)]
                pss = [k.ps(s1, [128, 512], F32, "pss6") for _ in range(4)]
                for c in range(16):
                    k.dma(yT[:, c, :], yT_d[c], [yT_r], [yT], eng=("sp" if c % 2 else "pool"))
                ci = 0
                for jg in range(8):
                    b = wb[jg % 2]
                    for h in range(2):
                        sg = stg[(jg * 2 + h) % 2]
                        k.dma(sg[:], wv[:, :, jg * 256 + h * 128:jg * 256 + (h + 1) * 128], [], [sg])
                        k.copy(b[:, :, h * 128:(h + 1) * 128], sg[:], [sg], [b], eng="pool")
                    for jj in range(2):
                        j = jg * 2 + jj
                        xt, ot, sqt = xb[j % 2], ob[j % 2], sq[j % 2]
                        k.dma(xt[:], xTv[j], [], [xt])
                        for tg in range(4):
                            p = pz[ci % 2]
                            ci += 1
                            sl = slice(tg * 512, (tg + 1) * 512)
                            for c in range(16):
                                k.mm(p[:], b[:, c, jj * 128:(jj + 1) * 128], yT[:, c, sl], c == 0, c == 15, [b, yT], [p])
                            k.stt(ot[:, sl], p[:], self.modT[:, 32 + j:33 + j], xt[:, sl], ALU.mult, ALU.add, [p, self.modT, xt], [ot])
                        k.dma(x1_d[j], ot[:], [ot], [x1_r])
                        k.act(sqt[:], ot[:], AF.Square, [ot], [sqt])
                        for tg in range(4):
                            k.mm(pss[tg][:], self.ones16[:], sqt[:, tg * 512:(tg + 1) * 512], j == 0, j == 15, [self.ones16, sqt], [pss[tg]])
                for tg in range(4):
                    sl = slice(tg * 512, (tg + 1) * 512)
                    k.act(rstd[:, sl], pss[tg][:], AF.Sqrt, [pss[tg], self.eps_t], [rstd], bias=self.eps_t[:], scale=1.0 / float(D))
                k.recip(rstd[:], rstd[:], [rstd], [rstd])
            k.barrier()
            with ExitStack() as s2:
                xb = [k.sb(s2, [128, S], F32, f"x6b{i}") for i in range(2)]
                tp = [k.sb(s2, [128, S], F32, f"t6b{i}") for i in range(2)]
                hb = [k.sb(s2, [128, S], BF16, f"h6b{i}") for i in range(2)]
                for j in range(16):
                    xt, tt_, ht = xb[j % 2], tp[j % 2], hb[j % 2]
                    k.dma(xt[:], x1_d[j], [x1_r], [xt])
                    k.stt(tt_[:], xt[:], self.G2[:, j:j + 1], rstd[:], ALU.mult, ALU.mult, [xt, self.G2, rstd], [tt_])
                    k.act(ht[:], tt_[:], AF.Identity, [tt_, self.modT], [ht], bias=self.modT[:, 48 + j:49 + j], scale=1.0)
                    k.dma(h2_d[j], ht[:], [ht], [h2_r])

    def p7_peer(self):
        k, I = self.k, self.I
        sc = self.scr
        h2_d, h2_r = sc["h2T"]
        x1_d, x1_r = sc["x1T"]
        qp_d, qp_r = self.scratch("qpT", [16, 128, S], BF16)
        wv = I["wq"].rearrange("(k p) n -> p k n", p=128)
        with ExitStack() as st:
            h2T = k.sb(st, [128, 16, S], BF16, "h2Ts")
            wb = [k.sb(st, [128, 16, 256], BF16, f"wq{i}") for i in range(2)]
            stg = [k.sb(st, [128, 16, 128], F32, f"wqs{i}") for i in range(2)]
            ob = [k.sb(st, [128, S], BF16, f"oq{i}") for i in range(2)]
            pz = [k.ps(st, [128, 512], F32, "pz7") for _ in range(2)]
            for c in range(16):
                k.dma(h2T[:, c, :], h2_d[c], [h2_r], [h2T], eng=("sp" if c % 2 else "pool"))
            ci = 0
            for jg in range(8):
                b = wb[jg % 2]
                for h in range(2):
                    sg = stg[(jg * 2 + h) % 2]
                    k.dma(sg[:], wv[:, :, jg * 256 + h * 128:jg * 256 + (h + 1) * 128], [], [sg])
                    k.copy(b[:, :, h * 128:(h + 1) * 128], sg[:], [sg], [b], eng="pool")
                for jj in range(2):
                    j = jg * 2 + jj
                    ot = ob[j % 2]
                    for tg in range(4):
                        p = pz[ci % 2]
                        ci += 1
                        sl = slice(tg * 512, (tg + 1) * 512)
                        for c in range(16):
                            k.mm(p[:], b[:, c, jj * 128:(jj + 1) * 128], h2T[:, c, sl], c == 0, c == 15, [b, h2T], [p])
                        k.copy(ot[:, sl], p[:], [p], [ot], eng=("act" if tg % 2 == 0 else "dve"))
                    k.dma(qp_d[j], ot[:], [ot], [qp_r])
        k.barrier()
        SLACK = 1.0 - 4e-6
        with ExitStack() as st:
            h2s = k.sb(st, [128, 16, 512], BF16, "h2s")
            E1s = k.sb(st, [128, 4, 8, 128], F32, "E1s")
            E2s = k.sb(st, [128, 4, 8, 128], F32, "E2s")
            dg16 = k.sb(st, [128, 4, 8, 128], BF16, "dg16")
            accT = k.sb(st, [128, 16, 512], F32, "accT")
            keys16 = k.sb(st, [128, 16, 128], BF16, "keys16")
            with ExitStack() as s0:
                k32 = k.sb(s0, [128, 16, 128], F32, "k32")
                k.dma(k32[:].rearrange("p a b -> p (a b)"), I["keysT"], [], [k32])
                k.copy(keys16[:], k32[:], [k32], [keys16])
            k.barrier()
            for su in range(4):
                tsl = slice(su * 512, (su + 1) * 512)
                k.dma(h2s[:], h2_d.rearrange("c p t -> p c t")[:, :, tsl], [h2_r], [h2s])
                with ExitStack() as sb_:
                    qps = k.sb(sb_, [128, 16, 512], BF16, "qps")
                    s_sb = k.sb(sb_, [128, 16, 128], F32, "s_sb")
                    wk = k.sb(sb_, [128, 16, 128], F32, "wk7")
                    wk2 = k.sb(sb_, [128, 8, 256], F32, "wk72")
                    v16R = [Res() for _ in range(16)]
                    wkR = [Res() for _ in range(16)]
                    c16R = [Res() for _ in range(8)]
                    wk2R = [Res() for _ in range(8)]
                    v16 = k.sb(sb_, [128, 16, 16], F32, "v16")
                    cand = k.sb(sb_, [128, 8, 256], F32, "cand")
                    c16 = k.sb(sb_, [128, 8, 16], F32, "c16")
                    en = k.sb(sb_, [128, 8, 16], F32, "en")
                    negm = k.sb(sb_, [128, 16], F32, "negm")
                    negM = k.sb(sb_, [128, 8], F32, "negM")
                    Z = k.sb(sb_, [128, 8], F32, "Z")
                    th = k.sb(sb_, [128, 8], F32, "th")
                    cf = k.sb(sb_, [128, 8], F32, "cf")
                    E1t = k.sb(sb_, [128, 8, 128], F32, "E1t")
                    pS_ = [k.ps(sb_, [128, 4, 128], F32, "pS7") for _ in range(2)]
                    k.dma(qps[:], qp_d.rearrange("c p t -> p c t")[:, :, tsl], [qp_r], [qps])
                    for tl in range(4):
                        for hg in range(4):
                            p = pS_[hg % 2]
                            for q4 in range(4):
                                hp = hg * 4 + q4
                                k.mm(p[:, q4, :], qps[:, hp, tl * 128:(tl + 1) * 128], keys16[:, hp, :], True, True, [qps, keys16], [p])
                            k.copy(s_sb[:, hg * 4:(hg + 1) * 4, :], p[:], [p], [s_sb], eng=("act" if hg % 2 == 0 else "dve"))
                        for hp in range(16):
                            k.fn("dve", lambda e, o=v16[:, hp, 0:8], a=s_sb[:, hp, :]: e.max(out=o, in_=a), [s_sb], [v16R[hp]])
                        for hp in range(16):
                            k.fn("dve", lambda e, o=wk[:, hp, :], a=v16[:, hp, 0:8], b=s_sb[:, hp, :]: e.match_replace(out=o, in_to_replace=a, in_values=b, imm_value=-1e30), [s_sb, v16R[hp]], [wkR[hp]])
                        for hp in range(16):
                            k.fn("dve", lambda e, o=v16[:, hp, 8:16], a=wk[:, hp, :]: e.max(out=o, in_=a), [wkR[hp]], [v16R[hp]])
                        v16r = v16[:].rearrange("p (h two) i -> p h two i", two=2)
                        k.tt(cand[:].rearrange("p h (i j) -> p h i j", i=16),
                             v16r[:, :, 0, :].unsqueeze(3).to_broadcast([128, 8, 16, 16]),
                             v16r[:, :, 1, :].unsqueeze(2).to_broadcast([128, 8, 16, 16]), ALU.add, v16R, [cand])
                        for h in range(8):
                            k.fn("dve", lambda e, o=c16[:, h, 0:8], a=cand[:, h, :]: e.max(out=o, in_=a), [cand], [c16R[h]])
                        for h in range(8):
                            k.fn("dve", lambda e, o=wk2[:, h, :], a=c16[:, h, 0:8], b=cand[:, h, :]: e.match_replace(out=o, in_to_replace=a, in_values=b, imm_value=-1e30), [cand, c16R[h]], [wk2R[h]])
                        for h in range(8):
                            k.fn("dve", lambda e, o=c16[:, h, 8:16], a=wk2[:, h, :]: e.max(out=o, in_=a), [wk2R[h]], [c16R[h]])
                        k.ts(negm[:], v16[:, :, 0], -1.0, None, ALU.mult, None, v16R, [negm])
                        k.ts(negM[:], c16[:, :, 0], -1.0, None, ALU.mult, None, c16R, [negM])
                        for h in range(8):
                            k.act(en[:, h, :], c16[:, h, :], AF.Exp, [c16R[h], negM], [en, Z], bias=negM[:, h:h + 1], scale=1.0, accum_out=Z[:, h:h + 1])
                        k.recip(Z[:], Z[:], [Z], [Z])
                        k.tt(th[:], en[:, :, 15], Z[:], ALU.mult, [en, Z], [th])
                        k.ts(th[:], th[:], SLACK, None, ALU.mult, None, [th], [th])
                        k.ts(cf[:], en[:, :, 15], SLACK, None, ALU.mult, None, [en], [cf])
                        k.recip(cf[:], cf[:], [cf], [cf])
                        for h in range(8):
                            k.act(E2s[:, tl, h, :], s_sb[:, 2 * h + 1, :], AF.Exp, [s_sb, negm], [E2s], bias=negm[:, 2 * h + 1:2 * h + 2], scale=1.0)
                            k.act(E1t[:, h, :], s_sb[:, 2 * h, :], AF.Exp, [s_sb, negm], [E1t], bias=negm[:, 2 * h:2 * h + 1], scale=1.0)
                            k.ts(dg16[:, tl, h, :], self.ident32[:], th[:, h:h + 1], None, ALU.mult, None, [self.ident32, th], [dg16], eng="pool")
                        k.tt(E1s[:, tl, :, :], E1t[:], cf[:].unsqueeze(2).to_broadcast([128, 8, 128]), ALU.mult, [E1t, cf], [E1s])
                k.barrier()
                if "peer_dbg" in self.dbg and su == 0:
                    for nm, tile_ in (("E1s", E1s), ("E2s", E2s)):
                        d, r = self.scratch(nm, [128, 4 * 8 * 128], F32)
                        k.dma(d, tile_[:].rearrange("p a b c -> p (a b c)"), [tile_], [r])
                with ExitStack() as sc_:
                    stgD = [k.sb(sc_, [128, 16, 128], F32, f"stgD{i}") for i in range(2)]
                    stgU = [k.sb(sc_, [128, 1024], F32, f"stgU{i}") for i in range(2)]
                    dn16 = [k.sb(sc_, [128, 16, 128], BF16, f"dn16{i}") for i in range(4)]
                    up16 = [[k.sb(sc_, [128, 2048], BF16, f"up16{j}_{i}") for i in range(4)] for j in range(2)]
                    GA16 = [k.sb(sc_, [128, 512], BF16, f"GA{i}") for i in range(2)]
                    Pt = [k.sb(sc_, [128, 8, 128], F32, f"Pt{i}") for i in range(4)]
                    mE = [k.sb(sc_, [128, 8, 128], BF16, f"mE{i}") for i in range(4)]
                    WA = [k.sb(sc_, [128, 512], BF16, f"WA{i}") for i in range(8)]
                    ev = [k.sb(sc_, [128, 512], F32, f"ev{i}") for i in range(2)]
                    pact = [k.ps(sc_, [128, 512], F32, "pact") for _ in range(2)]
                    pw = [k.ps(sc_, [128, 512], F32, "pw") for _ in range(2)]
                    po = [k.ps(sc_, [128, 512], F32, "po") for _ in range(2)]
                    cn = {"k": 0, "p": 0, "o": 0}
                    ngrp = 1 if "peer_short" in self.dbg else 32
                    pend_wa = []
                    pend_po = []

                    def flush_wa():
                        while pend_wa:
                            wa_, pw__, ga_ = pend_wa.pop(0)
                            k.tt(wa_[:], pw__[:], ga_[:], ALU.mult, [pw__, ga_], [wa_])

                    def drain_po(ndc):
                        while pend_po and ndc != 0:
                            gi_, was_, dcs = pend_po[0]
                            dc = dcs.pop(0)
                            po_ = po[cn["o"] % 2]
                            cn["o"] += 1
                            for kl in range(4):
                                k.mm(po_[:], up16[gi_ % 2][kl][:, dc * 128:(dc + 1) * 128], was_[kl][:], kl == 0, kl == 3, [up16[gi_ % 2][kl], was_[kl]], [po_])
                            if gi_ == 0:
                                k.copy(accT[:, dc, :], po_[:], [po_], [accR[dc]], eng="act")
                            else:
                                pend_add.append((dc, po_))
                            if not dcs:
                                pend_po.pop(0)
                            ndc -= 1
                            if len(pend_add) >= 2 and ndc != 0:
                                flush_add()

                    accR = [Res() for _ in range(16)]
                    pend_add = []

                    def flush_add():
                        while pend_add:
                            dc, po_ = pend_add.pop(0)
                            k.tt(accT[:, dc, :], accT[:, dc, :], po_[:], ALU.add, [accR[dc], po_], [accR[dc]])
                    nk = ngrp * 4
                    seq = [(kap, tl) for kap in range(nk) for tl in range(4)]
                    LA = 2

                    def load_dn(kap2):
                        kl2 = kap2 % 4
                        sd = stgD[kap2 % 2]
                        k.dma(sd[:].rearrange("p a b -> p (a b)"), I["downB"][kap2 * 128:(kap2 + 1) * 128, :], [], [sd], eng="sp")
                        k.copy(dn16[kl2][:], sd[:], [sd], [dn16[kl2]], eng="act")

                    def load_up(kap2):
                        gi2, kl2 = kap2 // 4, kap2 % 4
                        up = up16[gi2 % 2][kl2]
                        for hf in range(2):
                            su_ = stgU[(kap2 * 2 + hf) % 2]
                            k.dma(su_[:], I["up"][kap2 * 128:(kap2 + 1) * 128, hf * 1024:(hf + 1) * 1024], [], [su_], eng="sp")
                            k.copy(up[:, hf * 1024:(hf + 1) * 1024], su_[:], [su_], [up], eng="act")

                    def p1(n):
                        kap, tl = seq[n]
                        k.tt(Pt[n % 4][:], E2s[:, tl, :, :], E1s[:, tl, :, kap:kap + 1].to_broadcast([128, 8, 128]), ALU.mult, [E2s, E1s], [Pt[n % 4]])

                    for n in range(min(LA, len(seq))):
                        p1(n)
                    cur = {}
                    for n, (kap, tl) in enumerate(seq):
                        gi, kl = kap // 4, kap % 4
                        if tl == 0:
                            if kl == 0:
                                cur["was"] = []
                                if gi == 0:
                                    for kl2 in range(4):
                                        load_dn(kl2)
                            dn = dn16[kl]
                            pa_ = pact[cn["k"] % 2]
                            pw_ = pw[cn["k"] % 2]
                            ga = GA16[cn["k"] % 2]
                            wa = WA[cn["k"] % 8]
                            cn["k"] += 1
                            cur.update(pw=pw_, ga=ga, wa=wa)
                            for c in range(16):
                                k.mm(pa_[:], dn[:, c, :], h2s[:, c, :], c == 0, c == 15, [dn, h2s], [pa_])
                            k.act(ga[:], pa_[:], AF.Gelu, [pa_], [ga])
                            if kap + 4 < nk:
                                load_dn(kap + 4)
                            load_up(kap)
                        if n + LA < len(seq):
                            p1(n + LA)
                        me = mE[n % 4]
                        k.stt(me[:], Pt[n % 4][:], 1.0, Pt[n % 4][:], ALU.is_ge, ALU.mult, [Pt[n % 4]], [me])
                        flush_add()
                        if tl == 1:
                            flush_wa()
                        pw_ = cur["pw"]
                        for h in range(8):
                            k.mm(pw_[:, tl * 128:(tl + 1) * 128], me[:, h, :], dg16[:, tl, h, :], h == 0, h == 7, [me, dg16], [pw_])
                        if tl in (1, 3):
                            drain_po(2)
                        if tl == 3:
                            pend_wa.append((cur["wa"], pw_, cur["ga"]))
                            cur["was"].append(cur["wa"])
                            if kl == 3:
                                assert not pend_po
                                pend_po.append((gi, cur["was"], list(range(16))))
                    flush_add()
                    flush_wa()
                    drain_po(-1)
                    flush_add()
                    for dc in range(16):
                        x_ = ev[dc % 2]
                        k.dma(x_[:], x1_d[dc][:, tsl], [x1_r], [x_])
                        k.stt(x_[:], accT[:, dc, :], self.modT[:, 80 + dc:81 + dc], x_[:], ALU.mult, ALU.add, [accR[dc], self.modT, x_], [x_])
                        k.dma(self.outT[dc * 128:(dc + 1) * 128, tsl], x_[:], [x_], [])
                k.barrier()


def _consts():
    f = np.float32
    half = 64
    freqs = (10000.0 ** (-np.arange(half, dtype=f) / f(half))).astype(f)
    ang = np.arange(S, dtype=f)[:, None] * freqs[None, :]
    cos = np.cos(ang).astype(f).T
    sin = np.sin(ang).astype(f).T
    cosT = np.concatenate([cos, cos], 0)
    sinT = np.concatenate([sin, sin], 0)
    rotm = np.zeros((128, 128), f)
    for m in range(64):
        rotm[m + 64, m] = -1.0
        rotm[m, m + 64] = 1.0
    ident = np.eye(128, dtype=f)
    t = np.arange(S)
    cst = np.arange(NCMP) * 16
    maskc = np.zeros((128, S), f)
    maskc[:NCMP] = ((cst + 31)[:, None] <= t[None, :]).astype(f)
    kk = np.arange(128)[:, None]
    tt = np.arange(512)[None, :]
    cm = np.stack([(128 * o + kk <= tt).astype(f) for o in range(4)], 1).reshape(128, 4 * 512)
    wm = np.stack([(((128 * rel + kk - tt) <= 0) & ((128 * rel + kk - tt) > -512)).astype(f)
                   for rel in range(-4, 4)], 1).reshape(128, 8 * 512)
    ex = np.zeros((32, 16, 128), f)
    for kc in range(16):
        for kq in range(128):
            ex[2 * kc + kq // 64, kc, kq] = 1.0
    ex = ex.reshape(32, 16 * 128)
    jb = np.arange(32)[None, :]
    cur = (t // 64)[:, None]
    forced = (jb == 0) | (jb == cur) | (jb == cur - 1)
    valid = (jb * 64) <= t[:, None]
    vm_ = (valid & ~forced).astype(f)
    fb_ = np.where(forced, 1e4, np.where(valid, 0.0, -1e4)).astype(f)
    vm = vm_.reshape(16, 128, 32).transpose(1, 0, 2).reshape(128, 512)
    fb = fb_.reshape(16, 128, 32).transpose(1, 0, 2).reshape(128, 512)
    sst = np.arange(32) * 64
    ov = np.maximum(np.minimum(cst[:, None] + 32, sst[None, :] + 64) - np.maximum(cst[:, None], sst[None, :]), 0)
    ovl = np.zeros((128, 32), f)
    ovl[:NCMP] = ov.astype(f) / 32.0
    return dict(cosT=cosT, sinT=sinT, rotm=rotm, ident=ident, maskc=maskc, cm=cm, wm=wm, ex=ex, vm=vm, fb=fb, ovl=ovl)


def prep_shared(inp):
    f = np.float32
    A = lambda v: np.ascontiguousarray(np.asarray(v, dtype=f))
    sh = {}
    sh["ada_w"] = A(inp["ada_w"][0])
    sh["ada_bT"] = A(inp["ada_b"][0].reshape(96, 128).T)
    sh["g1T"] = A(inp["norm_mix_g"][0].reshape(16, 128).T)
    sh["g2T"] = A(inp["norm_ffn_g"][0].reshape(16, 128).T)
    sh["w_in"] = A(inp["w_in"][0])
    sh["w_out"] = A(inp["w_out"][0])
    sh["wq"] = A(inp["peer_wq"][0])
    sh["qg"] = A(inp["q_norm_g"][0].reshape(128, 1))
    sh["kgT"] = A(inp["k_norm_g"][0].T)
    sh["kg0b"] = A(np.broadcast_to(inp["k_norm_g"][0, 0][None, :], (128, 128)))
    sh["pekT"] = A(inp["cmp_pe_k"][0].T)
    sh["pevT"] = A(inp["cmp_pe_v"][0].T)
    sh["cwk"] = A(inp["cmp_w_k"][0])
    sh["cwv"] = A(inp["cmp_w_v"][0])
    sh["gateb"] = A(np.broadcast_to(inp["gate_b"][0][None, :], (128, 24)))
    sh["convw"] = A(inp["conv_w"][0].reshape(4, 8, 128).transpose(2, 1, 0).reshape(128, 32))
    for nm, key in (("convb", "conv_b"), ("lba", "lru_ba"), ("lbi", "lru_bi"), ("lam", "lru_lam"), ("gr", "out_g_rnn")):
        sh[nm] = A(inp[key][0].reshape(8, 128).T)
    sh["gab"] = A(np.broadcast_to(inp["out_g_attn"][0][None, :], (128, 1024)))
    sh["wa"] = A(inp["lru_wa"][0])
    sh["wi"] = A(inp["lru_wi"][0])
    sh["keysT"] = A(inp["peer_keys"][0].transpose(3, 0, 1, 2).reshape(128, 16 * 128))
    sh["downB"] = A(inp["peer_down"][0].reshape(128, 128, 16, 128).transpose(0, 3, 2, 1).reshape(128 * 128, 16 * 128))
    sh["up"] = A(inp["peer_up"][0])
    sh.update(_consts())
    return sh


def prep_core(inp, b):
    f = np.float32
    return {"xT": np.ascontiguousarray(np.asarray(inp["x"][b], dtype=f).T),
            "c_col": np.ascontiguousarray(np.asarray(inp["c"][b], dtype=f).reshape(16, 128).T)}


def kernel(**inputs):
    inp = {k_: np.asarray(v) for k_, v in inputs.items()}
    sh = prep_shared(inp)
    nc = Prog().build()
    in_maps = []
    for b in range(8):
        m = dict(sh)
        m.update(prep_core(inp, b))
        in_maps.append(m)
    res = run_bass_kernel_spmd(nc, in_maps, core_ids=list(range(8)))
    out = np.stack([np.asarray(r["outT"]).T for r in res.results], 0)
    return np.ascontiguousarray(out.astype(np.float32))
```

```python
import numpy as np
from contextlib import ExitStack
import concourse.bass as bass
import concourse.mybir as mybir
from concourse.bass_utils import run_bass_kernel_spmd

F32 = mybir.dt.float32
BF16 = mybir.dt.bfloat16
AF = mybir.ActivationFunctionType
ALU = mybir.AluOpType
AX = mybir.AxisListType

D = 2048
S = 2048
NT = 16
NIN = 4632
C_Q, C_KC, C_VC, C_KS, C_VS, C_KW, C_VW, C_GL, C_XR, C_XG = 0, 1024, 1280, 1536, 1792, 2048, 2304, 2560, 2584, 3608
EPS = 1e-6
NCMP = 127
SCALE = 128 ** -0.5


class Res:
    __slots__ = ("name", "w", "r")

    def __init__(self, name=""):
        self.name = name
        self.w = None
        self.r = []


class Op:
    __slots__ = ("eng", "fn", "deps", "flag", "cval", "dma", "dsem", "dval")

    def __init__(self, eng, fn, dma=False):
        self.eng = eng
        self.fn = fn
        self.deps = []
        self.flag = False
        self.cval = 0
        self.dma = dma
        self.dsem = None
        self.dval = 0


class Sched:
    ENGS = ("pe", "act", "dve", "pool", "sp")

    def __init__(self, nc, n_dma_sems=16):
        self.nc = nc
        self.q = {e: [] for e in self.ENGS}
        self.n_dma_sems = n_dma_sems
        self.dma_rr = 0
        self.dma_last = [None] * n_dma_sems
        self.dma_cnt = [0] * n_dma_sems
        self.pending = {e: [] for e in self.ENGS}

    def _add(self, eng, fn, reads, writes, dma=False):
        op = Op(eng, fn, dma)
        deps = list(self.pending[eng])
        self.pending[eng] = []
        for r in reads:
            if r.w is not None:
                deps.append(r.w)
        for w in writes:
            if w.w is not None:
                deps.append(w.w)
            deps.extend(w.r)
        for r in reads:
            r.r.append(op)
        for w in writes:
            w.w = op
            w.r = []
        if dma:
            s = self.dma_rr
            self.dma_rr = (self.dma_rr + 1) % self.n_dma_sems
            prev = self.dma_last[s]
            if prev is not None:
                deps.append(prev)
            self.dma_last[s] = op
            self.dma_cnt[s] += 1
            op.dsem = s
            op.dval = 16 * self.dma_cnt[s]
        seen = set()
        for d in deps:
            if d is op or id(d) in seen:
                continue
            if (not d.dma) and d.eng == eng and eng == "pe":
                continue
            seen.add(id(d))
            op.deps.append(d)
            d.flag = True
        self.q[eng].append(op)
        return op

    def op(self, eng, fn, reads=(), writes=()):
        return self._add(eng, fn, list(reads), list(writes))

    def dma(self, eng, out, in_, reads=(), writes=()):
        return self._add(eng, lambda e: e.dma_start(out=out, in_=in_), list(reads), list(writes), dma=True)

    def barrier(self):
        lasts = []
        for e in self.ENGS:
            for op in reversed(self.q[e]):
                if not op.dma:
                    lasts.append(op)
                    break
        for s in range(self.n_dma_sems):
            if self.dma_last[s] is not None:
                lasts.append(self.dma_last[s])
        for e in self.ENGS:
            self.pending[e] = list(lasts)

    def emit(self):
        nc = self.nc
        with ExitStack() as st:
            esem = {e: st.enter_context(nc.semaphore(f"s_{e}")) for e in self.ENGS}
            dsem = [st.enter_context(nc.semaphore(f"d_{i}")) for i in range(self.n_dma_sems)]
            for e in self.ENGS:
                c = 0
                for op in self.q[e]:
                    if op.dma:
                        continue
                    if op.flag:
                        c += 1
                        op.cval = c
            block = st.enter_context(nc.Block())

            def run(ename, eobj):
                waited = {}
                for op in self.q[ename]:
                    need = {}
                    for d in op.deps:
                        if d.dma:
                            key, val = ("d", d.dsem), d.dval
                        else:
                            key, val = ("e", d.eng), d.cval
                        if val > need.get(key, 0):
                            need[key] = val
                    for key, val in need.items():
                        if waited.get(key, 0) >= val:
                            continue
                        waited[key] = val
                        sem = dsem[key[1]] if key[0] == "d" else esem[key[1]]
                        eobj.wait_ge(sem, val)
                    ins = op.fn(eobj)
                    if op.dma:
                        ins.then_inc(dsem[op.dsem], 16)
                    elif op.flag:
                        ins.then_inc(esem[ename], 1)
                if ename == "sp":
                    for s in range(self.n_dma_sems):
                        if self.dma_cnt[s] > 0:
                            eobj.wait_ge(dsem[s], 16 * self.dma_cnt[s])

            block.tensor(lambda e: run("pe", e))
            block.scalar(lambda e: run("act", e))
            block.vector(lambda e: run("dve", e))
            block.gpsimd(lambda e: run("pool", e))
            block.sync(lambda e: run("sp", e))


class T:
    __slots__ = ("t", "r")

    def __init__(self, t, name=""):
        self.t = t
        self.r = Res(name)

    def __getitem__(self, k):
        return self.t[k]


class K:
    def __init__(self, nc):
        self.nc = nc
        self.S = Sched(nc)
        self.uid = 0

    def sb(self, st, shape, dt, name=None):
        self.uid += 1
        n = f"{name or 't'}_{self.uid}"
        return T(st.enter_context(self.nc.sbuf_tensor(n, list(shape), dt)), n)

    def ps(self, st, shape, dt=F32, name=None):
        self.uid += 1
        n = f"{name or 'p'}_{self.uid}"
        return T(st.enter_context(self.nc.psum_tensor(n, list(shape), dt)), n)

    @staticmethod
    def _rs(xs):
        return [x.r if isinstance(x, T) else x for x in xs]

    def mm(self, out, lhsT, rhs, start, stop, R, W):
        self.S.op("pe", lambda e: e.matmul(out, lhsT, rhs, start=start, stop=stop), self._rs(R), self._rs(W))

    def act(self, out, in_, func, R, W, bias=None, scale=None, accum_out=None, eng="act"):
        kw = {}
        if bias is not None:
            kw["bias"] = bias
        if scale is not None:
            kw["scale"] = scale
        if accum_out is not None:
            kw["accum_out"] = accum_out
        self.S.op(eng, lambda e: e.activation(out, in_, func, **kw), self._rs(R), self._rs(W))

    def tt(self, out, in0, in1, op, R, W, eng="dve"):
        self.S.op(eng, lambda e: e.tensor_tensor(out, in0, in1, op), self._rs(R), self._rs(W))

    def ts(self, out, in0, s1, s2, op0, op1, R, W, eng="dve", accum_out=None):
        if accum_out is None:
            if op1 is None:
                self.S.op(eng, lambda e: e.tensor_scalar(out, in0, s1, None, op0), self._rs(R), self._rs(W))
            else:
                self.S.op(eng, lambda e: e.tensor_scalar(out, in0, s1, s2, op0, op1), self._rs(R), self._rs(W))
        else:
            self.S.op(eng, lambda e: e.tensor_scalar(out, in0, s1, s2, op0, op1, accum_out=accum_out),
                      self._rs(R), self._rs(W))

    def stt(self, out, in0, scalar, in1, op0, op1, R, W, eng="dve"):
        self.S.op(eng, lambda e: e.scalar_tensor_tensor(out, in0, scalar, in1, op0, op1), self._rs(R), self._rs(W))

    def copy(self, out, in_, R, W, eng="dve"):
        if eng == "act":
            self.S.op("act", lambda e: e.copy(out, in_), self._rs(R), self._rs(W))
        else:
            self.S.op(eng, lambda e: e.tensor_copy(out, in_), self._rs(R), self._rs(W))

    def memset(self, ap, val, W, eng="pool"):
        self.S.op(eng, lambda e: e.memset(ap, val), [], self._rs(W))

    def recip(self, out, in_, R, W):
        self.S.op("dve", lambda e: e.reciprocal(out, in_), self._rs(R), self._rs(W))

    def dma(self, out, in_, R, W, eng="sp"):
        self.S.dma("sp", out, in_, self._rs(R), self._rs(W))

    def fn(self, eng, f, R, W):
        self.S.op(eng, f, self._rs(R), self._rs(W))

    def barrier(self):
        self.S.barrier()


IN_SPECS = [
    ("xT", [D, S], F32), ("c_col", [128, 16], F32), ("ada_w", [D, 6 * D], F32), ("ada_bT", [128, 96], F32),
    ("g1T", [128, 16], F32), ("g2T", [128, 16], F32), ("w_in", [D, NIN], F32), ("w_out", [D, D], F32),
    ("wq", [D, D], F32), ("qg", [128, 1], F32), ("kgT", [128, 3], F32), ("kg0b", [128, 128], F32),
    ("pekT", [128, 32], F32), ("pevT", [128, 32], F32), ("cwk", [4096, 128], F32), ("cwv", [4096, 128], F32),
    ("gateb", [128, 24], F32), ("convw", [128, 32], F32), ("convb", [128, 8], F32), ("lba", [128, 8], F32),
    ("lbi", [128, 8], F32), ("lam", [128, 8], F32), ("gr", [128, 8], F32), ("gab", [128, 1024], F32),
    ("wa", [8, 128, 128], F32), ("wi", [8, 128, 128], F32), ("keysT", [128, 16 * 128], F32),
    ("downB", [128 * 128, 16 * 128], F32), ("up", [16384, D], F32),
    ("cosT", [128, S], F32), ("sinT", [128, S], F32), ("rotm", [128, 128], F32), ("ident", [128, 128], F32),
    ("maskc", [128, S], F32), ("cm", [128, 4 * 512], F32), ("wm", [128, 8 * 512], F32),
    ("ex", [32, 16 * 128], F32), ("vm", [128, 16 * 32], F32), ("fb", [128, 16 * 32], F32), ("ovl", [128, 32], F32),
]


class Prog:
    def __init__(self, stop_after=99, dbg=()):
        self.nc = nc = bass.Bass("TRN2", target_bir_lowering=False)
        self.k = K(nc)
        self.stop_after = stop_after
        self.dbg = set(dbg)
        for d_ in self.dbg:
            if d_.startswith("qkind="):
                self.qkind = d_.split("=")[1]
        self.I = {}
        for name, shape, dt in IN_SPECS:
            self.I[name] = nc.dram_tensor(name, shape, dt, kind="ExternalInput").ap()
        self.outT = nc.dram_tensor("outT", [D, S], F32, kind="ExternalOutput").ap()
        self.scr = {}

    def scratch(self, name, shape, dt):
        kind = "ExternalOutput" if name in self.dbg else "Internal"
        ap = self.nc.dram_tensor(name, list(shape), dt, kind=kind).ap()
        self.scr[name] = (ap, Res(name))
        return ap, self.scr[name][1]

    def build(self):
        k = self.k
        with ExitStack() as g:
            self.g = g
            self.modT = k.sb(g, [128, 96], F32, "modT")
            self.G1 = k.sb(g, [128, 16], F32, "G1")
            self.G2 = k.sb(g, [128, 16], F32, "G2")
            self.eps_t = k.sb(g, [128, 1], F32, "eps")
            self.ones16 = k.sb(g, [128, 128], BF16, "ones16")
            self.ident16 = k.sb(g, [128, 128], BF16, "ident16")
            k.memset(self.eps_t[:], EPS, [self.eps_t])
            k.memset(self.ones16[:], 1.0, [self.ones16])
            self.ident32 = k.sb(g, [128, 128], F32, "ident32")
            k.dma(self.ident32[:], self.I["ident"], [], [self.ident32])
            k.copy(self.ident16[:], self.ident32[:], [self.ident32], [self.ident16])
            names = ["p0_mod", "p12_proj", "p3_cmp", "p4_attn", "p5_rnn", "p6_out", "p7_peer"]
            phases = [getattr(self, n) for n in names if hasattr(self, n)]
            for i, ph in enumerate(phases):
                if i > self.stop_after:
                    break
                ph()
                k.barrier()
            k.S.emit()
        return self.nc

    def p0_mod(self):
        k, I = self.k, self.I
        with ExitStack() as st:
            cc = k.sb(st, [128, 16], F32, "cc")
            sc = k.sb(st, [128, 16], F32, "sc")
            abT = k.sb(st, [128, 96], F32, "abT")
            g1 = k.sb(st, [128, 16], F32, "g1")
            g2 = k.sb(st, [128, 16], F32, "g2")
            tmp = k.sb(st, [128, 16], F32, "tmp")
            wb = [k.sb(st, [128, 16, 512], F32, f"adaw{i}") for i in range(3)]
            wb16 = [k.sb(st, [128, 16, 512], BF16, f"adaw16{i}") for i in range(2)]
            sc16 = k.sb(st, [128, 16], BF16, "sc16")
            pm = k.ps(st, [128, 96], F32, "pm")
            k.dma(cc[:], I["c_col"], [], [cc])
            k.dma(abT[:], I["ada_bT"], [], [abT])
            k.dma(g1[:], I["g1T"], [], [g1])
            k.dma(g2[:], I["g2T"], [], [g2])
            k.act(sc[:], cc[:], AF.Silu, [cc], [sc])
            k.copy(sc16[:], sc[:], [sc], [sc16])
            wv = I["ada_w"].rearrange("(k p) n -> p k n", p=128)
            cast_eng = ("act", "dve", "act", "dve")
            for gi in range(24):
                b = wb[gi % 3]
                b16 = wb16[gi % 2]
                k.dma(b[:], wv[:, :, gi * 512:(gi + 1) * 512], [], [b])
                for q in range(4):
                    k.copy(b16[:, q * 4:(q + 1) * 4, :], b[:, q * 4:(q + 1) * 4, :], [b], [b16], eng=cast_eng[q])
                for j in range(4):
                    col = gi * 4 + j
                    for kk in range(16):
                        k.mm(pm[:, col:col + 1], b16[:, kk, j * 128:(j + 1) * 128], sc16[:, kk:kk + 1],
                             kk == 0, kk == 15, [b16, sc16], [pm])
            k.tt(self.modT[:], pm[:], abT[:], ALU.add, [pm, abT], [self.modT])
            k.ts(tmp[:], self.modT[:, 16:32], 1.0, None, ALU.add, None, [self.modT], [tmp])
            k.tt(self.G1[:], tmp[:], g1[:], ALU.mult, [tmp, g1], [self.G1])
            k.ts(tmp[:], self.modT[:, 64:80], 1.0, None, ALU.add, None, [self.modT, self.G1], [tmp])
            k.tt(self.G2[:], tmp[:], g2[:], ALU.mult, [tmp, g2], [self.G2])
            if "modT" in self.dbg:
                d, r = self.scratch("modT", [128, 96], F32)
                k.dma(d, self.modT[:], [self.modT], [r])

    def rms_stats_fm(self, st, loader, nchunks, width, scale_div, name):
        k = self.k
        rstd = k.sb(st, [128, S], F32, name)
        with ExitStack() as s2:
            xb = [k.sb(s2, [128, S], F32, "xld") for _ in range(2)]
            sq = [k.sb(s2, [128, S], BF16, "sq") for _ in range(2)]
            pss = [k.ps(s2, [128, 512], F32, "pss") for _ in range(4)]
            for kk in range(nchunks):
                xt, sqt = xb[kk % 2], sq[kk % 2]
                loader(kk, xt)
                k.act(sqt[:], xt[:], AF.Square, [xt], [sqt])
                for tg in range(4):
                    k.mm(pss[tg][:], self.ones16[:], sqt[:, tg * 512:(tg + 1) * 512], kk == 0, kk == nchunks - 1,
                         [self.ones16, sqt], [pss[tg]])
            for tg in range(4):
                sl = slice(tg * 512, (tg + 1) * 512)
                k.act(rstd[:, sl], pss[tg][:], AF.Sqrt, [pss[tg], self.eps_t], [rstd], bias=self.eps_t[:], scale=1.0 / scale_div)
            k.recip(rstd[:], rstd[:], [rstd], [rstd])
        k.barrier()
        return rstd

    def p12_proj(self):
        k, I = self.k, self.I
        xTv = I["xT"].rearrange("(k p) t -> k p t", p=128)
        qT_d, qT_r = self.scratch("qT", [8, 128, S], BF16)
        kcT_d, kcT_r = self.scratch("kcT", [2, 128, S], BF16)
        vcT_d, vcT_r = self.scratch("vcT", [2, 128, S], BF16)
        ksT_d, ksT_r = self.scratch("ksT", [2, 128, S], BF16)
        kwT_d, kwT_r = self.scratch("kwT", [2, 128, S], BF16)
        vs_d, vs_r = self.scratch("vs", [S, 256], BF16)
        vw_d, vw_r = self.scratch("vw", [S, 256], BF16)
        gt_d, gt_r = self.scratch("gates", [128, 16 * 24], F32)
        xrT_d, xrT_r = self.scratch("xrT", [8, 128, S], F32)
        xgT_d, xgT_r = self.scratch("xgT", [8, 128, S], F32)
        with ExitStack() as st:
            hT = k.sb(st, [128, 16, S], BF16, "hT")
            with ExitStack() as s1:
                rstd = self.rms_stats_fm(s1, lambda kk, dst: k.dma(dst[:], xTv[kk], [], [dst]), 16, S, float(D), "rstd1")
                xb = [k.sb(s1, [128, S], F32, "xld2") for _ in range(2)]
                tmp = [k.sb(s1, [128, S], F32, "tmp") for _ in range(2)]
                for kk in range(16):
                    xt, tp = xb[kk % 2], tmp[kk % 2]
                    k.dma(xt[:], xTv[kk], [], [xt])
                    k.stt(tp[:], xt[:], self.G1[:, kk:kk + 1], rstd[:], ALU.mult, ALU.mult, [xt, self.G1, rstd], [tp])
                    k.act(hT[:, kk, :], tp[:], AF.Identity, [tp, self.modT], [hT], bias=self.modT[:, kk:kk + 1], scale=1.0)
            if "hT" in self.dbg:
                d, r = self.scratch("hT", [16, 128, S], BF16)
                k.dma(d.rearrange("k p t -> p k t"), hT[:], [hT], [r])
            k.barrier()
            if "stop_p1" in self.dbg:
                return
            with ExitStack() as s2:
                wb = [k.sb(s2, [128, 16, 544], BF16, f"win{i}") for i in range(2)]
                cosT = k.sb(s2, [128, S], F32, "cosT")
                sinT = k.sb(s2, [128, S], F32, "sinT")
                rot16 = k.sb(s2, [128, 128], BF16, "rot16")
                gq = k.sb(s2, [128, 4], F32, "gq")
                gateb = k.sb(s2, [128, 24], F32, "gateb")
                k.dma(cosT[:], I["cosT"], [], [cosT])
                k.dma(sinT[:], I["sinT"], [], [sinT])
                rot32 = k.sb(s2, [128, 128], F32, "rot32")
                k.dma(rot32[:], I["rotm"], [], [rot32])
                k.copy(rot16[:], rot32[:], [rot32], [rot16])
                k.dma(gq[:, 0:1], I["qg"], [], [gq])
                k.dma(gq[:, 1:4], I["kgT"], [], [gq])
                k.dma(gateb[:], I["gateb"], [], [gateb])
                pz = [k.ps(s2, [128, 512], F32, "pz") for _ in range(4)]
                pss = [k.ps(s2, [128, 512], F32, "pss2") for _ in range(2)]
                prot = [k.ps(s2, [128, 512], F32, "prot") for _ in range(2)]
                ptm = pz[2:4]
                sq = [k.sb(s2, [128, 512], BF16, "sq2") for _ in range(2)]
                rs = [k.sb(s2, [128, 512], F32, "rs") for _ in range(2)]
                xn = [k.sb(s2, [128, 512], F32, "xn") for _ in range(2)]
                xn16 = [k.sb(s2, [128, 512], BF16, "xn16") for _ in range(2)]
                t1 = [k.sb(s2, [128, 512], F32, "t1") for _ in range(2)]
                t2 = [k.sb(s2, [128, 512], F32, "t2") for _ in range(2)]
                o16 = [k.sb(s2, [128, S], BF16, "o16") for _ in range(2)]
                o32 = [k.sb(s2, [128, S], F32, "o32") for _ in range(2)]
                vtm = k.sb(s2, [128, 16, 256], BF16, "vtm")
                gtm = k.sb(s2, [128, 16, 24], F32, "gtm")
                gtmp = k.sb(s2, [128, 24], F32, "gtmp")
                wv = I["w_in"].rearrange("(k p) n -> p k n", p=128)
                cnt = {"c": 0, "i": 0}

                pend = []

                def fm_chunk(b, co, kind, gcol, dst_ap, dst_res):
                    ci = cnt["c"]
                    cnt["c"] += 1
                    ob = (o32 if kind == "f32" else o16)[ci % 2]
                    for tg in range(4):
                        pend.append((b, co, kind, gcol, dst_ap, dst_res, ob, tg))

                def fm_s0(st_):
                    b, co, kind, gcol, dst_ap, dst_res, ob, tg = st_
                    i = cnt["i"]
                    cnt["i"] += 1
                    p = pz[i % 4]
                    sl = slice(tg * 512, (tg + 1) * 512)
                    for kk in range(16):
                        k.mm(p[:], b[:, kk, co:co + 128], hT[:, kk, sl], kk == 0, kk == 15, [b, hT], [p])
                    return (i, p)

                def fm_s1(st_, ip):
                    b, co, kind, gcol, dst_ap, dst_res, ob, tg = st_
                    i, p = ip
                    sl = slice(tg * 512, (tg + 1) * 512)
                    if kind in ("f32", "bf16"):
                        k.copy(ob[:, sl], p[:], [p], [ob], eng=("act" if tg % 2 == 0 else "dve"))
                    else:
                        a, a16 = xn[i % 2], xn16[i % 2]
                        if kind == "normrope":
                            sqt, pst, rst = sq[i % 2], pss[i % 2], rs[i % 2]
                            k.act(sqt[:], p[:], AF.Square, [p], [sqt])
                            k.mm(pst[:], self.ones16[:], sqt[:], True, True, [self.ones16, sqt], [pst])
                            k.act(rst[:], pst[:], AF.Sqrt, [pst, self.eps_t], [rst], bias=self.eps_t[:], scale=1.0 / 128.0)
                            k.recip(rst[:], rst[:], [rst], [rst])
                            k.stt(a[:], p[:], gq[:, gcol:gcol + 1], rst[:], ALU.mult, ALU.mult, [p, gq, rst], [a])
                        else:
                            k.copy(a[:], p[:], [p], [a], eng="dve")
                        k.copy(a16[:], a[:], [a], [a16], eng="act")
                        pr = prot[i % 2]
                        k.mm(pr[:], rot16[:], a16[:], True, True, [rot16, a16], [pr])
                        k.tt(t1[i % 2][:], a[:], cosT[:, sl], ALU.mult, [a, cosT], [t1[i % 2]])
                        k.tt(t2[i % 2][:], pr[:], sinT[:, sl], ALU.mult, [pr, sinT], [t2[i % 2]])
                        k.tt(ob[:, sl], t1[i % 2][:], t2[i % 2][:], ALU.add, [t1[i % 2], t2[i % 2]], [ob], eng="pool")
                    if tg == 3:
                        k.dma(dst_ap, ob[:], [ob], [dst_res])

                def fm_flush():
                    steps = list(pend)
                    del pend[:]
                    inflight = []
                    LA = 2
                    for n in range(min(LA, len(steps))):
                        inflight.append(fm_s0(steps[n]))
                    for n in range(len(steps)):
                        if n + LA < len(steps):
                            inflight.append(fm_s0(steps[n + LA]))
                        fm_s1(steps[n], inflight.pop(0))

                stg = [k.sb(s2, [128, 16, 272], F32, f"stg{i}") for i in range(2)]
                lcnt = {"g": 0, "s": 0}

                def load_group(c0, c1):
                    fm_flush()
                    b = wb[lcnt["g"] % 2]
                    lcnt["g"] += 1
                    w = c1 - c0
                    pieces = [(a, min(a + 256, w)) for a in range(0, w, 256)]
                    for (a0, a1) in pieces:
                        sg = stg[lcnt["s"] % 2]
                        lcnt["s"] += 1
                        k.dma(sg[:, :, 0:a1 - a0], wv[:, :, c0 + a0:c0 + a1], [], [sg], eng=("sp" if lcnt["s"] % 2 else "pool"))
                        k.copy(b[:, :, a0:a1], sg[:, :, 0:a1 - a0], [sg], [b], eng="pool")
                    return b

                def tm_block(b, co, width, post):
                    for tt_ in range(16):
                        p = ptm[tt_ % 2]
                        for kk in range(16):
                            k.mm(p[:, 0:width], hT[:, kk, tt_ * 128:(tt_ + 1) * 128], b[:, kk, co:co + width], kk == 0, kk == 15, [b, hT], [p])
                        post(tt_, p)

                b = load_group(0, 512)
                for j in range(4):
                    fm_chunk(b, j * 128, self.qkind if hasattr(self, "qkind") else "normrope", 0, qT_d[j], qT_r)
                b = load_group(512, 1024)
                for j in range(4):
                    fm_chunk(b, j * 128, self.qkind if hasattr(self, "qkind") else "normrope", 0, qT_d[4 + j], qT_r)
                if "stop_g0" in self.dbg:
                    fm_flush()
                    return
                b = load_group(1024, 1536)
                for j in range(2):
                    fm_chunk(b, j * 128, "rope", 0, kcT_d[j], kcT_r)
                for j in range(2):
                    fm_chunk(b, 256 + j * 128, "bf16", 0, vcT_d[j], vcT_r)
                b = load_group(1536, 2048)
                for j in range(2):
                    fm_chunk(b, j * 128, "normrope", 2, ksT_d[j], ksT_r)
                fm_flush()
                tm_block(b, 256, 256, lambda tt_, p: k.copy(vtm[:, tt_, :], p[:, 0:256], [p], [vtm], eng=("act" if tt_ % 2 == 0 else "dve")))
                k.dma(vs_d.rearrange("(t p) c -> p t c", p=128), vtm[:], [vtm], [vs_r])
                if "stop_g1" in self.dbg:
                    return
                if "rep_g3" in self.dbg:
                    b = load_group(1536, 2048)
                    for j in range(2):
                        fm_chunk(b, j * 128, "normrope", 2, ksT_d[j], ksT_r)
                    tm_block(b, 256, 256, lambda tt_, p: k.copy(vtm[:, tt_, :], p[:, 0:256], [p], [vtm], eng=("act" if tt_ % 2 == 0 else "dve")))
                    k.dma(vs_d.rearrange("(t p) c -> p t c", p=128), vtm[:], [vtm], [vs_r])
                    return
                b = load_group(2048, 2560)
                for j in range(2):
                    fm_chunk(b, j * 128, "normrope", 3, kwT_d[j], kwT_r)
                fm_flush()
                tm_block(b, 256, 256, lambda tt_, p: k.copy(vtm[:, tt_, :], p[:, 0:256], [p], [vtm], eng=("act" if tt_ % 2 == 0 else "dve")))
                b = load_group(2560, 2584)

                def post_g(tt_, p):
                    k.tt(gtmp[:], p[:, 0:24], gateb[:], ALU.add, [p, gateb], [gtmp])
                    k.act(gtmp[:], gtmp[:], AF.Exp, [gtmp], [gtmp], scale=-1.0)
                    k.ts(gtmp[:], gtmp[:], 1.0, None, ALU.add, None, [gtmp], [gtmp])
                    k.recip(gtm[:, tt_, :], gtmp[:], [gtmp], [gtm])
                tm_block(b, 0, 24, post_g)
                k.dma(vw_d.rearrange("(t p) c -> p t c", p=128), vtm[:], [vtm], [vw_r])
                k.dma(gt_d, gtm[:].rearrange("p t c -> p (t c)"), [gtm], [gt_r])
                if "stop_g2" in self.dbg:
                    return
                for half in range(2):
                    b = load_group(C_XR + half * 512, C_XR + (half + 1) * 512)
                    for j in range(4):
                        fm_chunk(b, j * 128, "f32", 0, xrT_d[half * 4 + j], xrT_r)
                for half in range(2):
                    b = load_group(C_XG + half * 512, C_XG + (half + 1) * 512)
                    for j in range(4):
                        fm_chunk(b, j * 128, "f32", 0, xgT_d[half * 4 + j], xgT_r)
                fm_flush()

    def p3_cmp(self):
        k, I = self.k, self.I
        kcT_d = self.scr["kcT"][0]
        vcT_d = self.scr["vcT"][0]
        kcmpT_d, kcmpT_r = self.scratch("kcmpT", [128, 2, 128], BF16)
        vcmp_d, vcmp_r = self.scratch("vcmp", [128, 2, 162], BF16)
        with ExitStack() as st:
            src = k.sb(st, [128, 4, S], BF16, "cmpsrc")
            wst = k.sb(st, [128, 32, 128], F32, "wst")
            w16 = [k.sb(st, [128, 32, 128], BF16, f"w16{i}") for i in range(2)]
            pe32 = k.sb(st, [128, 2, 32], F32, "pe32")
            peB = [k.sb(st, [128, 32, 127], BF16, f"peB{i}") for i in range(2)]
            kg0b = k.sb(st, [128, 128], F32, "kg0b")
            ovl = k.sb(st, [128, 32], F32, "ovl")
            ss = k.sb(st, [128, 1], F32, "ss")
            junk = k.sb(st, [128, 128], F32, "junk")
            kn16 = k.sb(st, [128, 128], BF16, "kn16")
            kT16 = k.sb(st, [128, 2, 128], BF16, "kT16")
            va16 = k.sb(st, [128, 2, 162], BF16, "va16")
            pc = [k.ps(st, [128, 128], F32, "pc") for _ in range(2)]
            ptr = k.ps(st, [128, 128], F32, "ptr")
            for g in range(2):
                k.dma(src[:, g, :], kcT_d[g], [self.scr["kcT"][1]], [src])
                k.dma(src[:, 2 + g, :], vcT_d[g], [self.scr["vcT"][1]], [src])
            k.dma(pe32[:, 0, :], I["pekT"], [], [pe32])
            k.dma(pe32[:, 1, :], I["pevT"], [], [pe32])
            k.dma(kg0b[:], I["kg0b"], [], [kg0b])
            k.dma(ovl[:], I["ovl"], [], [ovl])
            k.memset(kT16[:], 0.0, [kT16])
            k.memset(va16[:], 0.0, [va16])
            for kv, wname in enumerate(("cwk", "cwv")):
                k.dma(wst[:], I[wname].rearrange("(l d) o -> d l o", d=128), [], [wst])
                k.copy(w16[kv][:], wst[:], [wst], [w16[kv]], eng="pool")
                k.copy(peB[kv][:], pe32[:, kv, :].unsqueeze(2).to_broadcast([128, 32, 127]), [pe32], [peB[kv]])
            for kv in range(2):
                for g in range(2):
                    p = pc[(kv * 2 + g) % 2]
                    for l in range(32):
                        k.mm(p[0:127, :], src[:, kv * 2 + g, l:l + 16 * 126 + 1:16], w16[kv][:, l, :], l == 0, False, [src, w16[kv]], [p])
                    for l in range(32):
                        k.mm(p[0:127, :], peB[kv][:, l, :], w16[kv][:, l, :], False, l == 31, [peB[kv], w16[kv]], [p])
                    if kv == 0:
                        k.act(junk[0:127, :], p[0:127, :], AF.Square, [p], [junk, ss], accum_out=ss[0:127, :])
                        k.act(ss[0:127, :], ss[0:127, :], AF.Sqrt, [ss, self.eps_t], [ss], bias=self.eps_t[0:127, :], scale=1.0 / 128.0)
                        k.recip(ss[0:127, :], ss[0:127, :], [ss], [ss])
                        k.stt(kn16[0:127, :], p[0:127, :], ss[0:127, :], kg0b[0:127, :], ALU.mult, ALU.mult, [p, ss, kg0b], [kn16])
                        k.mm(ptr[:, 0:127], kn16[0:127, :], self.ident16[0:127, 0:127], True, True, [kn16, self.ident16], [ptr])
                        k.copy(kT16[:, g, 0:127], ptr[:, 0:127], [ptr], [kT16])
                    else:
                        k.copy(va16[0:127, g, 0:128], p[0:127, :], [p], [va16])
                        k.memset(va16[0:127, g, 128:129], 1.0, [va16])
                        k.copy(va16[0:127, g, 129:161], ovl[0:127, :], [ovl], [va16])
            k.dma(kcmpT_d, kT16[:], [kT16], [kcmpT_r])
            k.dma(vcmp_d, va16[:], [va16], [vcmp_r])

    def p4_attn(self):
        k, I = self.k, self.I
        sc = self.scr
        yT_d, yT_r = self.scratch("yT", [16, 128, S], BF16)
        with ExitStack() as st:
            qT = k.sb(st, [128, 8, S], BF16, "qT")
            ksT = k.sb(st, [128, 2, S], BF16, "ksT")
            kwT = k.sb(st, [128, 2, S], BF16, "kwT")
            kcT = k.sb(st, [128, 2, 128], BF16, "kcT")
            vsa = k.sb(st, [128, 16, 2, 130], BF16, "vsa")
            vwa = k.sb(st, [128, 16, 2, 130], BF16, "vwa")
            vca = k.sb(st, [128, 2, 162], BF16, "vca")
            gts = k.sb(st, [128, 16, 24], F32, "gts")
            maskc = k.sb(st, [128, S], F32, "maskc")
            cm = k.sb(st, [128, 4, 512], F32, "cm")
            wm = k.sb(st, [128, 8, 512], F32, "wm")
            ex32 = k.sb(st, [32, 16, 128], F32, "ex32")
            ex16 = k.sb(st, [32, 16, 128], BF16, "ex16")
            vm = k.sb(st, [128, 16, 32], F32, "vm")
            fb = k.sb(st, [128, 16, 32], F32, "fb")
            gab = k.sb(st, [128, 1024], F32, "gab")
            for j in range(8):
                k.dma(qT[:, j, :], sc["qT"][0][j], [sc["qT"][1]], [qT], eng=("sp" if j % 2 else "pool"))
            for g in range(2):
                k.dma(ksT[:, g, :], sc["ksT"][0][g], [sc["ksT"][1]], [ksT])
                k.dma(kwT[:, g, :], sc["kwT"][0][g], [sc["kwT"][1]], [kwT])
            k.dma(kcT[:], sc["kcmpT"][0], [sc["kcmpT"][1]], [kcT])
            k.dma(vca[:], sc["vcmp"][0], [sc["vcmp"][1]], [vca])
            k.memset(vsa[:], 1.0, [vsa])
            k.memset(vwa[:], 1.0, [vwa])
            for g in range(2):
                k.dma(vsa[:, :, g, 0:128], sc["vs"][0].rearrange("(c p) x -> p c x", p=128)[:, :, g * 128:(g + 1) * 128], [sc["vs"][1]], [vsa])
                k.dma(vwa[:, :, g, 0:128], sc["vw"][0].rearrange("(c p) x -> p c x", p=128)[:, :, g * 128:(g + 1) * 128], [sc["vw"][1]], [vwa])
            k.dma(gts[:].rearrange("p t c -> p (t c)"), sc["gates"][0], [sc["gates"][1]], [gts])
            k.dma(maskc[:], I["maskc"], [], [maskc])
            k.dma(cm[:].rearrange("p a b -> p (a b)"), I["cm"], [], [cm])
            k.dma(wm[:].rearrange("p a b -> p (a b)"), I["wm"], [], [wm])
            k.dma(ex32[:].rearrange("p a b -> p (a b)"), I["ex"], [], [ex32])
            k.copy(ex16[:], ex32[:], [ex32], [ex16])
            k.dma(vm[:].rearrange("p a b -> p (a b)"), I["vm"], [], [vm])
            k.dma(fb[:].rearrange("p a b -> p (a b)"), I["fb"], [], [fb])
            k.dma(gab[:], I["gab"], [], [gab])
            O = k.sb(st, [128, 4, 1024], F32, "O")
            Osub = [Res() for _ in range(4)]
            imp = [k.sb(st, [128, 32], F32, f"imp{i}") for i in range(4)]
            e16 = [k.sb(st, [128, 512], BF16, f"e16{i}") for i in range(3)]
            p16 = [k.sb(st, [128, 512], BF16, f"p16{i}") for i in range(3)]
            mskS = k.sb(st, [128, 16, 512], BF16, "mskS")
            selT16 = k.sb(st, [32, 512], BF16, "selT16")
            sel16 = [k.sb(st, [128, 32], BF16, f"sel16{i}") for i in range(2)]
            imp2 = [k.sb(st, [128, 32], F32, f"imp2{i}") for i in range(2)]
            wk = [k.sb(st, [128, 32], F32, f"wk{i}") for i in range(2)]
            m8 = [k.sb(st, [128, 16], F32, f"m8{i}") for i in range(2)]
            den = [k.sb(st, [128, 2], F32, f"den{i}") for i in range(4)]
            ssq = k.sb(st, [128, 1], F32, "ssq")
            junk = k.sb(st, [128, 1024], F32, "junk4")
            yn16 = k.sb(st, [128, 1024], BF16, "yn16")
            yT16 = [k.sb(st, [128, 512], BF16, f"yT16{i}") for i in range(2)]
            pS = [k.ps(st, [128, 512], F32, "pS") for _ in range(2)]
            pA = k.ps(st, [128, 512], F32, "pA")
            pACC = [k.ps(st, [128, 512], F32, "pACC") for _ in range(4)]
            pM = [k.ps(st, [128, 512], F32, "pM")] * 2
            pT = pA
            cnt = {"s": 0, "e": 0, "m": 0, "d": 0, "y": 0, "sel": 0}

            def finish_head(sub, hd, acc_ap, den_ap, tt_, gcol, first, Rp):
                dn = den[cnt["d"] % 4]
                cnt["d"] += 1
                k.ts(dn[:, 0:1], den_ap, 1e-30, None, ALU.max, None, Rp, [dn])
                k.recip(dn[:, 0:1], dn[:, 0:1], [dn], [dn])
                k.tt(dn[:, 1:2], dn[:, 0:1], gts[:, tt_, gcol:gcol + 1], ALU.mult, [dn, gts], [dn])
                osl = O[:, sub, hd * 128:(hd + 1) * 128]
                if first:
                    k.ts(osl, acc_ap, dn[:, 1:2], None, ALU.mult, None, Rp + [dn], [Osub[sub]])
                else:
                    k.stt(osl, acc_ap, dn[:, 1:2], osl, ALU.mult, ALU.add, Rp + [dn, Osub[sub]], [Osub[sub]])
                return dn

            def run_steps(i, steps):
                qsl = slice(i * 512, (i + 1) * 512)
                state = {}

                def s0(n):
                    keyT, vaug, g, r, kc, mask_of, first, last, gbranch = steps[n]
                    ps_ = pS[cnt["s"] % 2]
                    cnt["s"] += 1
                    k.mm(ps_[:], keyT[:, g, kc * 128:(kc + 1) * 128], qT[:, g * 4 + r, qsl], True, True, [keyT, qT], [ps_])
                    state[n] = ps_

                def s12(n):
                    keyT, vaug, g, r, kc, mask_of, first, last, gbranch = steps[n]
                    ps_ = state.pop(n)
                    hd = g * 4 + r
                    e = e16[cnt["e"] % 3]
                    p_ = p16[cnt["e"] % 3]
                    cnt["e"] += 1
                    k.act(e[:], ps_[:], AF.Exp, [ps_], [e], scale=SCALE)
                    mk, mkR, eng = mask_of(kc)
                    k.tt(p_[:], e[:], mk, ALU.mult, [e, mkR], [p_], eng=eng)
                    for sub in range(4):
                        acc = pACC[sub]
                        k.mm(acc[:, 0:129], p_[:, sub * 128:(sub + 1) * 128], vaug[:, kc, g, 0:129], first, last, [p_, vaug], [acc])
                    if last:
                        gcol = g * 12 + r * 3 + gbranch
                        dns = [den[sub] for sub in range(4)]
                        for sub in range(4):
                            k.ts(dns[sub][:, 0:1], pACC[sub][:, 128:129], 1e-30, None, ALU.max, None, [pACC[sub]], [dns[sub]])
                        for sub in range(4):
                            k.recip(dns[sub][:, 0:1], dns[sub][:, 0:1], [dns[sub]], [dns[sub]])
                        for sub in range(4):
                            k.tt(dns[sub][:, 1:2], dns[sub][:, 0:1], gts[:, i * 4 + sub, gcol:gcol + 1], ALU.mult, [dns[sub], gts], [dns[sub]])
                        for sub in range(4):
                            osl = O[:, sub, hd * 128:(hd + 1) * 128]
                            k.stt(osl, pACC[sub][:, 0:128], dns[sub][:, 1:2], osl, ALU.mult, ALU.add, [pACC[sub], dns[sub], Osub[sub]], [Osub[sub]])

                s0(0)
                for n in range(len(steps)):
                    if n + 1 < len(steps):
                        s0(n + 1)
                    s12(n)

            for i in range(4):
                qsl = slice(i * 512, (i + 1) * 512)
                for g in range(2):
                    for r in range(4):
                        hd = g * 4 + r
                        ps_ = pS[cnt["s"] % 2]
                        cnt["s"] += 1
                        k.mm(ps_[0:127, :], kcT[:, g, 0:127], qT[:, hd, qsl], True, True, [kcT, qT], [ps_])
                        e = e16[cnt["e"] % 3]
                        p_ = p16[cnt["e"] % 3]
                        cnt["e"] += 1
                        k.act(e[0:127, :], ps_[0:127, :], AF.Exp, [ps_], [e], scale=SCALE)
                        k.tt(p_[0:127, :], e[0:127, :], maskc[0:127, qsl], ALU.mult, [e, maskc], [p_])
                        for sub in range(4):
                            k.mm(pA[:, 0:161], p_[0:127, sub * 128:(sub + 1) * 128], vca[0:127, g, 0:161], True, True, [p_, vca], [pA])
                            dn = finish_head(sub, hd, pA[:, 0:128], pA[:, 128:129], i * 4 + sub, g * 12 + r * 3, True, [pA])
                            if r == 0:
                                k.ts(imp[sub][:], pA[:, 129:161], dn[:, 0:1], None, ALU.mult, None, [pA, dn], [imp[sub]])
                            else:
                                k.stt(imp[sub][:], pA[:, 129:161], dn[:, 0:1], imp[sub][:], ALU.mult, ALU.add, [pA, dn, imp[sub]], [imp[sub]])
                    psel = pM[cnt["m"] % 2]
                    cnt["m"] += 1
                    for sub in range(4):
                        tt_ = i * 4 + sub
                        j = cnt["sel"] % 2
                        cnt["sel"] += 1
                        k.tt(imp2[j][:], imp[sub][:], vm[:, tt_, :], ALU.mult, [imp[sub], vm], [imp2[j]])
                        k.tt(imp2[j][:], imp2[j][:], fb[:, tt_, :], ALU.add, [imp2[j], fb], [imp2[j]])
                        k.fn("dve", lambda e, o=m8[j][:, 0:8], a=imp2[j][:]: e.max(out=o, in_=a), [imp2[j]], [m8[j]])
                        k.fn("dve", lambda e, o=wk[j][:], a=m8[j][:, 0:8], b=imp2[j][:]: e.match_replace(out=o, in_to_replace=a, in_values=b, imm_value=-1e30), [imp2[j], m8[j]], [wk[j]])
                        k.fn("dve", lambda e, o=m8[j][:, 8:16], a=wk[j][:]: e.max(out=o, in_=a), [wk[j]], [m8[j]])
                        k.ts(sel16[j][:], imp2[j][:], m8[j][:, 15:16], None, ALU.is_ge, None, [imp2[j], m8[j]], [sel16[j]])
                        k.mm(psel[0:32, sub * 128:(sub + 1) * 128], sel16[j][:], self.ident16[:], True, True, [sel16[j], self.ident16], [psel])
                    k.copy(selT16[:], psel[0:32, :], [psel], [selT16], eng="act")
                    nkc = 4 * i + 4
                    for kc in range(nkc):
                        pm_ = pM[cnt["m"] % 2]
                        cnt["m"] += 1
                        k.mm(pm_[:], ex16[:, kc, :], selT16[:], True, True, [ex16, selT16], [pm_])
                        if kc >= 4 * i:
                            k.tt(mskS[:, kc, :], pm_[:], cm[:, kc - 4 * i, :], ALU.mult, [pm_, cm], [mskS])
                        else:
                            k.copy(mskS[:, kc, :], pm_[:], [pm_], [mskS], eng="act")
                    steps = []
                    for r in range(4):
                        for kc in range(nkc):
                            steps.append((ksT, vsa, g, r, kc, (lambda kc_: (mskS[:, kc_, :], mskS, "pool")), kc == 0, kc == nkc - 1, 1))
                    for r in range(4):
                        chunks = list(range(max(0, 4 * i - 4), 4 * i + 4))
                        for kc in chunks:
                            steps.append((kwT, vwa, g, r, kc, (lambda kc_, i_=i: (wm[:, kc_ - 4 * i_ + 4, :], wm, "dve")), kc == chunks[0], kc == chunks[-1], 2))
                    run_steps(i, steps)
                yt = yT16[i % 2]
                for c in range(8):
                    pass
                ytiles = []
                for sub in range(4):
                    k.act(junk[:], O[:, sub, :], AF.Square, [Osub[sub]], [junk, ssq], accum_out=ssq[:])
                    k.act(ssq[:], ssq[:], AF.Sqrt, [ssq, self.eps_t], [ssq], bias=self.eps_t[:], scale=1.0 / 1024.0)
                    k.recip(ssq[:], ssq[:], [ssq], [ssq])
                    k.stt(yn16[:], O[:, sub, :], ssq[:], gab[:], ALU.mult, ALU.mult, [Osub[sub], ssq, gab], [yn16])
                    for half in range(2):
                        for cc in range(4):
                            c = half * 4 + cc
                            k.mm(pT[:, cc * 128:(cc + 1) * 128], yn16[:, c * 128:(c + 1) * 128], self.ident16[:], True, True, [yn16, self.ident16], [pT])
                        dst = k.sb(st, [128, 4, 128], BF16, "ytmp") if False else None
                        yb = yT16[cnt["y"] % 2]
                        cnt["y"] += 1
                        k.copy(yb[:], pT[:], [pT], [yb], eng=("act" if half == 0 else "dve"))
                        for cc in range(4):
                            c = half * 4 + cc
                            k.dma(yT_d[c][:, i * 512 + sub * 128:i * 512 + (sub + 1) * 128], yb[:, cc * 128:(cc + 1) * 128], [yb], [yT_r])
            if "O_dbg" in self.dbg:
                pass

    def p5_rnn(self):
        k, I = self.k, self.I
        sc = self.scr
        yT_d, yT_r = sc["yT"]
        xr_d, xr_r = sc["xrT"]
        xg_d, xg_r = sc["xgT"]
        with ExitStack() as st:
            cw = k.sb(st, [128, 8, 4], F32, "cw")
            cb = k.sb(st, [128, 8], F32, "cb")
            nba = k.sb(st, [128, 8], F32, "nba")
            nbi = k.sb(st, [128, 8], F32, "nbi")
            lam = k.sb(st, [128, 8], F32, "lam")
            clam = k.sb(st, [128, 8], F32, "clam")
            gr = k.sb(st, [128, 8], F32, "gr")
            wst = k.sb(st, [128, 2, 128], F32, "wst5")
            w16 = [k.sb(st, [128, 2, 128], BF16, f"w165{i}") for i in range(2)]
            k.dma(cw[:].rearrange("p a b -> p (a b)"), I["convw"], [], [cw])
            k.dma(cb[:], I["convb"], [], [cb])
            k.dma(nba[:], I["lba"], [], [nba])
            k.dma(nbi[:], I["lbi"], [], [nbi])
            k.dma(lam[:], I["lam"], [], [lam])
            k.dma(gr[:], I["gr"], [], [gr])
            k.ts(nba[:], nba[:], -1.0, None, ALU.mult, None, [nba], [nba])
            k.ts(nbi[:], nbi[:], -1.0, None, ALU.mult, None, [nbi], [nbi])
            k.act(clam[:], lam[:], AF.Exp, [lam], [clam], scale=-1.0)
            k.ts(clam[:], clam[:], 1.0, None, ALU.add, None, [clam], [clam])
            k.act(clam[:], clam[:], AF.Ln, [clam], [clam])
            k.ts(clam[:], clam[:], -8.0, None, ALU.mult, None, [clam], [clam])
            orn = k.sb(st, [128, 8, S], F32, "orn")
            xp = k.sb(st, [128, S + 4], F32, "xp")
            xg = k.sb(st, [128, S], F32, "xg")
            u = k.sb(st, [128, S], F32, "u")
            u16 = k.sb(st, [128, S], BF16, "u16")
            ra = k.sb(st, [128, S], F32, "ra")
            ig = k.sb(st, [128, S], F32, "ig")
            bb = k.sb(st, [128, S], F32, "bb")
            sq16 = k.sb(st, [128, S], BF16, "sq165")
            pg = [k.ps(st, [128, 512], F32, "pg") for _ in range(2)]
            pss = [k.ps(st, [128, 512], F32, "pss5") for _ in range(4)]
            k.memset(xp[:, 0:4], 0.0, [xp])
            ci = 0
            for n in range(8):
                k.dma(xp[:, 4:S + 4], xr_d[n], [xr_r], [xp])
                k.dma(xg[:], xg_d[n], [xg_r], [xg], eng="pool")
                wb = w16[n % 2]
                k.dma(wst[:, 0, :], I["wa"][n], [], [wst])
                k.dma(wst[:, 1, :], I["wi"][n], [], [wst])
                k.copy(wb[:], wst[:], [wst], [wb], eng="pool")
                k.ts(u[:], xp[:, 1:S + 1], cw[:, n, 0:1], cb[:, n:n + 1], ALU.mult, ALU.add, [xp, cw, cb], [u])
                for i_ in range(1, 4):
                    k.stt(u[:], xp[:, 1 + i_:S + 1 + i_], cw[:, n, i_:i_ + 1], u[:], ALU.mult, ALU.add, [xp, cw, u], [u])
                k.copy(u16[:], u[:], [u], [u16], eng="act")
                for which, dst, nb in ((0, ra, nba), (1, ig, nbi)):
                    for tg in range(4):
                        p = pg[ci % 2]
                        ci += 1
                        sl = slice(tg * 512, (tg + 1) * 512)
                        k.mm(p[:], wb[:, which, :], u16[:, sl], True, True, [wb, u16], [p])
                        k.act(dst[:, sl], p[:], AF.Exp, [p, nb], [dst], bias=nb[:, n:n + 1], scale=-1.0)
                    k.ts(dst[:], dst[:], 1.0, None, ALU.add, None, [dst], [dst], eng="pool")
                    k.recip(dst[:], dst[:], [dst], [dst])
                k.act(ra[:], ra[:], AF.Exp, [ra, clam], [ra], scale=clam[:, n:n + 1])
                k.tt(bb[:], ra[:], ra[:], ALU.mult, [ra], [bb])
                k.ts(bb[:], bb[:], -1.0, 1.0, ALU.mult, ALU.add, [bb], [bb])
                k.act(bb[:], bb[:], AF.Sqrt, [bb], [bb])
                k.tt(bb[:], bb[:], ig[:], ALU.mult, [bb, ig], [bb], eng="pool")
                k.tt(bb[:], bb[:], u[:], ALU.mult, [bb, u], [bb])
                k.fn("dve", lambda e, o=ig[:], a=ra[:], b=bb[:]: e.tensor_tensor_scan(o, a, b, 0.0, ALU.mult, ALU.add), [ra, bb, ig], [ig])
                k.act(xg[:], xg[:], AF.Gelu, [xg], [xg])
                k.tt(orn[:, n, :], xg[:], ig[:], ALU.mult, [xg, ig], [orn])
                k.act(sq16[:], orn[:, n, :], AF.Square, [orn], [sq16])
                for tg in range(4):
                    k.mm(pss[tg][:], self.ones16[:], sq16[:, tg * 512:(tg + 1) * 512], n == 0, n == 7, [self.ones16, sq16], [pss[tg]])
            rstd = ra
            for tg in range(4):
                sl = slice(tg * 512, (tg + 1) * 512)
                k.act(rstd[:, sl], pss[tg][:], AF.Sqrt, [pss[tg], self.eps_t], [rstd], bias=self.eps_t[:], scale=1.0 / 1024.0)
            k.recip(rstd[:], rstd[:], [rstd], [rstd])
            for n in range(8):
                k.stt(u16[:], orn[:, n, :], gr[:, n:n + 1], rstd[:], ALU.mult, ALU.mult, [orn, gr, rstd], [u16])
                k.dma(yT_d[8 + n], u16[:], [u16], [yT_r])

    def p6_out(self):
        k, I = self.k, self.I
        sc = self.scr
        yT_d, yT_r = sc["yT"]
        x1_d, x1_r = self.scratch("x1T", [16, 128, S], F32)
        h2_d, h2_r = self.scratch("h2T", [16, 128, S], BF16)
        xTv = I["xT"].rearrange("(k p) t -> k p t", p=128)
        wv = I["w_out"].rearrange("(k p) n -> p k n", p=128)
        with ExitStack() as st:
            rstd = k.sb(st, [128, S], F32, "rstd2")
            with ExitStack() as s1:
                yT = k.sb(s1, [128, 16, S], BF16, "yTs")
                wb = [k.sb(s1, [128, 16, 256], BF16, f"wo{i}") for i in range(2)]
                stg = [k.sb(s1, [128, 16, 128], F32, f"wos{i}") for i in range(2)]
                xb = [k.sb(s1, [128, S], F32, f"x6{i}") for i in range(2)]
                ob = [k.sb(s1, [128, S], F32, f"o6{i}") for i in range(2)]
                sq = [k.sb(s1, [128, S], BF16, f"sq6{i}") for i in range(2)]
                pz = [k.ps(s1, [128, 512], F32, "pz6") for _ in range(2)]
                pss = [k.ps(s1, [128, 512], F32, "pss6") for _ in range(4)]
                for c in range(16):
                    k.dma(yT[:, c, :], yT_d[c], [yT_r], [yT], eng=("sp" if c % 2 else "pool"))
                ci = 0
                for jg in range(8):
                    b = wb[jg % 2]
                    for h in range(2):
                        sg = stg[(jg * 2 + h) % 2]
                        k.dma(sg[:], wv[:, :, jg * 256 + h * 128:jg * 256 + (h + 1) * 128], [], [sg])
                        k.copy(b[:, :, h * 128:(h + 1) * 128], sg[:], [sg], [b], eng="pool")
                    for jj in range(2):
                        j = jg * 2 + jj
                        xt, ot, sqt = xb[j % 2], ob[j % 2], sq[j % 2]
                        k.dma(xt[:], xTv[j], [], [xt])
                        for tg in range(4):
                            p = pz[ci % 2]
                            ci += 1
                            sl = slice(tg * 512, (tg + 1) * 512)
                            for c in range(16):
                                k.mm(p[:], b[:, c, jj * 128:(jj + 1) * 128], yT[:, c, sl], c == 0, c == 15, [b, yT], [p])
                            k.stt(ot[:, sl], p[:], self.modT[:, 32 + j:33 + j], xt[:, sl], ALU.mult, ALU.add, [p, self.modT, xt], [ot])
                        k.dma(x1_d[j], ot[:], [ot], [x1_r])
                        k.act(sqt[:], ot[:], AF.Square, [ot], [sqt])
                        for tg in range(4):
                            k.mm(pss[tg][:], self.ones16[:], sqt[:, tg * 512:(tg + 1) * 512], j == 0, j == 15, [self.ones16, sqt], [pss[tg]])
                for tg in range(4):
                    sl = slice(tg * 512, (tg + 1) * 512)
                    k.act(rstd[:, sl], pss[tg][:], AF.Sqrt, [pss[tg], self.eps_t], [rstd], bias=self.eps_t[:], scale=1.0 / float(D))
                k.recip(rstd[:], rstd[:], [rstd], [rstd])
            k.barrier()
            with ExitStack() as s2:
                xb = [k.sb(s2, [128, S], F32, f"x6b{i}") for i in range(2)]
                tp = [k.sb(s2, [128, S], F32, f"t6b{i}") for i in range(2)]
                hb = [k.sb(s2, [128, S], BF16, f"h6b{i}") for i in range(2)]
                for j in range(16):
                    xt, tt_, ht = xb[j % 2], tp[j % 2], hb[j % 2]
                    k.dma(xt[:], x1_d[j], [x1_r], [xt])
                    k.stt(tt_[:], xt[:], self.G2[:, j:j + 1], rstd[:], ALU.mult, ALU.mult, [xt, self.G2, rstd], [tt_])
                    k.act(ht[:], tt_[:], AF.Identity, [tt_, self.modT], [ht], bias=self.modT[:, 48 + j:49 + j], scale=1.0)
                    k.dma(h2_d[j], ht[:], [ht], [h2_r])

    def p7_peer(self):
        k, I = self.k, self.I
        sc = self.scr
        h2_d, h2_r = sc["h2T"]
        x1_d, x1_r = sc["x1T"]
        qp_d, qp_r = self.scratch("qpT", [16, 128, S], BF16)
        wv = I["wq"].rearrange("(k p) n -> p k n", p=128)
        with ExitStack() as st:
            h2T = k.sb(st, [128, 16, S], BF16, "h2Ts")
            wb = [k.sb(st, [128, 16, 256], BF16, f"wq{i}") for i in range(2)]
            stg = [k.sb(st, [128, 16, 128], F32, f"wqs{i}") for i in range(2)]
            ob = [k.sb(st, [128, S], BF16, f"oq{i}") for i in range(2)]
            pz = [k.ps(st, [128, 512], F32, "pz7") for _ in range(2)]
            for c in range(16):
                k.dma(h2T[:, c, :], h2_d[c], [h2_r], [h2T], eng=("sp" if c % 2 else "pool"))
            ci = 0
            for jg in range(8):
                b = wb[jg % 2]
                for h in range(2):
                    sg = stg[(jg * 2 + h) % 2]
                    k.dma(sg[:], wv[:, :, jg * 256 + h * 128:jg * 256 + (h + 1) * 128], [], [sg])
                    k.copy(b[:, :, h * 128:(h + 1) * 128], sg[:], [sg], [b], eng="pool")
                for jj in range(2):
                    j = jg * 2 + jj
                    ot = ob[j % 2]
                    for tg in range(4):
                        p = pz[ci % 2]
                        ci += 1
                        sl = slice(tg * 512, (tg + 1) * 512)
                        for c in range(16):
                            k.mm(p[:], b[:, c, jj * 128:(jj + 1) * 128], h2T[:, c, sl], c == 0, c == 15, [b, h2T], [p])
                        k.copy(ot[:, sl], p[:], [p], [ot], eng=("act" if tg % 2 == 0 else "dve"))
                    k.dma(qp_d[j], ot[:], [ot], [qp_r])
        k.barrier()
        SLACK = 1.0 - 4e-6
        with ExitStack() as st:
            h2s = k.sb(st, [128, 16, 512], BF16, "h2s")
            E1s = k.sb(st, [128, 4, 8, 128], F32, "E1s")
            E2s = k.sb(st, [128, 4, 8, 128], F32, "E2s")
            dg16 = k.sb(st, [128, 4, 8, 128], BF16, "dg16")
            accT = k.sb(st, [128, 16, 512], F32, "accT")
            keys16 = k.sb(st, [128, 16, 128], BF16, "keys16")
            with ExitStack() as s0:
                k32 = k.sb(s0, [128, 16, 128], F32, "k32")
                k.dma(k32[:].rearrange("p a b -> p (a b)"), I["keysT"], [], [k32])
                k.copy(keys16[:], k32[:], [k32], [keys16])
            k.barrier()
            for su in range(4):
                tsl = slice(su * 512, (su + 1) * 512)
                k.dma(h2s[:], h2_d.rearrange("c p t -> p c t")[:, :, tsl], [h2_r], [h2s])
                with ExitStack() as sb_:
                    qps = k.sb(sb_, [128, 16, 512], BF16, "qps")
                    s_sb = k.sb(sb_, [128, 16, 128], F32, "s_sb")
                    wk = k.sb(sb_, [128, 16, 128], F32, "wk7")
                    wk2 = k.sb(sb_, [128, 8, 256], F32, "wk72")
                    v16R = [Res() for _ in range(16)]
                    wkR = [Res() for _ in range(16)]
                    c16R = [Res() for _ in range(8)]
                    wk2R = [Res() for _ in range(8)]
                    v16 = k.sb(sb_, [128, 16, 16], F32, "v16")
                    cand = k.sb(sb_, [128, 8, 256], F32, "cand")
                    c16 = k.sb(sb_, [128, 8, 16], F32, "c16")
                    en = k.sb(sb_, [128, 8, 16], F32, "en")
                    negm = k.sb(sb_, [128, 16], F32, "negm")
                    negM = k.sb(sb_, [128, 8], F32, "negM")
                    Z = k.sb(sb_, [128, 8], F32, "Z")
                    th = k.sb(sb_, [128, 8], F32, "th")
                    cf = k.sb(sb_, [128, 8], F32, "cf")
                    E1t = k.sb(sb_, [128, 8, 128], F32, "E1t")
                    pS_ = [k.ps(sb_, [128, 4, 128], F32, "pS7") for _ in range(2)]
                    k.dma(qps[:], qp_d.rearrange("c p t -> p c t")[:, :, tsl], [qp_r], [qps])
                    for tl in range(4):
                        for hg in range(4):
                            p = pS_[hg % 2]
                            for q4 in range(4):
                                hp = hg * 4 + q4
                                k.mm(p[:, q4, :], qps[:, hp, tl * 128:(tl + 1) * 128], keys16[:, hp, :], True, True, [qps, keys16], [p])
                            k.copy(s_sb[:, hg * 4:(hg + 1) * 4, :], p[:], [p], [s_sb], eng=("act" if hg % 2 == 0 else "dve"))
                        for hp in range(16):
                            k.fn("dve", lambda e, o=v16[:, hp, 0:8], a=s_sb[:, hp, :]: e.max(out=o, in_=a), [s_sb], [v16R[hp]])
                        for hp in range(16):
                            k.fn("dve", lambda e, o=wk[:, hp, :], a=v16[:, hp, 0:8], b=s_sb[:, hp, :]: e.match_replace(out=o, in_to_replace=a, in_values=b, imm_value=-1e30), [s_sb, v16R[hp]], [wkR[hp]])
                        for hp in range(16):
                            k.fn("dve", lambda e, o=v16[:, hp, 8:16], a=wk[:, hp, :]: e.max(out=o, in_=a), [wkR[hp]], [v16R[hp]])
                        v16r = v16[:].rearrange("p (h two) i -> p h two i", two=2)
                        k.tt(cand[:].rearrange("p h (i j) -> p h i j", i=16),
                             v16r[:, :, 0, :].unsqueeze(3).to_broadcast([128, 8, 16, 16]),
                             v16r[:, :, 1, :].unsqueeze(2).to_broadcast([128, 8, 16, 16]), ALU.add, v16R, [cand])
                        for h in range(8):
                            k.fn("dve", lambda e, o=c16[:, h, 0:8], a=cand[:, h, :]: e.max(out=o, in_=a), [cand], [c16R[h]])
                        for h in range(8):
                            k.fn("dve", lambda e, o=wk2[:, h, :], a=c16[:, h, 0:8], b=cand[:, h, :]: e.match_replace(out=o, in_to_replace=a, in_values=b, imm_value=-1e30), [cand, c16R[h]], [wk2R[h]])
                        for h in range(8):
                            k.fn("dve", lambda e, o=c16[:, h, 8:16], a=wk2[:, h, :]: e.max(out=o, in_=a), [wk2R[h]], [c16R[h]])
                        k.ts(negm[:], v16[:, :, 0], -1.0, None, ALU.mult, None, v16R, [negm])
                        k.ts(negM[:], c16[:, :, 0], -1.0, None, ALU.mult, None, c16R, [negM])
                        for h in range(8):
                            k.act(en[:, h, :], c16[:, h, :], AF.Exp, [c16R[h], negM], [en, Z], bias=negM[:, h:h + 1], scale=1.0, accum_out=Z[:, h:h + 1])
                        k.recip(Z[:], Z[:], [Z], [Z])
                        k.tt(th[:], en[:, :, 15], Z[:], ALU.mult, [en, Z], [th])
                        k.ts(th[:], th[:], SLACK, None, ALU.mult, None, [th], [th])
                        k.ts(cf[:], en[:, :, 15], SLACK, None, ALU.mult, None, [en], [cf])
                        k.recip(cf[:], cf[:], [cf], [cf])
                        for h in range(8):
                            k.act(E2s[:, tl, h, :], s_sb[:, 2 * h + 1, :], AF.Exp, [s_sb, negm], [E2s], bias=negm[:, 2 * h + 1:2 * h + 2], scale=1.0)
                            k.act(E1t[:, h, :], s_sb[:, 2 * h, :], AF.Exp, [s_sb, negm], [E1t], bias=negm[:, 2 * h:2 * h + 1], scale=1.0)
                            k.ts(dg16[:, tl, h, :], self.ident32[:], th[:, h:h + 1], None, ALU.mult, None, [self.ident32, th], [dg16], eng="pool")
                        k.tt(E1s[:, tl, :, :], E1t[:], cf[:].unsqueeze(2).to_broadcast([128, 8, 128]), ALU.mult, [E1t, cf], [E1s])
                k.barrier()
                if "peer_dbg" in self.dbg and su == 0:
                    for nm, tile_ in (("E1s", E1s), ("E2s", E2s)):
                        d, r = self.scratch(nm, [128, 4 * 8 * 128], F32)
                        k.dma(d, tile_[:].rearrange("p a b c -> p (a b c)"), [tile_], [r])
                with ExitStack() as sc_:
                    stgD = [k.sb(sc_, [128, 16, 128], F32, f"stgD{i}") for i in range(2)]
                    stgU = [k.sb(sc_, [128, 1024], F32, f"stgU{i}") for i in range(2)]
                    dn16 = [k.sb(sc_, [128, 16, 128], BF16, f"dn16{i}") for i in range(4)]
                    up16 = [[k.sb(sc_, [128, 2048], BF16, f"up16{j}_{i}") for i in range(4)] for j in range(2)]
                    GA16 = [k.sb(sc_, [128, 512], BF16, f"GA{i}") for i in range(2)]
                    Pt = [k.sb(sc_, [128, 8, 128], F32, f"Pt{i}") for i in range(4)]
                    mE = [k.sb(sc_, [128, 8, 128], BF16, f"mE{i}") for i in range(4)]
                    WA = [k.sb(sc_, [128, 512], BF16, f"WA{i}") for i in range(8)]
                    ev = [k.sb(sc_, [128, 512], F32, f"ev{i}") for i in range(2)]
                    pact = [k.ps(sc_, [128, 512], F32, "pact") for _ in range(2)]
                    pw = [k.ps(sc_, [128, 512], F32, "pw") for _ in range(2)]
                    po = [k.ps(sc_, [128, 512], F32, "po") for _ in range(2)]
                    cn = {"k": 0, "p": 0, "o": 0}
                    ngrp = 1 if "peer_short" in self.dbg else 32
                    pend_wa = []
                    pend_po = []

                    def flush_wa():
                        while pend_wa:
                            wa_, pw__, ga_ = pend_wa.pop(0)
                            k.tt(wa_[:], pw__[:], ga_[:], ALU.mult, [pw__, ga_], [wa_])

                    def drain_po(ndc):
                        while pend_po and ndc != 0:
                            gi_, was_, dcs = pend_po[0]
                            dc = dcs.pop(0)
                            po_ = po[cn["o"] % 2]
                            cn["o"] += 1
                            for kl in range(4):
                                k.mm(po_[:], up16[gi_ % 2][kl][:, dc * 128:(dc + 1) * 128], was_[kl][:], kl == 0, kl == 3, [up16[gi_ % 2][kl], was_[kl]], [po_])
                            if gi_ == 0:
                                k.copy(accT[:, dc, :], po_[:], [po_], [accR[dc]], eng="act")
                            else:
                                pend_add.append((dc, po_))
                            if not dcs:
                                pend_po.pop(0)
                            ndc -= 1
                            if len(pend_add) >= 2 and ndc != 0:
                                flush_add()

                    accR = [Res() for _ in range(16)]
                    pend_add = []

                    def flush_add():
                        while pend_add:
                            dc, po_ = pend_add.pop(0)
                            k.tt(accT[:, dc, :], accT[:, dc, :], po_[:], ALU.add, [accR[dc], po_], [accR[dc]])
                    nk = ngrp * 4
                    seq = [(kap, tl) for kap in range(nk) for tl in range(4)]
                    LA = 2

                    def load_dn(kap2):
                        kl2 = kap2 % 4
                        sd = stgD[kap2 % 2]
                        k.dma(sd[:].rearrange("p a b -> p (a b)"), I["downB"][kap2 * 128:(kap2 + 1) * 128, :], [], [sd], eng="sp")
                        k.copy(dn16[kl2][:], sd[:], [sd], [dn16[kl2]], eng="act")

                    def load_up(kap2):
                        gi2, kl2 = kap2 // 4, kap2 % 4
                        up = up16[gi2 % 2][kl2]
                        for hf in range(2):
                            su_ = stgU[(kap2 * 2 + hf) % 2]
                            k.dma(su_[:], I["up"][kap2 * 128:(kap2 + 1) * 128, hf * 1024:(hf + 1) * 1024], [], [su_], eng="sp")
                            k.copy(up[:, hf * 1024:(hf + 1) * 1024], su_[:], [su_], [up], eng="act")

                    def p1(n):
                        kap, tl = seq[n]
                        k.tt(Pt[n % 4][:], E2s[:, tl, :, :], E1s[:, tl, :, kap:kap + 1].to_broadcast([128, 8, 128]), ALU.mult, [E2s, E1s], [Pt[n % 4]])

                    for n in range(min(LA, len(seq))):
                        p1(n)
                    cur = {}
                    for n, (kap, tl) in enumerate(seq):
                        gi, kl = kap // 4, kap % 4
                        if tl == 0:
                            if kl == 0:
                                cur["was"] = []
                                if gi == 0:
                                    for kl2 in range(4):
                                        load_dn(kl2)
                            dn = dn16[kl]
                            pa_ = pact[cn["k"] % 2]
                            pw_ = pw[cn["k"] % 2]
                            ga = GA16[cn["k"] % 2]
                            wa = WA[cn["k"] % 8]
                            cn["k"] += 1
                            cur.update(pw=pw_, ga=ga, wa=wa)
                            for c in range(16):
                                k.mm(pa_[:], dn[:, c, :], h2s[:, c, :], c == 0, c == 15, [dn, h2s], [pa_])
                            k.act(ga[:], pa_[:], AF.Gelu, [pa_], [ga])
                            if kap + 4 < nk:
                                load_dn(kap + 4)
                            load_up(kap)
                        if n + LA < len(seq):
                            p1(n + LA)
                        me = mE[n % 4]
                        k.stt(me[:], Pt[n % 4][:], 1.0, Pt[n % 4][:], ALU.is_ge, ALU.mult, [Pt[n % 4]], [me])
                        flush_add()
                        if tl == 1:
                            flush_wa()
                        pw_ = cur["pw"]
                        for h in range(8):
                            k.mm(pw_[:, tl * 128:(tl + 1) * 128], me[:, h, :], dg16[:, tl, h, :], h == 0, h == 7, [me, dg16], [pw_])
                        if tl in (1, 3):
                            drain_po(2)
                        if tl == 3:
                            pend_wa.append((cur["wa"], pw_, cur["ga"]))
                            cur["was"].append(cur["wa"])
                            if kl == 3:
                                assert not pend_po
                                pend_po.append((gi, cur["was"], list(range(16))))
                    flush_add()
                    flush_wa()
                    drain_po(-1)
                    flush_add()
                    for dc in range(16):
                        x_ = ev[dc % 2]
                        k.dma(x_[:], x1_d[dc][:, tsl], [x1_r], [x_])
                        k.stt(x_[:], accT[:, dc, :], self.modT[:, 80 + dc:81 + dc], x_[:], ALU.mult, ALU.add, [accR[dc], self.modT, x_], [x_])
                        k.dma(self.outT[dc * 128:(dc + 1) * 128, tsl], x_[:], [x_], [])
                k.barrier()


def _consts():
    f = np.float32
    half = 64
    freqs = (10000.0 ** (-np.arange(half, dtype=f) / f(half))).astype(f)
    ang = np.arange(S, dtype=f)[:, None] * freqs[None, :]
    cos = np.cos(ang).astype(f).T
    sin = np.sin(ang).astype(f).T
    cosT = np.concatenate([cos, cos], 0)
    sinT = np.concatenate([sin, sin], 0)
    rotm = np.zeros((128, 128), f)
    for m in range(64):
        rotm[m + 64, m] = -1.0
        rotm[m, m + 64] = 1.0
    ident = np.eye(128, dtype=f)
    t = np.arange(S)
    cst = np.arange(NCMP) * 16
    maskc = np.zeros((128, S), f)
    maskc[:NCMP] = ((cst + 31)[:, None] <= t[None, :]).astype(f)
    kk = np.arange(128)[:, None]
    tt = np.arange(512)[None, :]
    cm = np.stack([(128 * o + kk <= tt).astype(f) for o in range(4)], 1).reshape(128, 4 * 512)
    wm = np.stack([(((128 * rel + kk - tt) <= 0) & ((128 * rel + kk - tt) > -512)).astype(f)
                   for rel in range(-4, 4)], 1).reshape(128, 8 * 512)
    ex = np.zeros((32, 16, 128), f)
    for kc in range(16):
        for kq in range(128):
            ex[2 * kc + kq // 64, kc, kq] = 1.0
    ex = ex.reshape(32, 16 * 128)
    jb = np.arange(32)[None, :]
    cur = (t // 64)[:, None]
    forced = (jb == 0) | (jb == cur) | (jb == cur - 1)
    valid = (jb * 64) <= t[:, None]
    vm_ = (valid & ~forced).astype(f)
    fb_ = np.where(forced, 1e4, np.where(valid, 0.0, -1e4)).astype(f)
    vm = vm_.reshape(16, 128, 32).transpose(1, 0, 2).reshape(128, 512)
    fb = fb_.reshape(16, 128, 32).transpose(1, 0, 2).reshape(128, 512)
    sst = np.arange(32) * 64
    ov = np.maximum(np.minimum(cst[:, None] + 32, sst[None, :] + 64) - np.maximum(cst[:, None], sst[None, :]), 0)
    ovl = np.zeros((128, 32), f)
    ovl[:NCMP] = ov.astype(f) / 32.0
    return dict(cosT=cosT, sinT=sinT, rotm=rotm, ident=ident, maskc=maskc, cm=cm, wm=wm, ex=ex, vm=vm, fb=fb, ovl=ovl)


def prep_shared(inp):
    f = np.float32
    A = lambda v: np.ascontiguousarray(np.asarray(v, dtype=f))
    sh = {}
    sh["ada_w"] = A(inp["ada_w"][0])
    sh["ada_bT"] = A(inp["ada_b"][0].reshape(96, 128).T)
    sh["g1T"] = A(inp["norm_mix_g"][0].reshape(16, 128).T)
    sh["g2T"] = A(inp["norm_ffn_g"][0].reshape(16, 128).T)
    sh["w_in"] = A(inp["w_in"][0])
    sh["w_out"] = A(inp["w_out"][0])
    sh["wq"] = A(inp["peer_wq"][0])
    sh["qg"] = A(inp["q_norm_g"][0].reshape(128, 1))
    sh["kgT"] = A(inp["k_norm_g"][0].T)
    sh["kg0b"] = A(np.broadcast_to(inp["k_norm_g"][0, 0][None, :], (128, 128)))
    sh["pekT"] = A(inp["cmp_pe_k"][0].T)
    sh["pevT"] = A(inp["cmp_pe_v"][0].T)
    sh["cwk"] = A(inp["cmp_w_k"][0])
    sh["cwv"] = A(inp["cmp_w_v"][0])
    sh["gateb"] = A(np.broadcast_to(inp["gate_b"][0][None, :], (128, 24)))
    sh["convw"] = A(inp["conv_w"][0].reshape(4, 8, 128).transpose(2, 1, 0).reshape(128, 32))
    for nm, key in (("convb", "conv_b"), ("lba", "lru_ba"), ("lbi", "lru_bi"), ("lam", "lru_lam"), ("gr", "out_g_rnn")):
        sh[nm] = A(inp[key][0].reshape(8, 128).T)
    sh["gab"] = A(np.broadcast_to(inp["out_g_attn"][0][None, :], (128, 1024)))
    sh["wa"] = A(inp["lru_wa"][0])
    sh["wi"] = A(inp["lru_wi"][0])
    sh["keysT"] = A(inp["peer_keys"][0].transpose(3, 0, 1, 2).reshape(128, 16 * 128))
    sh["downB"] = A(inp["peer_down"][0].reshape(128, 128, 16, 128).transpose(0, 3, 2, 1).reshape(128 * 128, 16 * 128))
    sh["up"] = A(inp["peer_up"][0])
    sh.update(_consts())
    return sh


def prep_core(inp, b):
    f = np.float32
    return {"xT": np.ascontiguousarray(np.asarray(inp["x"][b], dtype=f).T),
            "c_col": np.ascontiguousarray(np.asarray(inp["c"][b], dtype=f).reshape(16, 128).T)}


def kernel(**inputs):
    inp = {k_: np.asarray(v) for k_, v in inputs.items()}
    sh = prep_shared(inp)
    nc = Prog().build()
    in_maps = []
    for b in range(8):
        m = dict(sh)
        m.update(prep_core(inp, b))
        in_maps.append(m)
    res = run_bass_kernel_spmd(nc, in_maps, core_ids=list(range(8)))
    out = np.stack([np.asarray(r["outT"]).T for r in res.results], 0)
    return np.ascontiguousarray(out.astype(np.float32))
```

```python
import numpy as np
from contextlib import ExitStack
import concourse.bass as bass
import concourse.mybir as mybir
from concourse.bass_utils import run_bass_kernel_spmd

F32 = mybir.dt.float32
BF16 = mybir.dt.bfloat16
AF = mybir.ActivationFunctionType
ALU = mybir.AluOpType
AX = mybir.AxisListType

D = 2048
S = 2048
NT = 16
NIN = 4632
C_Q, C_KC, C_VC, C_KS, C_VS, C_KW, C_VW, C_GL, C_XR, C_XG = 0, 1024, 1280, 1536, 1792, 2048, 2304, 2560, 2584, 3608
EPS = 1e-6
NCMP = 127
SCALE = 128 ** -0.5


class Res:
    __slots__ = ("name", "w", "r")

    def __init__(self, name=""):
        self.name = name
        self.w = None
        self.r = []


class Op:
    __slots__ = ("eng", "fn", "deps", "flag", "cval", "dma", "dsem", "dval")

    def __init__(self, eng, fn, dma=False):
        self.eng = eng
        self.fn = fn
        self.deps = []
        self.flag = False
        self.cval = 0
        self.dma = dma
        self.dsem = None
        self.dval = 0


class Sched:
    ENGS = ("pe", "act", "dve", "pool", "sp")

    def __init__(self, nc, n_dma_sems=16):
        self.nc = nc
        self.q = {e: [] for e in self.ENGS}
        self.n_dma_sems = n_dma_sems
        self.dma_rr = 0
        self.dma_last = [None] * n_dma_sems
        self.dma_cnt = [0] * n_dma_sems
        self.pending = {e: [] for e in self.ENGS}

    def _add(self, eng, fn, reads, writes, dma=False):
        op = Op(eng, fn, dma)
        deps = list(self.pending[eng])
        self.pending[eng] = []
        for r in reads:
            if r.w is not None:
                deps.append(r.w)
        for w in writes:
            if w.w is not None:
                deps.append(w.w)
            deps.extend(w.r)
        for r in reads:
            r.r.append(op)
        for w in writes:
            w.w = op
            w.r = []
        if dma:
            s = self.dma_rr
            self.dma_rr = (self.dma_rr + 1) % self.n_dma_sems
            prev = self.dma_last[s]
            if prev is not None:
                deps.append(prev)
            self.dma_last[s] = op
            self.dma_cnt[s] += 1
            op.dsem = s
            op.dval = 16 * self.dma_cnt[s]
        seen = set()
        for d in deps:
            if d is op or id(d) in seen:
                continue
            if (not d.dma) and d.eng == eng and eng == "pe":
                continue
            seen.add(id(d))
            op.deps.append(d)
            d.flag = True
        self.q[eng].append(op)
        return op

    def op(self, eng, fn, reads=(), writes=()):
        return self._add(eng, fn, list(reads), list(writes))

    def dma(self, eng, out, in_, reads=(), writes=()):
        return self._add(eng, lambda e: e.dma_start(out=out, in_=in_), list(reads), list(writes), dma=True)

    def barrier(self):
        lasts = []
        for e in self.ENGS:
            for op in reversed(self.q[e]):
                if not op.dma:
                    lasts.append(op)
                    break
        for s in range(self.n_dma_sems):
            if self.dma_last[s] is not None:
                lasts.append(self.dma_last[s])
        for e in self.ENGS:
            self.pending[e] = list(lasts)

    def emit(self):
        nc = self.nc
        with ExitStack() as st:
            esem = {e: st.enter_context(nc.semaphore(f"s_{e}")) for e in self.ENGS}
            dsem = [st.enter_context(nc.semaphore(f"d_{i}")) for i in range(self.n_dma_sems)]
            for e in self.ENGS:
                c = 0
                for op in self.q[e]:
                    if op.dma:
                        continue
                    if op.flag:
                        c += 1
                        op.cval = c
            block = st.enter_context(nc.Block())

            def run(ename, eobj):
                waited = {}
                for op in self.q[ename]:
                    need = {}
                    for d in op.deps:
                        if d.dma:
                            key, val = ("d", d.dsem), d.dval
                        else:
                            key, val = ("e", d.eng), d.cval
                        if val > need.get(key, 0):
                            need[key] = val
                    for key, val in need.items():
                        if waited.get(key, 0) >= val:
                            continue
                        waited[key] = val
                        sem = dsem[key[1]] if key[0] == "d" else esem[key[1]]
                        eobj.wait_ge(sem, val)
                    ins = op.fn(eobj)
                    if op.dma:
                        ins.then_inc(dsem[op.dsem], 16)
                    elif op.flag:
                        ins.then_inc(esem[ename], 1)
                if ename == "sp":
                    for s in range(self.n_dma_sems):
                        if self.dma_cnt[s] > 0:
                            eobj.wait_ge(dsem[s], 16 * self.dma_cnt[s])

            block.tensor(lambda e: run("pe", e))
            block.scalar(lambda e: run("act", e))
            block.vector(lambda e: run("dve", e))
            block.gpsimd(lambda e: run("pool", e))
            block.sync(lambda e: run("sp", e))


class T:
    __slots__ = ("t", "r")

    def __init__(self, t, name=""):
        self.t = t
        self.r = Res(name)

    def __getitem__(self, k):
        return self.t[k]


class K:
    def __init__(self, nc):
        self.nc = nc
        self.S = Sched(nc)
        self.uid = 0

    def sb(self, st, shape, dt, name=None):
        self.uid += 1
        n = f"{name or 't'}_{self.uid}"
        return T(st.enter_context(self.nc.sbuf_tensor(n, list(shape), dt)), n)

    def ps(self, st, shape, dt=F32, name=None):
        self.uid += 1
        n = f"{name or 'p'}_{self.uid}"
        return T(st.enter_context(self.nc.psum_tensor(n, list(shape), dt)), n)

    @staticmethod
    def _rs(xs):
        return [x.r if isinstance(x, T) else x for x in xs]

    def mm(self, out, lhsT, rhs, start, stop, R, W):
        self.S.op("pe", lambda e: e.matmul(out, lhsT, rhs, start=start, stop=stop), self._rs(R), self._rs(W))

    def act(self, out, in_, func, R, W, bias=None, scale=None, accum_out=None, eng="act"):
        kw = {}
        if bias is not None:
            kw["bias"] = bias
        if scale is not None:
            kw["scale"] = scale
        if accum_out is not None:
            kw["accum_out"] = accum_out
        self.S.op(eng, lambda e: e.activation(out, in_, func, **kw), self._rs(R), self._rs(W))

    def tt(self, out, in0, in1, op, R, W, eng="dve"):
        self.S.op(eng, lambda e: e.tensor_tensor(out, in0, in1, op), self._rs(R), self._rs(W))

    def ts(self, out, in0, s1, s2, op0, op1, R, W, eng="dve", accum_out=None):
        if accum_out is None:
            if op1 is None:
                self.S.op(eng, lambda e: e.tensor_scalar(out, in0, s1, None, op0), self._rs(R), self._rs(W))
            else:
                self.S.op(eng, lambda e: e.tensor_scalar(out, in0, s1, s2, op0, op1), self._rs(R), self._rs(W))
        else:
            self.S.op(eng, lambda e: e.tensor_scalar(out, in0, s1, s2, op0, op1, accum_out=accum_out),
                      self._rs(R), self._rs(W))

    def stt(self, out, in0, scalar, in1, op0, op1, R, W, eng="dve"):
        self.S.op(eng, lambda e: e.scalar_tensor_tensor(out, in0, scalar, in1, op0, op1), self._rs(R), self._rs(W))

    def copy(self, out, in_, R, W, eng="dve"):
        if eng == "act":
            self.S.op("act", lambda e: e.copy(out, in_), self._rs(R), self._rs(W))
        else:
            self.S.op(eng, lambda e: e.tensor_copy(out, in_), self._rs(R), self._rs(W))

    def memset(self, ap, val, W, eng="pool"):
        self.S.op(eng, lambda e: e.memset(ap, val), [], self._rs(W))

    def recip(self, out, in_, R, W):
        self.S.op("dve", lambda e: e.reciprocal(out, in_), self._rs(R), self._rs(W))

    def dma(self, out, in_, R, W, eng="sp"):
        self.S.dma("sp", out, in_, self._rs(R), self._rs(W))

    def fn(self, eng, f, R, W):
        self.S.op(eng, f, self._rs(R), self._rs(W))

    def barrier(self):
        self.S.barrier()


IN_SPECS = [
    ("xT", [D, S], F32), ("c_col", [128, 16], F32), ("ada_w", [D, 6 * D], F32), ("ada_bT", [128, 96], F32),
    ("g1T", [128, 16], F32), ("g2T", [128, 16], F32), ("w_in", [D, NIN], F32), ("w_out", [D, D], F32),
    ("wq", [D, D], F32), ("qg", [128, 1], F32), ("kgT", [128, 3], F32), ("kg0b", [128, 128], F32),
    ("pekT", [128, 32], F32), ("pevT", [128, 32], F32), ("cwk", [4096, 128], F32), ("cwv", [4096, 128], F32),
    ("gateb", [128, 24], F32), ("convw", [128, 32], F32), ("convb", [128, 8], F32), ("lba", [128, 8], F32),
    ("lbi", [128, 8], F32), ("lam", [128, 8], F32), ("gr", [128, 8], F32), ("gab", [128, 1024], F32),
    ("wa", [8, 128, 128], F32), ("wi", [8, 128, 128], F32), ("keysT", [128, 16 * 128], F32),
    ("downB", [128 * 128, 16 * 128], F32), ("up", [16384, D], F32),
    ("cosT", [128, S], F32), ("sinT", [128, S], F32), ("rotm", [128, 128], F32), ("ident", [128, 128], F32),
    ("maskc", [128, S], F32), ("cm", [128, 4 * 512], F32), ("wm", [128, 8 * 512], F32),
    ("ex", [32, 16 * 128], F32), ("vm", [128, 16 * 32], F32), ("fb", [128, 16 * 32], F32), ("ovl", [128, 32], F32),
]


class Prog:
    def __init__(self, stop_after=99, dbg=()):
        self.nc = nc = bass.Bass("TRN2", target_bir_lowering=False)
        self.k = K(nc)
        self.stop_after = stop_after
        self.dbg = set(dbg)
        for d_ in self.dbg:
            if d_.startswith("qkind="):
                self.qkind = d_.split("=")[1]
        self.I = {}
        for name, shape, dt in IN_SPECS:
            self.I[name] = nc.dram_tensor(name, shape, dt, kind="ExternalInput").ap()
        self.outT = nc.dram_tensor("outT", [D, S], F32, kind="ExternalOutput").ap()
        self.scr = {}

    def scratch(self, name, shape, dt):
        kind = "ExternalOutput" if name in self.dbg else "Internal"
        ap = self.nc.dram_tensor(name, list(shape), dt, kind=kind).ap()
        self.scr[name] = (ap, Res(name))
        return ap, self.scr[name][1]

    def build(self):
        k = self.k
        with ExitStack() as g:
            self.g = g
            self.modT = k.sb(g, [128, 96], F32, "modT")
            self.G1 = k.sb(g, [128, 16], F32, "G1")
            self.G2 = k.sb(g, [128, 16], F32, "G2")
            self.eps_t = k.sb(g, [128, 1], F32, "eps")
            self.ones16 = k.sb(g, [128, 128], BF16, "ones16")
            self.ident16 = k.sb(g, [128, 128], BF16, "ident16")
            k.memset(self.eps_t[:], EPS, [self.eps_t])
            k.memset(self.ones16[:], 1.0, [self.ones16])
            self.ident32 = k.sb(g, [128, 128], F32, "ident32")
            k.dma(self.ident32[:], self.I["ident"], [], [self.ident32])
            k.copy(self.ident16[:], self.ident32[:], [self.ident32], [self.ident16])
            names = ["p0_mod", "p12_proj", "p3_cmp", "p4_attn", "p5_rnn", "p6_out", "p7_peer"]
            phases = [getattr(self, n) for n in names if hasattr(self, n)]
            for i, ph in enumerate(phases):
                if i > self.stop_after:
                    break
                ph()
                k.barrier()
            k.S.emit()
        return self.nc

    def p0_mod(self):
        k, I = self.k, self.I
        with ExitStack() as st:
            cc = k.sb(st, [128, 16], F32, "cc")
            sc = k.sb(st, [128, 16], F32, "sc")
            abT = k.sb(st, [128, 96], F32, "abT")
            g1 = k.sb(st, [128, 16], F32, "g1")
            g2 = k.sb(st, [128, 16], F32, "g2")
            tmp = k.sb(st, [128, 16], F32, "tmp")
            wb = [k.sb(st, [128, 16, 512], F32, f"adaw{i}") for i in range(3)]
            wb16 = [k.sb(st, [128, 16, 512], BF16, f"adaw16{i}") for i in range(2)]
            sc16 = k.sb(st, [128, 16], BF16, "sc16")
            pm = k.ps(st, [128, 96], F32, "pm")
            k.dma(cc[:], I["c_col"], [], [cc])
            k.dma(abT[:], I["ada_bT"], [], [abT])
            k.dma(g1[:], I["g1T"], [], [g1])
            k.dma(g2[:], I["g2T"], [], [g2])
            k.act(sc[:], cc[:], AF.Silu, [cc], [sc])
            k.copy(sc16[:], sc[:], [sc], [sc16])
            wv = I["ada_w"].rearrange("(k p) n -> p k n", p=128)
            cast_eng = ("act", "dve", "act", "dve")
            for gi in range(24):
                b = wb[gi % 3]
                b16 = wb16[gi % 2]
                k.dma(b[:], wv[:, :, gi * 512:(gi + 1) * 512], [], [b])
                for q in range(4):
                    k.copy(b16[:, q * 4:(q + 1) * 4, :], b[:, q * 4:(q + 1) * 4, :], [b], [b16], eng=cast_eng[q])
                for j in range(4):
                    col = gi * 4 + j
                    for kk in range(16):
                        k.mm(pm[:, col:col + 1], b16[:, kk, j * 128:(j + 1) * 128], sc16[:, kk:kk + 1],
                             kk == 0, kk == 15, [b16, sc16], [pm])
            k.tt(self.modT[:], pm[:], abT[:], ALU.add, [pm, abT], [self.modT])
            k.ts(tmp[:], self.modT[:, 16:32], 1.0, None, ALU.add, None, [self.modT], [tmp])
            k.tt(self.G1[:], tmp[:], g1[:], ALU.mult, [tmp, g1], [self.G1])
            k.ts(tmp[:], self.modT[:, 64:80], 1.0, None, ALU.add, None, [self.modT, self.G1], [tmp])
            k.tt(self.G2[:], tmp[:], g2[:], ALU.mult, [tmp, g2], [self.G2])
            if "modT" in self.dbg:
                d, r = self.scratch("modT", [128, 96], F32)
                k.dma(d, self.modT[:], [self.modT], [r])

    def rms_stats_fm(self, st, loader, nchunks, width, scale_div, name):
        k = self.k
        rstd = k.sb(st, [128, S], F32, name)
        with ExitStack() as s2:
            xb = [k.sb(s2, [128, S], F32, "xld") for _ in range(2)]
            sq = [k.sb(s2, [128, S], BF16, "sq") for _ in range(2)]
            pss = [k.ps(s2, [128, 512], F32, "pss") for _ in range(4)]
            for kk in range(nchunks):
                xt, sqt = xb[kk % 2], sq[kk % 2]
                loader(kk, xt)
                k.act(sqt[:], xt[:], AF.Square, [xt], [sqt])
                for tg in range(4):
                    k.mm(pss[tg][:], self.ones16[:], sqt[:, tg * 512:(tg + 1) * 512], kk == 0, kk == nchunks - 1,
                         [self.ones16, sqt], [pss[tg]])
            for tg in range(4):
                sl = slice(tg * 512, (tg + 1) * 512)
                k.act(rstd[:, sl], pss[tg][:], AF.Sqrt, [pss[tg], self.eps_t], [rstd], bias=self.eps_t[:], scale=1.0 / scale_div)
            k.recip(rstd[:], rstd[:], [rstd], [rstd])
        k.barrier()
        return rstd

    def p12_proj(self):
        k, I = self.k, self.I
        xTv = I["xT"].rearrange("(k p) t -> k p t", p=128)
        qT_d, qT_r = self.scratch("qT", [8, 128, S], BF16)
        kcT_d, kcT_r = self.scratch("kcT", [2, 128, S], BF16)
        vcT_d, vcT_r = self.scratch("vcT", [2, 128, S], BF16)
        ksT_d, ksT_r = self.scratch("ksT", [2, 128, S], BF16)
        kwT_d, kwT_r = self.scratch("kwT", [2, 128, S], BF16)
        vs_d, vs_r = self.scratch("vs", [S, 256], BF16)
        vw_d, vw_r = self.scratch("vw", [S, 256], BF16)
        gt_d, gt_r = self.scratch("gates", [128, 16 * 24], F32)
        xrT_d, xrT_r = self.scratch("xrT", [8, 128, S], F32)
        xgT_d, xgT_r = self.scratch("xgT", [8, 128, S], F32)
        with ExitStack() as st:
            hT = k.sb(st, [128, 16, S], BF16, "hT")
            with ExitStack() as s1:
                rstd = self.rms_stats_fm(s1, lambda kk, dst: k.dma(dst[:], xTv[kk], [], [dst]), 16, S, float(D), "rstd1")
                xb = [k.sb(s1, [128, S], F32, "xld2") for _ in range(2)]
                tmp = [k.sb(s1, [128, S], F32, "tmp") for _ in range(2)]
                for kk in range(16):
                    xt, tp = xb[kk % 2], tmp[kk % 2]
                    k.dma(xt[:], xTv[kk], [], [xt])
                    k.stt(tp[:], xt[:], self.G1[:, kk:kk + 1], rstd[:], ALU.mult, ALU.mult, [xt, self.G1, rstd], [tp])
                    k.act(hT[:, kk, :], tp[:], AF.Identity, [tp, self.modT], [hT], bias=self.modT[:, kk:kk + 1], scale=1.0)
            if "hT" in self.dbg:
                d, r = self.scratch("hT", [16, 128, S], BF16)
                k.dma(d.rearrange("k p t -> p k t"), hT[:], [hT], [r])
            k.barrier()
            if "stop_p1" in self.dbg:
                return
            with ExitStack() as s2:
                wb = [k.sb(s2, [128, 16, 544], BF16, f"win{i}") for i in range(2)]
                cosT = k.sb(s2, [128, S], F32, "cosT")
                sinT = k.sb(s2, [128, S], F32, "sinT")
                rot16 = k.sb(s2, [128, 128], BF16, "rot16")
                gq = k.sb(s2, [128, 4], F32, "gq")
                gateb = k.sb(s2, [128, 24], F32, "gateb")
                k.dma(cosT[:], I["cosT"], [], [cosT])
                k.dma(sinT[:], I["sinT"], [], [sinT])
                rot32 = k.sb(s2, [128, 128], F32, "rot32")
                k.dma(rot32[:], I["rotm"], [], [rot32])
                k.copy(rot16[:], rot32[:], [rot32], [rot16])
                k.dma(gq[:, 0:1], I["qg"], [], [gq])
                k.dma(gq[:, 1:4], I["kgT"], [], [gq])
                k.dma(gateb[:], I["gateb"], [], [gateb])
                pz = [k.ps(s2, [128, 512], F32, "pz") for _ in range(4)]
                pss = [k.ps(s2, [128, 512], F32, "pss2") for _ in range(2)]
                prot = [k.ps(s2, [128, 512], F32, "prot") for _ in range(2)]
                ptm = pz[2:4]
                sq = [k.sb(s2, [128, 512], BF16, "sq2") for _ in range(2)]
                rs = [k.sb(s2, [128, 512], F32, "rs") for _ in range(2)]
                xn = [k.sb(s2, [128, 512], F32, "xn") for _ in range(2)]
                xn16 = [k.sb(s2, [128, 512], BF16, "xn16") for _ in range(2)]
                t1 = [k.sb(s2, [128, 512], F32, "t1") for _ in range(2)]
                t2 = [k.sb(s2, [128, 512], F32, "t2") for _ in range(2)]
                o16 = [k.sb(s2, [128, S], BF16, "o16") for _ in range(2)]
                o32 = [k.sb(s2, [128, S], F32, "o32") for _ in range(2)]
                vtm = k.sb(s2, [128, 16, 256], BF16, "vtm")
                gtm = k.sb(s2, [128, 16, 24], F32, "gtm")
                gtmp = k.sb(s2, [128, 24], F32, "gtmp")
                wv = I["w_in"].rearrange("(k p) n -> p k n", p=128)
                cnt = {"c": 0, "i": 0}

                pend = []

                def fm_chunk(b, co, kind, gcol, dst_ap, dst_res):
                    ci = cnt["c"]
                    cnt["c"] += 1
                    ob = (o32 if kind == "f32" else o16)[ci % 2]
                    for tg in range(4):
                        pend.append((b, co, kind, gcol, dst_ap, dst_res, ob, tg))

                def fm_s0(st_):
                    b, co, kind, gcol, dst_ap, dst_res, ob, tg = st_
                    i = cnt["i"]
                    cnt["i"] += 1
                    p = pz[i % 4]
                    sl = slice(tg * 512, (tg + 1) * 512)
                    for kk in range(16):
                        k.mm(p[:], b[:, kk, co:co + 128], hT[:, kk, sl], kk == 0, kk == 15, [b, hT], [p])
                    return (i, p)

                def fm_s1(st_, ip):
                    b, co, kind, gcol, dst_ap, dst_res, ob, tg = st_
                    i, p = ip
                    sl = slice(tg * 512, (tg + 1) * 512)
                    if kind in ("f32", "bf16"):
                        k.copy(ob[:, sl], p[:], [p], [ob], eng=("act" if tg % 2 == 0 else "dve"))
                    else:
                        a, a16 = xn[i % 2], xn16[i % 2]
                        if kind == "normrope":
                            sqt, pst, rst = sq[i % 2], pss[i % 2], rs[i % 2]
                            k.act(sqt[:], p[:], AF.Square, [p], [sqt])
                            k.mm(pst[:], self.ones16[:], sqt[:], True, True, [self.ones16, sqt], [pst])
                            k.act(rst[:], pst[:], AF.Sqrt, [pst, self.eps_t], [rst], bias=self.eps_t[:], scale=1.0 / 128.0)
                            k.recip(rst[:], rst[:], [rst], [rst])
                            k.stt(a[:], p[:], gq[:, gcol:gcol + 1], rst[:], ALU.mult, ALU.mult, [p, gq, rst], [a])
                        else:
                            k.copy(a[:], p[:], [p], [a], eng="dve")
                        k.copy(a16[:], a[:], [a], [a16], eng="act")
                        pr = prot[i % 2]
                        k.mm(pr[:], rot16[:], a16[:], True, True, [rot16, a16], [pr])
                        k.tt(t1[i % 2][:], a[:], cosT[:, sl], ALU.mult, [a, cosT], [t1[i % 2]])
                        k.tt(t2[i % 2][:], pr[:], sinT[:, sl], ALU.mult, [pr, sinT], [t2[i % 2]])
                        k.tt(ob[:, sl], t1[i % 2][:], t2[i % 2][:], ALU.add, [t1[i % 2], t2[i % 2]], [ob])
                    if tg == 3:
                        k.dma(dst_ap, ob[:], [ob], [dst_res])

                def fm_flush():
                    steps = list(pend)
                    del pend[:]
                    inflight = []
                    LA = 2
                    for n in range(min(LA, len(steps))):
                        inflight.append(fm_s0(steps[n]))
                    for n in range(len(steps)):
                        if n + LA < len(steps):
                            inflight.append(fm_s0(steps[n + LA]))
                        fm_s1(steps[n], inflight.pop(0))

                stg = [k.sb(s2, [128, 16, 272], F32, f"stg{i}") for i in range(2)]
                lcnt = {"g": 0, "s": 0}

                def load_group(c0, c1):
                    fm_flush()
                    b = wb[lcnt["g"] % 2]
                    lcnt["g"] += 1
                    w = c1 - c0
                    pieces = [(a, min(a + 256, w)) for a in range(0, w, 256)]
                    for (a0, a1) in pieces:
                        sg = stg[lcnt["s"] % 2]
                        lcnt["s"] += 1
                        k.dma(sg[:, :, 0:a1 - a0], wv[:, :, c0 + a0:c0 + a1], [], [sg], eng=("sp" if lcnt["s"] % 2 else "pool"))
                        k.copy(b[:, :, a0:a1], sg[:, :, 0:a1 - a0], [sg], [b], eng="pool")
                    return b

                def tm_block(b, co, width, post):
                    for tt_ in range(16):
                        p = ptm[tt_ % 2]
                        for kk in range(16):
                            k.mm(p[:, 0:width], hT[:, kk, tt_ * 128:(tt_ + 1) * 128], b[:, kk, co:co + width], kk == 0, kk == 15, [b, hT], [p])
                        post(tt_, p)

                b = load_group(0, 512)
                for j in range(4):
                    fm_chunk(b, j * 128, self.qkind if hasattr(self, "qkind") else "normrope", 0, qT_d[j], qT_r)
                b = load_group(512, 1024)
                for j in range(4):
                    fm_chunk(b, j * 128, self.qkind if hasattr(self, "qkind") else "normrope", 0, qT_d[4 + j], qT_r)
                if "stop_g0" in self.dbg:
                    fm_flush()
                    return
                b = load_group(1024, 1536)
                for j in range(2):
                    fm_chunk(b, j * 128, "rope", 0, kcT_d[j], kcT_r)
                for j in range(2):
                    fm_chunk(b, 256 + j * 128, "bf16", 0, vcT_d[j], vcT_r)
                b = load_group(1536, 2048)
                for j in range(2):
                    fm_chunk(b, j * 128, "normrope", 2, ksT_d[j], ksT_r)
                fm_flush()
                tm_block(b, 256, 256, lambda tt_, p: k.copy(vtm[:, tt_, :], p[:, 0:256], [p], [vtm], eng=("act" if tt_ % 2 == 0 else "dve")))
                k.dma(vs_d.rearrange("(t p) c -> p t c", p=128), vtm[:], [vtm], [vs_r])
                if "stop_g1" in self.dbg:
                    return
                if "rep_g3" in self.dbg:
                    b = load_group(1536, 2048)
                    for j in range(2):
                        fm_chunk(b, j * 128, "normrope", 2, ksT_d[j], ksT_r)
                    tm_block(b, 256, 256, lambda tt_, p: k.copy(vtm[:, tt_, :], p[:, 0:256], [p], [vtm], eng=("act" if tt_ % 2 == 0 else "dve")))
                    k.dma(vs_d.rearrange("(t p) c -> p t c", p=128), vtm[:], [vtm], [vs_r])
                    return
                b = load_group(2048, 2560)
                for j in range(2):
                    fm_chunk(b, j * 128, "normrope", 3, kwT_d[j], kwT_r)
                fm_flush()
                tm_block(b, 256, 256, lambda tt_, p: k.copy(vtm[:, tt_, :], p[:, 0:256], [p], [vtm], eng=("act" if tt_ % 2 == 0 else "dve")))
                b = load_group(2560, 2584)

                def post_g(tt_, p):
                    k.tt(gtmp[:], p[:, 0:24], gateb[:], ALU.add, [p, gateb], [gtmp])
                    k.act(gtmp[:], gtmp[:], AF.Exp, [gtmp], [gtmp], scale=-1.0)
                    k.ts(gtmp[:], gtmp[:], 1.0, None, ALU.add, None, [gtmp], [gtmp])
                    k.recip(gtm[:, tt_, :], gtmp[:], [gtmp], [gtm])
                tm_block(b, 0, 24, post_g)
                k.dma(vw_d.rearrange("(t p) c -> p t c", p=128), vtm[:], [vtm], [vw_r])
                k.dma(gt_d, gtm[:].rearrange("p t c -> p (t c)"), [gtm], [gt_r])
                if "stop_g2" in self.dbg:
                    return
                for half in range(2):
                    b = load_group(C_XR + half * 512, C_XR + (half + 1) * 512)
                    for j in range(4):
                        fm_chunk(b, j * 128, "f32", 0, xrT_d[half * 4 + j], xrT_r)
                for half in range(2):
                    b = load_group(C_XG + half * 512, C_XG + (half + 1) * 512)
                    for j in range(4):
                        fm_chunk(b, j * 128, "f32", 0, xgT_d[half * 4 + j], xgT_r)
                fm_flush()

    def p3_cmp(self):
        k, I = self.k, self.I
        kcT_d = self.scr["kcT"][0]
        vcT_d = self.scr["vcT"][0]
        kcmpT_d, kcmpT_r = self.scratch("kcmpT", [128, 2, 128], BF16)
        vcmp_d, vcmp_r = self.scratch("vcmp", [128, 2, 162], BF16)
        with ExitStack() as st:
            src = k.sb(st, [128, 4, S], BF16, "cmpsrc")
            wst = k.sb(st, [128, 32, 128], F32, "wst")
            w16 = [k.sb(st, [128, 32, 128], BF16, f"w16{i}") for i in range(2)]
            pe32 = k.sb(st, [128, 2, 32], F32, "pe32")
            peB = [k.sb(st, [128, 32, 127], BF16, f"peB{i}") for i in range(2)]
            kg0b = k.sb(st, [128, 128], F32, "kg0b")
            ovl = k.sb(st, [128, 32], F32, "ovl")
            ss = k.sb(st, [128, 1], F32, "ss")
            junk = k.sb(st, [128, 128], F32, "junk")
            kn16 = k.sb(st, [128, 128], BF16, "kn16")
            kT16 = k.sb(st, [128, 2, 128], BF16, "kT16")
            va16 = k.sb(st, [128, 2, 162], BF16, "va16")
            pc = [k.ps(st, [128, 128], F32, "pc") for _ in range(2)]
            ptr = k.ps(st, [128, 128], F32, "ptr")
            for g in range(2):
                k.dma(src[:, g, :], kcT_d[g], [self.scr["kcT"][1]], [src])
                k.dma(src[:, 2 + g, :], vcT_d[g], [self.scr["vcT"][1]], [src])
            k.dma(pe32[:, 0, :], I["pekT"], [], [pe32])
            k.dma(pe32[:, 1, :], I["pevT"], [], [pe32])
            k.dma(kg0b[:], I["kg0b"], [], [kg0b])
            k.dma(ovl[:], I["ovl"], [], [ovl])
            k.memset(kT16[:], 0.0, [kT16])
            k.memset(va16[:], 0.0, [va16])
            for kv, wname in enumerate(("cwk", "cwv")):
                k.dma(wst[:], I[wname].rearrange("(l d) o -> d l o", d=128), [], [wst])
                k.copy(w16[kv][:], wst[:], [wst], [w16[kv]], eng="pool")
                k.copy(peB[kv][:], pe32[:, kv, :].unsqueeze(2).to_broadcast([128, 32, 127]), [pe32], [peB[kv]])
            for kv in range(2):
                for g in range(2):
                    p = pc[(kv * 2 + g) % 2]
                    for l in range(32):
                        k.mm(p[0:127, :], src[:, kv * 2 + g, l:l + 16 * 126 + 1:16], w16[kv][:, l, :], l == 0, False, [src, w16[kv]], [p])
                    for l in range(32):
                        k.mm(p[0:127, :], peB[kv][:, l, :], w16[kv][:, l, :], False, l == 31, [peB[kv], w16[kv]], [p])
                    if kv == 0:
                        k.act(junk[0:127, :], p[0:127, :], AF.Square, [p], [junk, ss], accum_out=ss[0:127, :])
                        k.act(ss[0:127, :], ss[0:127, :], AF.Sqrt, [ss, self.eps_t], [ss], bias=self.eps_t[0:127, :], scale=1.0 / 128.0)
                        k.recip(ss[0:127, :], ss[0:127, :], [ss], [ss])
                        k.stt(kn16[0:127, :], p[0:127, :], ss[0:127, :], kg0b[0:127, :], ALU.mult, ALU.mult, [p, ss, kg0b], [kn16])
                        k.mm(ptr[:, 0:127], kn16[0:127, :], self.ident16[0:127, 0:127], True, True, [kn16, self.ident16], [ptr])
                        k.copy(kT16[:, g, 0:127], ptr[:, 0:127], [ptr], [kT16])
                    else:
                        k.copy(va16[0:127, g, 0:128], p[0:127, :], [p], [va16])
                        k.memset(va16[0:127, g, 128:129], 1.0, [va16])
                        k.copy(va16[0:127, g, 129:161], ovl[0:127, :], [ovl], [va16])
            k.dma(kcmpT_d, kT16[:], [kT16], [kcmpT_r])
            k.dma(vcmp_d, va16[:], [va16], [vcmp_r])

    def p4_attn(self):
        k, I = self.k, self.I
        sc = self.scr
        yT_d, yT_r = self.scratch("yT", [16, 128, S], BF16)
        with ExitStack() as st:
            qT = k.sb(st, [128, 8, S], BF16, "qT")
            ksT = k.sb(st, [128, 2, S], BF16, "ksT")
            kwT = k.sb(st, [128, 2, S], BF16, "kwT")
            kcT = k.sb(st, [128, 2, 128], BF16, "kcT")
            vsa = k.sb(st, [128, 16, 2, 130], BF16, "vsa")
            vwa = k.sb(st, [128, 16, 2, 130], BF16, "vwa")
            vca = k.sb(st, [128, 2, 162], BF16, "vca")
            gts = k.sb(st, [128, 16, 24], F32, "gts")
            maskc = k.sb(st, [128, S], F32, "maskc")
            cm = k.sb(st, [128, 4, 512], F32, "cm")
            wm = k.sb(st, [128, 8, 512], F32, "wm")
            ex32 = k.sb(st, [32, 16, 128], F32, "ex32")
            ex16 = k.sb(st, [32, 16, 128], BF16, "ex16")
            vm = k.sb(st, [128, 16, 32], F32, "vm")
            fb = k.sb(st, [128, 16, 32], F32, "fb")
            gab = k.sb(st, [128, 1024], F32, "gab")
            for j in range(8):
                k.dma(qT[:, j, :], sc["qT"][0][j], [sc["qT"][1]], [qT], eng=("sp" if j % 2 else "pool"))
            for g in range(2):
                k.dma(ksT[:, g, :], sc["ksT"][0][g], [sc["ksT"][1]], [ksT])
                k.dma(kwT[:, g, :], sc["kwT"][0][g], [sc["kwT"][1]], [kwT])
            k.dma(kcT[:], sc["kcmpT"][0], [sc["kcmpT"][1]], [kcT])
            k.dma(vca[:], sc["vcmp"][0], [sc["vcmp"][1]], [vca])
            k.memset(vsa[:], 1.0, [vsa])
            k.memset(vwa[:], 1.0, [vwa])
            for g in range(2):
                k.dma(vsa[:, :, g, 0:128], sc["vs"][0].rearrange("(c p) x -> p c x", p=128)[:, :, g * 128:(g + 1) * 128], [sc["vs"][1]], [vsa])
                k.dma(vwa[:, :, g, 0:128], sc["vw"][0].rearrange("(c p) x -> p c x", p=128)[:, :, g * 128:(g + 1) * 128], [sc["vw"][1]], [vwa])
            k.dma(gts[:].rearrange("p t c -> p (t c)"), sc["gates"][0], [sc["gates"][1]], [gts])
            k.dma(maskc[:], I["maskc"], [], [maskc])
            k.dma(cm[:].rearrange("p a b -> p (a b)"), I["cm"], [], [cm])
            k.dma(wm[:].rearrange("p a b -> p (a b)"), I["wm"], [], [wm])
            k.dma(ex32[:].rearrange("p a b -> p (a b)"), I["ex"], [], [ex32])
            k.copy(ex16[:], ex32[:], [ex32], [ex16])
            k.dma(vm[:].rearrange("p a b -> p (a b)"), I["vm"], [], [vm])
            k.dma(fb[:].rearrange("p a b -> p (a b)"), I["fb"], [], [fb])
            k.dma(gab[:], I["gab"], [], [gab])
            O = k.sb(st, [128, 4, 1024], F32, "O")
            Osub = [Res() for _ in range(4)]
            imp = [k.sb(st, [128, 32], F32, f"imp{i}") for i in range(4)]
            e16 = [k.sb(st, [128, 512], BF16, f"e16{i}") for i in range(3)]
            p16 = [k.sb(st, [128, 512], BF16, f"p16{i}") for i in range(3)]
            mskS = k.sb(st, [128, 16, 512], BF16, "mskS")
            selT16 = k.sb(st, [32, 512], BF16, "selT16")
            sel16 = [k.sb(st, [128, 32], BF16, f"sel16{i}") for i in range(2)]
            imp2 = [k.sb(st, [128, 32], F32, f"imp2{i}") for i in range(2)]
            wk = [k.sb(st, [128, 32], F32, f"wk{i}") for i in range(2)]
            m8 = [k.sb(st, [128, 16], F32, f"m8{i}") for i in range(2)]
            den = [k.sb(st, [128, 2], F32, f"den{i}") for i in range(4)]
            ssq = k.sb(st, [128, 1], F32, "ssq")
            junk = k.sb(st, [128, 1024], F32, "junk4")
            yn16 = k.sb(st, [128, 1024], BF16, "yn16")
            yT16 = [k.sb(st, [128, 512], BF16, f"yT16{i}") for i in range(2)]
            pS = [k.ps(st, [128, 512], F32, "pS") for _ in range(2)]
            pA = k.ps(st, [128, 512], F32, "pA")
            pACC = [k.ps(st, [128, 512], F32, "pACC") for _ in range(4)]
            pM = [k.ps(st, [128, 512], F32, "pM")] * 2
            pT = pA
            cnt = {"s": 0, "e": 0, "m": 0, "d": 0, "y": 0, "sel": 0}

            def finish_head(sub, hd, acc_ap, den_ap, tt_, gcol, first, Rp):
                dn = den[cnt["d"] % 4]
                cnt["d"] += 1
                k.ts(dn[:, 0:1], den_ap, 1e-30, None, ALU.max, None, Rp, [dn])
                k.recip(dn[:, 0:1], dn[:, 0:1], [dn], [dn])
                k.tt(dn[:, 1:2], dn[:, 0:1], gts[:, tt_, gcol:gcol + 1], ALU.mult, [dn, gts], [dn])
                osl = O[:, sub, hd * 128:(hd + 1) * 128]
                if first:
                    k.ts(osl, acc_ap, dn[:, 1:2], None, ALU.mult, None, Rp + [dn], [Osub[sub]])
                else:
                    k.stt(osl, acc_ap, dn[:, 1:2], osl, ALU.mult, ALU.add, Rp + [dn, Osub[sub]], [Osub[sub]])
                return dn

            def run_steps(i, steps):
                qsl = slice(i * 512, (i + 1) * 512)
                state = {}

                def s0(n):
                    keyT, vaug, g, r, kc, mask_of, first, last, gbranch = steps[n]
                    ps_ = pS[cnt["s"] % 2]
                    cnt["s"] += 1
                    k.mm(ps_[:], keyT[:, g, kc * 128:(kc + 1) * 128], qT[:, g * 4 + r, qsl], True, True, [keyT, qT], [ps_])
                    state[n] = ps_

                def s12(n):
                    keyT, vaug, g, r, kc, mask_of, first, last, gbranch = steps[n]
                    ps_ = state.pop(n)
                    hd = g * 4 + r
                    e = e16[cnt["e"] % 3]
                    p_ = p16[cnt["e"] % 3]
                    cnt["e"] += 1
                    k.act(e[:], ps_[:], AF.Exp, [ps_], [e], scale=SCALE)
                    mk, mkR, eng = mask_of(kc)
                    k.tt(p_[:], e[:], mk, ALU.mult, [e, mkR], [p_], eng=eng)
                    for sub in range(4):
                        acc = pACC[sub]
                        k.mm(acc[:, 0:129], p_[:, sub * 128:(sub + 1) * 128], vaug[:, kc, g, 0:129], first, last, [p_, vaug], [acc])
                    if last:
                        gcol = g * 12 + r * 3 + gbranch
                        dns = [den[sub] for sub in range(4)]
                        for sub in range(4):
                            k.ts(dns[sub][:, 0:1], pACC[sub][:, 128:129], 1e-30, None, ALU.max, None, [pACC[sub]], [dns[sub]])
                        for sub in range(4):
                            k.recip(dns[sub][:, 0:1], dns[sub][:, 0:1], [dns[sub]], [dns[sub]])
                        for sub in range(4):
                            k.tt(dns[sub][:, 1:2], dns[sub][:, 0:1], gts[:, i * 4 + sub, gcol:gcol + 1], ALU.mult, [dns[sub], gts], [dns[sub]])
                        for sub in range(4):
                            osl = O[:, sub, hd * 128:(hd + 1) * 128]
                            k.stt(osl, pACC[sub][:, 0:128], dns[sub][:, 1:2], osl, ALU.mult, ALU.add, [pACC[sub], dns[sub], Osub[sub]], [Osub[sub]])

                s0(0)
                for n in range(len(steps)):
                    if n + 1 < len(steps):
                        s0(n + 1)
                    s12(n)

            for i in range(4):
                qsl = slice(i * 512, (i + 1) * 512)
                for g in range(2):
                    for r in range(4):
                        hd = g * 4 + r
                        ps_ = pS[cnt["s"] % 2]
                        cnt["s"] += 1
                        k.mm(ps_[0:127, :], kcT[:, g, 0:127], qT[:, hd, qsl], True, True, [kcT, qT], [ps_])
                        e = e16[cnt["e"] % 3]
                        p_ = p16[cnt["e"] % 3]
                        cnt["e"] += 1
                        k.act(e[0:127, :], ps_[0:127, :], AF.Exp, [ps_], [e], scale=SCALE)
                        k.tt(p_[0:127, :], e[0:127, :], maskc[0:127, qsl], ALU.mult, [e, maskc], [p_])
                        for sub in range(4):
                            k.mm(pA[:, 0:161], p_[0:127, sub * 128:(sub + 1) * 128], vca[0:127, g, 0:161], True, True, [p_, vca], [pA])
                            dn = finish_head(sub, hd, pA[:, 0:128], pA[:, 128:129], i * 4 + sub, g * 12 + r * 3, True, [pA])
                            if r == 0:
                                k.ts(imp[sub][:], pA[:, 129:161], dn[:, 0:1], None, ALU.mult, None, [pA, dn], [imp[sub]])
                            else:
                                k.stt(imp[sub][:], pA[:, 129:161], dn[:, 0:1], imp[sub][:], ALU.mult, ALU.add, [pA, dn, imp[sub]], [imp[sub]])
                    psel = pM[cnt["m"] % 2]
                    cnt["m"] += 1
                    for sub in range(4):
                        tt_ = i * 4 + sub
                        j = cnt["sel"] % 2
                        cnt["sel"] += 1
                        k.tt(imp2[j][:], imp[sub][:], vm[:, tt_, :], ALU.mult, [imp[sub], vm], [imp2[j]])
                        k.tt(imp2[j][:], imp2[j][:], fb[:, tt_, :], ALU.add, [imp2[j], fb], [imp2[j]])
                        k.fn("dve", lambda e, o=m8[j][:, 0:8], a=imp2[j][:]: e.max(out=o, in_=a), [imp2[j]], [m8[j]])
                        k.fn("dve", lambda e, o=wk[j][:], a=m8[j][:, 0:8], b=imp2[j][:]: e.match_replace(out=o, in_to_replace=a, in_values=b, imm_value=-1e30), [imp2[j], m8[j]], [wk[j]])
                        k.fn("dve", lambda e, o=m8[j][:, 8:16], a=wk[j][:]: e.max(out=o, in_=a), [wk[j]], [m8[j]])
                        k.ts(sel16[j][:], imp2[j][:], m8[j][:, 15:16], None, ALU.is_ge, None, [imp2[j], m8[j]], [sel16[j]])
                        k.mm(psel[0:32, sub * 128:(sub + 1) * 128], sel16[j][:], self.ident16[:], True, True, [sel16[j], self.ident16], [psel])
                    k.copy(selT16[:], psel[0:32, :], [psel], [selT16], eng="act")
                    nkc = 4 * i + 4
                    for kc in range(nkc):
                        pm_ = pM[cnt["m"] % 2]
                        cnt["m"] += 1
                        k.mm(pm_[:], ex16[:, kc, :], selT16[:], True, True, [ex16, selT16], [pm_])
                        if kc >= 4 * i:
                            k.tt(mskS[:, kc, :], pm_[:], cm[:, kc - 4 * i, :], ALU.mult, [pm_, cm], [mskS])
                        else:
                            k.copy(mskS[:, kc, :], pm_[:], [pm_], [mskS], eng="act")
                    steps = []
                    for r in range(4):
                        for kc in range(nkc):
                            steps.append((ksT, vsa, g, r, kc, (lambda kc_: (mskS[:, kc_, :], mskS, "dve")), kc == 0, kc == nkc - 1, 1))
                    for r in range(4):
                        chunks = list(range(max(0, 4 * i - 4), 4 * i + 4))
                        for kc in chunks:
                            steps.append((kwT, vwa, g, r, kc, (lambda kc_, i_=i: (wm[:, kc_ - 4 * i_ + 4, :], wm, "dve")), kc == chunks[0], kc == chunks[-1], 2))
                    run_steps(i, steps)
                yt = yT16[i % 2]
                for c in range(8):
                    pass
                ytiles = []
                for sub in range(4):
                    k.act(junk[:], O[:, sub, :], AF.Square, [Osub[sub]], [junk, ssq], accum_out=ssq[:])
                    k.act(ssq[:], ssq[:], AF.Sqrt, [ssq, self.eps_t], [ssq], bias=self.eps_t[:], scale=1.0 / 1024.0)
                    k.recip(ssq[:], ssq[:], [ssq], [ssq])
                    k.stt(yn16[:], O[:, sub, :], ssq[:], gab[:], ALU.mult, ALU.mult, [Osub[sub], ssq, gab], [yn16])
                    for half in range(2):
                        for cc in range(4):
                            c = half * 4 + cc
                            k.mm(pT[:, cc * 128:(cc + 1) * 128], yn16[:, c * 128:(c + 1) * 128], self.ident16[:], True, True, [yn16, self.ident16], [pT])
                        dst = k.sb(st, [128, 4, 128], BF16, "ytmp") if False else None
                        yb = yT16[cnt["y"] % 2]
                        cnt["y"] += 1
                        k.copy(yb[:], pT[:], [pT], [yb], eng=("act" if half == 0 else "dve"))
                        for cc in range(4):
                            c = half * 4 + cc
                            k.dma(yT_d[c][:, i * 512 + sub * 128:i * 512 + (sub + 1) * 128], yb[:, cc * 128:(cc + 1) * 128], [yb], [yT_r])
            if "O_dbg" in self.dbg:
                pass

    def p5_rnn(self):
        k, I = self.k, self.I
        sc = self.scr
        yT_d, yT_r = sc["yT"]
        xr_d, xr_r = sc["xrT"]
        xg_d, xg_r = sc["xgT"]
        with ExitStack() as st:
            cw = k.sb(st, [128, 8, 4], F32, "cw")
            cb = k.sb(st, [128, 8], F32, "cb")
            nba = k.sb(st, [128, 8], F32, "nba")
            nbi = k.sb(st, [128, 8], F32, "nbi")
            lam = k.sb(st, [128, 8], F32, "lam")
            clam = k.sb(st, [128, 8], F32, "clam")
            gr = k.sb(st, [128, 8], F32, "gr")
            wst = k.sb(st, [128, 2, 128], F32, "wst5")
            w16 = [k.sb(st, [128, 2, 128], BF16, f"w165{i}") for i in range(2)]
            k.dma(cw[:].rearrange("p a b -> p (a b)"), I["convw"], [], [cw])
            k.dma(cb[:], I["convb"], [], [cb])
            k.dma(nba[:], I["lba"], [], [nba])
            k.dma(nbi[:], I["lbi"], [], [nbi])
            k.dma(lam[:], I["lam"], [], [lam])
            k.dma(gr[:], I["gr"], [], [gr])
            k.ts(nba[:], nba[:], -1.0, None, ALU.mult, None, [nba], [nba])
            k.ts(nbi[:], nbi[:], -1.0, None, ALU.mult, None, [nbi], [nbi])
            k.act(clam[:], lam[:], AF.Exp, [lam], [clam], scale=-1.0)
            k.ts(clam[:], clam[:], 1.0, None, ALU.add, None, [clam], [clam])
            k.act(clam[:], clam[:], AF.Ln, [clam], [clam])
            k.ts(clam[:], clam[:], -8.0, None, ALU.mult, None, [clam], [clam])
            orn = k.sb(st, [128, 8, S], F32, "orn")
            xp = k.sb(st, [128, S + 4], F32, "xp")
            xg = k.sb(st, [128, S], F32, "xg")
            u = k.sb(st, [128, S], F32, "u")
            u16 = k.sb(st, [128, S], BF16, "u16")
            ra = k.sb(st, [128, S], F32, "ra")
            ig = k.sb(st, [128, S], F32, "ig")
            bb = k.sb(st, [128, S], F32, "bb")
            sq16 = k.sb(st, [128, S], BF16, "sq165")
            pg = [k.ps(st, [128, 512], F32, "pg") for _ in range(2)]
            pss = [k.ps(st, [128, 512], F32, "pss5") for _ in range(4)]
            k.memset(xp[:, 0:4], 0.0, [xp])
            ci = 0
            for n in range(8):
                k.dma(xp[:, 4:S + 4], xr_d[n], [xr_r], [xp])
                k.dma(xg[:], xg_d[n], [xg_r], [xg], eng="pool")
                wb = w16[n % 2]
                k.dma(wst[:, 0, :], I["wa"][n], [], [wst])
                k.dma(wst[:, 1, :], I["wi"][n], [], [wst])
                k.copy(wb[:], wst[:], [wst], [wb], eng="pool")
                k.ts(u[:], xp[:, 1:S + 1], cw[:, n, 0:1], cb[:, n:n + 1], ALU.mult, ALU.add, [xp, cw, cb], [u])
                for i_ in range(1, 4):
                    k.stt(u[:], xp[:, 1 + i_:S + 1 + i_], cw[:, n, i_:i_ + 1], u[:], ALU.mult, ALU.add, [xp, cw, u], [u])
                k.copy(u16[:], u[:], [u], [u16], eng="act")
                for which, dst, nb in ((0, ra, nba), (1, ig, nbi)):
                    for tg in range(4):
                        p = pg[ci % 2]
                        ci += 1
                        sl = slice(tg * 512, (tg + 1) * 512)
                        k.mm(p[:], wb[:, which, :], u16[:, sl], True, True, [wb, u16], [p])
                        k.act(dst[:, sl], p[:], AF.Exp, [p, nb], [dst], bias=nb[:, n:n + 1], scale=-1.0)
                    k.ts(dst[:], dst[:], 1.0, None, ALU.add, None, [dst], [dst])
                    k.recip(dst[:], dst[:], [dst], [dst])
                k.act(ra[:], ra[:], AF.Exp, [ra, clam], [ra], scale=clam[:, n:n + 1])
                k.tt(bb[:], ra[:], ra[:], ALU.mult, [ra], [bb])
                k.ts(bb[:], bb[:], -1.0, 1.0, ALU.mult, ALU.add, [bb], [bb])
                k.act(bb[:], bb[:], AF.Sqrt, [bb], [bb])
                k.tt(bb[:], bb[:], ig[:], ALU.mult, [bb, ig], [bb])
                k.tt(bb[:], bb[:], u[:], ALU.mult, [bb, u], [bb])
                k.fn("dve", lambda e, o=ig[:], a=ra[:], b=bb[:]: e.tensor_tensor_scan(o, a, b, 0.0, ALU.mult, ALU.add), [ra, bb, ig], [ig])
                k.act(xg[:], xg[:], AF.Gelu, [xg], [xg])
                k.tt(orn[:, n, :], xg[:], ig[:], ALU.mult, [xg, ig], [orn])
                k.act(sq16[:], orn[:, n, :], AF.Square, [orn], [sq16])
                for tg in range(4):
                    k.mm(pss[tg][:], self.ones16[:], sq16[:, tg * 512:(tg + 1) * 512], n == 0, n == 7, [self.ones16, sq16], [pss[tg]])
            rstd = ra
            for tg in range(4):
                sl = slice(tg * 512, (tg + 1) * 512)
                k.act(rstd[:, sl], pss[tg][:], AF.Sqrt, [pss[tg], self.eps_t], [rstd], bias=self.eps_t[:], scale=1.0 / 1024.0)
            k.recip(rstd[:], rstd[:], [rstd], [rstd])
            for n in range(8):
                k.stt(u16[:], orn[:, n, :], gr[:, n:n + 1], rstd[:], ALU.mult, ALU.mult, [orn, gr, rstd], [u16])
                k.dma(yT_d[8 + n], u16[:], [u16], [yT_r])

    def p6_out(self):
        k, I = self.k, self.I
        sc = self.scr
        yT_d, yT_r = sc["yT"]
        x1_d, x1_r = self.scratch("x1T", [16, 128, S], F32)
        h2_d, h2_r = self.scratch("h2T", [16, 128, S], BF16)
        xTv = I["xT"].rearrange("(k p) t -> k p t", p=128)
        wv = I["w_out"].rearrange("(k p) n -> p k n", p=128)
        with ExitStack() as st:
            rstd = k.sb(st, [128, S], F32, "rstd2")
            with ExitStack() as s1:
                yT = k.sb(s1, [128, 16, S], BF16, "yTs")
                wb = [k.sb(s1, [128, 16, 256], BF16, f"wo{i}") for i in range(2)]
                stg = [k.sb(s1, [128, 16, 128], F32, f"wos{i}") for i in range(2)]
                xb = [k.sb(s1, [128, S], F32, f"x6{i}") for i in range(2)]
                ob = [k.sb(s1, [128, S], F32, f"o6{i}") for i in range(2)]
                sq = [k.sb(s1, [128, S], BF16, f"sq6{i}") for i in range(2)]
                pz = [k.ps(s1, [128, 512], F32, "pz6") for _ in range(2)]
                pss = [k.ps(s1, [128, 512], F32, "pss6") for _ in range(4)]
                for c in range(16):
                    k.dma(yT[:, c, :], yT_d[c], [yT_r], [yT], eng=("sp" if c % 2 else "pool"))
                ci = 0
                for jg in range(8):
                    b = wb[jg % 2]
                    for h in range(2):
                        sg = stg[(jg * 2 + h) % 2]
                        k.dma(sg[:], wv[:, :, jg * 256 + h * 128:jg * 256 + (h + 1) * 128], [], [sg])
                        k.copy(b[:, :, h * 128:(h + 1) * 128], sg[:], [sg], [b], eng="pool")
                    for jj in range(2):
                        j = jg * 2 + jj
                        xt, ot, sqt = xb[j % 2], ob[j % 2], sq[j % 2]
                        k.dma(xt[:], xTv[j], [], [xt])
                        for tg in range(4):
                            p = pz[ci % 2]
                            ci += 1
                            sl = slice(tg * 512, (tg + 1) * 512)
                            for c in range(16):
                                k.mm(p[:], b[:, c, jj * 128:(jj + 1) * 128], yT[:, c, sl], c == 0, c == 15, [b, yT], [p])
                            k.stt(ot[:, sl], p[:], self.modT[:, 32 + j:33 + j], xt[:, sl], ALU.mult, ALU.add, [p, self.modT, xt], [ot])
                        k.dma(x1_d[j], ot[:], [ot], [x1_r])
                        k.act(sqt[:], ot[:], AF.Square, [ot], [sqt])
                        for tg in range(4):
                            k.mm(pss[tg][:], self.ones16[:], sqt[:, tg * 512:(tg + 1) * 512], j == 0, j == 15, [self.ones16, sqt], [pss[tg]])
                for tg in range(4):
                    sl = slice(tg * 512, (tg + 1) * 512)
                    k.act(rstd[:, sl], pss[tg][:], AF.Sqrt, [pss[tg], self.eps_t], [rstd], bias=self.eps_t[:], scale=1.0 / float(D))
                k.recip(rstd[:], rstd[:], [rstd], [rstd])
            k.barrier()
            with ExitStack() as s2:
                xb = [k.sb(s2, [128, S], F32, f"x6b{i}") for i in range(2)]
                tp = [k.sb(s2, [128, S], F32, f"t6b{i}") for i in range(2)]
                hb = [k.sb(s2, [128, S], BF16, f"h6b{i}") for i in range(2)]
                for j in range(16):
                    xt, tt_, ht = xb[j % 2], tp[j % 2], hb[j % 2]
                    k.dma(xt[:], x1_d[j], [x1_r], [xt])
                    k.stt(tt_[:], xt[:], self.G2[:, j:j + 1], rstd[:], ALU.mult, ALU.mult, [xt, self.G2, rstd], [tt_])
                    k.act(ht[:], tt_[:], AF.Identity, [tt_, self.modT], [ht], bias=self.modT[:, 48 + j:49 + j], scale=1.0)
                    k.dma(h2_d[j], ht[:], [ht], [h2_r])

    def p7_peer(self):
        k, I = self.k, self.I
        sc = self.scr
        h2_d, h2_r = sc["h2T"]
        x1_d, x1_r = sc["x1T"]
        qp_d, qp_r = self.scratch("qpT", [16, 128, S], BF16)
        wv = I["wq"].rearrange("(k p) n -> p k n", p=128)
        with ExitStack() as st:
            h2T = k.sb(st, [128, 16, S], BF16, "h2Ts")
            wb = [k.sb(st, [128, 16, 256], BF16, f"wq{i}") for i in range(2)]
            stg = [k.sb(st, [128, 16, 128], F32, f"wqs{i}") for i in range(2)]
            ob = [k.sb(st, [128, S], BF16, f"oq{i}") for i in range(2)]
            pz = [k.ps(st, [128, 512], F32, "pz7") for _ in range(2)]
            for c in range(16):
                k.dma(h2T[:, c, :], h2_d[c], [h2_r], [h2T], eng=("sp" if c % 2 else "pool"))
            ci = 0
            for jg in range(8):
                b = wb[jg % 2]
                for h in range(2):
                    sg = stg[(jg * 2 + h) % 2]
                    k.dma(sg[:], wv[:, :, jg * 256 + h * 128:jg * 256 + (h + 1) * 128], [], [sg])
                    k.copy(b[:, :, h * 128:(h + 1) * 128], sg[:], [sg], [b], eng="pool")
                for jj in range(2):
                    j = jg * 2 + jj
                    ot = ob[j % 2]
                    for tg in range(4):
                        p = pz[ci % 2]
                        ci += 1
                        sl = slice(tg * 512, (tg + 1) * 512)
                        for c in range(16):
                            k.mm(p[:], b[:, c, jj * 128:(jj + 1) * 128], h2T[:, c, sl], c == 0, c == 15, [b, h2T], [p])
                        k.copy(ot[:, sl], p[:], [p], [ot], eng=("act" if tg % 2 == 0 else "dve"))
                    k.dma(qp_d[j], ot[:], [ot], [qp_r])
        k.barrier()
        SLACK = 1.0 - 4e-6
        with ExitStack() as st:
            h2s = k.sb(st, [128, 16, 512], BF16, "h2s")
            E1s = k.sb(st, [128, 4, 8, 128], F32, "E1s")
            E2s = k.sb(st, [128, 4, 8, 128], F32, "E2s")
            dg16 = k.sb(st, [128, 4, 8, 128], BF16, "dg16")
            accT = k.sb(st, [128, 16, 512], F32, "accT")
            keys16 = k.sb(st, [128, 16, 128], BF16, "keys16")
            with ExitStack() as s0:
                k32 = k.sb(s0, [128, 16, 128], F32, "k32")
                k.dma(k32[:].rearrange("p a b -> p (a b)"), I["keysT"], [], [k32])
                k.copy(keys16[:], k32[:], [k32], [keys16])
            k.barrier()
            for su in range(4):
                tsl = slice(su * 512, (su + 1) * 512)
                k.dma(h2s[:], h2_d.rearrange("c p t -> p c t")[:, :, tsl], [h2_r], [h2s])
                with ExitStack() as sb_:
                    qps = k.sb(sb_, [128, 16, 512], BF16, "qps")
                    s_sb = k.sb(sb_, [128, 16, 128], F32, "s_sb")
                    wk = k.sb(sb_, [128, 16, 128], F32, "wk7")
                    wk2 = k.sb(sb_, [128, 8, 256], F32, "wk72")
                    v16R = [Res() for _ in range(16)]
                    wkR = [Res() for _ in range(16)]
                    c16R = [Res() for _ in range(8)]
                    wk2R = [Res() for _ in range(8)]
                    v16 = k.sb(sb_, [128, 16, 16], F32, "v16")
                    cand = k.sb(sb_, [128, 8, 256], F32, "cand")
                    c16 = k.sb(sb_, [128, 8, 16], F32, "c16")
                    en = k.sb(sb_, [128, 8, 16], F32, "en")
                    negm = k.sb(sb_, [128, 16], F32, "negm")
                    negM = k.sb(sb_, [128, 8], F32, "negM")
                    Z = k.sb(sb_, [128, 8], F32, "Z")
                    th = k.sb(sb_, [128, 8], F32, "th")
                    cf = k.sb(sb_, [128, 8], F32, "cf")
                    E1t = k.sb(sb_, [128, 8, 128], F32, "E1t")
                    pS_ = [k.ps(sb_, [128, 4, 128], F32, "pS7") for _ in range(2)]
                    k.dma(qps[:], qp_d.rearrange("c p t -> p c t")[:, :, tsl], [qp_r], [qps])
                    for tl in range(4):
                        for hg in range(4):
                            p = pS_[hg % 2]
                            for q4 in range(4):
                                hp = hg * 4 + q4
                                k.mm(p[:, q4, :], qps[:, hp, tl * 128:(tl + 1) * 128], keys16[:, hp, :], True, True, [qps, keys16], [p])
                            k.copy(s_sb[:, hg * 4:(hg + 1) * 4, :], p[:], [p], [s_sb], eng=("act" if hg % 2 == 0 else "dve"))
                        for hp in range(16):
                            k.fn("dve", lambda e, o=v16[:, hp, 0:8], a=s_sb[:, hp, :]: e.max(out=o, in_=a), [s_sb], [v16R[hp]])
                        for hp in range(16):
                            k.fn("dve", lambda e, o=wk[:, hp, :], a=v16[:, hp, 0:8], b=s_sb[:, hp, :]: e.match_replace(out=o, in_to_replace=a, in_values=b, imm_value=-1e30), [s_sb, v16R[hp]], [wkR[hp]])
                        for hp in range(16):
                            k.fn("dve", lambda e, o=v16[:, hp, 8:16], a=wk[:, hp, :]: e.max(out=o, in_=a), [wkR[hp]], [v16R[hp]])
                        v16r = v16[:].rearrange("p (h two) i -> p h two i", two=2)
                        k.tt(cand[:].rearrange("p h (i j) -> p h i j", i=16),
                             v16r[:, :, 0, :].unsqueeze(3).to_broadcast([128, 8, 16, 16]),
                             v16r[:, :, 1, :].unsqueeze(2).to_broadcast([128, 8, 16, 16]), ALU.add, v16R, [cand])
                        for h in range(8):
                            k.fn("dve", lambda e, o=c16[:, h, 0:8], a=cand[:, h, :]: e.max(out=o, in_=a), [cand], [c16R[h]])
                        for h in range(8):
                            k.fn("dve", lambda e, o=wk2[:, h, :], a=c16[:, h, 0:8], b=cand[:, h, :]: e.match_replace(out=o, in_to_replace=a, in_values=b, imm_value=-1e30), [cand, c16R[h]], [wk2R[h]])
                        for h in range(8):
                            k.fn("dve", lambda e, o=c16[:, h, 8:16], a=wk2[:, h, :]: e.max(out=o, in_=a), [wk2R[h]], [c16R[h]])
                        k.ts(negm[:], v16[:, :, 0], -1.0, None, ALU.mult, None, v16R, [negm])
                        k.ts(negM[:], c16[:, :, 0], -1.0, None, ALU.mult, None, c16R, [negM])
                        for h in range(8):
                            k.act(en[:, h, :], c16[:, h, :], AF.Exp, [c16R[h], negM], [en, Z], bias=negM[:, h:h + 1], scale=1.0, accum_out=Z[:, h:h + 1])
                        k.recip(Z[:], Z[:], [Z], [Z])
                        k.tt(th[:], en[:, :, 15], Z[:], ALU.mult, [en, Z], [th])
                        k.ts(th[:], th[:], SLACK, None, ALU.mult, None, [th], [th])
                        k.ts(cf[:], en[:, :, 15], SLACK, None, ALU.mult, None, [en], [cf])
                        k.recip(cf[:], cf[:], [cf], [cf])
                        for h in range(8):
                            k.act(E2s[:, tl, h, :], s_sb[:, 2 * h + 1, :], AF.Exp, [s_sb, negm], [E2s], bias=negm[:, 2 * h + 1:2 * h + 2], scale=1.0)
                            k.act(E1t[:, h, :], s_sb[:, 2 * h, :], AF.Exp, [s_sb, negm], [E1t], bias=negm[:, 2 * h:2 * h + 1], scale=1.0)
                            k.ts(dg16[:, tl, h, :], self.ident32[:], th[:, h:h + 1], None, ALU.mult, None, [self.ident32, th], [dg16], eng="pool")
                        k.tt(E1s[:, tl, :, :], E1t[:], cf[:].unsqueeze(2).to_broadcast([128, 8, 128]), ALU.mult, [E1t, cf], [E1s])
                k.barrier()
                if "peer_dbg" in self.dbg and su == 0:
                    for nm, tile_ in (("E1s", E1s), ("E2s", E2s)):
                        d, r = self.scratch(nm, [128, 4 * 8 * 128], F32)
                        k.dma(d, tile_[:].rearrange("p a b c -> p (a b c)"), [tile_], [r])
                with ExitStack() as sc_:
                    stgD = [k.sb(sc_, [128, 16, 128], F32, f"stgD{i}") for i in range(2)]
                    stgU = [k.sb(sc_, [128, 1024], F32, f"stgU{i}") for i in range(2)]
                    dn16 = [k.sb(sc_, [128, 16, 128], BF16, f"dn16{i}") for i in range(4)]
                    up16 = [[k.sb(sc_, [128, 2048], BF16, f"up16{j}_{i}") for i in range(4)] for j in range(2)]
                    GA16 = [k.sb(sc_, [128, 512], BF16, f"GA{i}") for i in range(2)]
                    Pt = [k.sb(sc_, [128, 8, 128], F32, f"Pt{i}") for i in range(4)]
                    mE = [k.sb(sc_, [128, 8, 128], BF16, f"mE{i}") for i in range(4)]
                    WA = [k.sb(sc_, [128, 512], BF16, f"WA{i}") for i in range(8)]
                    ev = [k.sb(sc_, [128, 512], F32, f"ev{i}") for i in range(2)]
                    pact = [k.ps(sc_, [128, 512], F32, "pact") for _ in range(2)]
                    pw = [k.ps(sc_, [128, 512], F32, "pw") for _ in range(2)]
                    po = [k.ps(sc_, [128, 512], F32, "po") for _ in range(2)]
                    cn = {"k": 0, "p": 0, "o": 0}
                    ngrp = 1 if "peer_short" in self.dbg else 32
                    pend_wa = []
                    pend_po = []

                    def flush_wa():
                        while pend_wa:
                            wa_, pw__, ga_ = pend_wa.pop(0)
                            k.tt(wa_[:], pw__[:], ga_[:], ALU.mult, [pw__, ga_], [wa_])

                    def drain_po(ndc):
                        while pend_po and ndc != 0:
                            gi_, was_, dcs = pend_po[0]
                            dc = dcs.pop(0)
                            po_ = po[cn["o"] % 2]
                            cn["o"] += 1
                            for kl in range(4):
                                k.mm(po_[:], up16[gi_ % 2][kl][:, dc * 128:(dc + 1) * 128], was_[kl][:], kl == 0, kl == 3, [up16[gi_ % 2][kl], was_[kl]], [po_])
                            if gi_ == 0:
                                k.copy(accT[:, dc, :], po_[:], [po_], [accR[dc]], eng="act")
                            else:
                                pend_add.append((dc, po_))
                            if not dcs:
                                pend_po.pop(0)
                            ndc -= 1
                            if len(pend_add) >= 2 and ndc != 0:
                                flush_add()

                    accR = [Res() for _ in range(16)]
                    pend_add = []

                    def flush_add():
                        while pend_add:
                            dc, po_ = pend_add.pop(0)
                            k.tt(accT[:, dc, :], accT[:, dc, :], po_[:], ALU.add, [accR[dc], po_], [accR[dc]])
                    nk = ngrp * 4
                    seq = [(kap, tl) for kap in range(nk) for tl in range(4)]
                    LA = 2

                    def load_dn(kap2):
                        kl2 = kap2 % 4
                        sd = stgD[kap2 % 2]
                        k.dma(sd[:].rearrange("p a b -> p (a b)"), I["downB"][kap2 * 128:(kap2 + 1) * 128, :], [], [sd], eng="sp")
                        k.copy(dn16[kl2][:], sd[:], [sd], [dn16[kl2]], eng="act")

                    def load_up(kap2):
                        gi2, kl2 = kap2 // 4, kap2 % 4
                        up = up16[gi2 % 2][kl2]
                        for hf in range(2):
                            su_ = stgU[(kap2 * 2 + hf) % 2]
                            k.dma(su_[:], I["up"][kap2 * 128:(kap2 + 1) * 128, hf * 1024:(hf + 1) * 1024], [], [su_], eng="sp")
                            k.copy(up[:, hf * 1024:(hf + 1) * 1024], su_[:], [su_], [up], eng="act")

                    def p1(n):
                        kap, tl = seq[n]
                        k.tt(Pt[n % 4][:], E2s[:, tl, :, :], E1s[:, tl, :, kap:kap + 1].to_broadcast([128, 8, 128]), ALU.mult, [E2s, E1s], [Pt[n % 4]])

                    for n in range(min(LA, len(seq))):
                        p1(n)
                    cur = {}
                    for n, (kap, tl) in enumerate(seq):
                        gi, kl = kap // 4, kap % 4
                        if tl == 0:
                            if kl == 0:
                                cur["was"] = []
                                if gi == 0:
                                    for kl2 in range(4):
                                        load_dn(kl2)
                            dn = dn16[kl]
                            pa_ = pact[cn["k"] % 2]
                            pw_ = pw[cn["k"] % 2]
                            ga = GA16[cn["k"] % 2]
                            wa = WA[cn["k"] % 8]
                            cn["k"] += 1
                            cur.update(pw=pw_, ga=ga, wa=wa)
                            for c in range(16):
                                k.mm(pa_[:], dn[:, c, :], h2s[:, c, :], c == 0, c == 15, [dn, h2s], [pa_])
                            k.act(ga[:], pa_[:], AF.Gelu, [pa_], [ga])
                            if kap + 4 < nk:
                                load_dn(kap + 4)
                            load_up(kap)
                        if n + LA < len(seq):
                            p1(n + LA)
                        me = mE[n % 4]
                        k.stt(me[:], Pt[n % 4][:], 1.0, Pt[n % 4][:], ALU.is_ge, ALU.mult, [Pt[n % 4]], [me])
                        flush_add()
                        if tl == 1:
                            flush_wa()
                        pw_ = cur["pw"]
                        for h in range(8):
                            k.mm(pw_[:, tl * 128:(tl + 1) * 128], me[:, h, :], dg16[:, tl, h, :], h == 0, h == 7, [me, dg16], [pw_])
                        if tl in (1, 3):
                            drain_po(2)
                        if tl == 3:
                            pend_wa.append((cur["wa"], pw_, cur["ga"]))
                            cur["was"].append(cur["wa"])
                            if kl == 3:
                                assert not pend_po
                                pend_po.append((gi, cur["was"], list(range(16))))
                    flush_add()
                    flush_wa()
                    drain_po(-1)
                    flush_add()
                    for dc in range(16):
                        x_ = ev[dc % 2]
                        k.dma(x_[:], x1_d[dc][:, tsl], [x1_r], [x_])
                        k.stt(x_[:], accT[:, dc, :], self.modT[:, 80 + dc:81 + dc], x_[:], ALU.mult, ALU.add, [accR[dc], self.modT, x_], [x_])
                        k.dma(self.outT[dc * 128:(dc + 1) * 128, tsl], x_[:], [x_], [])
                k.barrier()


def _consts():
    f = np.float32
    half = 64
    freqs = (10000.0 ** (-np.arange(half, dtype=f) / f(half))).astype(f)
    ang = np.arange(S, dtype=f)[:, None] * freqs[None, :]
    cos = np.cos(ang).astype(f).T
    sin = np.sin(ang).astype(f).T
    cosT = np.concatenate([cos, cos], 0)
    sinT = np.concatenate([sin, sin], 0)
    rotm = np.zeros((128, 128), f)
    for m in range(64):
        rotm[m + 64, m] = -1.0
        rotm[m, m + 64] = 1.0
    ident = np.eye(128, dtype=f)
    t = np.arange(S)
    cst = np.arange(NCMP) * 16
    maskc = np.zeros((128, S), f)
    maskc[:NCMP] = ((cst + 31)[:, None] <= t[None, :]).astype(f)
    kk = np.arange(128)[:, None]
    tt = np.arange(512)[None, :]
    cm = np.stack([(128 * o + kk <= tt).astype(f) for o in range(4)], 1).reshape(128, 4 * 512)
    wm = np.stack([(((128 * rel + kk - tt) <= 0) & ((128 * rel + kk - tt) > -512)).astype(f)
                   for rel in range(-4, 4)], 1).reshape(128, 8 * 512)
    ex = np.zeros((32, 16, 128), f)
    for kc in range(16):
        for kq in range(128):
            ex[2 * kc + kq // 64, kc, kq] = 1.0
    ex = ex.reshape(32, 16 * 128)
    jb = np.arange(32)[None, :]
    cur = (t // 64)[:, None]
    forced = (jb == 0) | (jb == cur) | (jb == cur - 1)
    valid = (jb * 64) <= t[:, None]
    vm_ = (valid & ~forced).astype(f)
    fb_ = np.where(forced, 1e4, np.where(valid, 0.0, -1e4)).astype(f)
    vm = vm_.reshape(16, 128, 32).transpose(1, 0, 2).reshape(128, 512)
    fb = fb_.reshape(16, 128, 32).transpose(1, 0, 2).reshape(128, 512)
    sst = np.arange(32) * 64
    ov = np.maximum(np.minimum(cst[:, None] + 32, sst[None, :] + 64) - np.maximum(cst[:, None], sst[None, :]), 0)
    ovl = np.zeros((128, 32), f)
    ovl[:NCMP] = ov.astype(f) / 32.0
    return dict(cosT=cosT, sinT=sinT, rotm=rotm, ident=ident, maskc=maskc, cm=cm, wm=wm, ex=ex, vm=vm, fb=fb, ovl=ovl)


def prep_shared(inp):
    f = np.float32
    A = lambda v: np.ascontiguousarray(np.asarray(v, dtype=f))
    sh = {}
    sh["ada_w"] = A(inp["ada_w"][0])
    sh["ada_bT"] = A(inp["ada_b"][0].reshape(96, 128).T)
    sh["g1T"] = A(inp["norm_mix_g"][0].reshape(16, 128).T)
    sh["g2T"] = A(inp["norm_ffn_g"][0].reshape(16, 128).T)
    sh["w_in"] = A(inp["w_in"][0])
    sh["w_out"] = A(inp["w_out"][0])
    sh["wq"] = A(inp["peer_wq"][0])
    sh["qg"] = A(inp["q_norm_g"][0].reshape(128, 1))
    sh["kgT"] = A(inp["k_norm_g"][0].T)
    sh["kg0b"] = A(np.broadcast_to(inp["k_norm_g"][0, 0][None, :], (128, 128)))
    sh["pekT"] = A(inp["cmp_pe_k"][0].T)
    sh["pevT"] = A(inp["cmp_pe_v"][0].T)
    sh["cwk"] = A(inp["cmp_w_k"][0])
    sh["cwv"] = A(inp["cmp_w_v"][0])
    sh["gateb"] = A(np.broadcast_to(inp["gate_b"][0][None, :], (128, 24)))
    sh["convw"] = A(inp["conv_w"][0].reshape(4, 8, 128).transpose(2, 1, 0).reshape(128, 32))
    for nm, key in (("convb", "conv_b"), ("lba", "lru_ba"), ("lbi", "lru_bi"), ("lam", "lru_lam"), ("gr", "out_g_rnn")):
        sh[nm] = A(inp[key][0].reshape(8, 128).T)
    sh["gab"] = A(np.broadcast_to(inp["out_g_attn"][0][None, :], (128, 1024)))
    sh["wa"] = A(inp["lru_wa"][0])
    sh["wi"] = A(inp["lru_wi"][0])
    sh["keysT"] = A(inp["peer_keys"][0].transpose(3, 0, 1, 2).reshape(128, 16 * 128))
    sh["downB"] = A(inp["peer_down"][0].reshape(128, 128, 16, 128).transpose(0, 3, 2, 1).reshape(128 * 128, 16 * 128))
    sh["up"] = A(inp["peer_up"][0])
    sh.update(_consts())
    return sh


def prep_core(inp, b):
    f = np.float32
    return {"xT": np.ascontiguousarray(np.asarray(inp["x"][b], dtype=f).T),
            "c_col": np.ascontiguousarray(np.asarray(inp["c"][b], dtype=f).reshape(16, 128).T)}


def kernel(**inputs):
    inp = {k_: np.asarray(v) for k_, v in inputs.items()}
    sh = prep_shared(inp)
    nc = Prog().build()
    in_maps = []
    for b in range(8):
        m = dict(sh)
        m.update(prep_core(inp, b))
        in_maps.append(m)
    res = run_bass_kernel_spmd(nc, in_maps, core_ids=list(range(8)))
    out = np.stack([np.asarray(r["outT"]).T for r in res.results], 0)
    return np.ascontiguousarray(out.astype(np.float32))
```
